# Optimizing a Trainium2 kernel written in Bass

```python
import jax, jax.numpy as jnp
from jax import lax
import numpy as np

D_MODEL = 1024
BATCH = 2
SEQ = 8192
DEPTH = 2

N_MIXERS = 2
GRID_W = 64
EPS = 1e-6

ATT_HEADS = 8
ATT_KV_HEADS = 2
ATT_HEAD_DIM = 128
ATT_GROUP = ATT_HEADS // ATT_KV_HEADS
ATT_Q_BLOCK = 128
ROPE_THETA = 10000.0
ATT_Q_W = ATT_HEADS * ATT_HEAD_DIM
ATT_KV_W = ATT_KV_HEADS * ATT_HEAD_DIM
ATT_IN_W = 2 * ATT_Q_W + 2 * ATT_KV_W

GDN_QK_HEADS = 8
GDN_V_HEADS = 16
GDN_DK = 128
GDN_DV = 128
GDN_CONV = 5
GDN_CHUNK = 64
GDN_QK_W = GDN_QK_HEADS * GDN_DK
GDN_V_W = GDN_V_HEADS * GDN_DV
GDN_CONV_W = 2 * GDN_QK_W + GDN_V_W
GDN_IN_W = GDN_CONV_W + GDN_V_W + 4 * GDN_V_HEADS

D_FF = 2816
FFN_CONV = 3

kernel_name = "hybrid_axial_gqa_gated_deltanet_convffn_encoder"


def rmsnorm(x, g):
    xf = x.astype(jnp.float32)
    y = xf * lax.rsqrt(jnp.mean(xf * xf, axis=-1, keepdims=True) + EPS)
    return (y * g.astype(jnp.float32)).astype(x.dtype)


def l2norm(x):
    return x * lax.rsqrt(jnp.sum(x * x, axis=-1, keepdims=True) + EPS)


def depthwise_conv_centred(x, w, b):
    K = w.shape[0]
    S = x.shape[1]
    pad = K // 2
    xp = jnp.pad(x, ((0, 0), (pad, pad), (0, 0)))
    y = xp[:, 0:S] * w[0]
    for t in range(1, K):
        y = y + xp[:, t:t + S] * w[t]
    return y + b


def axial_rope_tables(S):
    rows = S // GRID_W
    row = jnp.repeat(jnp.arange(rows), GRID_W).astype(jnp.float32)
    col = jnp.tile(jnp.arange(GRID_W), rows).astype(jnp.float32)
    half = ATT_HEAD_DIM // 2
    n_freq = half // 2
    inv_freq = ROPE_THETA ** (-(jnp.arange(n_freq, dtype=jnp.float32) * 2.0 / half))
    ang_r = row[:, None] * inv_freq[None, :]
    ang_c = col[:, None] * inv_freq[None, :]
    return (jnp.cos(ang_r), jnp.sin(ang_r), jnp.cos(ang_c), jnp.sin(ang_c))


def _rotate(z, cos, sin):
    n = cos.shape[-1]
    z1, z2 = z[..., :n], z[..., n:]
    c = cos[None, :, None, :]
    s = sin[None, :, None, :]
    return jnp.concatenate([z1 * c - z2 * s, z1 * s + z2 * c], axis=-1)


def apply_axial_rope(x, rope):
    cos_r, sin_r, cos_c, sin_c = rope
    half = ATT_HEAD_DIM // 2
    return jnp.concatenate([_rotate(x[..., :half], cos_r, sin_r),
                            _rotate(x[..., half:], cos_c, sin_c)], axis=-1)


def attention_mixer(h, w_in, q_norm, k_norm, w_out, rope):
    B, S, _ = h.shape
    proj = h @ w_in
    q, k, v, gate = jnp.split(proj, [ATT_Q_W, ATT_Q_W + ATT_KV_W, ATT_Q_W + 2 * ATT_KV_W], axis=-1)
    q = q.reshape(B, S, ATT_HEADS, ATT_HEAD_DIM)
    k = k.reshape(B, S, ATT_KV_HEADS, ATT_HEAD_DIM)
    v = v.reshape(B, S, ATT_KV_HEADS, ATT_HEAD_DIM).astype(jnp.float32)
    q = apply_axial_rope(rmsnorm(q, q_norm).astype(jnp.float32), rope) * (ATT_HEAD_DIM ** -0.5)
    k = apply_axial_rope(rmsnorm(k, k_norm).astype(jnp.float32), rope)
    nb = S // ATT_Q_BLOCK
    qb = q.reshape(B, nb, ATT_Q_BLOCK, ATT_KV_HEADS, ATT_GROUP, ATT_HEAD_DIM)
    qb = jnp.moveaxis(qb, 1, 0)

    def block(q_blk):
        s = jnp.einsum('bqkgd,bskd->bkgqs', q_blk, k)
        p = jax.nn.softmax(s, axis=-1)
        return jnp.einsum('bkgqs,bskd->bqkgd', p, v)

    o = lax.map(block, qb)
    o = jnp.moveaxis(o, 0, 1).reshape(B, S, ATT_Q_W)
    o = o * jax.nn.sigmoid(gate.astype(jnp.float32))
    return o.astype(h.dtype) @ w_out


def chunk_gated_delta_rule(q, k, v, g, beta):
    B, S, H, dk = q.shape
    dv = v.shape[-1]
    C = GDN_CHUNK
    N = S // C

    def chunks(t):
        t = t.reshape((B, N, C, H) + t.shape[3:])
        return jnp.moveaxis(t, 3, 1)

    q, k, v, g, beta = chunks(q), chunks(k), chunks(v), chunks(g), chunks(beta)
    gc = jnp.cumsum(g, axis=-1)
    idx = jnp.arange(C)
    incl = idx[:, None] >= idx[None, :]
    strict = idx[:, None] > idx[None, :]
    decay = jnp.exp(jnp.where(incl, gc[..., :, None] - gc[..., None, :], -jnp.inf))
    kb = k * beta[..., None]
    L = jnp.einsum('bhncd,bhnsd->bhncs', kb, k) * decay * strict.astype(jnp.float32)
    unit_lower = L + jnp.eye(C, dtype=jnp.float32)
    rhs = jnp.concatenate([v * beta[..., None], kb * jnp.exp(gc)[..., None]], axis=-1)
    sol = lax.linalg.triangular_solve(unit_lower, rhs, left_side=True, lower=True,
                                      unit_diagonal=True)
    u, w = sol[..., :dv], sol[..., dv:]
    attn = jnp.einsum('bhncd,bhnsd->bhncs', q, k) * decay
    q_dec = q * jnp.exp(gc)[..., None]
    k_dec = k * jnp.exp(gc[..., -1:] - gc)[..., None]
    g_last = jnp.exp(gc[..., -1])

    xs = tuple(jnp.moveaxis(t, 2, 0) for t in (u, w, attn, q_dec, k_dec, g_last))

    def step(state, inp):
        u_i, w_i, a_i, qd_i, kd_i, gl_i = inp
        v_new = u_i - jnp.einsum('bhck,bhkv->bhcv', w_i, state)
        o_i = jnp.einsum('bhck,bhkv->bhcv', qd_i, state) + jnp.einsum('bhcs,bhsv->bhcv', a_i, v_new)
        state = state * gl_i[..., None, None] + jnp.einsum('bhck,bhcv->bhkv', kd_i, v_new)
        return state, o_i

    state0 = jnp.zeros((B, H, dk, dv), jnp.float32)
    _, o = lax.scan(step, state0, xs)
    return jnp.transpose(o, (1, 0, 3, 2, 4)).reshape(B, S, H, dv)


def gated_deltanet_mixer(h, w_in, conv_w, conv_b, a_log, dt_bias, o_norm, w_out):
    B, S, _ = h.shape
    proj = h @ w_in
    qkv, z, ab = jnp.split(proj, [GDN_CONV_W, GDN_CONV_W + GDN_V_W], axis=-1)
    qkv = jax.nn.silu(depthwise_conv_centred(qkv, conv_w, conv_b)).astype(jnp.float32)
    q, k, v = jnp.split(qkv, [GDN_QK_W, 2 * GDN_QK_W], axis=-1)
    rep = GDN_V_HEADS // GDN_QK_HEADS
    q = jnp.repeat(l2norm(q.reshape(B, S, GDN_QK_HEADS, GDN_DK)) * (GDN_DK ** -0.5), rep, axis=2)
    k = jnp.repeat(l2norm(k.reshape(B, S, GDN_QK_HEADS, GDN_DK)), rep, axis=2)
    v = v.reshape(B, S, GDN_V_HEADS, GDN_DV)
    ab = ab.astype(jnp.float32).reshape(B, S, 2, 2, GDN_V_HEADS)
    a, b = ab[:, :, :, 0], ab[:, :, :, 1]
    g = -jnp.exp(a_log.astype(jnp.float32)) * jax.nn.softplus(a + dt_bias.astype(jnp.float32))
    beta = jax.nn.sigmoid(b)
    o_fwd = chunk_gated_delta_rule(q, k, v, g[:, :, 0], beta[:, :, 0])
    flip = lambda t: jnp.flip(t, axis=1)
    o_bwd = flip(chunk_gated_delta_rule(flip(q), flip(k), flip(v),
                                        flip(g[:, :, 1]), flip(beta[:, :, 1])))
    o = rmsnorm(o_fwd + o_bwd, o_norm) * jax.nn.silu(
        z.astype(jnp.float32).reshape(B, S, GDN_V_HEADS, GDN_DV))
    return o.reshape(B, S, GDN_V_W).astype(h.dtype) @ w_out


def conv_ffn(h, w_up, conv_w, conv_b, w_down):
    u = depthwise_conv_centred(h @ w_up, conv_w, conv_b)
    a, b = jnp.split(u, 2, axis=-1)
    return (jax.nn.silu(a) * b) @ w_down


def setup_inputs(seed: int = 0) -> dict:
    key = jax.random.key(seed)
    ks = jax.random.split(key, 24)
    n_attn = (DEPTH + N_MIXERS - 1) // N_MIXERS
    n_gdn = DEPTH // N_MIXERS
    f32 = jnp.float32

    def w(k, shape, fan_in):
        return jax.random.normal(k, shape, f32) * (fan_in ** -0.5)

    def gain(k, shape):
        return 1.0 + 0.02 * jax.random.normal(k, shape, f32)

    a_log = jnp.log(jax.random.uniform(ks[12], (n_gdn, 2, GDN_V_HEADS), f32, 1.0, 16.0))
    dt = jnp.exp(jax.random.uniform(ks[13], (n_gdn, 2, GDN_V_HEADS), f32,
                                    float(np.log(1e-3)), float(np.log(1e-1))))
    dt_bias = dt + jnp.log(-jnp.expm1(-dt))
    return {
        "x": jax.random.normal(ks[0], (BATCH, SEQ, D_MODEL), f32),
        "norm_mix": gain(ks[1], (DEPTH, D_MODEL)),
        "norm_ffn": gain(ks[2], (DEPTH, D_MODEL)),
        "norm_final": gain(ks[3], (D_MODEL,)),
        "attn_w_in": w(ks[4], (n_attn, D_MODEL, ATT_IN_W), D_MODEL),
        "attn_q_norm": gain(ks[5], (n_attn, ATT_HEAD_DIM)),
        "attn_k_norm": gain(ks[6], (n_attn, ATT_HEAD_DIM)),
        "attn_w_out": w(ks[7], (n_attn, ATT_Q_W, D_MODEL), ATT_Q_W),
        "gdn_w_in": w(ks[8], (n_gdn, D_MODEL, GDN_IN_W), D_MODEL),
        "gdn_conv_w": w(ks[9], (n_gdn, GDN_CONV, GDN_CONV_W), GDN_CONV),
        "gdn_conv_b": 0.02 * jax.random.normal(ks[10], (n_gdn, GDN_CONV_W), f32),
        "gdn_a_log": a_log,
        "gdn_dt_bias": dt_bias,
        "gdn_o_norm": gain(ks[14], (n_gdn, GDN_DV)),
        "gdn_w_out": w(ks[15], (n_gdn, GDN_V_W, D_MODEL), GDN_V_W),
        "ffn_w_up": w(ks[16], (DEPTH, D_MODEL, 2 * D_FF), D_MODEL),
        "ffn_conv_w": w(ks[17], (DEPTH, FFN_CONV, 2 * D_FF), FFN_CONV),
        "ffn_conv_b": 0.02 * jax.random.normal(ks[18], (DEPTH, 2 * D_FF), f32),
        "ffn_w_down": w(ks[19], (DEPTH, D_FF, D_MODEL), D_FF),
    }


def reference(x, norm_mix, norm_ffn, norm_final, attn_w_in, attn_q_norm, attn_k_norm,
              attn_w_out, gdn_w_in, gdn_conv_w, gdn_conv_b, gdn_a_log, gdn_dt_bias,
              gdn_o_norm, gdn_w_out, ffn_w_up, ffn_conv_w, ffn_conv_b, ffn_w_down):
    S = x.shape[1]
    rope = axial_rope_tables(S)
    h = x
    for i in range(DEPTH):
        j = i // N_MIXERS
        y = rmsnorm(h, norm_mix[i])
        if i % N_MIXERS == 0:
            h = h + attention_mixer(y, attn_w_in[j], attn_q_norm[j], attn_k_norm[j],
                                    attn_w_out[j], rope)
        else:
            h = h + gated_deltanet_mixer(y, gdn_w_in[j], gdn_conv_w[j], gdn_conv_b[j],
                                         gdn_a_log[j], gdn_dt_bias[j], gdn_o_norm[j],
                                         gdn_w_out[j])
        h = h + conv_ffn(rmsnorm(h, norm_ffn[i]), ffn_w_up[i], ffn_conv_w[i],
                         ffn_conv_b[i], ffn_w_down[i])
    return rmsnorm(h, norm_final)
```

```python
import contextlib
import numpy as np
import concourse.bass as bass
import concourse.mybir as mybir
from concourse.bass_utils import run_bass_kernel_spmd

F32 = mybir.dt.float32
BF16 = mybir.dt.bfloat16
AF = mybir.ActivationFunctionType
ALU = mybir.AluOpType
AX = mybir.AxisListType

D_MODEL = 1024
BATCH = 2
SEQ = 8192
EPS = 1e-6
D_FF = 2816
NCORES = 8
TOK = 2048

COMPUTE = ("tensor", "vector", "scalar", "gpsimd")
N_DMA_SEMS = 24


class Prog:
    def __init__(self, nc, same_engine_sync=True):
        self.nc = nc
        self.ops = []
        self.last_w = {}
        self.readers = {}
        self.same_engine_sync = same_engine_sync
        self.n_dma = 0
        self.dma_sem_last = {}
        self.fence_deps = set()

    def fence(self):
        self.fence_deps = set(i for i, o in enumerate(self.ops) if not o["dma"])

    def op(self, eng, fn, reads=(), writes=(), dma=False):
        idx = len(self.ops)
        deps = set()
        for r in reads:
            if r in self.last_w:
                deps.add(self.last_w[r])
        for w in writes:
            if w in self.last_w:
                deps.add(self.last_w[w])
            for rd in self.readers.get(w, ()):
                deps.add(rd)
        sem_slot = None
        if dma:
            if eng == "gpsimd":
                self.n_dma_sw = getattr(self, "n_dma_sw", 0) + 1
                sem_slot = 16 + self.n_dma_sw % (N_DMA_SEMS - 16)
            else:
                sem_slot = self.n_dma % 16
            self.n_dma += 1
            prev = self.dma_sem_last.get(sem_slot)
            if prev is not None:
                deps.add(prev)
            self.dma_sem_last[sem_slot] = idx
        deps |= self.fence_deps
        deps.discard(idx)
        self.ops.append(dict(eng=eng, fn=fn, deps=deps, dma=dma, sem_slot=sem_slot))
        for w in writes:
            self.last_w[w] = idx
            self.readers[w] = []
        for r in reads:
            if r not in writes:
                self.readers.setdefault(r, []).append(idx)
        return idx

    def I(self, eng, method, reads, writes, *args, **kwargs):
        def fn(e, method=method, args=args, kwargs=kwargs):
            return getattr(e, method)(*args, **kwargs)
        idx = self.op(eng, fn, reads, writes)
        try:
            if method == "matmul":
                r = kwargs["rhs"]
                n = int(np.prod(r.shape[1:]))
                dur = 40 + n * (2.0 if r.dtype == F32 else 0.5)
            elif method == "transpose":
                dur = 110.0
            else:
                o = kwargs.get("out", args[0] if args else None)
                n = int(np.prod(o.shape[1:]))
                dur = {"scalar": 220 + n / 1.4, "vector": 90 + n / 0.96, "gpsimd": 160 + n / 0.45}[eng]
        except Exception:
            dur = 300.0
        self.ops[idx]["dur"] = dur
        return idx

    def dma(self, queue, out, in_, reads=(), writes=()):
        def fn(e, out=out, in_=in_):
            return e.dma_start(out=out, in_=in_)
        idx = self.op(queue, fn, reads, writes, dma=True)
        try:
            nbytes = int(np.prod(out.shape)) * (4 if out.dtype == F32 else 2)
        except Exception:
            nbytes = 1 << 16
        self.ops[idx]["dur"] = 2000 + nbytes / 150.0
        return idx

    def reorder(self, final_wait_ops):
        import heapq
        ops = self.ops
        n = len(ops)
        succ = [[] for _ in range(n)]
        indeg = [0] * n
        for i, o in enumerate(ops):
            for d in o["deps"]:
                succ[d].append(i)
            indeg[i] = len(o["deps"])
        ready_t = [0.0] * n
        heap = [(0.0, i) for i in range(n) if indeg[i] == 0]
        heapq.heapify(heap)
        eng_free = {}
        order = []
        LAT = 900.0
        while heap:
            rt, i = heapq.heappop(heap)
            o = ops[i]
            e = o["eng"]
            start = max(rt, eng_free.get(e, 0.0))
            dur = o.get("dur", 300.0)
            if o["dma"]:
                eng_free[e] = start + 60.0
                fin = start + dur
            else:
                eng_free[e] = start + dur
                fin = start + dur
            order.append(i)
            for s_ in succ[i]:
                so = ops[s_]
                lat = 0.0 if (so["eng"] == e and not o["dma"] and e == "tensor") else LAT
                t = fin + lat
                if t > ready_t[s_]:
                    ready_t[s_] = t
                indeg[s_] -= 1
                if indeg[s_] == 0:
                    heapq.heappush(heap, (max(ready_t[s_], 0.0), s_))
        assert len(order) == n
        pos = {old: new for new, old in enumerate(order)}
        new_ops = []
        for old in order:
            o = ops[old]
            o["deps"] = set(pos[d] for d in o["deps"])
            new_ops.append(o)
        self.ops = new_ops
        self.est_ns = max(eng_free.values()) if eng_free else 0
        return [pos[d] for d in final_wait_ops]

    def emit(self, final_wait_ops=(), schedule=True):
        nc = self.nc
        if schedule:
            final_wait_ops = self.reorder(list(final_wait_ops))
        ops = self.ops
        engines = ("sync",) + COMPUTE
        waited_eng = {e: {p: -1 for p in COMPUTE} for e in engines}
        waited_dma = {e: set() for e in engines}
        for i, o in enumerate(ops):
            e = o["eng"]
            need_eng = {}
            need_dma = []
            for d in sorted(o["deps"]):
                po = ops[d]
                if po["dma"]:
                    if d not in waited_dma[e]:
                        need_dma.append(d)
                        waited_dma[e].add(d)
                else:
                    pe = po["eng"]
                    if pe == e and (pe == "tensor" or not self.same_engine_sync):
                        continue
                    if d > waited_eng[e][pe]:
                        need_eng[pe] = max(need_eng.get(pe, -1), d)
            for pe, d in need_eng.items():
                waited_eng[e][pe] = d
            o["waits"] = list(need_eng.values()) + need_dma
        final_e = "sync"
        fw = []
        for d in final_wait_ops:
            fw.append(d)
        signal = set()
        for o in ops:
            for d in o["waits"]:
                signal.add(d)
        for d in fw:
            signal.add(d)
        cnt = {e: 0 for e in COMPUTE}
        dma_cnt = {}
        for i, o in enumerate(ops):
            if o["dma"]:
                s = o["sem_slot"]
                dma_cnt[s] = dma_cnt.get(s, 0) + 16
                o["sig"] = ("dma", s, dma_cnt[s])
            elif i in signal:
                cnt[o["eng"]] += 1
                o["sig"] = ("eng", o["eng"], cnt[o["eng"]])
            else:
                o["sig"] = None
        self.stats = dict(n_ops=len(ops), signals=dict(cnt), n_dma=self.n_dma)
        with contextlib.ExitStack() as st:
            esem = {e: st.enter_context(nc.semaphore("s_" + e)) for e in COMPUTE}
            dsem = [st.enter_context(nc.semaphore("d_%d" % k)) for k in range(N_DMA_SEMS)]
            block = st.enter_context(nc.Block())

            def semof(sig):
                if sig[0] == "dma":
                    return dsem[sig[1]], sig[2]
                return esem[sig[1]], sig[2]

            def run(ename):
                def body(eng):
                    for i, o in enumerate(ops):
                        if o["eng"] != ename:
                            continue
                        for d in o["waits"]:
                            s, v = semof(ops[d]["sig"])
                            eng.wait_ge(s, v)
                        ins = o["fn"](eng)
                        sig = o["sig"]
                        if sig is not None:
                            s, v = semof(sig)
                            ins.then_inc(s, 16 if sig[0] == "dma" else 1)
                    if ename == final_e:
                        for d in fw:
                            s, v = semof(ops[d]["sig"])
                            eng.wait_ge(s, v)
                return body

            block.sync(run("sync"))
            block.tensor(run("tensor"))
            block.vector(run("vector"))
            block.scalar(run("scalar"))
            block.gpsimd(run("gpsimd"))


class Ctx:
    def __init__(self, nc, prog, st):
        self.nc, self.p, self.st = nc, prog, st
        self.k = 0

    def sb(self, name, shape, dt):
        return self.st.enter_context(self.nc.sbuf_tensor(name, list(shape), dt))

    def ps(self, name, shape, dt):
        return self.st.enter_context(self.nc.psum_tensor(name, list(shape), dt))


def emit_rmsnorm_T(c, src_ap, n, g_sb, dstT, col0, tag, bufs):
    p = c.p
    sq, ss, rs, yb, pT, ident = (bufs[k] for k in ("sq", "ss", "rs", "yb", "pT", "ident"))
    nm = bufs["names"]
    p.I("scalar", "activation", [tag], [nm["sq"], nm["ss"]],
        out=sq[0:n, :], in_=src_ap, func=AF.Square, accum_out=ss[0:n, :])
    p.I("scalar", "activation", [nm["ss"], "eps"], [nm["rs"]],
        out=rs[0:n, :], in_=ss[0:n, :], func=AF.Sqrt, scale=1.0 / D_MODEL, bias=bufs["eps"][0:n, :])
    p.I("vector", "reciprocal", [nm["rs"]], [nm["rs"]], out=rs[0:n, :], in_=rs[0:n, :])
    p.I("vector", "scalar_tensor_tensor", [tag, nm["rs"], "g_sb"], [nm["yb"]],
        out=yb[0:n, :], in0=src_ap, scalar=rs[0:n, :], in1=g_sb[0:n, :], op0=ALU.mult, op1=ALU.mult)
    for k in range(8):
        p.I("tensor", "transpose", [nm["yb"]], [nm["pT"]],
            out=pT[:, k, 0:n], in_=yb[0:n, k * 128:(k + 1) * 128], identity=ident[0:n, 0:n])
    p.I("vector", "tensor_copy", [nm["pT"]], [bufs["dst_name"]],
        out=dstT[:, :, col0:col0 + n], in_=pT[:, :, 0:n])


def make_ident(c, name="ident"):
    ident = c.sb(name, [128, 128], BF16)
    c.p.I("gpsimd", "memset", [], [name], ident[:], 1.0)
    c.p.I("gpsimd", "affine_select", [name], [name], out=ident[:], in_=ident[:], pattern=[[-1, 128]],
          compare_op=ALU.is_equal, fill=0.0, base=0, channel_multiplier=1)
    return ident


def ffn_blocks(ntok):
    out = []
    t = 0
    while t < ntok:
        n = min(254, ntok - t)
        out.append((t, n))
        t += n
    return out


def build_ffn_program(final_norm, pre=None):
    nc = bass.Bass("TRN2", target_bir_lowering=False)
    h_d = nc.dram_tensor("h", [TOK + 2, D_MODEL], F32, kind="ExternalInput").ap()
    g_d = nc.dram_tensor("g_ffn", [128, D_MODEL], F32, kind="ExternalInput").ap()
    gf_d = nc.dram_tensor("g_fin", [128, D_MODEL], F32, kind="ExternalInput").ap()
    wup_d = nc.dram_tensor("w_up", [D_MODEL, 2 * D_FF], F32, kind="ExternalInput").ap()
    wdn_d = nc.dram_tensor("w_down", [D_FF, D_MODEL], F32, kind="ExternalInput").ap()
    cw_d = nc.dram_tensor("cw", [128, 44, 4], F32, kind="ExternalInput").ap()
    out_d = nc.dram_tensor("out", [TOK, D_MODEL], F32, kind="ExternalOutput").ap()
    prog = Prog(nc)
    with contextlib.ExitStack() as st:
        c = Ctx(nc, prog, st)
        emit_ffn(c, h_d, g_d, gf_d, wup_d, wdn_d, cw_d, out_d, final_norm)
    return nc, prog


def emit_ffn(c, h_d, g_d, gf_d, wup_d, wdn_d, cw_d, out_d, final_norm):
    p = c.p
    ident = make_ident(c)
    wup = c.sb("wup", [128, 8, 2 * D_FF], BF16)
    wdn = c.sb("wdn", [128, 22, D_MODEL], BF16)
    cw = c.sb("cw_sb", [128, 44, 4], F32)
    g_sb = c.sb("g_sb", [128, D_MODEL], F32)
    gf_sb = c.sb("gf_sb", [128, D_MODEL], F32)
    eps = c.sb("eps", [128, 1], F32)
    p.I("vector", "memset", [], ["eps"], eps[:], EPS)
    p.fence()
    p.dma("sync", g_sb[:], g_d, writes=["g_sb"])
    p.dma("sync", gf_sb[:], gf_d, writes=["gf_sb"])
    p.dma("sync", cw[:], cw_d, writes=["cw"])
    wup_v = wup_d.rearrange("(k p) c -> p k c", p=128)
    for k in range(8):
        for hh in range(2):
            p.dma("gpsimd", wup[:, k, hh * D_FF:(hh + 1) * D_FF], wup_v[:, k, hh * D_FF:(hh + 1) * D_FF],
                  writes=["wup"])
    wdn_v = wdn_d.rearrange("(j p) c -> p j c", p=128)
    for j0 in range(0, 22, 4):
        j1 = min(22, j0 + 4)
        p.dma("gpsimd", wdn[:, j0:j1, :], wdn_v[:, j0:j1, :], writes=["wdn"])

    NB = 2
    xt = [c.sb("xt%d" % i, [128, D_MODEL], F32) for i in range(NB)]
    sq = c.sb("sq", [128, D_MODEL], F32)
    ss = c.sb("ss", [128, 1], F32)
    rs = c.sb("rs", [128, 1], F32)
    yb = c.sb("yb", [128, D_MODEL], BF16)
    yT = [c.sb("yT%d" % i, [128, 8, 256], BF16) for i in range(2)]
    NR = 3
    t1 = [c.sb("t1_%d" % i, [128, 254], F32) for i in range(2 * NR)]
    t2 = [c.sb("t2_%d" % i, [128, 254], F32) for i in range(2 * NR)]
    t3 = [c.sb("t3_%d" % i, [128, 254], F32) for i in range(2 * NR)]
    sas = [c.sb("sa%d" % i, [128, 254], F32) for i in range(NR)]
    mT = [c.sb("mT%d" % i, [128, 254], BF16) for i in range(NR)]
    hres = [c.sb("hres%d" % i, [128, D_MODEL], F32) for i in range(2)]
    hout = [c.sb("hout%d" % i, [128, D_MODEL], F32) for i in range(2)]
    pT = c.ps("pT", [128, 8, 128], BF16)
    U = [c.ps("U%d" % i, [128, 512], F32) for i in range(3)]
    Dp = [c.ps("D%d" % i, [128, 512], F32) for i in range(4)]
    nbufs = dict(sq=sq, ss=ss, rs=rs, yb=yb, pT=pT, ident=ident, eps=eps,
                 names=dict(sq="sq", ss="ss", rs="rs", yb="yb", pT="pT"))

    out_ops = []
    xi = 0
    for bi, (t0, n) in enumerate(ffn_blocks(TOK)):
        ncol = n + 2
        yTb = yT[bi % 2]
        yname = "yT%d" % (bi % 2)
        nbufs["dst_name"] = yname
        r = 0
        while r < ncol:
            rn = min(128, ncol - r)
            x = xt[xi % NB]
            xname = "xt%d" % (xi % NB)
            xi += 1
            p.dma("sync", x[0:rn, :], h_d[t0 + r:t0 + r + rn, :], writes=[xname])
            emit_rmsnorm_T(c, x[0:rn, :], rn, g_sb, yTb, r, xname, nbufs)
            r += rn
        subs = []
        s0 = 0
        while s0 < n:
            sn = min(127, n - s0)
            subs.append((s0, sn))
            s0 += sn

        def up(j):
            Uj = U[j % 3]
            un = "U%d" % (j % 3)
            for half, cj in ((0, j), (1, j + 22)):
                for k in range(8):
                    p.I("tensor", "matmul", ["wup", yname], [un],
                        Uj[:, half * 256:half * 256 + ncol], lhsT=wup[:, k, cj * 128:(cj + 1) * 128],
                        rhs=yTb[:, k, 0:ncol], start=(k == 0), stop=(k == 7))

        def ew(j):
            Uj = U[j % 3]
            un = "U%d" % (j % 3)
            b = j % NR
            for half, cj in ((0, j), (1, j + 22)):
                o = half * 256
                ti_ = half * NR + b
                tt1, tt2, tt3 = t1[ti_], t2[ti_], t3[ti_]
                p.I("scalar", "activation", ["cw"], [un, "t1_%d" % ti_],
                    out=tt1[:, 0:n], in_=Uj[:, o + 1:o + 1 + n], func=AF.Identity,
                    scale=cw[:, cj, 1:2], bias=cw[:, cj, 3:4])
                p.I("vector", "scalar_tensor_tensor", ["cw", "t1_%d" % ti_], [un, "t2_%d" % ti_],
                    out=tt2[:, 0:n], in0=Uj[:, o:o + n], scalar=cw[:, cj, 0:1], in1=tt1[:, 0:n],
                    op0=ALU.mult, op1=ALU.add)
                p.I("vector", "scalar_tensor_tensor", ["cw", "t2_%d" % ti_], [un, "t3_%d" % ti_],
                    out=tt3[:, 0:n], in0=Uj[:, o + 2:o + 2 + n], scalar=cw[:, cj, 2:3], in1=tt2[:, 0:n],
                    op0=ALU.mult, op1=ALU.add)
            p.I("scalar", "activation", ["t3_%d" % b], ["sa%d" % b], out=sas[b][:, 0:n], in_=t3[b][:, 0:n], func=AF.Silu)
            p.I("gpsimd", "tensor_tensor", ["sa%d" % b, "t3_%d" % (NR + b)], ["mT%d" % b],
                out=mT[b][:, 0:n], in0=sas[b][:, 0:n], in1=t3[NR + b][:, 0:n], op=ALU.mult)

        def down(j):
            b = j % NR
            m = mT[b]
            for si, (s0, sn) in enumerate(subs):
                for hf in range(2):
                    p.I("tensor", "matmul", ["mT%d" % b, "wdn"], ["D%d" % (si * 2 + hf)],
                        Dp[si * 2 + hf][0:sn, :], lhsT=m[:, s0:s0 + sn],
                        rhs=wdn[:, j, hf * 512:(hf + 1) * 512], start=(j == 0), stop=(j == 21))

        for si, (s0, sn) in enumerate(subs):
            p.dma("sync", hres[si][0:sn, :], h_d[1 + t0 + s0:1 + t0 + s0 + sn, :], writes=["hres%d" % si])
        up(0)
        for j in range(22):
            if j + 1 < 22:
                up(j + 1)
            ew(j)
            down(j)
        for si, (s0, sn) in enumerate(subs):
            hr = hres[si]
            ho = hout[si]
            hn = "hout%d" % si
            for hf in range(2):
                p.I("vector", "tensor_tensor", ["hres%d" % si], ["D%d" % (si * 2 + hf), hn],
                    out=ho[0:sn, hf * 512:(hf + 1) * 512], in0=Dp[si * 2 + hf][0:sn, :],
                    in1=hr[0:sn, hf * 512:(hf + 1) * 512], op=ALU.add)
            if final_norm:
                p.I("scalar", "activation", [hn], ["sq", "ss"],
                    out=sq[0:sn, :], in_=ho[0:sn, :], func=AF.Square, accum_out=ss[0:sn, :])
                p.I("scalar", "activation", ["ss", "eps"], ["rs"],
                    out=rs[0:sn, :], in_=ss[0:sn, :], func=AF.Sqrt, scale=1.0 / D_MODEL, bias=eps[0:sn, :])
                p.I("vector", "reciprocal", ["rs"], ["rs"], out=rs[0:sn, :], in_=rs[0:sn, :])
                p.I("vector", "scalar_tensor_tensor", ["rs", "gf_sb"], [hn],
                    out=ho[0:sn, :], in0=ho[0:sn, :], scalar=rs[0:sn, :], in1=gf_sb[0:sn, :],
                    op0=ALU.mult, op1=ALU.mult)
            d = p.dma("sync", out_d[t0 + s0:t0 + s0 + sn, :], ho[0:sn, :], reads=[hn])
            out_ops.append(d)
    p.emit(final_wait_ops=out_ops)


def build_attn_proj_program():
    nc = bass.Bass("TRN2", target_bir_lowering=False)
    x_d = nc.dram_tensor("x", [TOK, D_MODEL], F32, kind="ExternalInput").ap()
    g_d = nc.dram_tensor("g_mix", [128, D_MODEL], F32, kind="ExternalInput").ap()
    w_d = nc.dram_tensor("w_in", [D_MODEL, 2560], F32, kind="ExternalInput").ap()
    qg_d = nc.dram_tensor("qg", [128, 128], F32, kind="ExternalInput").ap()
    kg_d = nc.dram_tensor("kg", [128, 128], F32, kind="ExternalInput").ap()
    rope_d = nc.dram_tensor("rope", [TOK, 128], F32, kind="ExternalInput").ap()
    qT_d = nc.dram_tensor("qT", [128, 8, TOK], BF16, kind="ExternalOutput").ap()
    kT_d = nc.dram_tensor("kT", [128, 2, TOK], BF16, kind="ExternalOutput").ap()
    v_d = nc.dram_tensor("v", [128, 16, 256], BF16, kind="ExternalOutput").ap()
    gt_d = nc.dram_tensor("gate", [128, 16, 1024], BF16, kind="ExternalOutput").ap()
    prog = Prog(nc)
    p = prog
    with contextlib.ExitStack() as st:
        c = Ctx(nc, prog, st)
        ident = make_ident(c)
        w = c.sb("w", [128, 8, 2560], BF16)
        g_sb = c.sb("g_sb", [128, D_MODEL], F32)
        gains = c.sb("gains", [128, 2, 128], F32)
        eps = c.sb("eps", [128, 1], F32)
        p.I("vector", "memset", [], ["eps"], eps[:], EPS)
        p.fence()
        p.dma("sync", g_sb[:], g_d, writes=["g_sb"])
        p.dma("sync", gains[:, 0, :], qg_d, writes=["gains"])
        p.dma("sync", gains[:, 1, :], kg_d, writes=["gains"])
        p.I("scalar", "mul", ["gains"], ["gains"], out=gains[:, 0, :], in_=gains[:, 0, :], mul=128.0 ** -0.5)
        w_v = w_d.rearrange("(k p) c -> p k c", p=128)
        for k in range(8):
            p.dma("gpsimd", w[:, k, :], w_v[:, k, :], writes=["w"])
        xt = [c.sb("xt%d" % i, [128, D_MODEL], F32) for i in range(2)]
        rp = [c.sb("rp%d" % i, [128, 2, 2, 32], F32) for i in range(2)]
        sq = c.sb("sq", [128, D_MODEL], F32)
        ss = c.sb("ss", [128, 1], F32)
        rs = c.sb("rs", [128, 1], F32)
        yb = c.sb("yb", [128, D_MODEL], BF16)
        yT = [c.sb("yT%d" % i, [128, 8, 128], BF16) for i in range(2)]
        sq10 = c.sb("sq10", [128, 10, 128], F32)
        ss10 = c.sb("ss10", [128, 10], F32)
        rs10 = c.sb("rs10", [128, 10], F32)
        z0 = c.sb("z0", [128, 10, 128], F32)
        zz = c.sb("zz", [128, 10, 128], F32)
        ra = [c.sb("ra%d" % i, [128, 10, 2, 32], F32) for i in range(4)]
        zr = c.sb("zr", [128, 10, 128], BF16)
        qT_all = c.sb("qT_all", [128, 8, TOK], BF16)
        kT_all = c.sb("kT_all", [128, 2, TOK], BF16)
        v_all = c.sb("v_all", [128, 16, 256], BF16)
        gt_all = c.sb("gt_all", [128, 16, 1024], BF16)
        Q2 = c.ps("Q2", [128, 1024], F32)
        G2 = c.ps("G2", [128, 1024], F32)
        KV = c.ps("KV", [128, 512], F32)
        pT = c.ps("pT", [128, 8, 128], BF16)
        TQ = c.ps("TQ", [128, 8, 128], BF16)
        TK = c.ps("TK", [128, 8, 128], BF16)
        nbufs = dict(sq=sq, ss=ss, rs=rs, yb=yb, pT=pT, ident=ident, eps=eps,
                     names=dict(sq="sq", ss="ss", rs="rs", yb="yb", pT="pT"))
        for t in range(16):
            b = t % 2
            x = xt[b]
            p.dma("sync", x[:], x_d[t * 128:(t + 1) * 128, :], writes=["xt%d" % b])
            p.dma("sync", rp[b][:], rope_d[t * 128:(t + 1) * 128, :].rearrange("p (a r e) -> p a r e", a=2, r=2),
                  writes=["rp%d" % b])
            nbufs["dst_name"] = "yT%d" % b
            emit_rmsnorm_T(c, x[:], 128, g_sb, yT[b], 0, "xt%d" % b, nbufs)
            for (dst, dn, c0) in ((Q2[:, 0:512], "Q2", 0), (Q2[:, 512:1024], "Q2", 512), (KV[:, :], "KV", 1024),
                                  (G2[:, 0:512], "G2", 1536), (G2[:, 512:1024], "G2", 2048)):
                for k in range(8):
                    p.I("tensor", "matmul", ["w", "yT%d" % b], [dn], dst, lhsT=yT[b][:, k, :],
                        rhs=w[:, k, c0:c0 + 512], start=(k == 0), stop=(k == 7))
            p.I("scalar", "activation", [], ["Q2", "sq10"], out=sq10[:, 0:8, :],
                in_=Q2[:, :].rearrange("p (h d) -> p h d", h=8), func=AF.Square)
            p.I("scalar", "activation", [], ["KV", "sq10"], out=sq10[:, 8:10, :],
                in_=KV[:, 0:256].rearrange("p (h d) -> p h d", h=2), func=AF.Square)
            p.I("vector", "tensor_reduce", ["sq10"], ["ss10"], out=ss10[:, :], in_=sq10[:, :, :], axis=AX.X, op=ALU.add)
            p.I("scalar", "activation", ["ss10", "eps"], ["rs10"], out=rs10[:, :], in_=ss10[:, :], func=AF.Sqrt,
                scale=1.0 / 128, bias=eps[:, :])
            p.I("vector", "reciprocal", ["rs10"], ["rs10"], out=rs10[:, :], in_=rs10[:, :])
            p.I("vector", "tensor_tensor", ["rs10"], ["Q2", "z0"], out=z0[:, 0:8, :],
                in0=Q2[:, :].rearrange("p (h d) -> p h d", h=8),
                in1=rs10[:, 0:8].unsqueeze(2).broadcast_to([128, 8, 128]), op=ALU.mult)
            p.I("vector", "tensor_tensor", ["rs10"], ["KV", "z0"], out=z0[:, 8:10, :],
                in0=KV[:, 0:256].rearrange("p (h d) -> p h d", h=2),
                in1=rs10[:, 8:10].unsqueeze(2).broadcast_to([128, 2, 128]), op=ALU.mult)
            p.I("gpsimd", "tensor_tensor", ["z0", "gains"], ["zz"], out=zz[:, 0:8, :], in0=z0[:, 0:8, :],
                in1=gains[:, 0:1, :].broadcast_to([128, 8, 128]), op=ALU.mult)
            p.I("gpsimd", "tensor_tensor", ["z0", "gains"], ["zz"], out=zz[:, 8:10, :], in0=z0[:, 8:10, :],
                in1=gains[:, 1:2, :].broadcast_to([128, 2, 128]), op=ALU.mult)
            zv = zz[:, :, :].rearrange("p h (r f e) -> p h r f e", r=2, f=2)
            ov = zr[:, :, :].rearrange("p h (r f e) -> p h r f e", r=2, f=2)
            z1, z2 = zv[:, :, :, 0, :], zv[:, :, :, 1, :]
            cosb = rp[b][:, 0:1, :, :].broadcast_to([128, 10, 2, 32])
            sinb = rp[b][:, 1:2, :, :].broadcast_to([128, 10, 2, 32])
            rn = "rp%d" % b
            p.I("vector", "tensor_tensor", ["zz", rn], ["ra0"], out=ra[0][:], in0=z1, in1=cosb, op=ALU.mult)
            p.I("gpsimd", "tensor_tensor", ["zz", rn], ["ra1"], out=ra[1][:], in0=z2, in1=sinb, op=ALU.mult)
            p.I("gpsimd", "tensor_tensor", ["zz", rn], ["ra2"], out=ra[2][:], in0=z1, in1=sinb, op=ALU.mult)
            p.I("vector", "tensor_tensor", ["zz", rn], ["ra3"], out=ra[3][:], in0=z2, in1=cosb, op=ALU.mult)
            p.I("vector", "tensor_tensor", ["ra0", "ra1"], ["zr"], out=ov[:, :, :, 0, :], in0=ra[0][:], in1=ra[1][:],
                op=ALU.subtract)
            p.I("gpsimd", "tensor_tensor", ["ra2", "ra3"], ["zr"], out=ov[:, :, :, 1, :], in0=ra[2][:], in1=ra[3][:],
                op=ALU.add)
            for h in range(8):
                p.I("tensor", "transpose", ["zr"], ["TQ"], out=TQ[:, h, :], in_=zr[:, h, :], identity=ident[:, :])
            for h in range(2):
                p.I("tensor", "transpose", ["zr"], ["TK"], out=TK[:, h, :], in_=zr[:, 8 + h, :], identity=ident[:, :])
            p.I("scalar", "copy", [], ["TQ", "qT_all"], out=qT_all[:, :, t * 128:(t + 1) * 128], in_=TQ[:, :, :])
            p.I("vector", "tensor_copy", [], ["TK", "kT_all"], out=kT_all[:, :, t * 128:(t + 1) * 128], in_=TK[:, 0:2, :])
            p.I("scalar", "copy", [], ["KV", "v_all"], out=v_all[:, t, :], in_=KV[:, 256:512])
            p.I("scalar", "activation", [], ["G2", "gt_all"], out=gt_all[:, t, :], in_=G2[:, :], func=AF.Sigmoid)
        outs = [p.dma("sync", qT_d, qT_all[:], reads=["qT_all"]),
                p.dma("sync", kT_d, kT_all[:], reads=["kT_all"]),
                p.dma("sync", v_d, v_all[:], reads=["v_all"]),
                p.dma("sync", gt_d, gt_all[:], reads=["gt_all"])]
        p.emit(final_wait_ops=outs)
    return nc, prog


def block_masks_np():
    i = np.arange(128)
    b32 = (i[:, None] // 32) == (i[None, :] // 32)
    b64 = (i[:, None] // 64) == (i[None, :] // 64)
    return np.ascontiguousarray(np.stack([b32, b64 & ~b32, ~b64], axis=1).astype(np.float32))


def rope_tables_np():
    t = np.arange(SEQ)
    row = (t // 64).astype(np.float32)
    col = (t % 64).astype(np.float32)
    inv = (np.float32(10000.0) ** (-(np.arange(32, dtype=np.float32) * np.float32(2.0) / np.float32(64)))).astype(np.float32)
    ar = row[:, None] * inv[None, :]
    ac = col[:, None] * inv[None, :]
    return np.concatenate([np.cos(ar), np.cos(ac), np.sin(ar), np.sin(ac)], axis=1).astype(np.float32)


def build_attn_core_program(nheads=8, nqb=4, nsp=32):
    nc = bass.Bass("TRN2", target_bir_lowering=False)
    qT_d = nc.dram_tensor("qT", [128, 8, TOK], BF16, kind="ExternalInput").ap()
    kT_d = nc.dram_tensor("kT", [128, 2, SEQ], BF16, kind="ExternalInput").ap()
    v_d = nc.dram_tensor("v", [128, 64, 256], BF16, kind="ExternalInput").ap()
    gt_d = nc.dram_tensor("gate", [128, 16, 1024], BF16, kind="ExternalInput").ap()
    x_d = nc.dram_tensor("x", [TOK, D_MODEL], F32, kind="ExternalInput").ap()
    wo_d = nc.dram_tensor("w_out", [D_MODEL, D_MODEL], F32, kind="ExternalInput").ap()
    qg_d = nc.dram_tensor("qg", [128, 128], F32, kind="ExternalInput").ap()
    kg_d = nc.dram_tensor("kg", [128, 128], F32, kind="ExternalInput").ap()
    out_d = nc.dram_tensor("out", [TOK, D_MODEL], F32, kind="ExternalOutput").ap()
    prog = Prog(nc)
    p = prog
    with contextlib.ExitStack() as st:
        c = Ctx(nc, prog, st)
        ident = make_ident(c)
        qT = c.sb("qT_sb", [128, 8, TOK], BF16)
        kT = c.sb("kT_sb", [128, 2, SEQ], BF16)
        va = c.sb("v_aug", [128, 64, 2, 129], BF16)
        gt = c.sb("gt_sb", [128, 16, 1024], BF16)
        wo = c.sb("wo_sb", [128, 8, D_MODEL], BF16)
        gq = c.sb("gq", [128, 2, 128], F32)
        m2 = c.sb("m2", [128, 2], F32)
        negb = c.sb("negb", [128, 1], F32)
        p.fence()
        p.dma("sync", gq[:, 0, :], qg_d, writes=["gq"])
        p.dma("sync", gq[:, 1, :], kg_d, writes=["gq"])
        for h in range(8):
            p.dma("sync", qT[:, h, :], qT_d[:, h, :], writes=["qT"])
        for h in range(2):
            for s4 in range(4):
                p.dma("sync", kT[:, h, s4 * 2048:(s4 + 1) * 2048], kT_d[:, h, s4 * 2048:(s4 + 1) * 2048], writes=["kT"])
        p.I("gpsimd", "memset", [], ["va"], va[:, :, :, 128:129], 1.0)
        for s4 in range(4):
            p.dma("sync", va[:, s4 * 16:(s4 + 1) * 16, :, 0:128],
                  v_d[:, s4 * 16:(s4 + 1) * 16, :].rearrange("p s (h d) -> p s h d", h=2), writes=["va"])
        for t4 in range(4):
            p.dma("sync", gt[:, t4 * 4:(t4 + 1) * 4, :], gt_d[:, t4 * 4:(t4 + 1) * 4, :], writes=["gt"])
        wo_v = wo_d.rearrange("(k p) c -> p k c", p=128)
        for k in range(0, 8, 2):
            p.dma("gpsimd", wo[:, k:k + 2, :], wo_v[:, k:k + 2, :], writes=["wo"])
        p.I("vector", "tensor_tensor", ["gq"], ["gq"], out=gq[:], in0=gq[:], in1=gq[:], op=ALU.mult)
        p.I("vector", "tensor_reduce", ["gq"], ["m2"], out=m2[:, :], in_=gq[:, :, :], axis=AX.X, op=ALU.max)
        p.I("vector", "tensor_tensor", ["m2"], ["negb"], out=negb[:, :], in0=m2[:, 0:1], in1=m2[:, 1:2], op=ALU.mult)
        p.I("scalar", "activation", ["negb"], ["negb"], out=negb[:, :], in_=negb[:, :], func=AF.Sqrt, scale=128.0)
        p.I("scalar", "mul", ["negb"], ["negb"], out=negb[:, :], in_=negb[:, :], mul=-1.0)

        SC = [c.ps("SC%d" % i, [128, 1024], F32) for i in range(2)]
        O = [c.ps("O%d" % i, [128, 512], F32) for i in range(4)]
        NP = 3
        pT = [c.sb("pT%d" % i, [128, 1024], BF16) for i in range(NP)]
        rinv = c.sb("rinv", [128, 4], F32)
        step = 0
        for h in range(nheads):
            kv = h // 4
            for qb in range(nqb):
                def qk(sp, st_):
                    b = st_ % 2
                    for cc in range(2):
                        s = 2 * sp + cc
                        p.I("tensor", "matmul", ["kT", "qT"], ["SC%d" % b], SC[b][:, cc * 512:(cc + 1) * 512],
                            lhsT=kT[:, kv, s * 128:(s + 1) * 128], rhs=qT[:, h, qb * 512:(qb + 1) * 512],
                            start=True, stop=True)

                def ex(sp, st_):
                    b = st_ % 2
                    pb = st_ % NP
                    p.I("scalar", "activation", ["negb"], ["SC%d" % b, "pT%d" % pb], out=pT[pb][:, :], in_=SC[b][:, :],
                        func=AF.Exp, bias=negb[:, :])

                def pv(sp, st_):
                    pb = st_ % NP
                    for cc in range(2):
                        s = 2 * sp + cc
                        for qs in range(4):
                            p.I("tensor", "matmul", ["pT%d" % pb, "va"], ["O%d" % qs], O[qs][:, 0:129],
                                lhsT=pT[pb][:, cc * 512 + qs * 128:cc * 512 + (qs + 1) * 128], rhs=va[:, s, kv, :],
                                start=(sp == 0 and cc == 0), stop=(sp == nsp - 1 and cc == 1))

                qk(0, step)
                for sp in range(nsp):
                    if sp + 1 < nsp:
                        qk(sp + 1, step + 1)
                    ex(sp, step)
                    pv(sp, step)
                    step += 1
                for qs in range(4):
                    tile = qb * 4 + qs
                    p.I("vector", "reciprocal", [], ["O%d" % qs, "rinv"], out=rinv[:, qs:qs + 1], in_=O[qs][:, 128:129])
                    p.I("vector", "scalar_tensor_tensor", ["rinv"], ["O%d" % qs, "gt"],
                        out=gt[:, tile, h * 128:(h + 1) * 128], in0=O[qs][:, 0:128], scalar=rinv[:, qs:qs + 1],
                        in1=gt[:, tile, h * 128:(h + 1) * 128], op0=ALU.mult, op1=ALU.mult)
        xt = [c.sb("xt%d" % i, [128, D_MODEL], F32) for i in range(2)]
        ho = [c.sb("ho%d" % i, [128, D_MODEL], F32) for i in range(2)]
        ogT = [c.sb("ogT%d" % i, [128, 8, 128], BF16) for i in range(2)]
        TP = O[0].bitcast(BF16) if hasattr(O[0], "bitcast") else None
        outs = []
        for t in range(16):
            b = t % 2
            p.dma("sync", xt[b][:], x_d[t * 128:(t + 1) * 128, :], writes=["xt%d" % b])
            for k in range(8):
                p.I("tensor", "transpose", ["gt"], ["O0"], out=TP[:, k * 128:(k + 1) * 128],
                    in_=gt[:, t, k * 128:(k + 1) * 128], identity=ident[:, :])
            p.I("vector", "tensor_copy", [], ["O0", "ogT%d" % b], out=ogT[b][:, :, :],
                in_=TP[:, :].rearrange("p (k t) -> p k t", k=8))
            for hf in range(2):
                for k in range(8):
                    p.I("tensor", "matmul", ["ogT%d" % b, "wo"], ["SC0"], SC[0][:, hf * 512:(hf + 1) * 512],
                        lhsT=ogT[b][:, k, :], rhs=wo[:, k, hf * 512:(hf + 1) * 512], start=(k == 0), stop=(k == 7))
            p.I("vector", "tensor_tensor", ["xt%d" % b], ["SC0", "ho%d" % b], out=ho[b][:, :], in0=SC[0][:, :],
                in1=xt[b][:, :], op=ALU.add)
            outs.append(p.dma("sync", out_d[t * 128:(t + 1) * 128, :], ho[b][:, :], reads=["ho%d" % b]))
        p.emit(final_wait_ops=outs)
    return nc, prog


GDN_IN = 6208


def build_gdn_proj_program():
    nc = bass.Bass("TRN2", target_bir_lowering=False)
    HT = TOK + 4
    h_d = nc.dram_tensor("h", [HT, D_MODEL], F32, kind="ExternalInput").ap()
    g_d = nc.dram_tensor("g_mix", [128, D_MODEL], F32, kind="ExternalInput").ap()
    w_d = nc.dram_tensor("w_in", [D_MODEL, GDN_IN], F32, kind="ExternalInput").ap()
    cw_d = nc.dram_tensor("cw", [128, 32, 6], F32, kind="ExternalInput").ap()
    ad_d = nc.dram_tensor("ad", [128, 2, 32], F32, kind="ExternalInput").ap()
    qT_d = nc.dram_tensor("qT", [128, 8, TOK], BF16, kind="ExternalOutput").ap()
    kT_d = nc.dram_tensor("kT", [128, 8, TOK], BF16, kind="ExternalOutput").ap()
    ktm_d = nc.dram_tensor("k_tm", [TOK, 1024], BF16, kind="ExternalOutput").ap()
    vtm_d = nc.dram_tensor("v_tm", [TOK, 2048], BF16, kind="ExternalOutput").ap()
    z_d = nc.dram_tensor("z", [TOK, 2048], BF16, kind="ExternalOutput").ap()
    gb_d = nc.dram_tensor("gb", [TOK, 64], F32, kind="ExternalOutput").ap()
    prog = Prog(nc)
    p = prog
    outs = []
    with contextlib.ExitStack() as st:
        c = Ctx(nc, prog, st)
        ident = make_ident(c)
        ones = c.sb("ones", [128, 128], F32)
        p.I("gpsimd", "memset", [], ["ones"], ones[:], 1.0)
        g_sb = c.sb("g_sb", [128, D_MODEL], F32)
        cw = c.sb("cw_sb", [128, 32, 6], F32)
        ad = c.sb("ad_sb", [128, 2, 32], F32)
        eps = c.sb("eps", [128, 1], F32)
        one1 = c.sb("one1", [128, 1], F32)
        p.I("vector", "memset", [], ["eps"], eps[:], EPS)
        p.I("vector", "memset", [], ["one1"], one1[:], 1.0)
        p.fence()
        p.dma("sync", g_sb[:], g_d, writes=["g_sb"])
        p.dma("sync", cw[:], cw_d, writes=["cw"])
        p.dma("sync", ad[:], ad_d, writes=["ad"])
        p.I("scalar", "activation", ["ad"], ["ad"], out=ad[:, 0, :], in_=ad[:, 0, :], func=AF.Exp)
        p.I("scalar", "mul", ["ad"], ["ad"], out=ad[:, 0, :], in_=ad[:, 0, :], mul=-1.0)
        wv = w_d.rearrange("(k p) c -> p k c", p=128)
        wb = [c.sb("wb%d" % i, [128, 8, 1024], BF16) for i in range(2)]
        yT = c.sb("yT_all", [128, 8, HT], BF16)
        xt = [c.sb("xt%d" % i, [128, D_MODEL], F32) for i in range(2)]
        sq = c.sb("sq", [128, D_MODEL], F32)
        ss = c.sb("ss", [128, 1], F32)
        rs = c.sb("rs", [128, 1], F32)
        yb = c.sb("yb", [128, D_MODEL], BF16)
        pT = c.ps("pT", [128, 8, 128], BF16)
        U = [c.ps("U%d" % i, [128, 512], F32) for i in range(3)]
        L = c.ps("L", [128, 512], F32)
        TT = c.ps("TT", [128, 8, 128], BF16)
        Z = [U[0], U[1]]
        nbufs = dict(sq=sq, ss=ss, rs=rs, yb=yb, pT=pT, ident=ident, eps=eps, dst_name="yT",
                     names=dict(sq="sq", ss="ss", rs="rs", yb="yb", pT="pT"))
        r = 0
        xi = 0
        while r < HT:
            rn = min(128, HT - r)
            b = xi % 2
            xi += 1
            p.dma("sync", xt[b][0:rn, :], h_d[r:r + rn, :], writes=["xt%d" % b])
            emit_rmsnorm_T(c, xt[b][0:rn, :], rn, g_sb, yT, r, "xt%d" % b, nbufs)
            r += rn
        NR = 3
        tAs = [c.sb("tA%d" % i, [128, 508], F32) for i in range(NR)]
        tBs = [c.sb("tB%d" % i, [128, 508], F32) for i in range(NR)]
        acts = [c.sb("act%d" % i, [128, 508], F32) for i in range(NR)]
        sqvs = [c.sb("sqv%d" % i, [128, 508], F32) for i in range(NR)]
        rts = [c.sb("rt%d" % i, [128, 508], F32) for i in range(NR)]
        fm = [c.sb("fm%d" % i, [128, 8, 508], BF16) for i in range(2)]
        tm = [c.sb("tm%d" % i, [128, 1024], BF16) for i in range(2)]
        blocks = []
        t0 = 0
        while t0 < TOK:
            n = min(508, TOK - t0)
            blocks.append((t0, n))
            t0 += n
        fi = 0
        ti = 0
        ui = 0
        for grp in range(4):
            wbuf = wb[grp % 2]
            wn = "wb%d" % (grp % 2)
            for k in range(0, 8, 2):
                p.dma("gpsimd", wbuf[:, k:k + 2, :], wv[:, k:k + 2, grp * 1024:(grp + 1) * 1024], writes=[wn])
            for (t0, n) in blocks:
                ncol = n + 4
                fmb = fm[fi % 2]
                fn_ = "fm%d" % (fi % 2)
                fi += 1
                for j in range(8):
                    cj = grp * 8 + j
                    Uj = U[ui % 3]
                    un = "U%d" % (ui % 3)
                    rr = ui % NR
                    tA, tB, act, sqv, rt = tAs[rr], tBs[rr], acts[rr], sqvs[rr], rts[rr]
                    nA, nB, nact, nsqv, nrt = "tA%d" % rr, "tB%d" % rr, "act%d" % rr, "sqv%d" % rr, "rt%d" % rr
                    ui += 1
                    for k in range(8):
                        p.I("tensor", "matmul", [wn, "yT"], [un], Uj[:, 0:ncol], lhsT=wbuf[:, k, j * 128:(j + 1) * 128],
                            rhs=yT[:, k, t0:t0 + ncol], start=(k == 0), stop=(k == 7))
                    p.I("scalar", "activation", ["cw"], [un, nA], out=tA[:, 0:n], in_=Uj[:, 2:2 + n], func=AF.Identity,
                        scale=cw[:, cj, 2:3], bias=cw[:, cj, 5:6])
                    src, dst = tA, tB
                    sn_, dn_ = nA, nB
                    for tap in (0, 1, 3, 4):
                        p.I("vector", "scalar_tensor_tensor", ["cw", sn_], [un, dn_], out=dst[:, 0:n],
                            in0=Uj[:, tap:tap + n], scalar=cw[:, cj, tap:tap + 1], in1=src[:, 0:n],
                            op0=ALU.mult, op1=ALU.add)
                        src, dst = dst, src
                        sn_, dn_ = dn_, sn_
                    if grp < 2:
                        p.I("scalar", "activation", [nA], [nact], out=act[:, 0:n], in_=tA[:, 0:n], func=AF.Silu)
                        p.I("gpsimd", "tensor_tensor", [nact], [nsqv], out=sqv[:, 0:n], in0=act[:, 0:n], in1=act[:, 0:n], op=ALU.mult)
                        p.I("tensor", "matmul", ["ones", nsqv], ["L"], L[:, 0:n], lhsT=ones[:, :], rhs=sqv[:, 0:n],
                            start=True, stop=True)
                        p.I("scalar", "activation", ["eps"], ["L", nrt], out=rt[:, 0:n], in_=L[:, 0:n], func=AF.Sqrt,
                            bias=eps[:, :])
                        p.I("vector", "reciprocal", [nrt], [nrt], out=rt[:, 0:n], in_=rt[:, 0:n])
                        p.I("gpsimd", "scalar_tensor_tensor" if False else "tensor_tensor", [nact, nrt], [nsqv], out=sqv[:, 0:n],
                            in0=act[:, 0:n], in1=rt[:, 0:n], op=ALU.mult)
                        p.I("scalar", "mul", [nsqv], [fn_], out=fmb[:, j, 0:n], in_=sqv[:, 0:n],
                            mul=(128.0 ** -0.5 if grp == 0 else 1.0))
                    else:
                        p.I("scalar", "activation", [nA], [fn_], out=fmb[:, j, 0:n], in_=tA[:, 0:n], func=AF.Silu)
                if grp == 0:
                    outs.append(p.dma("sync", qT_d[:, :, t0:t0 + n], fmb[:, :, 0:n], reads=[fn_]))
                if grp == 1:
                    outs.append(p.dma("sync", kT_d[:, :, t0:t0 + n], fmb[:, :, 0:n], reads=[fn_]))
                if grp >= 1:
                    s0 = 0
                    while s0 < n:
                        sn = min(128, n - s0)
                        tmb = tm[ti % 2]
                        tn = "tm%d" % (ti % 2)
                        ti += 1
                        for j in range(8):
                            p.I("tensor", "transpose", [fn_], ["TT"], out=TT[0:sn, j, :], in_=fmb[:, j, s0:s0 + sn],
                                identity=ident[:, :])
                        p.I("vector", "tensor_copy", [], ["TT", tn], out=tmb[0:sn, :],
                            in_=TT[0:sn, :, :].rearrange("p j d -> p (j d)"))
                        if grp == 1:
                            dst_ap = ktm_d[t0 + s0:t0 + s0 + sn, :]
                        else:
                            dst_ap = vtm_d[t0 + s0:t0 + s0 + sn, (grp - 2) * 1024:(grp - 1) * 1024]
                        outs.append(p.dma("sync", dst_ap, tmb[0:sn, :], reads=[tn]))
                        s0 += sn
        wz = wb
        zs = [c.sb("zs%d" % i, [128, 1024], BF16) for i in range(2)]
        for half in range(2):
            wbuf = wz[half % 2]
            wn = "wb%d" % (half % 2)
            for k in range(0, 8, 2):
                p.dma("gpsimd", wbuf[:, k:k + 2, :], wv[:, k:k + 2, 4096 + half * 1024:4096 + (half + 1) * 1024], writes=[wn])
            for t in range(16):
                zb = zs[t % 2]
                zn = "zs%d" % (t % 2)
                for hf in range(2):
                    for k in range(8):
                        p.I("tensor", "matmul", [wn, "yT"], ["U%d" % hf], Z[hf][:, :], lhsT=yT[:, k, 2 + t * 128:2 + (t + 1) * 128],
                            rhs=wbuf[:, k, hf * 512:(hf + 1) * 512], start=(k == 0), stop=(k == 7))
                    p.I("scalar", "activation", [], ["U%d" % hf, zn], out=zb[:, hf * 512:(hf + 1) * 512], in_=Z[hf][:, :],
                        func=AF.Silu)
                outs.append(p.dma("sync", z_d[t * 128:(t + 1) * 128, half * 1024:(half + 1) * 1024], zb[:, :], reads=[zn]))
        wab = c.sb("wab", [128, 8, 64], BF16)
        p.dma("gpsimd", wab[:, :, :], wv[:, :, 6144:6208], writes=["wab"])
        xs = c.sb("xs", [128, 32], F32)
        ax = c.sb("ax", [128, 32], F32)
        gbs = [c.sb("gbs%d" % i, [128, 64], F32) for i in range(2)]
        for t in range(16):
            gbt = gbs[t % 2]
            gn = "gbs%d" % (t % 2)
            for k in range(8):
                p.I("tensor", "matmul", ["wab", "yT"], ["U0"], Z[0][:, 0:64], lhsT=yT[:, k, 2 + t * 128:2 + (t + 1) * 128],
                    rhs=wab[:, k, :], start=(k == 0), stop=(k == 7))
            Zv = Z[0][:, 0:64].rearrange("p (d a h) -> p d a h", d=2, a=2)
            p.I("vector", "tensor_tensor", ["ad"], ["U0", "xs"], out=xs[:, :].rearrange("p (d h) -> p d h", d=2),
                in0=Zv[:, :, 0, :], in1=ad[:, 1, :].rearrange("p (d h) -> p d h", d=2), op=ALU.add)
            p.I("scalar", "activation", [], ["U0", gn], out=gbt[:, 32:64].rearrange("p (d h) -> p d h", d=2),
                in_=Zv[:, :, 1, :], func=AF.Sigmoid)
            p.I("scalar", "activation", ["xs"], ["ax"], out=ax[:, :], in_=xs[:, :], func=AF.Abs)
            p.I("scalar", "activation", ["ax"], ["ax"], out=ax[:, :], in_=ax[:, :], func=AF.Exp, scale=-1.0)
            p.I("scalar", "activation", ["ax", "one1"], ["ax"], out=ax[:, :], in_=ax[:, :], func=AF.Ln, bias=one1[:, :])
            p.I("vector", "scalar_tensor_tensor", ["xs", "ax"], ["xs"], out=xs[:, :], in0=xs[:, :], scalar=0.0,
                in1=ax[:, :], op0=ALU.max, op1=ALU.add)
            p.I("vector", "tensor_tensor", ["xs", "ad"], [gn], out=gbt[:, 0:32], in0=xs[:, :], in1=ad[:, 0, :], op=ALU.mult)
            outs.append(p.dma("sync", gb_d[t * 128:(t + 1) * 128, :], gbt[:, :], reads=[gn]))
        p.emit(final_wait_ops=outs)
    return nc, prog


def build_gdn_scan_program(nchunks=64):
    nc = bass.Bass("TRN2", target_bir_lowering=False)
    S_ = nchunks * 128
    qT_d = nc.dram_tensor("qT", [128, 2, S_], BF16, kind="ExternalInput").ap()
    kT_d = nc.dram_tensor("kT", [128, 2, S_], BF16, kind="ExternalInput").ap()
    ktm_d = nc.dram_tensor("k_tm", [S_, 256], BF16, kind="ExternalInput").ap()
    vtm_d = nc.dram_tensor("v_tm", [S_, 512], BF16, kind="ExternalInput").ap()
    z_d = nc.dram_tensor("z", [S_, 512], BF16, kind="ExternalInput").ap()
    gb_d = nc.dram_tensor("gb", [S_, 16], F32, kind="ExternalInput").ap()
    on_d = nc.dram_tensor("on", [128, 128], F32, kind="ExternalInput").ap()
    bm_d = nc.dram_tensor("bm", [128, 3, 128], F32, kind="ExternalInput").ap()
    og_d = nc.dram_tensor("og", [S_, 512], BF16, kind="ExternalOutput").ap()
    ost_d = nc.dram_tensor("ost", [S_, 512], F32, kind="Internal").ap()
    prog = Prog(nc)
    p = prog
    outs = []
    with contextlib.ExitStack() as st:
        c = Ctx(nc, prog, st)
        ident = make_ident(c)
        ones = c.sb("ones", [128, 128], F32)
        p.I("gpsimd", "memset", [], ["ones"], ones[:], 1.0)
        masks = {}
        for nm_, cmp, sg in (("LE", ALU.is_ge, -1), ("GT", ALU.is_gt, 1), ("GE", ALU.is_ge, 1), ("LT", ALU.is_gt, -1)):
            m = c.sb("m" + nm_, [128, 128], F32)
            p.I("gpsimd", "memset", [], ["m" + nm_], m[:], 1.0)
            p.I("gpsimd", "affine_select", ["m" + nm_], ["m" + nm_], out=m[:], in_=m[:], pattern=[[-sg, 128]],
                compare_op=cmp, fill=0.0, base=0, channel_multiplier=sg)
            masks[nm_] = m
        on = c.sb("on_sb", [128, 128], F32)
        eps = c.sb("eps", [128, 1], F32)
        p.I("vector", "memset", [], ["eps"], eps[:], EPS)
        p.fence()
        p.dma("sync", on[:], on_d, writes=["on"])
        bm = c.sb("bm_sb", [128, 3, 128], F32)
        p.dma("sync", bm[:], bm_d, writes=["bm"])
        B = [c.ps("B%d" % i, [128, 512], F32) for i in range(8)]

        def bk(i):
            return B[i][:, :].rearrange("p (h d) -> p h d", h=4)

        D = {}
        for d in range(2):
            for par in range(2):
                dd = {}
                sfx = "_%d%d" % (d, par)
                for nm_, shp, dt_ in (("qTc", [128, 2, 128], BF16), ("kTc", [128, 2, 128], BF16), ("ktm", [128, 256], BF16),
                                      ("vtm", [128, 512], BF16), ("zc", [128, 512], BF16), ("gb", [128, 16], F32),
                                      ("X0", [128, 4, 128], BF16), ("X1", [128, 4, 128], BF16),
                                      ("Y0", [128, 4, 128], BF16), ("Y1", [128, 4, 128], BF16),
                                      ("P0", [128, 4, 128], BF16), ("P1", [128, 4, 128], BF16),
                                      ("rhsD", [128, 4, 128], F32), ("E", [128, 4, 128], F32), ("Es", [128, 4, 128], F32),
                                      ("Ei", [128, 4, 128], F32), ("KQ", [128, 4, 128], F32), ("attn", [128, 4, 128], BF16),
                                      ("XA", [128, 8, 128], BF16), ("kbg", [128, 4, 128], BF16), ("vb", [128, 4, 128], BF16),
                                      ("kdec", [128, 4, 128], BF16), ("nwT", [128, 4, 128], BF16),
                                      ("gs", [128, 8], F32), ("ex", [128, 12], F32), ("nbe", [128, 4], F32),
                                      ("sso", [128, 4], F32), ("ol", [128, 4, 128], F32),
                                      ("Pf0", [128, 4, 128], F32), ("Pf1", [128, 4, 128], F32),
                                      ("N1", [128, 4, 128], BF16), ("N2", [128, 4, 128], BF16), ("N1T", [128, 4, 128], BF16),
                                      ("WP", [128, 4, 128], BF16), ("WQ", [128, 4, 128], BF16),
                                      ("Q0", [128, 4, 128], BF16), ("Q1", [128, 4, 128], BF16)):
                    dd[nm_] = c.sb(nm_ + sfx, shp, dt_)
                dd["tq"], dd["od"], dd["sqo"] = dd["rhsD"], dd["E"], dd["Es"]
                dd["vn"], dd["ogc"] = dd["attn"], dd["kbg"]
                dd["_alias"] = dict(tq="rhsD", od="E", sqo="Es", vn="attn", ogc="kbg")
                D[d, par] = dd
        S32, Sbf = {}, {}
        for d in range(2):
            S32[d] = c.sb("S32_%d" % d, [128, 4, 128], F32)
            Sbf[d] = c.sb("Sbf_%d" % d, [128, 4, 128], BF16)
            p.I("vector", "memset", [], ["S32_%d" % d], S32[d][:], 0.0)
            p.I("vector", "memset", [], ["Sbf_%d" % d], Sbf[d][:], 0.0)

        def b4(ap2):
            return ap2.unsqueeze(2).broadcast_to([128, 4, 128])

        def bh(ap2):
            return ap2.unsqueeze(1).broadcast_to([128, 4, 128])

        def rep2(ap3):
            return ap3.unsqueeze(2).broadcast_to([128, 2, 2, 128])

        def v4(ap3):
            return ap3.rearrange("p (q r) d -> p q r d", q=2)

        def pre(d, i, par, second):
            dd = D[d, par]
            sfx = "_%d%d" % (d, par)
            al = dd["_alias"]
            N = lambda x: al.get(x, x) + sfx
            b0, b1, b2 = 3 * d, 3 * d + 1, 3 * d + 2
            n0, n1, n2 = "B%d" % b0, "B%d" % b1, "B%d" % b2
            r0 = i * 128
            qTc, kTc, ktm, vtm, zc, gb = (dd[k] for k in ("qTc", "kTc", "ktm", "vtm", "zc", "gb"))
            p.dma("sync", qTc[:], qT_d[:, :, r0:r0 + 128], writes=[N("qTc")])
            p.dma("sync", kTc[:], kT_d[:, :, r0:r0 + 128], writes=[N("kTc")])
            p.dma("sync", ktm[:], ktm_d[r0:r0 + 128, :], writes=[N("ktm")])
            p.dma("sync", vtm[:], vtm_d[r0:r0 + 128, :], writes=[N("vtm")])
            p.dma("sync", zc[:], z_d[r0:r0 + 128, :], writes=[N("zc")])
            p.dma("sync", gb[:], gb_d[r0:r0 + 128, :], writes=[N("gb")])
            Mm = masks["LE"] if d == 0 else masks["GE"]
            Vm = masks["GT"] if d == 0 else masks["LT"]
            mS = masks["GT"] if d == 0 else masks["LT"]
            mI = masks["GE"] if d == 0 else masks["LE"]
            g_ = gb[:, d * 4:(d + 1) * 4]
            be = gb[:, 8 + d * 4:8 + (d + 1) * 4]
            gs, ex, nbe = dd["gs"], dd["ex"], dd["nbe"]
            for q in range(2):
                p.I("tensor", "matmul", [N("kTc")], [n0], bk(b0)[:, q, :], lhsT=kTc[:, q, :], rhs=kTc[:, q, :],
                    start=True, stop=True)
                p.I("tensor", "matmul", [N("kTc"), N("qTc")], [n0], bk(b0)[:, 2 + q, :], lhsT=qTc[:, q, :],
                    rhs=kTc[:, q, :], start=True, stop=True)
            p.I("scalar", "copy", [], [n0, N("KQ")], out=dd["KQ"][:], in_=bk(b0))
            p.I("tensor", "matmul", [N("gb")], [n1], B[b1][:, 0:4], lhsT=Mm[:, :], rhs=g_, start=True, stop=True)
            p.I("tensor", "matmul", [N("gb"), "ones"], [n1], B[b1][:, 4:8], lhsT=ones[:, :], rhs=g_, start=True, stop=True)
            p.I("vector", "tensor_copy", [], [n1, N("gs")], out=gs[:, :], in_=B[b1][:, 0:8])
            p.I("scalar", "activation", [N("gs")], [N("ex")], out=ex[:, 0:4], in_=gs[:, 0:4], func=AF.Exp)
            p.I("vector", "tensor_tensor", [N("gs")], [N("gs")], out=gs[:, 0:4], in0=gs[:, 4:8], in1=gs[:, 0:4], op=ALU.subtract)
            p.I("scalar", "activation", [N("gs")], [N("ex")], out=ex[:, 4:12], in_=gs[:, 0:8], func=AF.Exp)
            p.I("vector", "tensor_scalar", [N("gb")], [N("nbe")], out=nbe[:, :], in0=be, scalar1=-1.0, scalar2=None, op0=ALU.mult)
            p.I("vector", "tensor_tensor", [N("nbe"), N("ex")], [N("sso")], out=dd["sso"][:, :], in0=nbe[:, :], in1=ex[:, 0:4], op=ALU.mult)
            k3 = ktm[:, :].rearrange("p (q d) -> p q d", q=2)
            p.I("gpsimd", "tensor_tensor", [N("ktm"), N("ex")], [N("kdec")], out=v4(dd["kdec"][:]), in0=rep2(k3),
                in1=v4(b4(ex[:, 4:8])), op=ALU.mult)
            p.I("gpsimd", "tensor_tensor", [N("vtm"), N("gb")], [N("vb")], out=dd["vb"][:],
                in0=vtm[:, :].rearrange("p (h d) -> p h d", h=4), in1=b4(be), op=ALU.mult)
            p.I("gpsimd", "tensor_tensor", [N("gb")], [N("rhsD")], out=dd["rhsD"][:], in0=bh(Vm[:, :]), in1=b4(g_), op=ALU.mult)
            p.I("tensor", "matmul", [N("rhsD")], [n2], B[b2][:, :], lhsT=Mm[:, :], rhs=dd["rhsD"][:].rearrange("p h s -> p (h s)"),
                start=True, stop=True)
            p.I("scalar", "activation", [], [n2, N("E")], out=dd["E"][:], in_=bk(b2), func=AF.Exp)
            p.I("gpsimd", "tensor_tensor", [N("E")], [N("Es")], out=dd["Es"][:], in0=dd["E"][:], in1=bh(mS[:, :]), op=ALU.mult)
            p.I("gpsimd", "tensor_tensor", [N("E")], [N("Ei")], out=dd["Ei"][:], in0=dd["E"][:], in1=bh(mI[:, :]), op=ALU.mult)
            p.I("vector", "tensor_tensor", [N("Es"), N("nbe")], [N("Es")], out=dd["Es"][:], in0=dd["Es"][:], in1=b4(nbe[:, :]), op=ALU.mult)
            Y0 = dd["Y0"]
            p.I("vector", "tensor_tensor", [N("Es"), N("KQ")], [N("Es")], out=v4(dd["Es"][:]), in0=v4(dd["Es"][:]),
                in1=rep2(dd["KQ"][:, 0:2, :]), op=ALU.mult)
            p.I("gpsimd", "tensor_tensor", [N("Es"), "bm"], [N("Y0")], out=Y0[:], in0=dd["Es"][:], in1=bh(bm[:, 0, :]), op=ALU.mult)
            p.I("gpsimd", "tensor_tensor", [N("Es"), "bm"], [N("N1")], out=dd["N1"][:], in0=dd["Es"][:], in1=bh(bm[:, 1, :]), op=ALU.mult)
            p.I("vector", "tensor_tensor", [N("Es"), "bm"], [N("N2")], out=dd["N2"][:], in0=dd["Es"][:], in1=bh(bm[:, 2, :]), op=ALU.mult)
            p.I("gpsimd", "tensor_tensor", [N("Ei"), N("KQ")], [N("attn")], out=v4(dd["attn"][:]), in0=v4(dd["Ei"][:]),
                in1=rep2(dd["KQ"][:, 2:4, :]), op=ALU.mult)
            TRb = B[b0].bitcast(BF16)
            TRb1 = B[b1].bitcast(BF16)
            for h in range(4):
                p.I("tensor", "transpose", [N("Y0")], [n0], out=TRb[:, h * 128:(h + 1) * 128], in_=Y0[:, h, :],
                    identity=ident[:, :])
            for h in range(4):
                p.I("tensor", "transpose", [N("attn")], [n0], out=TRb[:, (4 + h) * 128:(5 + h) * 128], in_=dd["attn"][:, h, :],
                    identity=ident[:, :])
            for h in range(4):
                p.I("tensor", "transpose", [N("N1")], [n1], out=TRb1[:, h * 128:(h + 1) * 128], in_=dd["N1"][:, h, :],
                    identity=ident[:, :])
            XA = dd["XA"]
            p.I("scalar", "copy", [], [n0, N("XA")], out=XA[:], in_=TRb[:, :].rearrange("p (h s) -> p h s", h=8))
            p.I("vector", "tensor_copy", [], [n1, N("N1T")], out=dd["N1T"][:],
                in_=TRb1[:, 0:512].rearrange("p (h s) -> p h s", h=4))
            P1 = dd["P0"]
            p.I("vector", "tensor_tensor", [N("XA")], [N("Pf0")], out=dd["Pf0"][:], in0=XA[:, 0:4, :], in1=bh(ident[:, :]), op=ALU.add)
            p.I("gpsimd", "tensor_copy", [N("Pf0")], [N("P0")], out=P1[:], in_=dd["Pf0"][:])
            Pf, Pfn = dd["Pf0"], N("Pf0")
            Xc, Xn_ = XA[:, 0:4, :], N("XA")
            Yc, Yn_ = Y0, N("Y0")
            Pc, Pn_ = P1, N("P0")
            for it in range(5):
                nb = (it + 1) % 2
                doP = it >= 1
                doX = it <= 2
                doY = it <= 3
                if doP:
                    for h in range(4):
                        p.I("tensor", "matmul", [Yn_, Pn_], [n2], bk(b2)[:, h, :], lhsT=Yc[:, h, :], rhs=Pc[:, h, :], start=True, stop=True)
                if doX:
                    for h in range(4):
                        p.I("tensor", "matmul", [Yn_, Xn_], [n0], bk(b0)[:, h, :], lhsT=Yc[:, h, :], rhs=Xc[:, h, :], start=True, stop=True)
                if doY:
                    for h in range(4):
                        p.I("tensor", "matmul", [Xn_, Yn_], [n1], bk(b1)[:, h, :], lhsT=Xc[:, h, :], rhs=Yc[:, h, :], start=True, stop=True)
                if doP:
                    Pnew, Pnn = dd["P%d" % nb], N("P%d" % nb)
                    Pfnew, Pfnn = dd["Pf%d" % nb], N("Pf%d" % nb)
                    p.I("vector", "tensor_tensor", [Pfn], [n2, Pfnn], out=Pfnew[:], in0=bk(b2), in1=Pf[:], op=ALU.add)
                    p.I("scalar", "copy", [Pfnn], [Pnn], out=Pnew[:], in_=Pfnew[:])
                    Pf, Pfn = Pfnew, Pfnn
                    Pc, Pn_ = Pnew, Pnn
                if doX:
                    Xnew, Xnn = dd["X%d" % nb], N("X%d" % nb)
                    p.I("scalar", "copy", [], [n0, Xnn], out=Xnew[:], in_=bk(b0))
                if doY:
                    Ynew, Ynn = dd["Y%d" % nb], N("Y%d" % nb)
                    p.I("vector", "tensor_copy", [], [n1, Ynn], out=Ynew[:], in_=bk(b1))
                if doX:
                    Xc, Xn_ = Xnew[:], Xnn
                if doY:
                    Yc, Yn_ = Ynew, Ynn
            for h in range(4):
                p.I("tensor", "transpose", [Pn_], [n0], out=TRb[:, h * 128:(h + 1) * 128], in_=Pc[:, h, :], identity=ident[:, :])
            Q0, Q1 = dd["Q0"], dd["Q1"]
            p.I("vector", "tensor_copy", [], [n0, N("Q0")], out=Q0[:], in_=TRb[:, 0:512].rearrange("p (h s) -> p h s", h=4))
            for h in range(4):
                p.I("tensor", "matmul", [N("N1"), Pn_], [n1], bk(b1)[:, h, :], lhsT=dd["N1"][:, h, :], rhs=Pc[:, h, :], start=True, stop=True)
            for h in range(4):
                p.I("tensor", "matmul", [N("N1T"), N("Q0")], [n2], bk(b2)[:, h, :], lhsT=dd["N1T"][:, h, :], rhs=Q0[:, h, :], start=True, stop=True)
            p.I("scalar", "copy", [], [n1, N("WP")], out=dd["WP"][:], in_=bk(b1))
            p.I("vector", "tensor_copy", [], [n2, N("WQ")], out=dd["WQ"][:], in_=bk(b2))
            for h in range(4):
                p.I("tensor", "matmul", [N("Q0"), N("WP")], [n0], bk(b0)[:, h, :], lhsT=Q0[:, h, :], rhs=dd["WP"][:, h, :], start=True, stop=True)
            for h in range(4):
                p.I("tensor", "matmul", [Pn_, N("WQ")], [n1], bk(b1)[:, h, :], lhsT=Pc[:, h, :], rhs=dd["WQ"][:, h, :], start=True, stop=True)
            nbp = 0 if Pc is dd["P1"] else 1
            Pnew, Pnn = dd["P%d" % nbp], N("P%d" % nbp)
            Pfnew, Pfnn = dd["Pf%d" % nbp], N("Pf%d" % nbp)
            p.I("vector", "tensor_tensor", [Pfn], [n0, Pfnn], out=Pfnew[:], in0=bk(b0), in1=Pf[:], op=ALU.add)
            p.I("scalar", "copy", [Pfnn], [Pnn], out=Pnew[:], in_=Pfnew[:])
            p.I("vector", "tensor_tensor", [N("Q0")], [n1, N("Q1")], out=Q1[:], in0=bk(b1), in1=Q0[:], op=ALU.add)
            Pf, Pfn, Pc, Pn_ = Pfnew, Pfnn, Pnew, Pnn
            for h in range(4):
                p.I("tensor", "matmul", [N("N2"), Pn_], [n2], bk(b2)[:, h, :], lhsT=dd["N2"][:, h, :], rhs=Pc[:, h, :], start=True, stop=True)
            p.I("scalar", "copy", [], [n2, N("WP")], out=dd["WP"][:], in_=bk(b2))
            for h in range(4):
                p.I("tensor", "matmul", [N("Q1"), N("WP")], [n0], bk(b0)[:, h, :], lhsT=Q1[:, h, :], rhs=dd["WP"][:, h, :], start=True, stop=True)
            nbp = 1 - nbp
            Pnew, Pnn = dd["P%d" % nbp], N("P%d" % nbp)
            p.I("vector", "tensor_tensor", [Pfn], [n0, Pnn], out=Pnew[:], in0=bk(b0), in1=Pf[:], op=ALU.add)
            Pc, Pn_ = Pnew, Pnn
            return Pc, Pn_

        def scan(d, i, par, second, Pc, Pn_):
            dd = D[d, par]
            sfx = "_%d%d" % (d, par)
            al = dd["_alias"]
            N = lambda x: al.get(x, x) + sfx
            Sn, Sb = "S32_%d" % d, "Sbf_%d" % d
            XA, qTc, zc, ex = dd["XA"], dd["qTc"], dd["zc"], dd["ex"]
            if second:
                p.dma("sync", dd["ol"][:], ost_d[i * 128:(i + 1) * 128, :].rearrange("p (h d) -> p h d", h=4),
                      reads=["ost%d" % i], writes=[N("ol")])
            kTc = dd["kTc"]
            for h in range(4):
                p.I("tensor", "matmul", [N("kTc"), Sb], ["B6"], bk(6)[:, h, :], lhsT=kTc[:, h // 2, :], rhs=Sbf[d][:, h, :],
                    start=True, stop=True)
            for h in range(4):
                p.I("tensor", "matmul", [N("qTc"), Sb], ["B7"], bk(7)[:, h, :], lhsT=qTc[:, h // 2, :], rhs=Sbf[d][:, h, :],
                    start=True, stop=True)
            rr = dd["kbg"]
            for h in range(4):
                p.I("vector", "scalar_tensor_tensor", [N("sso"), N("vb")], ["B6", N("kbg")], out=rr[:, h, :], in0=bk(6)[:, h, :],
                    scalar=dd["sso"][:, h:h + 1], in1=dd["vb"][:, h, :], op0=ALU.mult, op1=ALU.add)
            for h in range(4):
                p.I("tensor", "matmul", [Pn_, N("kbg")], ["B6"], bk(6)[:, h, :], lhsT=Pc[:, h, :], rhs=rr[:, h, :],
                    start=True, stop=True)
            p.I("vector", "tensor_copy", [], ["B6", N("vn")], out=dd["vn"][:], in_=bk(6))
            p.I("vector", "tensor_tensor", [N("ex")], ["B7", N("tq")], out=dd["tq"][:], in0=bk(7), in1=b4(ex[:, 0:4]), op=ALU.mult)
            for h in range(4):
                p.I("tensor", "matmul", [N("kdec"), N("vn")], ["B7"], bk(7)[:, h, :], lhsT=dd["kdec"][:, h, :], rhs=dd["vn"][:, h, :],
                    start=True, stop=True)
            for h in range(4):
                p.I("tensor", "matmul", [N("XA"), N("vn")], ["B6"], bk(6)[:, h, :], lhsT=XA[:, 4 + h, :], rhs=dd["vn"][:, h, :],
                    start=True, stop=True)
            for h in range(4):
                p.I("vector", "scalar_tensor_tensor", [N("ex")], ["B7", Sn], out=S32[d][:, h, :], in0=S32[d][:, h, :],
                    scalar=ex[:, 8 + h:9 + h], in1=bk(7)[:, h, :], op0=ALU.mult, op1=ALU.add)
            p.I("scalar", "copy", [Sn], [Sb], out=Sbf[d][:], in_=S32[d][:])
            od = dd["od"]
            p.I("vector", "tensor_tensor", [N("tq")], ["B6", N("od")], out=od[:], in0=bk(6), in1=dd["tq"][:], op=ALU.add)
            if not second:
                p.dma("sync", ost_d[i * 128:(i + 1) * 128, :], od[:].rearrange("p h d -> p (h d)"), reads=[N("od")],
                      writes=["ost%d" % i])
                return
            p.I("gpsimd", "tensor_tensor", [N("od"), N("ol")], [N("od")], out=od[:], in0=od[:], in1=dd["ol"][:], op=ALU.add)
            p.I("scalar", "activation", [N("od")], [N("sqo")], out=dd["sqo"][:], in_=od[:], func=AF.Square)
            p.I("vector", "tensor_reduce", [N("sqo")], [N("sso")], out=dd["sso"][:, :], in_=dd["sqo"][:], axis=AX.X, op=ALU.add)
            p.I("scalar", "activation", [N("sso"), "eps"], [N("sso")], out=dd["sso"][:, :], in_=dd["sso"][:, :], func=AF.Sqrt,
                scale=1.0 / 128, bias=eps[:, :])
            p.I("vector", "reciprocal", [N("sso")], [N("sso")], out=dd["sso"][:, :], in_=dd["sso"][:, :])
            p.I("vector", "tensor_tensor", [N("sso"), N("od")], [N("od")], out=od[:], in0=od[:], in1=b4(dd["sso"][:, :]), op=ALU.mult)
            p.I("gpsimd", "tensor_tensor", [N("od"), "on"], [N("od")], out=od[:], in0=od[:], in1=bh(on[:, :]), op=ALU.mult)
            p.I("gpsimd", "tensor_tensor", [N("od"), N("zc")], [N("ogc")], out=dd["ogc"][:], in0=od[:],
                in1=zc[:, :].rearrange("p (h d) -> p h d", h=4), op=ALU.mult)
            outs.append(p.dma("sync", og_d[i * 128:(i + 1) * 128, :], dd["ogc"][:].rearrange("p h d -> p (h d)"), reads=[N("ogc")]))

        pend = {}
        pend[0, 0] = pre(0, 0, 0, False)
        pend[1, nchunks - 1] = pre(1, nchunks - 1, 0, False)
        for t in range(nchunks):
            par = t % 2
            if t + 1 < nchunks:
                sec1 = (t + 1) >= nchunks // 2
                pend[0, t + 1] = pre(0, t + 1, 1 - par, sec1)
                pend[1, nchunks - 2 - t] = pre(1, nchunks - 2 - t, 1 - par, sec1)
            second = t >= nchunks // 2
            scan(0, t, par, second, *pend.pop((0, t)))
            scan(1, nchunks - 1 - t, par, second, *pend.pop((1, nchunks - 1 - t)))
        p.emit(final_wait_ops=outs)
    return nc, prog


def build_gdn_out_program():
    nc = bass.Bass("TRN2", target_bir_lowering=False)
    og_d = nc.dram_tensor("og", [TOK, 2048], BF16, kind="ExternalInput").ap()
    h_d = nc.dram_tensor("h", [TOK, D_MODEL], F32, kind="ExternalInput").ap()
    w_d = nc.dram_tensor("w_out", [2048, D_MODEL], F32, kind="ExternalInput").ap()
    out_d = nc.dram_tensor("out", [TOK, D_MODEL], F32, kind="ExternalOutput").ap()
    prog = Prog(nc)
    p = prog
    outs = []
    with contextlib.ExitStack() as st:
        c = Ctx(nc, prog, st)
        ident = make_ident(c)
        p.fence()
        w = c.sb("w", [128, 16, D_MODEL], BF16)
        wv = w_d.rearrange("(k p) c -> p k c", p=128)
        for k in range(0, 16, 4):
            p.dma("gpsimd", w[:, k:k + 4, :], wv[:, k:k + 4, :], writes=["w"])
        ogt = [c.sb("ogt%d" % i, [128, 2048], BF16) for i in range(2)]
        ht = [c.sb("ht%d" % i, [128, D_MODEL], F32) for i in range(2)]
        ho = [c.sb("ho%d" % i, [128, D_MODEL], F32) for i in range(2)]
        ogT = [c.sb("ogT%d" % i, [128, 16, 128], BF16) for i in range(2)]
        TP = c.ps("TP", [128, 16, 128], BF16)
        AC = c.ps("AC", [128, 1024], F32)
        for t in range(16):
            b = t % 2
            p.dma("sync", ogt[b][:], og_d[t * 128:(t + 1) * 128, :], writes=["ogt%d" % b])
            p.dma("sync", ht[b][:], h_d[t * 128:(t + 1) * 128, :], writes=["ht%d" % b])
            for k in range(16):
                p.I("tensor", "transpose", ["ogt%d" % b], ["TP"], out=TP[:, k, :], in_=ogt[b][:, k * 128:(k + 1) * 128],
                    identity=ident[:, :])
            p.I("vector", "tensor_copy", [], ["TP", "ogT%d" % b], out=ogT[b][:], in_=TP[:])
            for hf in range(2):
                for k in range(16):
                    p.I("tensor", "matmul", ["ogT%d" % b, "w"], ["AC"], AC[:, hf * 512:(hf + 1) * 512], lhsT=ogT[b][:, k, :],
                        rhs=w[:, k, hf * 512:(hf + 1) * 512], start=(k == 0), stop=(k == 15))
            p.I("vector", "tensor_tensor", ["ht%d" % b], ["AC", "ho%d" % b], out=ho[b][:], in0=AC[:, :], in1=ht[b][:], op=ALU.add)
            outs.append(p.dma("sync", out_d[t * 128:(t + 1) * 128, :], ho[b][:], reads=["ho%d" % b]))
        p.emit(final_wait_ops=outs)
    return nc, prog


_CACHE = {}


def _prog(name, builder, *a):
    key = (name,) + a
    if key not in _CACHE:
        _CACHE[key] = builder(*a)[0]
    return _CACHE[key]


def _rep(v, n=128):
    return np.ascontiguousarray(np.broadcast_to(np.asarray(v, np.float32)[None], (n,) + tuple(np.shape(v))))


def _halo(hb, c, pad):
    b, j = c // 4, c % 4
    z = np.zeros((pad, hb.shape[-1]), hb.dtype)
    ext = np.concatenate([z, hb[b], z], 0)
    return np.ascontiguousarray(ext[j * TOK:j * TOK + TOK + 2 * pad])


def _run(nc, maps):
    return run_bass_kernel_spmd(nc, maps, core_ids=list(range(NCORES))).results


def _ffn_layer(h, g, gf, wup, cwt, cb, wdn, final):
    nc = _prog("ffn", build_ffn_program, final)
    cw = np.concatenate([cwt, cb[None]], 0)
    cw_l = np.ascontiguousarray(cw.reshape(4, 44, 128).transpose(2, 1, 0))
    maps = []
    for c in range(NCORES):
        hh = _halo(h, c, 1)
        maps.append(dict(h=hh, g_ffn=_rep(g), g_fin=_rep(gf), w_up=wup, w_down=wdn, cw=cw_l))
    res = _run(nc, maps)
    return np.stack([np.concatenate([res[b * 4 + j]["out"] for j in range(4)], 0) for b in range(BATCH)], 0)


def kernel(x, norm_mix, norm_ffn, norm_final, attn_w_in, attn_q_norm, attn_k_norm, attn_w_out, gdn_w_in,
           gdn_conv_w, gdn_conv_b, gdn_a_log, gdn_dt_bias, gdn_o_norm, gdn_w_out, ffn_w_up, ffn_conv_w,
           ffn_conv_b, ffn_w_down):
    f32 = lambda a: np.ascontiguousarray(np.asarray(a, dtype=np.float32))
    x = f32(x)
    rope = rope_tables_np()
    nc = _prog("ap", build_attn_proj_program)
    maps = []
    for c in range(NCORES):
        b, j = c // 4, c % 4
        maps.append(dict(x=np.ascontiguousarray(x[b, j * TOK:(j + 1) * TOK]), g_mix=_rep(norm_mix[0]), w_in=f32(attn_w_in[0]),
                         qg=_rep(attn_q_norm[0]), kg=_rep(attn_k_norm[0]), rope=np.ascontiguousarray(rope[j * TOK:(j + 1) * TOK])))
    r1 = _run(nc, maps)
    nc = _prog("ac", build_attn_core_program)
    maps = []
    for c in range(NCORES):
        b, j = c // 4, c % 4
        kT = np.ascontiguousarray(np.concatenate([r1[b * 4 + i]["kT"] for i in range(4)], axis=2))
        v = np.ascontiguousarray(np.concatenate([r1[b * 4 + i]["v"] for i in range(4)], axis=1))
        maps.append(dict(qT=r1[c]["qT"], kT=kT, v=v, gate=r1[c]["gate"], x=np.ascontiguousarray(x[b, j * TOK:(j + 1) * TOK]),
                         w_out=f32(attn_w_out[0]), qg=_rep(attn_q_norm[0]), kg=_rep(attn_k_norm[0])))
    r2 = _run(nc, maps)
    h = np.stack([np.concatenate([r2[b * 4 + j]["out"] for j in range(4)], 0) for b in range(BATCH)], 0)
    h = _ffn_layer(h, f32(norm_ffn[0]), f32(norm_final), f32(ffn_w_up[0]), f32(ffn_conv_w[0]), f32(ffn_conv_b[0]),
                   f32(ffn_w_down[0]), False)
    nc = _prog("gp", build_gdn_proj_program)
    cwf = np.concatenate([f32(gdn_conv_w[0]), f32(gdn_conv_b[0])[None]], 0)
    cw_l = np.ascontiguousarray(cwf.reshape(6, 32, 128).transpose(2, 1, 0))
    ad = np.stack([f32(gdn_a_log[0]).reshape(32), f32(gdn_dt_bias[0]).reshape(32)], 0)
    maps = []
    for c in range(NCORES):
        maps.append(dict(h=_halo(h, c, 2), g_mix=_rep(norm_mix[1]), w_in=f32(gdn_w_in[0]), cw=cw_l, ad=_rep(ad)))
    r3 = _run(nc, maps)
    nc = _prog("gs", build_gdn_scan_program, 64)
    maps = []
    for c in range(NCORES):
        b, j = c // 4, c % 4
        grp = [r3[b * 4 + i] for i in range(4)]
        qT = np.ascontiguousarray(np.concatenate([g_["qT"][:, 2 * j:2 * j + 2] for g_ in grp], axis=2))
        kT = np.ascontiguousarray(np.concatenate([g_["kT"][:, 2 * j:2 * j + 2] for g_ in grp], axis=2))
        ktm = np.ascontiguousarray(np.concatenate([g_["k_tm"][:, 256 * j:256 * (j + 1)] for g_ in grp], axis=0))
        vtm = np.ascontiguousarray(np.concatenate([g_["v_tm"][:, 512 * j:512 * (j + 1)] for g_ in grp], axis=0))
        zz = np.ascontiguousarray(np.concatenate([g_["z"][:, 512 * j:512 * (j + 1)] for g_ in grp], axis=0))
        gball = np.concatenate([g_["gb"] for g_ in grp], axis=0)
        hs = slice(4 * j, 4 * j + 4)
        gb = np.ascontiguousarray(np.concatenate([gball[:, 0:16][:, hs], gball[:, 16:32][:, hs],
                                                  gball[:, 32:48][:, hs], gball[:, 48:64][:, hs]], axis=1))
        maps.append(dict(qT=qT, kT=kT, k_tm=ktm, v_tm=vtm, z=zz, gb=gb, on=_rep(gdn_o_norm[0]), bm=block_masks_np()))
    r4 = _run(nc, maps)
    nc = _prog("go", build_gdn_out_program)
    maps = []
    for c in range(NCORES):
        b, j = c // 4, c % 4
        og = np.ascontiguousarray(np.concatenate([r4[b * 4 + i]["og"][j * TOK:(j + 1) * TOK] for i in range(4)], axis=1))
        maps.append(dict(og=og, h=np.ascontiguousarray(h[b, j * TOK:(j + 1) * TOK]), w_out=f32(gdn_w_out[0])))
    r5 = _run(nc, maps)
    h = np.stack([np.concatenate([r5[b * 4 + j]["out"] for j in range(4)], 0) for b in range(BATCH)], 0)
    out = _ffn_layer(h, f32(norm_ffn[1]), f32(norm_final), f32(ffn_w_up[1]), f32(ffn_conv_w[1]), f32(ffn_conv_b[1]),
                     f32(ffn_w_down[1]), True)
    return out.astype(np.float32)
```

```python
import contextlib
import numpy as np
import concourse.bass as bass
import concourse.mybir as mybir
from concourse.bass_utils import run_bass_kernel_spmd

F32 = mybir.dt.float32
BF16 = mybir.dt.bfloat16
AF = mybir.ActivationFunctionType
ALU = mybir.AluOpType
AX = mybir.AxisListType

D_MODEL = 1024
BATCH = 2
SEQ = 8192
EPS = 1e-6
D_FF = 2816
NCORES = 8
TOK = 2048

COMPUTE = ("tensor", "vector", "scalar", "gpsimd")
N_DMA_SEMS = 24


class Prog:
    def __init__(self, nc, same_engine_sync=True):
        self.nc = nc
        self.ops = []
        self.last_w = {}
        self.readers = {}
        self.same_engine_sync = same_engine_sync
        self.n_dma = 0
        self.dma_sem_last = {}
        self.fence_deps = set()

    def fence(self):
        self.fence_deps = set(i for i, o in enumerate(self.ops) if not o["dma"])

    def op(self, eng, fn, reads=(), writes=(), dma=False):
        idx = len(self.ops)
        deps = set()
        for r in reads:
            if r in self.last_w:
                deps.add(self.last_w[r])
        for w in writes:
            if w in self.last_w:
                deps.add(self.last_w[w])
            for rd in self.readers.get(w, ()):
                deps.add(rd)
        sem_slot = None
        if dma:
            if eng == "gpsimd":
                self.n_dma_sw = getattr(self, "n_dma_sw", 0) + 1
                sem_slot = 16 + self.n_dma_sw % (N_DMA_SEMS - 16)
            else:
                sem_slot = self.n_dma % 16
            self.n_dma += 1
            prev = self.dma_sem_last.get(sem_slot)
            if prev is not None:
                deps.add(prev)
            self.dma_sem_last[sem_slot] = idx
        deps |= self.fence_deps
        deps.discard(idx)
        self.ops.append(dict(eng=eng, fn=fn, deps=deps, dma=dma, sem_slot=sem_slot))
        for w in writes:
            self.last_w[w] = idx
            self.readers[w] = []
        for r in reads:
            if r not in writes:
                self.readers.setdefault(r, []).append(idx)
        return idx

    def I(self, eng, method, reads, writes, *args, **kwargs):
        def fn(e, method=method, args=args, kwargs=kwargs):
            return getattr(e, method)(*args, **kwargs)
        idx = self.op(eng, fn, reads, writes)
        try:
            if method == "matmul":
                r = kwargs["rhs"]
                n = int(np.prod(r.shape[1:]))
                dur = 40 + n * (2.0 if r.dtype == F32 else 0.5)
            elif method == "transpose":
                dur = 110.0
            else:
                o = kwargs.get("out", args[0] if args else None)
                n = int(np.prod(o.shape[1:]))
                dur = {"scalar": 220 + n / 1.4, "vector": 90 + n / 0.96, "gpsimd": 160 + n / 0.45}[eng]
        except Exception:
            dur = 300.0
        self.ops[idx]["dur"] = dur
        return idx

    def dma(self, queue, out, in_, reads=(), writes=()):
        def fn(e, out=out, in_=in_):
            return e.dma_start(out=out, in_=in_)
        idx = self.op(queue, fn, reads, writes, dma=True)
        try:
            nbytes = int(np.prod(out.shape)) * (4 if out.dtype == F32 else 2)
        except Exception:
            nbytes = 1 << 16
        self.ops[idx]["dur"] = 2000 + nbytes / 150.0
        return idx

    def reorder(self, final_wait_ops):
        import heapq
        ops = self.ops
        n = len(ops)
        succ = [[] for _ in range(n)]
        indeg = [0] * n
        for i, o in enumerate(ops):
            for d in o["deps"]:
                succ[d].append(i)
            indeg[i] = len(o["deps"])
        ready_t = [0.0] * n
        heap = [(0.0, i) for i in range(n) if indeg[i] == 0]
        heapq.heapify(heap)
        eng_free = {}
        order = []
        LAT = 900.0
        while heap:
            rt, i = heapq.heappop(heap)
            o = ops[i]
            e = o["eng"]
            start = max(rt, eng_free.get(e, 0.0))
            dur = o.get("dur", 300.0)
            if o["dma"]:
                eng_free[e] = start + 60.0
                fin = start + dur
            else:
                eng_free[e] = start + dur
                fin = start + dur
            order.append(i)
            for s_ in succ[i]:
                so = ops[s_]
                lat = 0.0 if (so["eng"] == e and not o["dma"] and e == "tensor") else LAT
                t = fin + lat
                if t > ready_t[s_]:
                    ready_t[s_] = t
                indeg[s_] -= 1
                if indeg[s_] == 0:
                    heapq.heappush(heap, (max(ready_t[s_], 0.0), s_))
        assert len(order) == n
        pos = {old: new for new, old in enumerate(order)}
        new_ops = []
        for old in order:
            o = ops[old]
            o["deps"] = set(pos[d] for d in o["deps"])
            new_ops.append(o)
        self.ops = new_ops
        self.est_ns = max(eng_free.values()) if eng_free else 0
        return [pos[d] for d in final_wait_ops]

    def emit(self, final_wait_ops=(), schedule=True):
        nc = self.nc
        if schedule:
            final_wait_ops = self.reorder(list(final_wait_ops))
        ops = self.ops
        engines = ("sync",) + COMPUTE
        waited_eng = {e: {p: -1 for p in COMPUTE} for e in engines}
        waited_dma = {e: set() for e in engines}
        for i, o in enumerate(ops):
            e = o["eng"]
            need_eng = {}
            need_dma = []
            for d in sorted(o["deps"]):
                po = ops[d]
                if po["dma"]:
                    if d not in waited_dma[e]:
                        need_dma.append(d)
                        waited_dma[e].add(d)
                else:
                    pe = po["eng"]
                    if pe == e and (pe == "tensor" or not self.same_engine_sync):
                        continue
                    if d > waited_eng[e][pe]:
                        need_eng[pe] = max(need_eng.get(pe, -1), d)
            for pe, d in need_eng.items():
                waited_eng[e][pe] = d
            o["waits"] = list(need_eng.values()) + need_dma
        final_e = "sync"
        fw = []
        for d in final_wait_ops:
            fw.append(d)
        signal = set()
        for o in ops:
            for d in o["waits"]:
                signal.add(d)
        for d in fw:
            signal.add(d)
        cnt = {e: 0 for e in COMPUTE}
        dma_cnt = {}
        for i, o in enumerate(ops):
            if o["dma"]:
                s = o["sem_slot"]
                dma_cnt[s] = dma_cnt.get(s, 0) + 16
                o["sig"] = ("dma", s, dma_cnt[s])
            elif i in signal:
                cnt[o["eng"]] += 1
                o["sig"] = ("eng", o["eng"], cnt[o["eng"]])
            else:
                o["sig"] = None
        self.stats = dict(n_ops=len(ops), signals=dict(cnt), n_dma=self.n_dma)
        with contextlib.ExitStack() as st:
            esem = {e: st.enter_context(nc.semaphore("s_" + e)) for e in COMPUTE}
            dsem = [st.enter_context(nc.semaphore("d_%d" % k)) for k in range(N_DMA_SEMS)]
            block = st.enter_context(nc.Block())

            def semof(sig):
                if sig[0] == "dma":
                    return dsem[sig[1]], sig[2]
                return esem[sig[1]], sig[2]

            def run(ename):
                def body(eng):
                    for i, o in enumerate(ops):
                        if o["eng"] != ename:
                            continue
                        for d in o["waits"]:
                            s, v = semof(ops[d]["sig"])
                            eng.wait_ge(s, v)
                        ins = o["fn"](eng)
                        sig = o["sig"]
                        if sig is not None:
                            s, v = semof(sig)
                            ins.then_inc(s, 16 if sig[0] == "dma" else 1)
                    if ename == final_e:
                        for d in fw:
                            s, v = semof(ops[d]["sig"])
                            eng.wait_ge(s, v)
                return body

            block.sync(run("sync"))
            block.tensor(run("tensor"))
            block.vector(run("vector"))
            block.scalar(run("scalar"))
            block.gpsimd(run("gpsimd"))


class Ctx:
    def __init__(self, nc, prog, st):
        self.nc, self.p, self.st = nc, prog, st
        self.k = 0

    def sb(self, name, shape, dt):
        return self.st.enter_context(self.nc.sbuf_tensor(name, list(shape), dt))

    def ps(self, name, shape, dt):
        return self.st.enter_context(self.nc.psum_tensor(name, list(shape), dt))


def emit_rmsnorm_T(c, src_ap, n, g_sb, dstT, col0, tag, bufs):
    p = c.p
    sq, ss, rs, yb, pT, ident = (bufs[k] for k in ("sq", "ss", "rs", "yb", "pT", "ident"))
    nm = bufs["names"]
    p.I("scalar", "activation", [tag], [nm["sq"], nm["ss"]],
        out=sq[0:n, :], in_=src_ap, func=AF.Square, accum_out=ss[0:n, :])
    p.I("scalar", "activation", [nm["ss"], "eps"], [nm["rs"]],
        out=rs[0:n, :], in_=ss[0:n, :], func=AF.Sqrt, scale=1.0 / D_MODEL, bias=bufs["eps"][0:n, :])
    p.I("vector", "reciprocal", [nm["rs"]], [nm["rs"]], out=rs[0:n, :], in_=rs[0:n, :])
    p.I("vector", "scalar_tensor_tensor", [tag, nm["rs"], "g_sb"], [nm["yb"]],
        out=yb[0:n, :], in0=src_ap, scalar=rs[0:n, :], in1=g_sb[0:n, :], op0=ALU.mult, op1=ALU.mult)
    for k in range(8):
        p.I("tensor", "transpose", [nm["yb"]], [nm["pT"]],
            out=pT[:, k, 0:n], in_=yb[0:n, k * 128:(k + 1) * 128], identity=ident[0:n, 0:n])
    p.I("vector", "tensor_copy", [nm["pT"]], [bufs["dst_name"]],
        out=dstT[:, :, col0:col0 + n], in_=pT[:, :, 0:n])


def make_ident(c, name="ident"):
    ident = c.sb(name, [128, 128], BF16)
    c.p.I("gpsimd", "memset", [], [name], ident[:], 1.0)
    c.p.I("gpsimd", "affine_select", [name], [name], out=ident[:], in_=ident[:], pattern=[[-1, 128]],
          compare_op=ALU.is_equal, fill=0.0, base=0, channel_multiplier=1)
    return ident


def ffn_blocks(ntok):
    out = []
    t = 0
    while t < ntok:
        n = min(254, ntok - t)
        out.append((t, n))
        t += n
    return out


def build_ffn_program(final_norm, pre=None):
    nc = bass.Bass("TRN2", target_bir_lowering=False)
    h_d = nc.dram_tensor("h", [TOK + 2, D_MODEL], F32, kind="ExternalInput").ap()
    g_d = nc.dram_tensor("g_ffn", [128, D_MODEL], F32, kind="ExternalInput").ap()
    gf_d = nc.dram_tensor("g_fin", [128, D_MODEL], F32, kind="ExternalInput").ap()
    wup_d = nc.dram_tensor("w_up", [D_MODEL, 2 * D_FF], F32, kind="ExternalInput").ap()
    wdn_d = nc.dram_tensor("w_down", [D_FF, D_MODEL], F32, kind="ExternalInput").ap()
    cw_d = nc.dram_tensor("cw", [128, 44, 4], F32, kind="ExternalInput").ap()
    out_d = nc.dram_tensor("out", [TOK, D_MODEL], F32, kind="ExternalOutput").ap()
    prog = Prog(nc)
    with contextlib.ExitStack() as st:
        c = Ctx(nc, prog, st)
        emit_ffn(c, h_d, g_d, gf_d, wup_d, wdn_d, cw_d, out_d, final_norm)
    return nc, prog


def emit_ffn(c, h_d, g_d, gf_d, wup_d, wdn_d, cw_d, out_d, final_norm):
    p = c.p
    ident = make_ident(c)
    wup = c.sb("wup", [128, 8, 2 * D_FF], BF16)
    wdn = c.sb("wdn", [128, 22, D_MODEL], BF16)
    cw = c.sb("cw_sb", [128, 44, 4], F32)
    g_sb = c.sb("g_sb", [128, D_MODEL], F32)
    gf_sb = c.sb("gf_sb", [128, D_MODEL], F32)
    eps = c.sb("eps", [128, 1], F32)
    p.I("vector", "memset", [], ["eps"], eps[:], EPS)
    p.fence()
    p.dma("sync", g_sb[:], g_d, writes=["g_sb"])
    p.dma("sync", gf_sb[:], gf_d, writes=["gf_sb"])
    p.dma("sync", cw[:], cw_d, writes=["cw"])
    wup_v = wup_d.rearrange("(k p) c -> p k c", p=128)
    for k in range(8):
        for hh in range(2):
            p.dma("gpsimd", wup[:, k, hh * D_FF:(hh + 1) * D_FF], wup_v[:, k, hh * D_FF:(hh + 1) * D_FF],
                  writes=["wup"])
    wdn_v = wdn_d.rearrange("(j p) c -> p j c", p=128)
    for j0 in range(0, 22, 4):
        j1 = min(22, j0 + 4)
        p.dma("gpsimd", wdn[:, j0:j1, :], wdn_v[:, j0:j1, :], writes=["wdn"])

    NB = 2
    xt = [c.sb("xt%d" % i, [128, D_MODEL], F32) for i in range(NB)]
    sq = c.sb("sq", [128, D_MODEL], F32)
    ss = c.sb("ss", [128, 1], F32)
    rs = c.sb("rs", [128, 1], F32)
    yb = c.sb("yb", [128, D_MODEL], BF16)
    yT = [c.sb("yT%d" % i, [128, 8, 256], BF16) for i in range(2)]
    NR = 3
    t1 = [c.sb("t1_%d" % i, [128, 254], F32) for i in range(2 * NR)]
    t2 = [c.sb("t2_%d" % i, [128, 254], F32) for i in range(2 * NR)]
    t3 = [c.sb("t3_%d" % i, [128, 254], F32) for i in range(2 * NR)]
    sas = [c.sb("sa%d" % i, [128, 254], F32) for i in range(NR)]
    mT = [c.sb("mT%d" % i, [128, 254], BF16) for i in range(NR)]
    hres = [c.sb("hres%d" % i, [128, D_MODEL], F32) for i in range(2)]
    hout = [c.sb("hout%d" % i, [128, D_MODEL], F32) for i in range(2)]
    pT = c.ps("pT", [128, 8, 128], BF16)
    U = [c.ps("U%d" % i, [128, 512], F32) for i in range(3)]
    Dp = [c.ps("D%d" % i, [128, 512], F32) for i in range(4)]
    nbufs = dict(sq=sq, ss=ss, rs=rs, yb=yb, pT=pT, ident=ident, eps=eps,
                 names=dict(sq="sq", ss="ss", rs="rs", yb="yb", pT="pT"))

    out_ops = []
    xi = 0
    for bi, (t0, n) in enumerate(ffn_blocks(TOK)):
        ncol = n + 2
        yTb = yT[bi % 2]
        yname = "yT%d" % (bi % 2)
        nbufs["dst_name"] = yname
        r = 0
        while r < ncol:
            rn = min(128, ncol - r)
            x = xt[xi % NB]
            xname = "xt%d" % (xi % NB)
            xi += 1
            p.dma("sync", x[0:rn, :], h_d[t0 + r:t0 + r + rn, :], writes=[xname])
            emit_rmsnorm_T(c, x[0:rn, :], rn, g_sb, yTb, r, xname, nbufs)
            r += rn
        subs = []
        s0 = 0
        while s0 < n:
            sn = min(127, n - s0)
            subs.append((s0, sn))
            s0 += sn

        def up(j):
            Uj = U[j % 3]
            un = "U%d" % (j % 3)
            for half, cj in ((0, j), (1, j + 22)):
                for k in range(8):
                    p.I("tensor", "matmul", ["wup", yname], [un],
                        Uj[:, half * 256:half * 256 + ncol], lhsT=wup[:, k, cj * 128:(cj + 1) * 128],
                        rhs=yTb[:, k, 0:ncol], start=(k == 0), stop=(k == 7))

        def ew(j):
            Uj = U[j % 3]
            un = "U%d" % (j % 3)
            b = j % NR
            for half, cj in ((0, j), (1, j + 22)):
                o = half * 256
                ti_ = half * NR + b
                tt1, tt2, tt3 = t1[ti_], t2[ti_], t3[ti_]
                p.I("scalar", "activation", ["cw"], [un, "t1_%d" % ti_],
                    out=tt1[:, 0:n], in_=Uj[:, o + 1:o + 1 + n], func=AF.Identity,
                    scale=cw[:, cj, 1:2], bias=cw[:, cj, 3:4])
                p.I("vector", "scalar_tensor_tensor", ["cw", "t1_%d" % ti_], [un, "t2_%d" % ti_],
                    out=tt2[:, 0:n], in0=Uj[:, o:o + n], scalar=cw[:, cj, 0:1], in1=tt1[:, 0:n],
                    op0=ALU.mult, op1=ALU.add)
                p.I("vector", "scalar_tensor_tensor", ["cw", "t2_%d" % ti_], [un, "t3_%d" % ti_],
                    out=tt3[:, 0:n], in0=Uj[:, o + 2:o + 2 + n], scalar=cw[:, cj, 2:3], in1=tt2[:, 0:n],
                    op0=ALU.mult, op1=ALU.add)
            p.I("scalar", "activation", ["t3_%d" % b], ["sa%d" % b], out=sas[b][:, 0:n], in_=t3[b][:, 0:n], func=AF.Silu)
            p.I("gpsimd", "tensor_tensor", ["sa%d" % b, "t3_%d" % (NR + b)], ["mT%d" % b],
                out=mT[b][:, 0:n], in0=sas[b][:, 0:n], in1=t3[NR + b][:, 0:n], op=ALU.mult)

        def down(j):
            b = j % NR
            m = mT[b]
            for si, (s0, sn) in enumerate(subs):
                for hf in range(2):
                    p.I("tensor", "matmul", ["mT%d" % b, "wdn"], ["D%d" % (si * 2 + hf)],
                        Dp[si * 2 + hf][0:sn, :], lhsT=m[:, s0:s0 + sn],
                        rhs=wdn[:, j, hf * 512:(hf + 1) * 512], start=(j == 0), stop=(j == 21))

        for si, (s0, sn) in enumerate(subs):
            p.dma("sync", hres[si][0:sn, :], h_d[1 + t0 + s0:1 + t0 + s0 + sn, :], writes=["hres%d" % si])
        up(0)
        for j in range(22):
            if j + 1 < 22:
                up(j + 1)
            ew(j)
            down(j)
        for si, (s0, sn) in enumerate(subs):
            hr = hres[si]
            ho = hout[si]
            hn = "hout%d" % si
            for hf in range(2):
                p.I("vector", "tensor_tensor", ["hres%d" % si], ["D%d" % (si * 2 + hf), hn],
                    out=ho[0:sn, hf * 512:(hf + 1) * 512], in0=Dp[si * 2 + hf][0:sn, :],
                    in1=hr[0:sn, hf * 512:(hf + 1) * 512], op=ALU.add)
            if final_norm:
                p.I("scalar", "activation", [hn], ["sq", "ss"],
                    out=sq[0:sn, :], in_=ho[0:sn, :], func=AF.Square, accum_out=ss[0:sn, :])
                p.I("scalar", "activation", ["ss", "eps"], ["rs"],
                    out=rs[0:sn, :], in_=ss[0:sn, :], func=AF.Sqrt, scale=1.0 / D_MODEL, bias=eps[0:sn, :])
                p.I("vector", "reciprocal", ["rs"], ["rs"], out=rs[0:sn, :], in_=rs[0:sn, :])
                p.I("vector", "scalar_tensor_tensor", ["rs", "gf_sb"], [hn],
                    out=ho[0:sn, :], in0=ho[0:sn, :], scalar=rs[0:sn, :], in1=gf_sb[0:sn, :],
                    op0=ALU.mult, op1=ALU.mult)
            d = p.dma("sync", out_d[t0 + s0:t0 + s0 + sn, :], ho[0:sn, :], reads=[hn])
            out_ops.append(d)
    p.emit(final_wait_ops=out_ops)


def build_attn_proj_program():
    nc = bass.Bass("TRN2", target_bir_lowering=False)
    x_d = nc.dram_tensor("x", [TOK, D_MODEL], F32, kind="ExternalInput").ap()
    g_d = nc.dram_tensor("g_mix", [128, D_MODEL], F32, kind="ExternalInput").ap()
    w_d = nc.dram_tensor("w_in", [D_MODEL, 2560], F32, kind="ExternalInput").ap()
    qg_d = nc.dram_tensor("qg", [128, 128], F32, kind="ExternalInput").ap()
    kg_d = nc.dram_tensor("kg", [128, 128], F32, kind="ExternalInput").ap()
    rope_d = nc.dram_tensor("rope", [TOK, 128], F32, kind="ExternalInput").ap()
    qT_d = nc.dram_tensor("qT", [128, 8, TOK], BF16, kind="ExternalOutput").ap()
    kT_d = nc.dram_tensor("kT", [128, 2, TOK], BF16, kind="ExternalOutput").ap()
    v_d = nc.dram_tensor("v", [128, 16, 256], BF16, kind="ExternalOutput").ap()
    gt_d = nc.dram_tensor("gate", [128, 16, 1024], BF16, kind="ExternalOutput").ap()
    prog = Prog(nc)
    p = prog
    with contextlib.ExitStack() as st:
        c = Ctx(nc, prog, st)
        ident = make_ident(c)
        w = c.sb("w", [128, 8, 2560], BF16)
        g_sb = c.sb("g_sb", [128, D_MODEL], F32)
        gains = c.sb("gains", [128, 2, 128], F32)
        eps = c.sb("eps", [128, 1], F32)
        p.I("vector", "memset", [], ["eps"], eps[:], EPS)
        p.fence()
        p.dma("sync", g_sb[:], g_d, writes=["g_sb"])
        p.dma("sync", gains[:, 0, :], qg_d, writes=["gains"])
        p.dma("sync", gains[:, 1, :], kg_d, writes=["gains"])
        p.I("scalar", "mul", ["gains"], ["gains"], out=gains[:, 0, :], in_=gains[:, 0, :], mul=128.0 ** -0.5)
        w_v = w_d.rearrange("(k p) c -> p k c", p=128)
        for k in range(8):
            p.dma("gpsimd", w[:, k, :], w_v[:, k, :], writes=["w"])
        xt = [c.sb("xt%d" % i, [128, D_MODEL], F32) for i in range(2)]
        rp = [c.sb("rp%d" % i, [128, 2, 2, 32], F32) for i in range(2)]
        sq = c.sb("sq", [128, D_MODEL], F32)
        ss = c.sb("ss", [128, 1], F32)
        rs = c.sb("rs", [128, 1], F32)
        yb = c.sb("yb", [128, D_MODEL], BF16)
        yT = [c.sb("yT%d" % i, [128, 8, 128], BF16) for i in range(2)]
        sq10 = c.sb("sq10", [128, 10, 128], F32)
        ss10 = c.sb("ss10", [128, 10], F32)
        rs10 = c.sb("rs10", [128, 10], F32)
        z0 = c.sb("z0", [128, 10, 128], F32)
        zz = c.sb("zz", [128, 10, 128], F32)
        ra = [c.sb("ra%d" % i, [128, 10, 2, 32], F32) for i in range(4)]
        zr = c.sb("zr", [128, 10, 128], BF16)
        qT_all = c.sb("qT_all", [128, 8, TOK], BF16)
        kT_all = c.sb("kT_all", [128, 2, TOK], BF16)
        v_all = c.sb("v_all", [128, 16, 256], BF16)
        gt_all = c.sb("gt_all", [128, 16, 1024], BF16)
        Q2 = c.ps("Q2", [128, 1024], F32)
        G2 = c.ps("G2", [128, 1024], F32)
        KV = c.ps("KV", [128, 512], F32)
        pT = c.ps("pT", [128, 8, 128], BF16)
        TQ = c.ps("TQ", [128, 8, 128], BF16)
        TK = c.ps("TK", [128, 8, 128], BF16)
        nbufs = dict(sq=sq, ss=ss, rs=rs, yb=yb, pT=pT, ident=ident, eps=eps,
                     names=dict(sq="sq", ss="ss", rs="rs", yb="yb", pT="pT"))
        for t in range(16):
            b = t % 2
            x = xt[b]
            p.dma("sync", x[:], x_d[t * 128:(t + 1) * 128, :], writes=["xt%d" % b])
            p.dma("sync", rp[b][:], rope_d[t * 128:(t + 1) * 128, :].rearrange("p (a r e) -> p a r e", a=2, r=2),
                  writes=["rp%d" % b])
            nbufs["dst_name"] = "yT%d" % b
            emit_rmsnorm_T(c, x[:], 128, g_sb, yT[b], 0, "xt%d" % b, nbufs)
            for (dst, dn, c0) in ((Q2[:, 0:512], "Q2", 0), (Q2[:, 512:1024], "Q2", 512), (KV[:, :], "KV", 1024),
                                  (G2[:, 0:512], "G2", 1536), (G2[:, 512:1024], "G2", 2048)):
                for k in range(8):
                    p.I("tensor", "matmul", ["w", "yT%d" % b], [dn], dst, lhsT=yT[b][:, k, :],
                        rhs=w[:, k, c0:c0 + 512], start=(k == 0), stop=(k == 7))
            p.I("scalar", "activation", [], ["Q2", "sq10"], out=sq10[:, 0:8, :],
                in_=Q2[:, :].rearrange("p (h d) -> p h d", h=8), func=AF.Square)
            p.I("scalar", "activation", [], ["KV", "sq10"], out=sq10[:, 8:10, :],
                in_=KV[:, 0:256].rearrange("p (h d) -> p h d", h=2), func=AF.Square)
            p.I("vector", "tensor_reduce", ["sq10"], ["ss10"], out=ss10[:, :], in_=sq10[:, :, :], axis=AX.X, op=ALU.add)
            p.I("scalar", "activation", ["ss10", "eps"], ["rs10"], out=rs10[:, :], in_=ss10[:, :], func=AF.Sqrt,
                scale=1.0 / 128, bias=eps[:, :])
            p.I("vector", "reciprocal", ["rs10"], ["rs10"], out=rs10[:, :], in_=rs10[:, :])
            p.I("vector", "tensor_tensor", ["rs10"], ["Q2", "z0"], out=z0[:, 0:8, :],
                in0=Q2[:, :].rearrange("p (h d) -> p h d", h=8),
                in1=rs10[:, 0:8].unsqueeze(2).broadcast_to([128, 8, 128]), op=ALU.mult)
            p.I("vector", "tensor_tensor", ["rs10"], ["KV", "z0"], out=z0[:, 8:10, :],
                in0=KV[:, 0:256].rearrange("p (h d) -> p h d", h=2),
                in1=rs10[:, 8:10].unsqueeze(2).broadcast_to([128, 2, 128]), op=ALU.mult)
            p.I("gpsimd", "tensor_tensor", ["z0", "gains"], ["zz"], out=zz[:, 0:8, :], in0=z0[:, 0:8, :],
                in1=gains[:, 0:1, :].broadcast_to([128, 8, 128]), op=ALU.mult)
            p.I("gpsimd", "tensor_tensor", ["z0", "gains"], ["zz"], out=zz[:, 8:10, :], in0=z0[:, 8:10, :],
                in1=gains[:, 1:2, :].broadcast_to([128, 2, 128]), op=ALU.mult)
            zv = zz[:, :, :].rearrange("p h (r f e) -> p h r f e", r=2, f=2)
            ov = zr[:, :, :].rearrange("p h (r f e) -> p h r f e", r=2, f=2)
            z1, z2 = zv[:, :, :, 0, :], zv[:, :, :, 1, :]
            cosb = rp[b][:, 0:1, :, :].broadcast_to([128, 10, 2, 32])
            sinb = rp[b][:, 1:2, :, :].broadcast_to([128, 10, 2, 32])
            rn = "rp%d" % b
            p.I("vector", "tensor_tensor", ["zz", rn], ["ra0"], out=ra[0][:], in0=z1, in1=cosb, op=ALU.mult)
            p.I("gpsimd", "tensor_tensor", ["zz", rn], ["ra1"], out=ra[1][:], in0=z2, in1=sinb, op=ALU.mult)
            p.I("gpsimd", "tensor_tensor", ["zz", rn], ["ra2"], out=ra[2][:], in0=z1, in1=sinb, op=ALU.mult)
            p.I("vector", "tensor_tensor", ["zz", rn], ["ra3"], out=ra[3][:], in0=z2, in1=cosb, op=ALU.mult)
            p.I("vector", "tensor_tensor", ["ra0", "ra1"], ["zr"], out=ov[:, :, :, 0, :], in0=ra[0][:], in1=ra[1][:],
                op=ALU.subtract)
            p.I("gpsimd", "tensor_tensor", ["ra2", "ra3"], ["zr"], out=ov[:, :, :, 1, :], in0=ra[2][:], in1=ra[3][:],
                op=ALU.add)
            for h in range(8):
                p.I("tensor", "transpose", ["zr"], ["TQ"], out=TQ[:, h, :], in_=zr[:, h, :], identity=ident[:, :])
            for h in range(2):
                p.I("tensor", "transpose", ["zr"], ["TK"], out=TK[:, h, :], in_=zr[:, 8 + h, :], identity=ident[:, :])
            p.I("scalar", "copy", [], ["TQ", "qT_all"], out=qT_all[:, :, t * 128:(t + 1) * 128], in_=TQ[:, :, :])
            p.I("vector", "tensor_copy", [], ["TK", "kT_all"], out=kT_all[:, :, t * 128:(t + 1) * 128], in_=TK[:, 0:2, :])
            p.I("scalar", "copy", [], ["KV", "v_all"], out=v_all[:, t, :], in_=KV[:, 256:512])
            p.I("scalar", "activation", [], ["G2", "gt_all"], out=gt_all[:, t, :], in_=G2[:, :], func=AF.Sigmoid)
        outs = [p.dma("sync", qT_d, qT_all[:], reads=["qT_all"]),
                p.dma("sync", kT_d, kT_all[:], reads=["kT_all"]),
                p.dma("sync", v_d, v_all[:], reads=["v_all"]),
                p.dma("sync", gt_d, gt_all[:], reads=["gt_all"])]
        p.emit(final_wait_ops=outs)
    return nc, prog


def block_masks_np():
    i = np.arange(128)
    b32 = (i[:, None] // 32) == (i[None, :] // 32)
    b64 = (i[:, None] // 64) == (i[None, :] // 64)
    lo = i[:, None] > i[None, :]
    up = i[:, None] < i[None, :]
    ms = [t & m for t in (lo, up) for m in (b32, b64 & ~b32, ~b64)]
    return np.ascontiguousarray(np.stack(ms, axis=1).astype(np.float32))


def rope_tables_np():
    t = np.arange(SEQ)
    row = (t // 64).astype(np.float32)
    col = (t % 64).astype(np.float32)
    inv = (np.float32(10000.0) ** (-(np.arange(32, dtype=np.float32) * np.float32(2.0) / np.float32(64)))).astype(np.float32)
    ar = row[:, None] * inv[None, :]
    ac = col[:, None] * inv[None, :]
    return np.concatenate([np.cos(ar), np.cos(ac), np.sin(ar), np.sin(ac)], axis=1).astype(np.float32)


def build_attn_core_program(nheads=8, nqb=4, nsp=32):
    nc = bass.Bass("TRN2", target_bir_lowering=False)
    qT_d = nc.dram_tensor("qT", [128, 8, TOK], BF16, kind="ExternalInput").ap()
    kT_d = nc.dram_tensor("kT", [128, 2, SEQ], BF16, kind="ExternalInput").ap()
    v_d = nc.dram_tensor("v", [128, 64, 256], BF16, kind="ExternalInput").ap()
    gt_d = nc.dram_tensor("gate", [128, 16, 1024], BF16, kind="ExternalInput").ap()
    x_d = nc.dram_tensor("x", [TOK, D_MODEL], F32, kind="ExternalInput").ap()
    wo_d = nc.dram_tensor("w_out", [D_MODEL, D_MODEL], F32, kind="ExternalInput").ap()
    qg_d = nc.dram_tensor("qg", [128, 128], F32, kind="ExternalInput").ap()
    kg_d = nc.dram_tensor("kg", [128, 128], F32, kind="ExternalInput").ap()
    out_d = nc.dram_tensor("out", [TOK, D_MODEL], F32, kind="ExternalOutput").ap()
    prog = Prog(nc)
    p = prog
    with contextlib.ExitStack() as st:
        c = Ctx(nc, prog, st)
        ident = make_ident(c)
        qT = c.sb("qT_sb", [128, 8, TOK], BF16)
        kT = c.sb("kT_sb", [128, 2, SEQ], BF16)
        va = c.sb("v_aug", [128, 64, 2, 129], BF16)
        gt = c.sb("gt_sb", [128, 16, 1024], BF16)
        wo = c.sb("wo_sb", [128, 8, D_MODEL], BF16)
        gq = c.sb("gq", [128, 2, 128], F32)
        m2 = c.sb("m2", [128, 2], F32)
        negb = c.sb("negb", [128, 1], F32)
        p.fence()
        p.dma("sync", gq[:, 0, :], qg_d, writes=["gq"])
        p.dma("sync", gq[:, 1, :], kg_d, writes=["gq"])
        for h in range(8):
            p.dma("sync", qT[:, h, :], qT_d[:, h, :], writes=["qT"])
        for h in range(2):
            for s4 in range(4):
                p.dma("sync", kT[:, h, s4 * 2048:(s4 + 1) * 2048], kT_d[:, h, s4 * 2048:(s4 + 1) * 2048], writes=["kT"])
        p.I("gpsimd", "memset", [], ["va"], va[:, :, :, 128:129], 1.0)
        for s4 in range(4):
            p.dma("sync", va[:, s4 * 16:(s4 + 1) * 16, :, 0:128],
                  v_d[:, s4 * 16:(s4 + 1) * 16, :].rearrange("p s (h d) -> p s h d", h=2), writes=["va"])
        for t4 in range(4):
            p.dma("sync", gt[:, t4 * 4:(t4 + 1) * 4, :], gt_d[:, t4 * 4:(t4 + 1) * 4, :], writes=["gt"])
        wo_v = wo_d.rearrange("(k p) c -> p k c", p=128)
        for k in range(0, 8, 2):
            p.dma("gpsimd", wo[:, k:k + 2, :], wo_v[:, k:k + 2, :], writes=["wo"])
        p.I("vector", "tensor_tensor", ["gq"], ["gq"], out=gq[:], in0=gq[:], in1=gq[:], op=ALU.mult)
        p.I("vector", "tensor_reduce", ["gq"], ["m2"], out=m2[:, :], in_=gq[:, :, :], axis=AX.X, op=ALU.max)
        p.I("vector", "tensor_tensor", ["m2"], ["negb"], out=negb[:, :], in0=m2[:, 0:1], in1=m2[:, 1:2], op=ALU.mult)
        p.I("scalar", "activation", ["negb"], ["negb"], out=negb[:, :], in_=negb[:, :], func=AF.Sqrt, scale=128.0)
        p.I("scalar", "mul", ["negb"], ["negb"], out=negb[:, :], in_=negb[:, :], mul=-1.0)

        SC = [c.ps("SC%d" % i, [128, 1024], F32) for i in range(2)]
        O = [c.ps("O%d" % i, [128, 512], F32) for i in range(4)]
        NP = 3
        pT = [c.sb("pT%d" % i, [128, 1024], BF16) for i in range(NP)]
        rinv = c.sb("rinv", [128, 4], F32)
        step = 0
        for h in range(nheads):
            kv = h // 4
            for qb in range(nqb):
                def qk(sp, st_):
                    b = st_ % 2
                    for cc in range(2):
                        s = 2 * sp + cc
                        p.I("tensor", "matmul", ["kT", "qT"], ["SC%d" % b], SC[b][:, cc * 512:(cc + 1) * 512],
                            lhsT=kT[:, kv, s * 128:(s + 1) * 128], rhs=qT[:, h, qb * 512:(qb + 1) * 512],
                            start=True, stop=True)

                def ex(sp, st_):
                    b = st_ % 2
                    pb = st_ % NP
                    p.I("scalar", "activation", ["negb"], ["SC%d" % b, "pT%d" % pb], out=pT[pb][:, :], in_=SC[b][:, :],
                        func=AF.Exp, bias=negb[:, :])

                def pv(sp, st_):
                    pb = st_ % NP
                    for cc in range(2):
                        s = 2 * sp + cc
                        for qs in range(4):
                            p.I("tensor", "matmul", ["pT%d" % pb, "va"], ["O%d" % qs], O[qs][:, 0:129],
                                lhsT=pT[pb][:, cc * 512 + qs * 128:cc * 512 + (qs + 1) * 128], rhs=va[:, s, kv, :],
                                start=(sp == 0 and cc == 0), stop=(sp == nsp - 1 and cc == 1))

                qk(0, step)
                for sp in range(nsp):
                    if sp + 1 < nsp:
                        qk(sp + 1, step + 1)
                    ex(sp, step)
                    pv(sp, step)
                    step += 1
                for qs in range(4):
                    tile = qb * 4 + qs
                    p.I("vector", "reciprocal", [], ["O%d" % qs, "rinv"], out=rinv[:, qs:qs + 1], in_=O[qs][:, 128:129])
                    p.I("vector", "scalar_tensor_tensor", ["rinv"], ["O%d" % qs, "gt"],
                        out=gt[:, tile, h * 128:(h + 1) * 128], in0=O[qs][:, 0:128], scalar=rinv[:, qs:qs + 1],
                        in1=gt[:, tile, h * 128:(h + 1) * 128], op0=ALU.mult, op1=ALU.mult)
        xt = [c.sb("xt%d" % i, [128, D_MODEL], F32) for i in range(2)]
        ho = [c.sb("ho%d" % i, [128, D_MODEL], F32) for i in range(2)]
        ogT = [c.sb("ogT%d" % i, [128, 8, 128], BF16) for i in range(2)]
        TP = O[0].bitcast(BF16) if hasattr(O[0], "bitcast") else None
        outs = []
        for t in range(16):
            b = t % 2
            p.dma("sync", xt[b][:], x_d[t * 128:(t + 1) * 128, :], writes=["xt%d" % b])
            for k in range(8):
                p.I("tensor", "transpose", ["gt"], ["O0"], out=TP[:, k * 128:(k + 1) * 128],
                    in_=gt[:, t, k * 128:(k + 1) * 128], identity=ident[:, :])
            p.I("vector", "tensor_copy", [], ["O0", "ogT%d" % b], out=ogT[b][:, :, :],
                in_=TP[:, :].rearrange("p (k t) -> p k t", k=8))
            for hf in range(2):
                for k in range(8):
                    p.I("tensor", "matmul", ["ogT%d" % b, "wo"], ["SC0"], SC[0][:, hf * 512:(hf + 1) * 512],
                        lhsT=ogT[b][:, k, :], rhs=wo[:, k, hf * 512:(hf + 1) * 512], start=(k == 0), stop=(k == 7))
            p.I("vector", "tensor_tensor", ["xt%d" % b], ["SC0", "ho%d" % b], out=ho[b][:, :], in0=SC[0][:, :],
                in1=xt[b][:, :], op=ALU.add)
            outs.append(p.dma("sync", out_d[t * 128:(t + 1) * 128, :], ho[b][:, :], reads=["ho%d" % b]))
        p.emit(final_wait_ops=outs)
    return nc, prog


GDN_IN = 6208


def build_gdn_proj_program():
    nc = bass.Bass("TRN2", target_bir_lowering=False)
    HT = TOK + 4
    h_d = nc.dram_tensor("h", [HT, D_MODEL], F32, kind="ExternalInput").ap()
    g_d = nc.dram_tensor("g_mix", [128, D_MODEL], F32, kind="ExternalInput").ap()
    w_d = nc.dram_tensor("w_in", [D_MODEL, GDN_IN], F32, kind="ExternalInput").ap()
    cw_d = nc.dram_tensor("cw", [128, 32, 6], F32, kind="ExternalInput").ap()
    ad_d = nc.dram_tensor("ad", [128, 2, 32], F32, kind="ExternalInput").ap()
    qT_d = nc.dram_tensor("qT", [128, 8, TOK], BF16, kind="ExternalOutput").ap()
    kT_d = nc.dram_tensor("kT", [128, 8, TOK], BF16, kind="ExternalOutput").ap()
    ktm_d = nc.dram_tensor("k_tm", [TOK, 1024], BF16, kind="ExternalOutput").ap()
    vtm_d = nc.dram_tensor("v_tm", [TOK, 2048], BF16, kind="ExternalOutput").ap()
    z_d = nc.dram_tensor("z", [TOK, 2048], BF16, kind="ExternalOutput").ap()
    gb_d = nc.dram_tensor("gb", [TOK, 64], F32, kind="ExternalOutput").ap()
    prog = Prog(nc)
    p = prog
    outs = []
    with contextlib.ExitStack() as st:
        c = Ctx(nc, prog, st)
        ident = make_ident(c)
        ones = c.sb("ones", [128, 128], F32)
        p.I("gpsimd", "memset", [], ["ones"], ones[:], 1.0)
        g_sb = c.sb("g_sb", [128, D_MODEL], F32)
        cw = c.sb("cw_sb", [128, 32, 6], F32)
        ad = c.sb("ad_sb", [128, 2, 32], F32)
        eps = c.sb("eps", [128, 1], F32)
        one1 = c.sb("one1", [128, 1], F32)
        p.I("vector", "memset", [], ["eps"], eps[:], EPS)
        p.I("vector", "memset", [], ["one1"], one1[:], 1.0)
        p.fence()
        p.dma("sync", g_sb[:], g_d, writes=["g_sb"])
        p.dma("sync", cw[:], cw_d, writes=["cw"])
        p.dma("sync", ad[:], ad_d, writes=["ad"])
        p.I("scalar", "activation", ["ad"], ["ad"], out=ad[:, 0, :], in_=ad[:, 0, :], func=AF.Exp)
        p.I("scalar", "mul", ["ad"], ["ad"], out=ad[:, 0, :], in_=ad[:, 0, :], mul=-1.0)
        wv = w_d.rearrange("(k p) c -> p k c", p=128)
        wb = [c.sb("wb%d" % i, [128, 8, 1024], BF16) for i in range(2)]
        yT = c.sb("yT_all", [128, 8, HT], BF16)
        xt = [c.sb("xt%d" % i, [128, D_MODEL], F32) for i in range(2)]
        sq = c.sb("sq", [128, D_MODEL], F32)
        ss = c.sb("ss", [128, 1], F32)
        rs = c.sb("rs", [128, 1], F32)
        yb = c.sb("yb", [128, D_MODEL], BF16)
        pT = c.ps("pT", [128, 8, 128], BF16)
        U = [c.ps("U%d" % i, [128, 512], F32) for i in range(3)]
        L = c.ps("L", [128, 512], F32)
        TT = c.ps("TT", [128, 8, 128], BF16)
        Z = [U[0], U[1]]
        nbufs = dict(sq=sq, ss=ss, rs=rs, yb=yb, pT=pT, ident=ident, eps=eps, dst_name="yT",
                     names=dict(sq="sq", ss="ss", rs="rs", yb="yb", pT="pT"))
        r = 0
        xi = 0
        while r < HT:
            rn = min(128, HT - r)
            b = xi % 2
            xi += 1
            p.dma("sync", xt[b][0:rn, :], h_d[r:r + rn, :], writes=["xt%d" % b])
            emit_rmsnorm_T(c, xt[b][0:rn, :], rn, g_sb, yT, r, "xt%d" % b, nbufs)
            r += rn
        NR = 3
        tAs = [c.sb("tA%d" % i, [128, 508], F32) for i in range(NR)]
        tBs = [c.sb("tB%d" % i, [128, 508], F32) for i in range(NR)]
        acts = [c.sb("act%d" % i, [128, 508], F32) for i in range(NR)]
        sqvs = [c.sb("sqv%d" % i, [128, 508], F32) for i in range(NR)]
        rts = [c.sb("rt%d" % i, [128, 508], F32) for i in range(NR)]
        fm = [c.sb("fm%d" % i, [128, 8, 508], BF16) for i in range(2)]
        tm = [c.sb("tm%d" % i, [128, 1024], BF16) for i in range(2)]
        blocks = []
        t0 = 0
        while t0 < TOK:
            n = min(508, TOK - t0)
            blocks.append((t0, n))
            t0 += n
        fi = 0
        ti = 0
        ui = 0
        for grp in range(4):
            wbuf = wb[grp % 2]
            wn = "wb%d" % (grp % 2)
            for k in range(0, 8, 2):
                p.dma("gpsimd", wbuf[:, k:k + 2, :], wv[:, k:k + 2, grp * 1024:(grp + 1) * 1024], writes=[wn])
            for (t0, n) in blocks:
                ncol = n + 4
                fmb = fm[fi % 2]
                fn_ = "fm%d" % (fi % 2)
                fi += 1
                for j in range(8):
                    cj = grp * 8 + j
                    Uj = U[ui % 3]
                    un = "U%d" % (ui % 3)
                    rr = ui % NR
                    tA, tB, act, sqv, rt = tAs[rr], tBs[rr], acts[rr], sqvs[rr], rts[rr]
                    nA, nB, nact, nsqv, nrt = "tA%d" % rr, "tB%d" % rr, "act%d" % rr, "sqv%d" % rr, "rt%d" % rr
                    ui += 1
                    for k in range(8):
                        p.I("tensor", "matmul", [wn, "yT"], [un], Uj[:, 0:ncol], lhsT=wbuf[:, k, j * 128:(j + 1) * 128],
                            rhs=yT[:, k, t0:t0 + ncol], start=(k == 0), stop=(k == 7))
                    p.I("scalar", "activation", ["cw"], [un, nA], out=tA[:, 0:n], in_=Uj[:, 2:2 + n], func=AF.Identity,
                        scale=cw[:, cj, 2:3], bias=cw[:, cj, 5:6])
                    src, dst = tA, tB
                    sn_, dn_ = nA, nB
                    for tap in (0, 1, 3, 4):
                        p.I("vector", "scalar_tensor_tensor", ["cw", sn_], [un, dn_], out=dst[:, 0:n],
                            in0=Uj[:, tap:tap + n], scalar=cw[:, cj, tap:tap + 1], in1=src[:, 0:n],
                            op0=ALU.mult, op1=ALU.add)
                        src, dst = dst, src
                        sn_, dn_ = dn_, sn_
                    if grp < 2:
                        p.I("scalar", "activation", [nA], [nact], out=act[:, 0:n], in_=tA[:, 0:n], func=AF.Silu)
                        p.I("gpsimd", "tensor_tensor", [nact], [nsqv], out=sqv[:, 0:n], in0=act[:, 0:n], in1=act[:, 0:n], op=ALU.mult)
                        p.I("tensor", "matmul", ["ones", nsqv], ["L"], L[:, 0:n], lhsT=ones[:, :], rhs=sqv[:, 0:n],
                            start=True, stop=True)
                        p.I("scalar", "activation", ["eps"], ["L", nrt], out=rt[:, 0:n], in_=L[:, 0:n], func=AF.Sqrt,
                            bias=eps[:, :])
                        p.I("vector", "reciprocal", [nrt], [nrt], out=rt[:, 0:n], in_=rt[:, 0:n])
                        p.I("gpsimd", "scalar_tensor_tensor" if False else "tensor_tensor", [nact, nrt], [nsqv], out=sqv[:, 0:n],
                            in0=act[:, 0:n], in1=rt[:, 0:n], op=ALU.mult)
                        p.I("scalar", "mul", [nsqv], [fn_], out=fmb[:, j, 0:n], in_=sqv[:, 0:n],
                            mul=(128.0 ** -0.5 if grp == 0 else 1.0))
                    else:
                        p.I("scalar", "activation", [nA], [fn_], out=fmb[:, j, 0:n], in_=tA[:, 0:n], func=AF.Silu)
                if grp == 0:
                    outs.append(p.dma("sync", qT_d[:, :, t0:t0 + n], fmb[:, :, 0:n], reads=[fn_]))
                if grp == 1:
                    outs.append(p.dma("sync", kT_d[:, :, t0:t0 + n], fmb[:, :, 0:n], reads=[fn_]))
                if grp >= 1:
                    s0 = 0
                    while s0 < n:
                        sn = min(128, n - s0)
                        tmb = tm[ti % 2]
                        tn = "tm%d" % (ti % 2)
                        ti += 1
                        for j in range(8):
                            p.I("tensor", "transpose", [fn_], ["TT"], out=TT[0:sn, j, :], in_=fmb[:, j, s0:s0 + sn],
                                identity=ident[:, :])
                        p.I("vector", "tensor_copy", [], ["TT", tn], out=tmb[0:sn, :],
                            in_=TT[0:sn, :, :].rearrange("p j d -> p (j d)"))
                        if grp == 1:
                            dst_ap = ktm_d[t0 + s0:t0 + s0 + sn, :]
                        else:
                            dst_ap = vtm_d[t0 + s0:t0 + s0 + sn, (grp - 2) * 1024:(grp - 1) * 1024]
                        outs.append(p.dma("sync", dst_ap, tmb[0:sn, :], reads=[tn]))
                        s0 += sn
        wz = wb
        zs = [c.sb("zs%d" % i, [128, 1024], BF16) for i in range(2)]
        for half in range(2):
            wbuf = wz[half % 2]
            wn = "wb%d" % (half % 2)
            for k in range(0, 8, 2):
                p.dma("gpsimd", wbuf[:, k:k + 2, :], wv[:, k:k + 2, 4096 + half * 1024:4096 + (half + 1) * 1024], writes=[wn])
            for t in range(16):
                zb = zs[t % 2]
                zn = "zs%d" % (t % 2)
                for hf in range(2):
                    for k in range(8):
                        p.I("tensor", "matmul", [wn, "yT"], ["U%d" % hf], Z[hf][:, :], lhsT=yT[:, k, 2 + t * 128:2 + (t + 1) * 128],
                            rhs=wbuf[:, k, hf * 512:(hf + 1) * 512], start=(k == 0), stop=(k == 7))
                    p.I("scalar", "activation", [], ["U%d" % hf, zn], out=zb[:, hf * 512:(hf + 1) * 512], in_=Z[hf][:, :],
                        func=AF.Silu)
                outs.append(p.dma("sync", z_d[t * 128:(t + 1) * 128, half * 1024:(half + 1) * 1024], zb[:, :], reads=[zn]))
        wab = c.sb("wab", [128, 8, 64], BF16)
        p.dma("gpsimd", wab[:, :, :], wv[:, :, 6144:6208], writes=["wab"])
        xs = c.sb("xs", [128, 32], F32)
        ax = c.sb("ax", [128, 32], F32)
        gbs = [c.sb("gbs%d" % i, [128, 64], F32) for i in range(2)]
        for t in range(16):
            gbt = gbs[t % 2]
            gn = "gbs%d" % (t % 2)
            for k in range(8):
                p.I("tensor", "matmul", ["wab", "yT"], ["U0"], Z[0][:, 0:64], lhsT=yT[:, k, 2 + t * 128:2 + (t + 1) * 128],
                    rhs=wab[:, k, :], start=(k == 0), stop=(k == 7))
            Zv = Z[0][:, 0:64].rearrange("p (d a h) -> p d a h", d=2, a=2)
            p.I("vector", "tensor_tensor", ["ad"], ["U0", "xs"], out=xs[:, :].rearrange("p (d h) -> p d h", d=2),
                in0=Zv[:, :, 0, :], in1=ad[:, 1, :].rearrange("p (d h) -> p d h", d=2), op=ALU.add)
            p.I("scalar", "activation", [], ["U0", gn], out=gbt[:, 32:64].rearrange("p (d h) -> p d h", d=2),
                in_=Zv[:, :, 1, :], func=AF.Sigmoid)
            p.I("scalar", "activation", ["xs"], ["ax"], out=ax[:, :], in_=xs[:, :], func=AF.Abs)
            p.I("scalar", "activation", ["ax"], ["ax"], out=ax[:, :], in_=ax[:, :], func=AF.Exp, scale=-1.0)
            p.I("scalar", "activation", ["ax", "one1"], ["ax"], out=ax[:, :], in_=ax[:, :], func=AF.Ln, bias=one1[:, :])
            p.I("vector", "scalar_tensor_tensor", ["xs", "ax"], ["xs"], out=xs[:, :], in0=xs[:, :], scalar=0.0,
                in1=ax[:, :], op0=ALU.max, op1=ALU.add)
            p.I("vector", "tensor_tensor", ["xs", "ad"], [gn], out=gbt[:, 0:32], in0=xs[:, :], in1=ad[:, 0, :], op=ALU.mult)
            outs.append(p.dma("sync", gb_d[t * 128:(t + 1) * 128, :], gbt[:, :], reads=[gn]))
        p.emit(final_wait_ops=outs)
    return nc, prog


def build_gdn_scan_program(nchunks=64):
    nc = bass.Bass("TRN2", target_bir_lowering=False)
    S_ = nchunks * 128
    qT_d = nc.dram_tensor("qT", [128, 2, S_], BF16, kind="ExternalInput").ap()
    kT_d = nc.dram_tensor("kT", [128, 2, S_], BF16, kind="ExternalInput").ap()
    ktm_d = nc.dram_tensor("k_tm", [S_, 256], BF16, kind="ExternalInput").ap()
    vtm_d = nc.dram_tensor("v_tm", [S_, 512], BF16, kind="ExternalInput").ap()
    z_d = nc.dram_tensor("z", [S_, 512], BF16, kind="ExternalInput").ap()
    gb_d = nc.dram_tensor("gb", [S_, 16], F32, kind="ExternalInput").ap()
    on_d = nc.dram_tensor("on", [128, 128], F32, kind="ExternalInput").ap()
    bm_d = nc.dram_tensor("bm", [128, 6, 128], F32, kind="ExternalInput").ap()
    og_d = nc.dram_tensor("og", [S_, 512], BF16, kind="ExternalOutput").ap()
    ost_d = nc.dram_tensor("ost", [S_, 512], F32, kind="Internal").ap()
    prog = Prog(nc)
    p = prog
    outs = []
    with contextlib.ExitStack() as st:
        c = Ctx(nc, prog, st)
        ident = make_ident(c)
        ones = c.sb("ones", [128, 128], F32)
        p.I("gpsimd", "memset", [], ["ones"], ones[:], 1.0)
        masks = {}
        for nm_, cmp, sg in (("LE", ALU.is_ge, -1), ("GT", ALU.is_gt, 1), ("GE", ALU.is_ge, 1), ("LT", ALU.is_gt, -1)):
            m = c.sb("m" + nm_, [128, 128], F32)
            p.I("gpsimd", "memset", [], ["m" + nm_], m[:], 1.0)
            p.I("gpsimd", "affine_select", ["m" + nm_], ["m" + nm_], out=m[:], in_=m[:], pattern=[[-sg, 128]],
                compare_op=cmp, fill=0.0, base=0, channel_multiplier=sg)
            masks[nm_] = m
        on = c.sb("on_sb", [128, 128], F32)
        eps = c.sb("eps", [128, 1], F32)
        p.I("vector", "memset", [], ["eps"], eps[:], EPS)
        p.fence()
        p.dma("sync", on[:], on_d, writes=["on"])
        bm = c.sb("bm_sb", [128, 6, 128], F32)
        p.dma("sync", bm[:], bm_d, writes=["bm"])
        B = [c.ps("B%d" % i, [128, 512], F32) for i in range(8)]

        def bk(i):
            return B[i][:, :].rearrange("p (h d) -> p h d", h=4)

        D = {}
        for d in range(2):
            for par in range(2):
                dd = {}
                sfx = "_%d%d" % (d, par)
                for nm_, shp, dt_ in (("qTc", [128, 2, 128], BF16), ("kTc", [128, 2, 128], BF16), ("ktm", [128, 256], BF16),
                                      ("vtm", [128, 512], BF16), ("zc", [128, 512], BF16), ("gb", [128, 16], F32),
                                      ("X0", [128, 4, 128], BF16), ("X1", [128, 4, 128], BF16),
                                      ("Y0", [128, 4, 128], BF16), ("Y1", [128, 4, 128], BF16),
                                      ("P0", [128, 4, 128], BF16), ("P1", [128, 4, 128], BF16),
                                      ("rhsD", [128, 4, 128], F32), ("E", [128, 4, 128], F32), ("Es", [128, 4, 128], F32),
                                      ("Ei", [128, 4, 128], F32), ("KQ", [128, 4, 128], F32), ("attn", [128, 4, 128], BF16),
                                      ("XA", [128, 8, 128], BF16), ("kbg", [128, 4, 128], BF16), ("vb", [128, 4, 128], BF16),
                                      ("kdec", [128, 4, 128], BF16), ("nwT", [128, 4, 128], BF16),
                                      ("gs", [128, 8], F32), ("ex", [128, 12], F32), ("nbe", [128, 4], F32),
                                      ("sso", [128, 4], F32), ("ol", [128, 4, 128], F32),
                                      ("Pf0", [128, 4, 128], F32), ("Pf1", [128, 4, 128], F32),
                                      ("N1", [128, 4, 128], BF16), ("N2", [128, 4, 128], BF16), ("N1T", [128, 4, 128], BF16),
                                      ("WP", [128, 4, 128], BF16), ("WQ", [128, 4, 128], BF16),
                                      ("Q0", [128, 4, 128], BF16), ("Q1", [128, 4, 128], BF16)):
                    dd[nm_] = c.sb(nm_ + sfx, shp, dt_)
                dd["tq"], dd["od"], dd["sqo"] = dd["rhsD"], dd["E"], dd["Es"]
                dd["vn"], dd["ogc"] = dd["attn"], dd["kbg"]
                dd["_alias"] = dict(tq="rhsD", od="E", sqo="Es", vn="attn", ogc="kbg")
                D[d, par] = dd
        S32, Sbf = {}, {}
        for d in range(2):
            S32[d] = c.sb("S32_%d" % d, [128, 4, 128], F32)
            Sbf[d] = c.sb("Sbf_%d" % d, [128, 4, 128], BF16)
            p.I("vector", "memset", [], ["S32_%d" % d], S32[d][:], 0.0)
            p.I("vector", "memset", [], ["Sbf_%d" % d], Sbf[d][:], 0.0)

        def b4(ap2):
            return ap2.unsqueeze(2).broadcast_to([128, 4, 128])

        def bh(ap2):
            return ap2.unsqueeze(1).broadcast_to([128, 4, 128])

        def rep2(ap3):
            return ap3.unsqueeze(2).broadcast_to([128, 2, 2, 128])

        def v4(ap3):
            return ap3.rearrange("p (q r) d -> p q r d", q=2)

        def pre(d, i, par, second):
            dd = D[d, par]
            sfx = "_%d%d" % (d, par)
            al = dd["_alias"]
            N = lambda x: al.get(x, x) + sfx
            b0, b1, b2 = 3 * d, 3 * d + 1, 3 * d + 2
            n0, n1, n2 = "B%d" % b0, "B%d" % b1, "B%d" % b2
            r0 = i * 128
            qTc, kTc, ktm, vtm, zc, gb = (dd[k] for k in ("qTc", "kTc", "ktm", "vtm", "zc", "gb"))
            p.dma("sync", qTc[:], qT_d[:, :, r0:r0 + 128], writes=[N("qTc")])
            p.dma("sync", kTc[:], kT_d[:, :, r0:r0 + 128], writes=[N("kTc")])
            p.dma("sync", ktm[:], ktm_d[r0:r0 + 128, :], writes=[N("ktm")])
            p.dma("sync", vtm[:], vtm_d[r0:r0 + 128, :], writes=[N("vtm")])
            p.dma("sync", zc[:], z_d[r0:r0 + 128, :], writes=[N("zc")])
            p.dma("sync", gb[:], gb_d[r0:r0 + 128, :], writes=[N("gb")])
            Mm = masks["LE"] if d == 0 else masks["GE"]
            Vm = masks["GT"] if d == 0 else masks["LT"]
            mS = masks["GT"] if d == 0 else masks["LT"]
            mI = masks["GE"] if d == 0 else masks["LE"]
            g_ = gb[:, d * 4:(d + 1) * 4]
            be = gb[:, 8 + d * 4:8 + (d + 1) * 4]
            gs, ex, nbe = dd["gs"], dd["ex"], dd["nbe"]
            for q in range(2):
                p.I("tensor", "matmul", [N("kTc")], [n0], bk(b0)[:, q, :], lhsT=kTc[:, q, :], rhs=kTc[:, q, :],
                    start=True, stop=True)
                p.I("tensor", "matmul", [N("kTc"), N("qTc")], [n0], bk(b0)[:, 2 + q, :], lhsT=qTc[:, q, :],
                    rhs=kTc[:, q, :], start=True, stop=True)
            p.I("scalar", "copy", [], [n0, N("KQ")], out=dd["KQ"][:], in_=bk(b0))
            p.I("tensor", "matmul", [N("gb")], [n1], B[b1][:, 0:4], lhsT=Mm[:, :], rhs=g_, start=True, stop=True)
            p.I("tensor", "matmul", [N("gb"), "ones"], [n1], B[b1][:, 4:8], lhsT=ones[:, :], rhs=g_, start=True, stop=True)
            p.I("vector", "tensor_copy", [], [n1, N("gs")], out=gs[:, :], in_=B[b1][:, 0:8])
            p.I("scalar", "activation", [N("gs")], [N("ex")], out=ex[:, 0:4], in_=gs[:, 0:4], func=AF.Exp)
            p.I("vector", "tensor_tensor", [N("gs")], [N("gs")], out=gs[:, 0:4], in0=gs[:, 4:8], in1=gs[:, 0:4], op=ALU.subtract)
            p.I("scalar", "activation", [N("gs")], [N("ex")], out=ex[:, 4:12], in_=gs[:, 0:8], func=AF.Exp)
            p.I("vector", "tensor_scalar", [N("gb")], [N("nbe")], out=nbe[:, :], in0=be, scalar1=-1.0, scalar2=None, op0=ALU.mult)
            p.I("vector", "tensor_tensor", [N("nbe"), N("ex")], [N("sso")], out=dd["sso"][:, :], in0=nbe[:, :], in1=ex[:, 0:4], op=ALU.mult)
            k3 = ktm[:, :].rearrange("p (q d) -> p q d", q=2)
            for h in range(4):
                p.I("scalar", "activation", [N("ktm"), N("ex")], [N("kdec")], out=dd["kdec"][:, h, :], in_=k3[:, h // 2, :],
                    func=AF.Identity, scale=ex[:, 4 + h:5 + h])
                p.I("scalar", "activation", [N("vtm"), N("gb")], [N("vb")], out=dd["vb"][:, h, :], in_=vtm[:, h * 128:(h + 1) * 128],
                    func=AF.Identity, scale=be[:, h:h + 1])
            p.I("gpsimd", "tensor_tensor", [N("gb")], [N("rhsD")], out=dd["rhsD"][:], in0=bh(Vm[:, :]), in1=b4(g_), op=ALU.mult)
            p.I("tensor", "matmul", [N("rhsD")], [n2], B[b2][:, :], lhsT=Mm[:, :], rhs=dd["rhsD"][:].rearrange("p h s -> p (h s)"),
                start=True, stop=True)
            p.I("scalar", "activation", [], [n2, N("E")], out=dd["E"][:], in_=bk(b2), func=AF.Exp)
            p.I("gpsimd", "tensor_tensor", [N("E")], [N("Ei")], out=dd["Ei"][:], in0=dd["E"][:], in1=bh(mI[:, :]), op=ALU.mult)
            p.I("vector", "tensor_tensor", [N("E"), N("nbe")], [N("Es")], out=dd["Es"][:], in0=dd["E"][:], in1=b4(nbe[:, :]), op=ALU.mult)
            Y0 = dd["Y0"]
            p.I("vector", "tensor_tensor", [N("Es"), N("KQ")], [N("Es")], out=v4(dd["Es"][:]), in0=v4(dd["Es"][:]),
                in1=rep2(dd["KQ"][:, 0:2, :]), op=ALU.mult)
            p.I("gpsimd", "tensor_tensor", [N("Es"), "bm"], [N("Y0")], out=Y0[:], in0=dd["Es"][:], in1=bh(bm[:, 3 * d + 0, :]), op=ALU.mult)
            p.I("gpsimd", "tensor_tensor", [N("Es"), "bm"], [N("N1")], out=dd["N1"][:], in0=dd["Es"][:], in1=bh(bm[:, 3 * d + 1, :]), op=ALU.mult)
            p.I("vector", "tensor_tensor", [N("Es"), "bm"], [N("N2")], out=dd["N2"][:], in0=dd["Es"][:], in1=bh(bm[:, 3 * d + 2, :]), op=ALU.mult)
            p.I("gpsimd", "tensor_tensor", [N("Ei"), N("KQ")], [N("attn")], out=v4(dd["attn"][:]), in0=v4(dd["Ei"][:]),
                in1=rep2(dd["KQ"][:, 2:4, :]), op=ALU.mult)
            TRb = B[b0].bitcast(BF16)
            TRb1 = B[b1].bitcast(BF16)
            for h in range(4):
                p.I("tensor", "transpose", [N("Y0")], [n0], out=TRb[:, h * 128:(h + 1) * 128], in_=Y0[:, h, :],
                    identity=ident[:, :])
            for h in range(4):
                p.I("tensor", "transpose", [N("attn")], [n0], out=TRb[:, (4 + h) * 128:(5 + h) * 128], in_=dd["attn"][:, h, :],
                    identity=ident[:, :])
            for h in range(4):
                p.I("tensor", "transpose", [N("N1")], [n1], out=TRb1[:, h * 128:(h + 1) * 128], in_=dd["N1"][:, h, :],
                    identity=ident[:, :])
            XA = dd["XA"]
            p.I("scalar", "copy", [], [n0, N("XA")], out=XA[:], in_=TRb[:, :].rearrange("p (h s) -> p h s", h=8))
            p.I("scalar", "copy", [], [n1, N("N1T")], out=dd["N1T"][:],
                in_=TRb1[:, 0:512].rearrange("p (h s) -> p h s", h=4))
            P1 = dd["P0"]
            p.I("vector", "tensor_tensor", [N("XA")], [N("Pf0")], out=dd["Pf0"][:], in0=XA[:, 0:4, :], in1=bh(ident[:, :]), op=ALU.add)
            p.I("gpsimd", "tensor_copy", [N("Pf0")], [N("P0")], out=P1[:], in_=dd["Pf0"][:])
            Pf, Pfn = dd["Pf0"], N("Pf0")
            Xc, Xn_ = XA[:, 0:4, :], N("XA")
            Yc, Yn_ = Y0, N("Y0")
            Pc, Pn_ = P1, N("P0")
            for it in range(5):
                nb = (it + 1) % 2
                doP = it >= 1
                doX = it <= 2
                doY = it <= 3
                if doP:
                    for h in range(4):
                        p.I("tensor", "matmul", [Yn_, Pn_], [n2], bk(b2)[:, h, :], lhsT=Yc[:, h, :], rhs=Pc[:, h, :], start=True, stop=True)
                if doX:
                    for h in range(4):
                        p.I("tensor", "matmul", [Yn_, Xn_], [n0], bk(b0)[:, h, :], lhsT=Yc[:, h, :], rhs=Xc[:, h, :], start=True, stop=True)
                if doY:
                    for h in range(4):
                        p.I("tensor", "matmul", [Xn_, Yn_], [n1], bk(b1)[:, h, :], lhsT=Xc[:, h, :], rhs=Yc[:, h, :], start=True, stop=True)
                if doP:
                    Pnew, Pnn = dd["P%d" % nb], N("P%d" % nb)
                    Pfnew, Pfnn = dd["Pf%d" % nb], N("Pf%d" % nb)
                    p.I("vector", "tensor_tensor", [Pfn], [n2, Pfnn], out=Pfnew[:], in0=bk(b2), in1=Pf[:], op=ALU.add)
                    p.I("scalar", "copy", [Pfnn], [Pnn], out=Pnew[:], in_=Pfnew[:])
                    Pf, Pfn = Pfnew, Pfnn
                    Pc, Pn_ = Pnew, Pnn
                if doX:
                    Xnew, Xnn = dd["X%d" % nb], N("X%d" % nb)
                    p.I("scalar", "copy", [], [n0, Xnn], out=Xnew[:], in_=bk(b0))
                if doY:
                    Ynew, Ynn = dd["Y%d" % nb], N("Y%d" % nb)
                    if it % 2 == 0:
                        p.I("scalar", "copy", [], [n1, Ynn], out=Ynew[:], in_=bk(b1))
                    else:
                        p.I("vector", "tensor_copy", [], [n1, Ynn], out=Ynew[:], in_=bk(b1))
                if doX:
                    Xc, Xn_ = Xnew[:], Xnn
                if doY:
                    Yc, Yn_ = Ynew, Ynn
            for h in range(4):
                p.I("tensor", "transpose", [Pn_], [n0], out=TRb[:, h * 128:(h + 1) * 128], in_=Pc[:, h, :], identity=ident[:, :])
            Q0, Q1 = dd["Q0"], dd["Q1"]
            p.I("scalar", "copy", [], [n0, N("Q0")], out=Q0[:], in_=TRb[:, 0:512].rearrange("p (h s) -> p h s", h=4))
            for h in range(4):
                p.I("tensor", "matmul", [N("N1"), Pn_], [n1], bk(b1)[:, h, :], lhsT=dd["N1"][:, h, :], rhs=Pc[:, h, :], start=True, stop=True)
            for h in range(4):
                p.I("tensor", "matmul", [N("N1T"), N("Q0")], [n2], bk(b2)[:, h, :], lhsT=dd["N1T"][:, h, :], rhs=Q0[:, h, :], start=True, stop=True)
            p.I("scalar", "copy", [], [n1, N("WP")], out=dd["WP"][:], in_=bk(b1))
            p.I("vector", "tensor_copy", [], [n2, N("WQ")], out=dd["WQ"][:], in_=bk(b2))
            for h in range(4):
                p.I("tensor", "matmul", [N("Q0"), N("WP")], [n0], bk(b0)[:, h, :], lhsT=Q0[:, h, :], rhs=dd["WP"][:, h, :], start=True, stop=True)
            for h in range(4):
                p.I("tensor", "matmul", [Pn_, N("WQ")], [n1], bk(b1)[:, h, :], lhsT=Pc[:, h, :], rhs=dd["WQ"][:, h, :], start=True, stop=True)
            nbp = 0 if Pc is dd["P1"] else 1
            Pnew, Pnn = dd["P%d" % nbp], N("P%d" % nbp)
            Pfnew, Pfnn = dd["Pf%d" % nbp], N("Pf%d" % nbp)
            p.I("vector", "tensor_tensor", [Pfn], [n0, Pfnn], out=Pfnew[:], in0=bk(b0), in1=Pf[:], op=ALU.add)
            p.I("scalar", "copy", [Pfnn], [Pnn], out=Pnew[:], in_=Pfnew[:])
            p.I("vector", "tensor_tensor", [N("Q0")], [n1, N("Q1")], out=Q1[:], in0=bk(b1), in1=Q0[:], op=ALU.add)
            Pf, Pfn, Pc, Pn_ = Pfnew, Pfnn, Pnew, Pnn
            for h in range(4):
                p.I("tensor", "matmul", [N("N2"), Pn_], [n2], bk(b2)[:, h, :], lhsT=dd["N2"][:, h, :], rhs=Pc[:, h, :], start=True, stop=True)
            p.I("scalar", "copy", [], [n2, N("WP")], out=dd["WP"][:], in_=bk(b2))
            for h in range(4):
                p.I("tensor", "matmul", [N("Q1"), N("WP")], [n0], bk(b0)[:, h, :], lhsT=Q1[:, h, :], rhs=dd["WP"][:, h, :], start=True, stop=True)
            nbp = 1 - nbp
            Pnew, Pnn = dd["P%d" % nbp], N("P%d" % nbp)
            p.I("vector", "tensor_tensor", [Pfn], [n0, Pnn], out=Pnew[:], in0=bk(b0), in1=Pf[:], op=ALU.add)
            Pc, Pn_ = Pnew, Pnn
            return Pc, Pn_

        def scan(d, i, par, second, Pc, Pn_):
            dd = D[d, par]
            sfx = "_%d%d" % (d, par)
            al = dd["_alias"]
            N = lambda x: al.get(x, x) + sfx
            Sn, Sb = "S32_%d" % d, "Sbf_%d" % d
            XA, qTc, zc, ex = dd["XA"], dd["qTc"], dd["zc"], dd["ex"]
            if second:
                p.dma("sync", dd["ol"][:], ost_d[i * 128:(i + 1) * 128, :].rearrange("p (h d) -> p h d", h=4),
                      reads=["ost%d" % i], writes=[N("ol")])
            kTc = dd["kTc"]
            for h in range(4):
                p.I("tensor", "matmul", [N("kTc"), Sb], ["B6"], bk(6)[:, h, :], lhsT=kTc[:, h // 2, :], rhs=Sbf[d][:, h, :],
                    start=True, stop=True)
            for h in range(4):
                p.I("tensor", "matmul", [N("qTc"), Sb], ["B7"], bk(7)[:, h, :], lhsT=qTc[:, h // 2, :], rhs=Sbf[d][:, h, :],
                    start=True, stop=True)
            rr = dd["kbg"]
            for h in range(4):
                p.I("vector", "scalar_tensor_tensor", [N("sso"), N("vb")], ["B6", N("kbg")], out=rr[:, h, :], in0=bk(6)[:, h, :],
                    scalar=dd["sso"][:, h:h + 1], in1=dd["vb"][:, h, :], op0=ALU.mult, op1=ALU.add)
            for h in range(4):
                p.I("tensor", "matmul", [Pn_, N("kbg")], ["B6"], bk(6)[:, h, :], lhsT=Pc[:, h, :], rhs=rr[:, h, :],
                    start=True, stop=True)
            p.I("vector", "tensor_copy", [], ["B6", N("vn")], out=dd["vn"][:], in_=bk(6))
            p.I("vector", "tensor_tensor", [N("ex")], ["B7", N("tq")], out=dd["tq"][:], in0=bk(7), in1=b4(ex[:, 0:4]), op=ALU.mult)
            for h in range(4):
                p.I("tensor", "matmul", [N("kdec"), N("vn")], ["B7"], bk(7)[:, h, :], lhsT=dd["kdec"][:, h, :], rhs=dd["vn"][:, h, :],
                    start=True, stop=True)
            for h in range(4):
                p.I("tensor", "matmul", [N("XA"), N("vn")], ["B6"], bk(6)[:, h, :], lhsT=XA[:, 4 + h, :], rhs=dd["vn"][:, h, :],
                    start=True, stop=True)
            for h in range(4):
                p.I("vector", "scalar_tensor_tensor", [N("ex")], ["B7", Sn], out=S32[d][:, h, :], in0=S32[d][:, h, :],
                    scalar=ex[:, 8 + h:9 + h], in1=bk(7)[:, h, :], op0=ALU.mult, op1=ALU.add)
            p.I("scalar", "copy", [Sn], [Sb], out=Sbf[d][:], in_=S32[d][:])
            od = dd["od"]
            p.I("vector", "tensor_tensor", [N("tq")], ["B6", N("od")], out=od[:], in0=bk(6), in1=dd["tq"][:], op=ALU.add)
            if not second:
                p.dma("sync", ost_d[i * 128:(i + 1) * 128, :], od[:].rearrange("p h d -> p (h d)"), reads=[N("od")],
                      writes=["ost%d" % i])
                return
            p.I("gpsimd", "tensor_tensor", [N("od"), N("ol")], [N("od")], out=od[:], in0=od[:], in1=dd["ol"][:], op=ALU.add)
            p.I("scalar", "activation", [N("od")], [N("sqo")], out=dd["sqo"][:], in_=od[:], func=AF.Square)
            p.I("vector", "tensor_reduce", [N("sqo")], [N("sso")], out=dd["sso"][:, :], in_=dd["sqo"][:], axis=AX.X, op=ALU.add)
            p.I("scalar", "activation", [N("sso"), "eps"], [N("sso")], out=dd["sso"][:, :], in_=dd["sso"][:, :], func=AF.Sqrt,
                scale=1.0 / 128, bias=eps[:, :])
            p.I("vector", "reciprocal", [N("sso")], [N("sso")], out=dd["sso"][:, :], in_=dd["sso"][:, :])
            p.I("vector", "tensor_tensor", [N("sso"), N("od")], [N("od")], out=od[:], in0=od[:], in1=b4(dd["sso"][:, :]), op=ALU.mult)
            p.I("gpsimd", "tensor_tensor", [N("od"), "on"], [N("od")], out=od[:], in0=od[:], in1=bh(on[:, :]), op=ALU.mult)
            p.I("gpsimd", "tensor_tensor", [N("od"), N("zc")], [N("ogc")], out=dd["ogc"][:], in0=od[:],
                in1=zc[:, :].rearrange("p (h d) -> p h d", h=4), op=ALU.mult)
            outs.append(p.dma("sync", og_d[i * 128:(i + 1) * 128, :], dd["ogc"][:].rearrange("p h d -> p (h d)"), reads=[N("ogc")]))

        pend = {}
        pend[0, 0] = pre(0, 0, 0, False)
        pend[1, nchunks - 1] = pre(1, nchunks - 1, 0, False)
        for t in range(nchunks):
            par = t % 2
            if t + 1 < nchunks:
                sec1 = (t + 1) >= nchunks // 2
                pend[0, t + 1] = pre(0, t + 1, 1 - par, sec1)
                pend[1, nchunks - 2 - t] = pre(1, nchunks - 2 - t, 1 - par, sec1)
            second = t >= nchunks // 2
            scan(0, t, par, second, *pend.pop((0, t)))
            scan(1, nchunks - 1 - t, par, second, *pend.pop((1, nchunks - 1 - t)))
        p.emit(final_wait_ops=outs)
    return nc, prog


def build_gdn_out_program():
    nc = bass.Bass("TRN2", target_bir_lowering=False)
    og_d = nc.dram_tensor("og", [TOK, 2048], BF16, kind="ExternalInput").ap()
    h_d = nc.dram_tensor("h", [TOK, D_MODEL], F32, kind="ExternalInput").ap()
    w_d = nc.dram_tensor("w_out", [2048, D_MODEL], F32, kind="ExternalInput").ap()
    out_d = nc.dram_tensor("out", [TOK, D_MODEL], F32, kind="ExternalOutput").ap()
    prog = Prog(nc)
    p = prog
    outs = []
    with contextlib.ExitStack() as st:
        c = Ctx(nc, prog, st)
        ident = make_ident(c)
        p.fence()
        w = c.sb("w", [128, 16, D_MODEL], BF16)
        wv = w_d.rearrange("(k p) c -> p k c", p=128)
        for k in range(0, 16, 4):
            p.dma("gpsimd", w[:, k:k + 4, :], wv[:, k:k + 4, :], writes=["w"])
        ogt = [c.sb("ogt%d" % i, [128, 2048], BF16) for i in range(2)]
        ht = [c.sb("ht%d" % i, [128, D_MODEL], F32) for i in range(2)]
        ho = [c.sb("ho%d" % i, [128, D_MODEL], F32) for i in range(2)]
        ogT = [c.sb("ogT%d" % i, [128, 16, 128], BF16) for i in range(2)]
        TP = c.ps("TP", [128, 16, 128], BF16)
        AC = c.ps("AC", [128, 1024], F32)
        for t in range(16):
            b = t % 2
            p.dma("sync", ogt[b][:], og_d[t * 128:(t + 1) * 128, :], writes=["ogt%d" % b])
            p.dma("sync", ht[b][:], h_d[t * 128:(t + 1) * 128, :], writes=["ht%d" % b])
            for k in range(16):
                p.I("tensor", "transpose", ["ogt%d" % b], ["TP"], out=TP[:, k, :], in_=ogt[b][:, k * 128:(k + 1) * 128],
                    identity=ident[:, :])
            p.I("vector", "tensor_copy", [], ["TP", "ogT%d" % b], out=ogT[b][:], in_=TP[:])
            for hf in range(2):
                for k in range(16):
                    p.I("tensor", "matmul", ["ogT%d" % b, "w"], ["AC"], AC[:, hf * 512:(hf + 1) * 512], lhsT=ogT[b][:, k, :],
                        rhs=w[:, k, hf * 512:(hf + 1) * 512], start=(k == 0), stop=(k == 15))
            p.I("vector", "tensor_tensor", ["ht%d" % b], ["AC", "ho%d" % b], out=ho[b][:], in0=AC[:, :], in1=ht[b][:], op=ALU.add)
            outs.append(p.dma("sync", out_d[t * 128:(t + 1) * 128, :], ho[b][:], reads=["ho%d" % b]))
        p.emit(final_wait_ops=outs)
    return nc, prog


_CACHE = {}


def _prog(name, builder, *a):
    key = (name,) + a
    if key not in _CACHE:
        _CACHE[key] = builder(*a)[0]
    return _CACHE[key]


def _rep(v, n=128):
    return np.ascontiguousarray(np.broadcast_to(np.asarray(v, np.float32)[None], (n,) + tuple(np.shape(v))))


def _halo(hb, c, pad):
    b, j = c // 4, c % 4
    z = np.zeros((pad, hb.shape[-1]), hb.dtype)
    ext = np.concatenate([z, hb[b], z], 0)
    return np.ascontiguousarray(ext[j * TOK:j * TOK + TOK + 2 * pad])


def _run(nc, maps):
    return run_bass_kernel_spmd(nc, maps, core_ids=list(range(NCORES))).results


def _ffn_layer(h, g, gf, wup, cwt, cb, wdn, final):
    nc = _prog("ffn", build_ffn_program, final)
    cw = np.concatenate([cwt, cb[None]], 0)
    cw_l = np.ascontiguousarray(cw.reshape(4, 44, 128).transpose(2, 1, 0))
    maps = []
    for c in range(NCORES):
        hh = _halo(h, c, 1)
        maps.append(dict(h=hh, g_ffn=_rep(g), g_fin=_rep(gf), w_up=wup, w_down=wdn, cw=cw_l))
    res = _run(nc, maps)
    return np.stack([np.concatenate([res[b * 4 + j]["out"] for j in range(4)], 0) for b in range(BATCH)], 0)


def kernel(x, norm_mix, norm_ffn, norm_final, attn_w_in, attn_q_norm, attn_k_norm, attn_w_out, gdn_w_in,
           gdn_conv_w, gdn_conv_b, gdn_a_log, gdn_dt_bias, gdn_o_norm, gdn_w_out, ffn_w_up, ffn_conv_w,
           ffn_conv_b, ffn_w_down):
    f32 = lambda a: np.ascontiguousarray(np.asarray(a, dtype=np.float32))
    x = f32(x)
    rope = rope_tables_np()
    nc = _prog("ap", build_attn_proj_program)
    maps = []
    for c in range(NCORES):
        b, j = c // 4, c % 4
        maps.append(dict(x=np.ascontiguousarray(x[b, j * TOK:(j + 1) * TOK]), g_mix=_rep(norm_mix[0]), w_in=f32(attn_w_in[0]),
                         qg=_rep(attn_q_norm[0]), kg=_rep(attn_k_norm[0]), rope=np.ascontiguousarray(rope[j * TOK:(j + 1) * TOK])))
    r1 = _run(nc, maps)
    nc = _prog("ac", build_attn_core_program)
    maps = []
    for c in range(NCORES):
        b, j = c // 4, c % 4
        kT = np.ascontiguousarray(np.concatenate([r1[b * 4 + i]["kT"] for i in range(4)], axis=2))
        v = np.ascontiguousarray(np.concatenate([r1[b * 4 + i]["v"] for i in range(4)], axis=1))
        maps.append(dict(qT=r1[c]["qT"], kT=kT, v=v, gate=r1[c]["gate"], x=np.ascontiguousarray(x[b, j * TOK:(j + 1) * TOK]),
                         w_out=f32(attn_w_out[0]), qg=_rep(attn_q_norm[0]), kg=_rep(attn_k_norm[0])))
    r2 = _run(nc, maps)
    h = np.stack([np.concatenate([r2[b * 4 + j]["out"] for j in range(4)], 0) for b in range(BATCH)], 0)
    h = _ffn_layer(h, f32(norm_ffn[0]), f32(norm_final), f32(ffn_w_up[0]), f32(ffn_conv_w[0]), f32(ffn_conv_b[0]),
                   f32(ffn_w_down[0]), False)
    nc = _prog("gp", build_gdn_proj_program)
    cwf = np.concatenate([f32(gdn_conv_w[0]), f32(gdn_conv_b[0])[None]], 0)
    cw_l = np.ascontiguousarray(cwf.reshape(6, 32, 128).transpose(2, 1, 0))
    ad = np.stack([f32(gdn_a_log[0]).reshape(32), f32(gdn_dt_bias[0]).reshape(32)], 0)
    maps = []
    for c in range(NCORES):
        maps.append(dict(h=_halo(h, c, 2), g_mix=_rep(norm_mix[1]), w_in=f32(gdn_w_in[0]), cw=cw_l, ad=_rep(ad)))
    r3 = _run(nc, maps)
    nc = _prog("gs", build_gdn_scan_program, 64)
    maps = []
    for c in range(NCORES):
        b, j = c // 4, c % 4
        grp = [r3[b * 4 + i] for i in range(4)]
        qT = np.ascontiguousarray(np.concatenate([g_["qT"][:, 2 * j:2 * j + 2] for g_ in grp], axis=2))
        kT = np.ascontiguousarray(np.concatenate([g_["kT"][:, 2 * j:2 * j + 2] for g_ in grp], axis=2))
        ktm = np.ascontiguousarray(np.concatenate([g_["k_tm"][:, 256 * j:256 * (j + 1)] for g_ in grp], axis=0))
        vtm = np.ascontiguousarray(np.concatenate([g_["v_tm"][:, 512 * j:512 * (j + 1)] for g_ in grp], axis=0))
        zz = np.ascontiguousarray(np.concatenate([g_["z"][:, 512 * j:512 * (j + 1)] for g_ in grp], axis=0))
        gball = np.concatenate([g_["gb"] for g_ in grp], axis=0)
        hs = slice(4 * j, 4 * j + 4)
        gb = np.ascontiguousarray(np.concatenate([gball[:, 0:16][:, hs], gball[:, 16:32][:, hs],
                                                  gball[:, 32:48][:, hs], gball[:, 48:64][:, hs]], axis=1))
        maps.append(dict(qT=qT, kT=kT, k_tm=ktm, v_tm=vtm, z=zz, gb=gb, on=_rep(gdn_o_norm[0]), bm=block_masks_np()))
    r4 = _run(nc, maps)
    nc = _prog("go", build_gdn_out_program)
    maps = []
    for c in range(NCORES):
        b, j = c // 4, c % 4
        og = np.ascontiguousarray(np.concatenate([r4[b * 4 + i]["og"][j * TOK:(j + 1) * TOK] for i in range(4)], axis=1))
        maps.append(dict(og=og, h=np.ascontiguousarray(h[b, j * TOK:(j + 1) * TOK]), w_out=f32(gdn_w_out[0])))
    r5 = _run(nc, maps)
    h = np.stack([np.concatenate([r5[b * 4 + j]["out"] for j in range(4)], 0) for b in range(BATCH)], 0)
    out = _ffn_layer(h, f32(norm_ffn[1]), f32(norm_final), f32(ffn_w_up[1]), f32(ffn_conv_w[1]), f32(ffn_conv_b[1]),
                     f32(ffn_w_down[1]), True)
    return out.astype(np.float32)
```

```python
import contextlib
import numpy as np
import concourse.bass as bass
import concourse.mybir as mybir
from concourse.bass_utils import run_bass_kernel_spmd

F32 = mybir.dt.float32
BF16 = mybir.dt.bfloat16
AF = mybir.ActivationFunctionType
ALU = mybir.AluOpType
AX = mybir.AxisListType

D_MODEL = 1024
BATCH = 2
SEQ = 8192
EPS = 1e-6
D_FF = 2816
NCORES = 8
TOK = 2048

COMPUTE = ("tensor", "vector", "scalar", "gpsimd")
N_DMA_SEMS = 24


class Prog:
    def __init__(self, nc, same_engine_sync=True):
        self.nc = nc
        self.ops = []
        self.last_w = {}
        self.readers = {}
        self.same_engine_sync = same_engine_sync
        self.n_dma = 0
        self.dma_sem_last = {}
        self.fence_deps = set()

    def fence(self):
        self.fence_deps = set(i for i, o in enumerate(self.ops) if not o["dma"])

    def op(self, eng, fn, reads=(), writes=(), dma=False):
        idx = len(self.ops)
        deps = set()
        for r in reads:
            if r in self.last_w:
                deps.add(self.last_w[r])
        for w in writes:
            if w in self.last_w:
                deps.add(self.last_w[w])
            for rd in self.readers.get(w, ()):
                deps.add(rd)
        sem_slot = None
        if dma:
            if eng == "gpsimd":
                self.n_dma_sw = getattr(self, "n_dma_sw", 0) + 1
                sem_slot = 16 + self.n_dma_sw % (N_DMA_SEMS - 16)
            else:
                sem_slot = self.n_dma % 16
            self.n_dma += 1
            prev = self.dma_sem_last.get(sem_slot)
            if prev is not None:
                deps.add(prev)
            self.dma_sem_last[sem_slot] = idx
        deps |= self.fence_deps
        deps.discard(idx)
        self.ops.append(dict(eng=eng, fn=fn, deps=deps, dma=dma, sem_slot=sem_slot))
        for w in writes:
            self.last_w[w] = idx
            self.readers[w] = []
        for r in reads:
            if r not in writes:
                self.readers.setdefault(r, []).append(idx)
        return idx

    def I(self, eng, method, reads, writes, *args, **kwargs):
        def fn(e, method=method, args=args, kwargs=kwargs):
            return getattr(e, method)(*args, **kwargs)
        idx = self.op(eng, fn, reads, writes)
        try:
            if method == "matmul":
                r = kwargs["rhs"]
                n = int(np.prod(r.shape[1:]))
                dur = 40 + n * (2.0 if r.dtype == F32 else 0.5)
            elif method == "transpose":
                dur = 110.0
            else:
                o = kwargs.get("out", args[0] if args else None)
                n = int(np.prod(o.shape[1:]))
                dur = {"scalar": 220 + n / 1.4, "vector": 90 + n / 0.96, "gpsimd": 160 + n / 0.45}[eng]
        except Exception:
            dur = 300.0
        self.ops[idx]["dur"] = dur
        return idx

    def dma(self, queue, out, in_, reads=(), writes=()):
        def fn(e, out=out, in_=in_):
            return e.dma_start(out=out, in_=in_)
        idx = self.op(queue, fn, reads, writes, dma=True)
        try:
            nbytes = int(np.prod(out.shape)) * (4 if out.dtype == F32 else 2)
        except Exception:
            nbytes = 1 << 16
        self.ops[idx]["dur"] = 2000 + nbytes / 150.0
        return idx

    def reorder(self, final_wait_ops):
        import heapq
        ops = self.ops
        n = len(ops)
        succ = [[] for _ in range(n)]
        indeg = [0] * n
        for i, o in enumerate(ops):
            for d in o["deps"]:
                succ[d].append(i)
            indeg[i] = len(o["deps"])
        ready_t = [0.0] * n
        heap = [(0.0, i) for i in range(n) if indeg[i] == 0]
        heapq.heapify(heap)
        eng_free = {}
        order = []
        LAT = 900.0
        while heap:
            rt, i = heapq.heappop(heap)
            o = ops[i]
            e = o["eng"]
            start = max(rt, eng_free.get(e, 0.0))
            dur = o.get("dur", 300.0)
            if o["dma"]:
                eng_free[e] = start + 60.0
                fin = start + dur
            else:
                eng_free[e] = start + dur
                fin = start + dur
            order.append(i)
            for s_ in succ[i]:
                so = ops[s_]
                lat = 0.0 if (so["eng"] == e and not o["dma"] and e == "tensor") else LAT
                t = fin + lat
                if t > ready_t[s_]:
                    ready_t[s_] = t
                indeg[s_] -= 1
                if indeg[s_] == 0:
                    heapq.heappush(heap, (max(ready_t[s_], 0.0), s_))
        assert len(order) == n
        pos = {old: new for new, old in enumerate(order)}
        new_ops = []
        for old in order:
            o = ops[old]
            o["deps"] = set(pos[d] for d in o["deps"])
            new_ops.append(o)
        self.ops = new_ops
        self.est_ns = max(eng_free.values()) if eng_free else 0
        return [pos[d] for d in final_wait_ops]

    def emit(self, final_wait_ops=(), schedule=True):
        nc = self.nc
        if schedule:
            final_wait_ops = self.reorder(list(final_wait_ops))
        ops = self.ops
        engines = ("sync",) + COMPUTE
        waited_eng = {e: {p: -1 for p in COMPUTE} for e in engines}
        waited_dma = {e: set() for e in engines}
        for i, o in enumerate(ops):
            e = o["eng"]
            need_eng = {}
            need_dma = []
            for d in sorted(o["deps"]):
                po = ops[d]
                if po["dma"]:
                    if d not in waited_dma[e]:
                        need_dma.append(d)
                        waited_dma[e].add(d)
                else:
                    pe = po["eng"]
                    if pe == e and (pe == "tensor" or not self.same_engine_sync):
                        continue
                    if d > waited_eng[e][pe]:
                        need_eng[pe] = max(need_eng.get(pe, -1), d)
            for pe, d in need_eng.items():
                waited_eng[e][pe] = d
            o["waits"] = list(need_eng.values()) + need_dma
        final_e = "sync"
        fw = []
        for d in final_wait_ops:
            fw.append(d)
        signal = set()
        for o in ops:
            for d in o["waits"]:
                signal.add(d)
        for d in fw:
            signal.add(d)
        cnt = {e: 0 for e in COMPUTE}
        dma_cnt = {}
        for i, o in enumerate(ops):
            if o["dma"]:
                s = o["sem_slot"]
                dma_cnt[s] = dma_cnt.get(s, 0) + 16
                o["sig"] = ("dma", s, dma_cnt[s])
            elif i in signal:
                cnt[o["eng"]] += 1
                o["sig"] = ("eng", o["eng"], cnt[o["eng"]])
            else:
                o["sig"] = None
        self.stats = dict(n_ops=len(ops), signals=dict(cnt), n_dma=self.n_dma)
        with contextlib.ExitStack() as st:
            esem = {e: st.enter_context(nc.semaphore("s_" + e)) for e in COMPUTE}
            dsem = [st.enter_context(nc.semaphore("d_%d" % k)) for k in range(N_DMA_SEMS)]
            block = st.enter_context(nc.Block())

            def semof(sig):
                if sig[0] == "dma":
                    return dsem[sig[1]], sig[2]
                return esem[sig[1]], sig[2]

            def run(ename):
                def body(eng):
                    for i, o in enumerate(ops):
                        if o["eng"] != ename:
                            continue
                        for d in o["waits"]:
                            s, v = semof(ops[d]["sig"])
                            eng.wait_ge(s, v)
                        ins = o["fn"](eng)
                        sig = o["sig"]
                        if sig is not None:
                            s, v = semof(sig)
                            ins.then_inc(s, 16 if sig[0] == "dma" else 1)
                    if ename == final_e:
                        for d in fw:
                            s, v = semof(ops[d]["sig"])
                            eng.wait_ge(s, v)
                return body

            block.sync(run("sync"))
            block.tensor(run("tensor"))
            block.vector(run("vector"))
            block.scalar(run("scalar"))
            block.gpsimd(run("gpsimd"))


class Ctx:
    def __init__(self, nc, prog, st):
        self.nc, self.p, self.st = nc, prog, st
        self.k = 0

    def sb(self, name, shape, dt):
        return self.st.enter_context(self.nc.sbuf_tensor(name, list(shape), dt))

    def ps(self, name, shape, dt):
        return self.st.enter_context(self.nc.psum_tensor(name, list(shape), dt))


def emit_rmsnorm_T(c, src_ap, n, g_sb, dstT, col0, tag, bufs):
    p = c.p
    sq, ss, rs, yb, pT, ident = (bufs[k] for k in ("sq", "ss", "rs", "yb", "pT", "ident"))
    nm = bufs["names"]
    p.I("scalar", "activation", [tag], [nm["sq"], nm["ss"]],
        out=sq[0:n, :], in_=src_ap, func=AF.Square, accum_out=ss[0:n, :])
    p.I("scalar", "activation", [nm["ss"], "eps"], [nm["rs"]],
        out=rs[0:n, :], in_=ss[0:n, :], func=AF.Sqrt, scale=1.0 / D_MODEL, bias=bufs["eps"][0:n, :])
    p.I("vector", "reciprocal", [nm["rs"]], [nm["rs"]], out=rs[0:n, :], in_=rs[0:n, :])
    p.I("vector", "scalar_tensor_tensor", [tag, nm["rs"], "g_sb"], [nm["yb"]],
        out=yb[0:n, :], in0=src_ap, scalar=rs[0:n, :], in1=g_sb[0:n, :], op0=ALU.mult, op1=ALU.mult)
    for k in range(8):
        p.I("tensor", "transpose", [nm["yb"]], [nm["pT"]],
            out=pT[:, k, 0:n], in_=yb[0:n, k * 128:(k + 1) * 128], identity=ident[0:n, 0:n])
    p.I("vector", "tensor_copy", [nm["pT"]], [bufs["dst_name"]],
        out=dstT[:, :, col0:col0 + n], in_=pT[:, :, 0:n])


def make_ident(c, name="ident"):
    ident = c.sb(name, [128, 128], BF16)
    c.p.I("gpsimd", "memset", [], [name], ident[:], 1.0)
    c.p.I("gpsimd", "affine_select", [name], [name], out=ident[:], in_=ident[:], pattern=[[-1, 128]],
          compare_op=ALU.is_equal, fill=0.0, base=0, channel_multiplier=1)
    return ident


def ffn_blocks(ntok):
    out = []
    t = 0
    while t < ntok:
        n = min(254, ntok - t)
        out.append((t, n))
        t += n
    return out


def build_ffn_program(final_norm, pre=None):
    nc = bass.Bass("TRN2", target_bir_lowering=False)
    h_d = nc.dram_tensor("h", [TOK + 2, D_MODEL], F32, kind="ExternalInput").ap()
    g_d = nc.dram_tensor("g_ffn", [128, D_MODEL], F32, kind="ExternalInput").ap()
    gf_d = nc.dram_tensor("g_fin", [128, D_MODEL], F32, kind="ExternalInput").ap()
    wup_d = nc.dram_tensor("w_up", [D_MODEL, 2 * D_FF], F32, kind="ExternalInput").ap()
    wdn_d = nc.dram_tensor("w_down", [D_FF, D_MODEL], F32, kind="ExternalInput").ap()
    cw_d = nc.dram_tensor("cw", [128, 44, 4], F32, kind="ExternalInput").ap()
    out_d = nc.dram_tensor("out", [TOK, D_MODEL], F32, kind="ExternalOutput").ap()
    prog = Prog(nc)
    with contextlib.ExitStack() as st:
        c = Ctx(nc, prog, st)
        emit_ffn(c, h_d, g_d, gf_d, wup_d, wdn_d, cw_d, out_d, final_norm)
    return nc, prog


def emit_ffn(c, h_d, g_d, gf_d, wup_d, wdn_d, cw_d, out_d, final_norm):
    p = c.p
    ident = make_ident(c)
    wup = c.sb("wup", [128, 8, 2 * D_FF], BF16)
    wdn = c.sb("wdn", [128, 22, D_MODEL], BF16)
    cw = c.sb("cw_sb", [128, 44, 4], F32)
    g_sb = c.sb("g_sb", [128, D_MODEL], F32)
    gf_sb = c.sb("gf_sb", [128, D_MODEL], F32)
    eps = c.sb("eps", [128, 1], F32)
    p.I("vector", "memset", [], ["eps"], eps[:], EPS)
    p.fence()
    p.dma("sync", g_sb[:], g_d, writes=["g_sb"])
    p.dma("sync", gf_sb[:], gf_d, writes=["gf_sb"])
    p.dma("sync", cw[:], cw_d, writes=["cw"])
    wup_v = wup_d.rearrange("(k p) c -> p k c", p=128)
    wdn_v = wdn_d.rearrange("(j p) c -> p j c", p=128)
    WG = 2
    for j0 in range(0, 22, WG):
        j1 = min(22, j0 + WG)
        for hh in range(2):
            c0, c1 = (hh * 22 + j0) * 128, (hh * 22 + j1) * 128
            p.dma("gpsimd", wup[:, :, c0:c1], wup_v[:, :, c0:c1], writes=["wup%d" % (j0 // WG)])
        p.dma("gpsimd", wdn[:, j0:j1, :], wdn_v[:, j0:j1, :], writes=["wdn%d" % (j0 // WG)])

    NB = 2
    xt = [c.sb("xt%d" % i, [128, D_MODEL], F32) for i in range(NB)]
    sq = c.sb("sq", [128, D_MODEL], F32)
    ss = c.sb("ss", [128, 1], F32)
    rs = c.sb("rs", [128, 1], F32)
    yb = c.sb("yb", [128, D_MODEL], BF16)
    yT = [c.sb("yT%d" % i, [128, 8, 256], BF16) for i in range(2)]
    NR = 3
    t1 = [c.sb("t1_%d" % i, [128, 254], F32) for i in range(2 * NR)]
    t2 = [c.sb("t2_%d" % i, [128, 254], F32) for i in range(2 * NR)]
    t3 = [c.sb("t3_%d" % i, [128, 254], F32) for i in range(2 * NR)]
    sas = [c.sb("sa%d" % i, [128, 254], F32) for i in range(NR)]
    mT = [c.sb("mT%d" % i, [128, 254], BF16) for i in range(NR)]
    hres = [c.sb("hres%d" % i, [128, D_MODEL], F32) for i in range(2)]
    hout = [c.sb("hout%d" % i, [128, D_MODEL], F32) for i in range(2)]
    pT = c.ps("pT", [128, 8, 128], BF16)
    U = [c.ps("U%d" % i, [128, 512], F32) for i in range(3)]
    Dp = [c.ps("D%d" % i, [128, 512], F32) for i in range(4)]
    nbufs = dict(sq=sq, ss=ss, rs=rs, yb=yb, pT=pT, ident=ident, eps=eps,
                 names=dict(sq="sq", ss="ss", rs="rs", yb="yb", pT="pT"))

    out_ops = []
    xi = 0
    for bi, (t0, n) in enumerate(ffn_blocks(TOK)):
        ncol = n + 2
        yTb = yT[bi % 2]
        yname = "yT%d" % (bi % 2)
        nbufs["dst_name"] = yname
        r = 0
        while r < ncol:
            rn = min(128, ncol - r)
            x = xt[xi % NB]
            xname = "xt%d" % (xi % NB)
            xi += 1
            p.dma("sync", x[0:rn, :], h_d[t0 + r:t0 + r + rn, :], writes=[xname])
            emit_rmsnorm_T(c, x[0:rn, :], rn, g_sb, yTb, r, xname, nbufs)
            r += rn
        subs = []
        s0 = 0
        while s0 < n:
            sn = min(127, n - s0)
            subs.append((s0, sn))
            s0 += sn

        def up(j):
            Uj = U[j % 3]
            un = "U%d" % (j % 3)
            for half, cj in ((0, j), (1, j + 22)):
                for k in range(8):
                    p.I("tensor", "matmul", ["wup%d" % (j // 2), yname], [un],
                        Uj[:, half * 256:half * 256 + ncol], lhsT=wup[:, k, cj * 128:(cj + 1) * 128],
                        rhs=yTb[:, k, 0:ncol], start=(k == 0), stop=(k == 7))

        def ew(j):
            Uj = U[j % 3]
            un = "U%d" % (j % 3)
            b = j % NR
            for half, cj in ((0, j), (1, j + 22)):
                o = half * 256
                ti_ = half * NR + b
                tt1, tt2, tt3 = t1[ti_], t2[ti_], t3[ti_]
                p.I("scalar", "activation", ["cw"], [un, "t1_%d" % ti_],
                    out=tt1[:, 0:n], in_=Uj[:, o + 1:o + 1 + n], func=AF.Identity,
                    scale=cw[:, cj, 1:2], bias=cw[:, cj, 3:4])
                p.I("vector", "scalar_tensor_tensor", ["cw", "t1_%d" % ti_], [un, "t2_%d" % ti_],
                    out=tt2[:, 0:n], in0=Uj[:, o:o + n], scalar=cw[:, cj, 0:1], in1=tt1[:, 0:n],
                    op0=ALU.mult, op1=ALU.add)
                p.I("vector", "scalar_tensor_tensor", ["cw", "t2_%d" % ti_], [un, "t3_%d" % ti_],
                    out=tt3[:, 0:n], in0=Uj[:, o + 2:o + 2 + n], scalar=cw[:, cj, 2:3], in1=tt2[:, 0:n],
                    op0=ALU.mult, op1=ALU.add)
            p.I("scalar", "activation", ["t3_%d" % b], ["sa%d" % b], out=sas[b][:, 0:n], in_=t3[b][:, 0:n], func=AF.Silu)
            p.I("gpsimd", "tensor_tensor", ["sa%d" % b, "t3_%d" % (NR + b)], ["mT%d" % b],
                out=mT[b][:, 0:n], in0=sas[b][:, 0:n], in1=t3[NR + b][:, 0:n], op=ALU.mult)

        def down(j):
            b = j % NR
            m = mT[b]
            for si, (s0, sn) in enumerate(subs):
                for hf in range(2):
                    p.I("tensor", "matmul", ["mT%d" % b, "wdn%d" % (j // 2)], ["D%d" % (si * 2 + hf)],
                        Dp[si * 2 + hf][0:sn, :], lhsT=m[:, s0:s0 + sn],
                        rhs=wdn[:, j, hf * 512:(hf + 1) * 512], start=(j == 0), stop=(j == 21))

        for si, (s0, sn) in enumerate(subs):
            p.dma("sync", hres[si][0:sn, :], h_d[1 + t0 + s0:1 + t0 + s0 + sn, :], writes=["hres%d" % si])
        up(0)
        for j in range(22):
            if j + 1 < 22:
                up(j + 1)
            ew(j)
            down(j)
        for si, (s0, sn) in enumerate(subs):
            hr = hres[si]
            ho = hout[si]
            hn = "hout%d" % si
            for hf in range(2):
                p.I("vector", "tensor_tensor", ["hres%d" % si], ["D%d" % (si * 2 + hf), hn],
                    out=ho[0:sn, hf * 512:(hf + 1) * 512], in0=Dp[si * 2 + hf][0:sn, :],
                    in1=hr[0:sn, hf * 512:(hf + 1) * 512], op=ALU.add)
            if final_norm:
                p.I("scalar", "activation", [hn], ["sq", "ss"],
                    out=sq[0:sn, :], in_=ho[0:sn, :], func=AF.Square, accum_out=ss[0:sn, :])
                p.I("scalar", "activation", ["ss", "eps"], ["rs"],
                    out=rs[0:sn, :], in_=ss[0:sn, :], func=AF.Sqrt, scale=1.0 / D_MODEL, bias=eps[0:sn, :])
                p.I("vector", "reciprocal", ["rs"], ["rs"], out=rs[0:sn, :], in_=rs[0:sn, :])
                p.I("vector", "scalar_tensor_tensor", ["rs", "gf_sb"], [hn],
                    out=ho[0:sn, :], in0=ho[0:sn, :], scalar=rs[0:sn, :], in1=gf_sb[0:sn, :],
                    op0=ALU.mult, op1=ALU.mult)
            d = p.dma("sync", out_d[t0 + s0:t0 + s0 + sn, :], ho[0:sn, :], reads=[hn])
            out_ops.append(d)
    p.emit(final_wait_ops=out_ops)


def build_attn_proj_program():
    nc = bass.Bass("TRN2", target_bir_lowering=False)
    x_d = nc.dram_tensor("x", [TOK, D_MODEL], F32, kind="ExternalInput").ap()
    g_d = nc.dram_tensor("g_mix", [128, D_MODEL], F32, kind="ExternalInput").ap()
    w_d = nc.dram_tensor("w_in", [D_MODEL, 2560], F32, kind="ExternalInput").ap()
    qg_d = nc.dram_tensor("qg", [128, 128], F32, kind="ExternalInput").ap()
    kg_d = nc.dram_tensor("kg", [128, 128], F32, kind="ExternalInput").ap()
    rope_d = nc.dram_tensor("rope", [TOK, 128], F32, kind="ExternalInput").ap()
    qT_d = nc.dram_tensor("qT", [128, 8, TOK], BF16, kind="ExternalOutput").ap()
    kT_d = nc.dram_tensor("kT", [128, 2, TOK], BF16, kind="ExternalOutput").ap()
    v_d = nc.dram_tensor("v", [128, 16, 256], BF16, kind="ExternalOutput").ap()
    gt_d = nc.dram_tensor("gate", [128, 16, 1024], BF16, kind="ExternalOutput").ap()
    prog = Prog(nc)
    p = prog
    with contextlib.ExitStack() as st:
        c = Ctx(nc, prog, st)
        ident = make_ident(c)
        w = c.sb("w", [128, 8, 2560], BF16)
        g_sb = c.sb("g_sb", [128, D_MODEL], F32)
        gains = c.sb("gains", [128, 2, 128], F32)
        eps = c.sb("eps", [128, 1], F32)
        p.I("vector", "memset", [], ["eps"], eps[:], EPS)
        p.fence()
        p.dma("sync", g_sb[:], g_d, writes=["g_sb"])
        p.dma("sync", gains[:, 0, :], qg_d, writes=["gains"])
        p.dma("sync", gains[:, 1, :], kg_d, writes=["gains"])
        p.I("scalar", "mul", ["gains"], ["gains"], out=gains[:, 0, :], in_=gains[:, 0, :], mul=128.0 ** -0.5)
        w_v = w_d.rearrange("(k p) c -> p k c", p=128)
        for k in range(8):
            p.dma("gpsimd", w[:, k, :], w_v[:, k, :], writes=["w"])
        xt = [c.sb("xt%d" % i, [128, D_MODEL], F32) for i in range(2)]
        rp = [c.sb("rp%d" % i, [128, 2, 2, 32], F32) for i in range(2)]
        sq = c.sb("sq", [128, D_MODEL], F32)
        ss = c.sb("ss", [128, 1], F32)
        rs = c.sb("rs", [128, 1], F32)
        yb = c.sb("yb", [128, D_MODEL], BF16)
        yT = [c.sb("yT%d" % i, [128, 8, 128], BF16) for i in range(2)]
        sq10 = c.sb("sq10", [128, 10, 128], F32)
        ss10 = c.sb("ss10", [128, 10], F32)
        rs10 = c.sb("rs10", [128, 10], F32)
        z0 = c.sb("z0", [128, 10, 128], F32)
        zz = c.sb("zz", [128, 10, 128], F32)
        ra = [c.sb("ra%d" % i, [128, 10, 2, 32], F32) for i in range(4)]
        zr = c.sb("zr", [128, 10, 128], BF16)
        qT_all = c.sb("qT_all", [128, 8, TOK], BF16)
        kT_all = c.sb("kT_all", [128, 2, TOK], BF16)
        v_all = c.sb("v_all", [128, 16, 256], BF16)
        gt_all = c.sb("gt_all", [128, 16, 1024], BF16)
        Q2 = c.ps("Q2", [128, 1024], F32)
        G2 = c.ps("G2", [128, 1024], F32)
        KV = c.ps("KV", [128, 512], F32)
        pT = c.ps("pT", [128, 8, 128], BF16)
        TQ = c.ps("TQ", [128, 8, 128], BF16)
        TK = c.ps("TK", [128, 8, 128], BF16)
        nbufs = dict(sq=sq, ss=ss, rs=rs, yb=yb, pT=pT, ident=ident, eps=eps,
                     names=dict(sq="sq", ss="ss", rs="rs", yb="yb", pT="pT"))
        for t in range(16):
            b = t % 2
            x = xt[b]
            p.dma("sync", x[:], x_d[t * 128:(t + 1) * 128, :], writes=["xt%d" % b])
            p.dma("sync", rp[b][:], rope_d[t * 128:(t + 1) * 128, :].rearrange("p (a r e) -> p a r e", a=2, r=2),
                  writes=["rp%d" % b])
            nbufs["dst_name"] = "yT%d" % b
            emit_rmsnorm_T(c, x[:], 128, g_sb, yT[b], 0, "xt%d" % b, nbufs)
            for (dst, dn, c0) in ((Q2[:, 0:512], "Q2", 0), (Q2[:, 512:1024], "Q2", 512), (KV[:, :], "KV", 1024),
                                  (G2[:, 0:512], "G2", 1536), (G2[:, 512:1024], "G2", 2048)):
                for k in range(8):
                    p.I("tensor", "matmul", ["w", "yT%d" % b], [dn], dst, lhsT=yT[b][:, k, :],
                        rhs=w[:, k, c0:c0 + 512], start=(k == 0), stop=(k == 7))
            p.I("scalar", "activation", [], ["Q2", "sq10"], out=sq10[:, 0:8, :],
                in_=Q2[:, :].rearrange("p (h d) -> p h d", h=8), func=AF.Square)
            p.I("scalar", "activation", [], ["KV", "sq10"], out=sq10[:, 8:10, :],
                in_=KV[:, 0:256].rearrange("p (h d) -> p h d", h=2), func=AF.Square)
            p.I("vector", "tensor_reduce", ["sq10"], ["ss10"], out=ss10[:, :], in_=sq10[:, :, :], axis=AX.X, op=ALU.add)
            p.I("scalar", "activation", ["ss10", "eps"], ["rs10"], out=rs10[:, :], in_=ss10[:, :], func=AF.Sqrt,
                scale=1.0 / 128, bias=eps[:, :])
            p.I("vector", "reciprocal", ["rs10"], ["rs10"], out=rs10[:, :], in_=rs10[:, :])
            p.I("vector", "tensor_tensor", ["rs10"], ["Q2", "z0"], out=z0[:, 0:8, :],
                in0=Q2[:, :].rearrange("p (h d) -> p h d", h=8),
                in1=rs10[:, 0:8].unsqueeze(2).broadcast_to([128, 8, 128]), op=ALU.mult)
            p.I("vector", "tensor_tensor", ["rs10"], ["KV", "z0"], out=z0[:, 8:10, :],
                in0=KV[:, 0:256].rearrange("p (h d) -> p h d", h=2),
                in1=rs10[:, 8:10].unsqueeze(2).broadcast_to([128, 2, 128]), op=ALU.mult)
            p.I("gpsimd", "tensor_tensor", ["z0", "gains"], ["zz"], out=zz[:, 0:8, :], in0=z0[:, 0:8, :],
                in1=gains[:, 0:1, :].broadcast_to([128, 8, 128]), op=ALU.mult)
            p.I("gpsimd", "tensor_tensor", ["z0", "gains"], ["zz"], out=zz[:, 8:10, :], in0=z0[:, 8:10, :],
                in1=gains[:, 1:2, :].broadcast_to([128, 2, 128]), op=ALU.mult)
            zv = zz[:, :, :].rearrange("p h (r f e) -> p h r f e", r=2, f=2)
            ov = zr[:, :, :].rearrange("p h (r f e) -> p h r f e", r=2, f=2)
            z1, z2 = zv[:, :, :, 0, :], zv[:, :, :, 1, :]
            cosb = rp[b][:, 0:1, :, :].broadcast_to([128, 10, 2, 32])
            sinb = rp[b][:, 1:2, :, :].broadcast_to([128, 10, 2, 32])
            rn = "rp%d" % b
            p.I("vector", "tensor_tensor", ["zz", rn], ["ra0"], out=ra[0][:], in0=z1, in1=cosb, op=ALU.mult)
            p.I("gpsimd", "tensor_tensor", ["zz", rn], ["ra1"], out=ra[1][:], in0=z2, in1=sinb, op=ALU.mult)
            p.I("gpsimd", "tensor_tensor", ["zz", rn], ["ra2"], out=ra[2][:], in0=z1, in1=sinb, op=ALU.mult)
            p.I("vector", "tensor_tensor", ["zz", rn], ["ra3"], out=ra[3][:], in0=z2, in1=cosb, op=ALU.mult)
            p.I("vector", "tensor_tensor", ["ra0", "ra1"], ["zr"], out=ov[:, :, :, 0, :], in0=ra[0][:], in1=ra[1][:],
                op=ALU.subtract)
            p.I("gpsimd", "tensor_tensor", ["ra2", "ra3"], ["zr"], out=ov[:, :, :, 1, :], in0=ra[2][:], in1=ra[3][:],
                op=ALU.add)
            for h in range(8):
                p.I("tensor", "transpose", ["zr"], ["TQ"], out=TQ[:, h, :], in_=zr[:, h, :], identity=ident[:, :])
            for h in range(2):
                p.I("tensor", "transpose", ["zr"], ["TK"], out=TK[:, h, :], in_=zr[:, 8 + h, :], identity=ident[:, :])
            p.I("scalar", "copy", [], ["TQ", "qT_all"], out=qT_all[:, :, t * 128:(t + 1) * 128], in_=TQ[:, :, :])
            p.I("vector", "tensor_copy", [], ["TK", "kT_all"], out=kT_all[:, :, t * 128:(t + 1) * 128], in_=TK[:, 0:2, :])
            p.I("scalar", "copy", [], ["KV", "v_all"], out=v_all[:, t, :], in_=KV[:, 256:512])
            p.I("scalar", "activation", [], ["G2", "gt_all"], out=gt_all[:, t, :], in_=G2[:, :], func=AF.Sigmoid)
        outs = [p.dma("sync", qT_d, qT_all[:], reads=["qT_all"]),
                p.dma("sync", kT_d, kT_all[:], reads=["kT_all"]),
                p.dma("sync", v_d, v_all[:], reads=["v_all"]),
                p.dma("sync", gt_d, gt_all[:], reads=["gt_all"])]
        p.emit(final_wait_ops=outs)
    return nc, prog


def block_masks_np():
    i = np.arange(128)
    b32 = (i[:, None] // 32) == (i[None, :] // 32)
    b64 = (i[:, None] // 64) == (i[None, :] // 64)
    lo = i[:, None] > i[None, :]
    up = i[:, None] < i[None, :]
    ms = [t & m for t in (lo, up) for m in (b32, b64 & ~b32, ~b64)]
    return np.ascontiguousarray(np.stack(ms, axis=1).astype(np.float32))


def rope_tables_np():
    t = np.arange(SEQ)
    row = (t // 64).astype(np.float32)
    col = (t % 64).astype(np.float32)
    inv = (np.float32(10000.0) ** (-(np.arange(32, dtype=np.float32) * np.float32(2.0) / np.float32(64)))).astype(np.float32)
    ar = row[:, None] * inv[None, :]
    ac = col[:, None] * inv[None, :]
    return np.concatenate([np.cos(ar), np.cos(ac), np.sin(ar), np.sin(ac)], axis=1).astype(np.float32)


def build_attn_core_program(nheads=8, nqb=4, nsp=32):
    nc = bass.Bass("TRN2", target_bir_lowering=False)
    qT_d = nc.dram_tensor("qT", [128, 8, TOK], BF16, kind="ExternalInput").ap()
    kT_d = nc.dram_tensor("kT", [128, 2, SEQ], BF16, kind="ExternalInput").ap()
    v_d = nc.dram_tensor("v", [128, 64, 256], BF16, kind="ExternalInput").ap()
    gt_d = nc.dram_tensor("gate", [128, 16, 1024], BF16, kind="ExternalInput").ap()
    x_d = nc.dram_tensor("x", [TOK, D_MODEL], F32, kind="ExternalInput").ap()
    wo_d = nc.dram_tensor("w_out", [D_MODEL, D_MODEL], F32, kind="ExternalInput").ap()
    qg_d = nc.dram_tensor("qg", [128, 128], F32, kind="ExternalInput").ap()
    kg_d = nc.dram_tensor("kg", [128, 128], F32, kind="ExternalInput").ap()
    out_d = nc.dram_tensor("out", [TOK, D_MODEL], F32, kind="ExternalOutput").ap()
    prog = Prog(nc)
    p = prog
    with contextlib.ExitStack() as st:
        c = Ctx(nc, prog, st)
        ident = make_ident(c)
        qT = c.sb("qT_sb", [128, 8, TOK], BF16)
        kT = c.sb("kT_sb", [128, 2, SEQ], BF16)
        va = c.sb("v_aug", [128, 64, 2, 129], BF16)
        gt = c.sb("gt_sb", [128, 16, 1024], BF16)
        wo = c.sb("wo_sb", [128, 8, D_MODEL], BF16)
        gq = c.sb("gq", [128, 2, 128], F32)
        m2 = c.sb("m2", [128, 2], F32)
        negb = c.sb("negb", [128, 1], F32)
        p.fence()
        p.dma("sync", gq[:, 0, :], qg_d, writes=["gq"])
        p.dma("sync", gq[:, 1, :], kg_d, writes=["gq"])
        for h in range(8):
            p.dma("sync", qT[:, h, :], qT_d[:, h, :], writes=["qT"])
        for h in range(2):
            for s4 in range(4):
                p.dma("sync", kT[:, h, s4 * 2048:(s4 + 1) * 2048], kT_d[:, h, s4 * 2048:(s4 + 1) * 2048], writes=["kT"])
        p.I("gpsimd", "memset", [], ["va"], va[:, :, :, 128:129], 1.0)
        for s4 in range(4):
            p.dma("sync", va[:, s4 * 16:(s4 + 1) * 16, :, 0:128],
                  v_d[:, s4 * 16:(s4 + 1) * 16, :].rearrange("p s (h d) -> p s h d", h=2), writes=["va"])
        for t4 in range(4):
            p.dma("sync", gt[:, t4 * 4:(t4 + 1) * 4, :], gt_d[:, t4 * 4:(t4 + 1) * 4, :], writes=["gt"])
        wo_v = wo_d.rearrange("(k p) c -> p k c", p=128)
        for k in range(0, 8, 2):
            p.dma("gpsimd", wo[:, k:k + 2, :], wo_v[:, k:k + 2, :], writes=["wo"])
        p.I("vector", "tensor_tensor", ["gq"], ["gq"], out=gq[:], in0=gq[:], in1=gq[:], op=ALU.mult)
        p.I("vector", "tensor_reduce", ["gq"], ["m2"], out=m2[:, :], in_=gq[:, :, :], axis=AX.X, op=ALU.max)
        p.I("vector", "tensor_tensor", ["m2"], ["negb"], out=negb[:, :], in0=m2[:, 0:1], in1=m2[:, 1:2], op=ALU.mult)
        p.I("scalar", "activation", ["negb"], ["negb"], out=negb[:, :], in_=negb[:, :], func=AF.Sqrt, scale=128.0)
        p.I("scalar", "mul", ["negb"], ["negb"], out=negb[:, :], in_=negb[:, :], mul=-1.0)

        SC = [c.ps("SC%d" % i, [128, 1024], F32) for i in range(2)]
        O = [c.ps("O%d" % i, [128, 512], F32) for i in range(4)]
        NP = 3
        pT = [c.sb("pT%d" % i, [128, 1024], BF16) for i in range(NP)]
        rinv = c.sb("rinv", [128, 4], F32)
        step = 0
        for h in range(nheads):
            kv = h // 4
            for qb in range(nqb):
                def qk(sp, st_):
                    b = st_ % 2
                    for cc in range(2):
                        s = 2 * sp + cc
                        p.I("tensor", "matmul", ["kT", "qT"], ["SC%d" % b], SC[b][:, cc * 512:(cc + 1) * 512],
                            lhsT=kT[:, kv, s * 128:(s + 1) * 128], rhs=qT[:, h, qb * 512:(qb + 1) * 512],
                            start=True, stop=True)

                def ex(sp, st_):
                    b = st_ % 2
                    pb = st_ % NP
                    p.I("scalar", "activation", ["negb"], ["SC%d" % b, "pT%d" % pb], out=pT[pb][:, :], in_=SC[b][:, :],
                        func=AF.Exp, bias=negb[:, :])

                def pv(sp, st_):
                    pb = st_ % NP
                    for cc in range(2):
                        s = 2 * sp + cc
                        for qs in range(4):
                            p.I("tensor", "matmul", ["pT%d" % pb, "va"], ["O%d" % qs], O[qs][:, 0:129],
                                lhsT=pT[pb][:, cc * 512 + qs * 128:cc * 512 + (qs + 1) * 128], rhs=va[:, s, kv, :],
                                start=(sp == 0 and cc == 0), stop=(sp == nsp - 1 and cc == 1))

                qk(0, step)
                for sp in range(nsp):
                    if sp + 1 < nsp:
                        qk(sp + 1, step + 1)
                    ex(sp, step)
                    pv(sp, step)
                    step += 1
                for qs in range(4):
                    tile = qb * 4 + qs
                    p.I("vector", "reciprocal", [], ["O%d" % qs, "rinv"], out=rinv[:, qs:qs + 1], in_=O[qs][:, 128:129])
                    p.I("vector", "scalar_tensor_tensor", ["rinv"], ["O%d" % qs, "gt"],
                        out=gt[:, tile, h * 128:(h + 1) * 128], in0=O[qs][:, 0:128], scalar=rinv[:, qs:qs + 1],
                        in1=gt[:, tile, h * 128:(h + 1) * 128], op0=ALU.mult, op1=ALU.mult)
        xt = [c.sb("xt%d" % i, [128, D_MODEL], F32) for i in range(2)]
        ho = [c.sb("ho%d" % i, [128, D_MODEL], F32) for i in range(2)]
        ogT = [c.sb("ogT%d" % i, [128, 8, 128], BF16) for i in range(2)]
        TP = O[0].bitcast(BF16) if hasattr(O[0], "bitcast") else None
        outs = []
        for t in range(16):
            b = t % 2
            p.dma("sync", xt[b][:], x_d[t * 128:(t + 1) * 128, :], writes=["xt%d" % b])
            for k in range(8):
                p.I("tensor", "transpose", ["gt"], ["O0"], out=TP[:, k * 128:(k + 1) * 128],
                    in_=gt[:, t, k * 128:(k + 1) * 128], identity=ident[:, :])
            p.I("vector", "tensor_copy", [], ["O0", "ogT%d" % b], out=ogT[b][:, :, :],
                in_=TP[:, :].rearrange("p (k t) -> p k t", k=8))
            for hf in range(2):
                for k in range(8):
                    p.I("tensor", "matmul", ["ogT%d" % b, "wo"], ["SC0"], SC[0][:, hf * 512:(hf + 1) * 512],
                        lhsT=ogT[b][:, k, :], rhs=wo[:, k, hf * 512:(hf + 1) * 512], start=(k == 0), stop=(k == 7))
            p.I("vector", "tensor_tensor", ["xt%d" % b], ["SC0", "ho%d" % b], out=ho[b][:, :], in0=SC[0][:, :],
                in1=xt[b][:, :], op=ALU.add)
            outs.append(p.dma("sync", out_d[t * 128:(t + 1) * 128, :], ho[b][:, :], reads=["ho%d" % b]))
        p.emit(final_wait_ops=outs)
    return nc, prog


GDN_IN = 6208


def build_gdn_proj_program():
    nc = bass.Bass("TRN2", target_bir_lowering=False)
    HT = TOK + 4
    h_d = nc.dram_tensor("h", [HT, D_MODEL], F32, kind="ExternalInput").ap()
    g_d = nc.dram_tensor("g_mix", [128, D_MODEL], F32, kind="ExternalInput").ap()
    w_d = nc.dram_tensor("w_in", [D_MODEL, GDN_IN], F32, kind="ExternalInput").ap()
    cw_d = nc.dram_tensor("cw", [128, 32, 6], F32, kind="ExternalInput").ap()
    ad_d = nc.dram_tensor("ad", [128, 2, 32], F32, kind="ExternalInput").ap()
    qT_d = nc.dram_tensor("qT", [128, 8, TOK], BF16, kind="ExternalOutput").ap()
    kT_d = nc.dram_tensor("kT", [128, 8, TOK], BF16, kind="ExternalOutput").ap()
    ktm_d = nc.dram_tensor("k_tm", [TOK, 1024], BF16, kind="ExternalOutput").ap()
    vtm_d = nc.dram_tensor("v_tm", [TOK, 2048], BF16, kind="ExternalOutput").ap()
    z_d = nc.dram_tensor("z", [TOK, 2048], BF16, kind="ExternalOutput").ap()
    gb_d = nc.dram_tensor("gb", [TOK, 64], F32, kind="ExternalOutput").ap()
    prog = Prog(nc)
    p = prog
    outs = []
    with contextlib.ExitStack() as st:
        c = Ctx(nc, prog, st)
        ident = make_ident(c)
        ones = c.sb("ones", [128, 128], F32)
        p.I("gpsimd", "memset", [], ["ones"], ones[:], 1.0)
        g_sb = c.sb("g_sb", [128, D_MODEL], F32)
        cw = c.sb("cw_sb", [128, 32, 6], F32)
        ad = c.sb("ad_sb", [128, 2, 32], F32)
        eps = c.sb("eps", [128, 1], F32)
        one1 = c.sb("one1", [128, 1], F32)
        p.I("vector", "memset", [], ["eps"], eps[:], EPS)
        p.I("vector", "memset", [], ["one1"], one1[:], 1.0)
        p.fence()
        p.dma("sync", g_sb[:], g_d, writes=["g_sb"])
        p.dma("sync", cw[:], cw_d, writes=["cw"])
        p.dma("sync", ad[:], ad_d, writes=["ad"])
        p.I("scalar", "activation", ["ad"], ["ad"], out=ad[:, 0, :], in_=ad[:, 0, :], func=AF.Exp)
        p.I("scalar", "mul", ["ad"], ["ad"], out=ad[:, 0, :], in_=ad[:, 0, :], mul=-1.0)
        wv = w_d.rearrange("(k p) c -> p k c", p=128)
        wb = [c.sb("wb%d" % i, [128, 8, 1024], BF16) for i in range(2)]
        yT = c.sb("yT_all", [128, 8, HT], BF16)
        xt = [c.sb("xt%d" % i, [128, D_MODEL], F32) for i in range(2)]
        sq = c.sb("sq", [128, D_MODEL], F32)
        ss = c.sb("ss", [128, 1], F32)
        rs = c.sb("rs", [128, 1], F32)
        yb = c.sb("yb", [128, D_MODEL], BF16)
        pT = c.ps("pT", [128, 8, 128], BF16)
        U = [c.ps("U%d" % i, [128, 512], F32) for i in range(3)]
        L = c.ps("L", [128, 512], F32)
        TT = c.ps("TT", [128, 8, 128], BF16)
        Z = [U[0], U[1]]
        nbufs = dict(sq=sq, ss=ss, rs=rs, yb=yb, pT=pT, ident=ident, eps=eps, dst_name="yT",
                     names=dict(sq="sq", ss="ss", rs="rs", yb="yb", pT="pT"))
        r = 0
        xi = 0
        while r < HT:
            rn = min(128, HT - r)
            b = xi % 2
            xi += 1
            p.dma("sync", xt[b][0:rn, :], h_d[r:r + rn, :], writes=["xt%d" % b])
            emit_rmsnorm_T(c, xt[b][0:rn, :], rn, g_sb, yT, r, "xt%d" % b, nbufs)
            r += rn
        NR = 3
        tAs = [c.sb("tA%d" % i, [128, 508], F32) for i in range(NR)]
        tBs = [c.sb("tB%d" % i, [128, 508], F32) for i in range(NR)]
        acts = [c.sb("act%d" % i, [128, 508], F32) for i in range(NR)]
        sqvs = [c.sb("sqv%d" % i, [128, 508], F32) for i in range(NR)]
        rts = [c.sb("rt%d" % i, [128, 508], F32) for i in range(NR)]
        fm = [c.sb("fm%d" % i, [128, 8, 508], BF16) for i in range(2)]
        tm = [c.sb("tm%d" % i, [128, 1024], BF16) for i in range(2)]
        blocks = []
        t0 = 0
        while t0 < TOK:
            n = min(508, TOK - t0)
            blocks.append((t0, n))
            t0 += n
        fi = 0
        ti = 0
        ui = 0
        for grp in range(4):
            wbuf = wb[grp % 2]
            wn = "wb%d" % (grp % 2)
            for k in range(0, 8, 2):
                p.dma("gpsimd", wbuf[:, k:k + 2, :], wv[:, k:k + 2, grp * 1024:(grp + 1) * 1024], writes=[wn])
            for (t0, n) in blocks:
                ncol = n + 4
                fmb = fm[fi % 2]
                fn_ = "fm%d" % (fi % 2)
                fi += 1
                for j in range(8):
                    cj = grp * 8 + j
                    Uj = U[ui % 3]
                    un = "U%d" % (ui % 3)
                    rr = ui % NR
                    tA, tB, act, sqv, rt = tAs[rr], tBs[rr], acts[rr], sqvs[rr], rts[rr]
                    nA, nB, nact, nsqv, nrt = "tA%d" % rr, "tB%d" % rr, "act%d" % rr, "sqv%d" % rr, "rt%d" % rr
                    ui += 1
                    for k in range(8):
                        p.I("tensor", "matmul", [wn, "yT"], [un], Uj[:, 0:ncol], lhsT=wbuf[:, k, j * 128:(j + 1) * 128],
                            rhs=yT[:, k, t0:t0 + ncol], start=(k == 0), stop=(k == 7))
                    p.I("scalar", "activation", ["cw"], [un, nA], out=tA[:, 0:n], in_=Uj[:, 2:2 + n], func=AF.Identity,
                        scale=cw[:, cj, 2:3], bias=cw[:, cj, 5:6])
                    src, dst = tA, tB
                    sn_, dn_ = nA, nB
                    for tap in (0, 1, 3, 4):
                        p.I("vector", "scalar_tensor_tensor", ["cw", sn_], [un, dn_], out=dst[:, 0:n],
                            in0=Uj[:, tap:tap + n], scalar=cw[:, cj, tap:tap + 1], in1=src[:, 0:n],
                            op0=ALU.mult, op1=ALU.add)
                        src, dst = dst, src
                        sn_, dn_ = dn_, sn_
                    if grp < 2:
                        p.I("scalar", "activation", [nA], [nact], out=act[:, 0:n], in_=tA[:, 0:n], func=AF.Silu)
                        p.I("gpsimd", "tensor_tensor", [nact], [nsqv], out=sqv[:, 0:n], in0=act[:, 0:n], in1=act[:, 0:n], op=ALU.mult)
                        p.I("tensor", "matmul", ["ones", nsqv], ["L"], L[:, 0:n], lhsT=ones[:, :], rhs=sqv[:, 0:n],
                            start=True, stop=True)
                        p.I("scalar", "activation", ["eps"], ["L", nrt], out=rt[:, 0:n], in_=L[:, 0:n], func=AF.Sqrt,
                            bias=eps[:, :])
                        p.I("vector", "reciprocal", [nrt], [nrt], out=rt[:, 0:n], in_=rt[:, 0:n])
                        p.I("gpsimd", "scalar_tensor_tensor" if False else "tensor_tensor", [nact, nrt], [nsqv], out=sqv[:, 0:n],
                            in0=act[:, 0:n], in1=rt[:, 0:n], op=ALU.mult)
                        p.I("scalar", "mul", [nsqv], [fn_], out=fmb[:, j, 0:n], in_=sqv[:, 0:n],
                            mul=(128.0 ** -0.5 if grp == 0 else 1.0))
                    else:
                        p.I("scalar", "activation", [nA], [fn_], out=fmb[:, j, 0:n], in_=tA[:, 0:n], func=AF.Silu)
                if grp == 0:
                    outs.append(p.dma("sync", qT_d[:, :, t0:t0 + n], fmb[:, :, 0:n], reads=[fn_]))
                if grp == 1:
                    outs.append(p.dma("sync", kT_d[:, :, t0:t0 + n], fmb[:, :, 0:n], reads=[fn_]))
                if grp >= 1:
                    s0 = 0
                    while s0 < n:
                        sn = min(128, n - s0)
                        tmb = tm[ti % 2]
                        tn = "tm%d" % (ti % 2)
                        ti += 1
                        for j in range(8):
                            p.I("tensor", "transpose", [fn_], ["TT"], out=TT[0:sn, j, :], in_=fmb[:, j, s0:s0 + sn],
                                identity=ident[:, :])
                        p.I("vector", "tensor_copy", [], ["TT", tn], out=tmb[0:sn, :],
                            in_=TT[0:sn, :, :].rearrange("p j d -> p (j d)"))
                        if grp == 1:
                            dst_ap = ktm_d[t0 + s0:t0 + s0 + sn, :]
                        else:
                            dst_ap = vtm_d[t0 + s0:t0 + s0 + sn, (grp - 2) * 1024:(grp - 1) * 1024]
                        outs.append(p.dma("sync", dst_ap, tmb[0:sn, :], reads=[tn]))
                        s0 += sn
        wz = wb
        zs = [c.sb("zs%d" % i, [128, 1024], BF16) for i in range(2)]
        for half in range(2):
            wbuf = wz[half % 2]
            wn = "wb%d" % (half % 2)
            for k in range(0, 8, 2):
                p.dma("gpsimd", wbuf[:, k:k + 2, :], wv[:, k:k + 2, 4096 + half * 1024:4096 + (half + 1) * 1024], writes=[wn])
            for t in range(16):
                zb = zs[t % 2]
                zn = "zs%d" % (t % 2)
                for hf in range(2):
                    for k in range(8):
                        p.I("tensor", "matmul", [wn, "yT"], ["U%d" % hf], Z[hf][:, :], lhsT=yT[:, k, 2 + t * 128:2 + (t + 1) * 128],
                            rhs=wbuf[:, k, hf * 512:(hf + 1) * 512], start=(k == 0), stop=(k == 7))
                    p.I("scalar", "activation", [], ["U%d" % hf, zn], out=zb[:, hf * 512:(hf + 1) * 512], in_=Z[hf][:, :],
                        func=AF.Silu)
                outs.append(p.dma("sync", z_d[t * 128:(t + 1) * 128, half * 1024:(half + 1) * 1024], zb[:, :], reads=[zn]))
        wab = c.sb("wab", [128, 8, 64], BF16)
        p.dma("gpsimd", wab[:, :, :], wv[:, :, 6144:6208], writes=["wab"])
        xs = c.sb("xs", [128, 32], F32)
        ax = c.sb("ax", [128, 32], F32)
        gbs = [c.sb("gbs%d" % i, [128, 64], F32) for i in range(2)]
        for t in range(16):
            gbt = gbs[t % 2]
            gn = "gbs%d" % (t % 2)
            for k in range(8):
                p.I("tensor", "matmul", ["wab", "yT"], ["U0"], Z[0][:, 0:64], lhsT=yT[:, k, 2 + t * 128:2 + (t + 1) * 128],
                    rhs=wab[:, k, :], start=(k == 0), stop=(k == 7))
            Zv = Z[0][:, 0:64].rearrange("p (d a h) -> p d a h", d=2, a=2)
            p.I("vector", "tensor_tensor", ["ad"], ["U0", "xs"], out=xs[:, :].rearrange("p (d h) -> p d h", d=2),
                in0=Zv[:, :, 0, :], in1=ad[:, 1, :].rearrange("p (d h) -> p d h", d=2), op=ALU.add)
            p.I("scalar", "activation", [], ["U0", gn], out=gbt[:, 32:64].rearrange("p (d h) -> p d h", d=2),
                in_=Zv[:, :, 1, :], func=AF.Sigmoid)
            p.I("scalar", "activation", ["xs"], ["ax"], out=ax[:, :], in_=xs[:, :], func=AF.Abs)
            p.I("scalar", "activation", ["ax"], ["ax"], out=ax[:, :], in_=ax[:, :], func=AF.Exp, scale=-1.0)
            p.I("scalar", "activation", ["ax", "one1"], ["ax"], out=ax[:, :], in_=ax[:, :], func=AF.Ln, bias=one1[:, :])
            p.I("vector", "scalar_tensor_tensor", ["xs", "ax"], ["xs"], out=xs[:, :], in0=xs[:, :], scalar=0.0,
                in1=ax[:, :], op0=ALU.max, op1=ALU.add)
            p.I("vector", "tensor_tensor", ["xs", "ad"], [gn], out=gbt[:, 0:32], in0=xs[:, :], in1=ad[:, 0, :], op=ALU.mult)
            outs.append(p.dma("sync", gb_d[t * 128:(t + 1) * 128, :], gbt[:, :], reads=[gn]))
        p.emit(final_wait_ops=outs)
    return nc, prog


def build_gdn_scan_program(nchunks=64):
    nc = bass.Bass("TRN2", target_bir_lowering=False)
    S_ = nchunks * 128
    qT_d = nc.dram_tensor("qT", [128, 2, S_], BF16, kind="ExternalInput").ap()
    kT_d = nc.dram_tensor("kT", [128, 2, S_], BF16, kind="ExternalInput").ap()
    ktm_d = nc.dram_tensor("k_tm", [S_, 256], BF16, kind="ExternalInput").ap()
    vtm_d = nc.dram_tensor("v_tm", [S_, 512], BF16, kind="ExternalInput").ap()
    z_d = nc.dram_tensor("z", [S_, 512], BF16, kind="ExternalInput").ap()
    gb_d = nc.dram_tensor("gb", [S_, 16], F32, kind="ExternalInput").ap()
    on_d = nc.dram_tensor("on", [128, 128], F32, kind="ExternalInput").ap()
    bm_d = nc.dram_tensor("bm", [128, 6, 128], F32, kind="ExternalInput").ap()
    og_d = nc.dram_tensor("og", [S_, 512], BF16, kind="ExternalOutput").ap()
    ost_d = nc.dram_tensor("ost", [S_, 512], F32, kind="Internal").ap()
    prog = Prog(nc)
    p = prog
    outs = []
    with contextlib.ExitStack() as st:
        c = Ctx(nc, prog, st)
        ident = make_ident(c)
        ones = c.sb("ones", [128, 128], F32)
        p.I("gpsimd", "memset", [], ["ones"], ones[:], 1.0)
        masks = {}
        for nm_, cmp, sg in (("LE", ALU.is_ge, -1), ("GT", ALU.is_gt, 1), ("GE", ALU.is_ge, 1), ("LT", ALU.is_gt, -1)):
            m = c.sb("m" + nm_, [128, 128], F32)
            p.I("gpsimd", "memset", [], ["m" + nm_], m[:], 1.0)
            p.I("gpsimd", "affine_select", ["m" + nm_], ["m" + nm_], out=m[:], in_=m[:], pattern=[[-sg, 128]],
                compare_op=cmp, fill=0.0, base=0, channel_multiplier=sg)
            masks[nm_] = m
        on = c.sb("on_sb", [128, 128], F32)
        eps = c.sb("eps", [128, 1], F32)
        p.I("vector", "memset", [], ["eps"], eps[:], EPS)
        p.fence()
        p.dma("sync", on[:], on_d, writes=["on"])
        bm = c.sb("bm_sb", [128, 6, 128], F32)
        p.dma("sync", bm[:], bm_d, writes=["bm"])
        B = [c.ps("B%d" % i, [128, 512], F32) for i in range(8)]

        def bk(i):
            return B[i][:, :].rearrange("p (h d) -> p h d", h=4)

        D = {}
        for d in range(2):
            for par in range(2):
                dd = {}
                sfx = "_%d%d" % (d, par)
                for nm_, shp, dt_ in (("qTc", [128, 2, 128], BF16), ("kTc", [128, 2, 128], BF16), ("ktm", [128, 256], BF16),
                                      ("vtm", [128, 512], BF16), ("zc", [128, 512], BF16), ("gb", [128, 16], F32),
                                      ("X0", [128, 4, 128], BF16), ("X1", [128, 4, 128], BF16),
                                      ("Y0", [128, 4, 128], BF16), ("Y1", [128, 4, 128], BF16),
                                      ("P0", [128, 4, 128], BF16), ("P1", [128, 4, 128], BF16),
                                      ("rhsD", [128, 4, 128], F32), ("E", [128, 4, 128], F32), ("Es", [128, 4, 128], F32),
                                      ("Ei", [128, 4, 128], F32), ("KQ", [128, 4, 128], F32), ("attn", [128, 4, 128], BF16),
                                      ("XA", [128, 8, 128], BF16), ("kbg", [128, 4, 128], BF16), ("vb", [128, 4, 128], BF16),
                                      ("kdec", [128, 4, 128], BF16), ("nwT", [128, 4, 128], BF16),
                                      ("gs", [128, 8], F32), ("ex", [128, 12], F32), ("nbe", [128, 4], F32),
                                      ("sso", [128, 4], F32), ("ol", [128, 4, 128], F32),
                                      ("Pf0", [128, 4, 128], F32), ("Pf1", [128, 4, 128], F32),
                                      ("N1", [128, 4, 128], BF16), ("N2", [128, 4, 128], BF16), ("N1T", [128, 4, 128], BF16),
                                      ("WP", [128, 4, 128], BF16), ("WQ", [128, 4, 128], BF16),
                                      ("Q0", [128, 4, 128], BF16), ("Q1", [128, 4, 128], BF16)):
                    dd[nm_] = c.sb(nm_ + sfx, shp, dt_)
                dd["tq"], dd["od"], dd["sqo"] = dd["rhsD"], dd["E"], dd["Es"]
                dd["vn"], dd["ogc"] = dd["attn"], dd["kbg"]
                dd["_alias"] = dict(tq="rhsD", od="E", sqo="Es", vn="attn", ogc="kbg")
                D[d, par] = dd
        S32, Sbf = {}, {}
        for d in range(2):
            S32[d] = c.sb("S32_%d" % d, [128, 4, 128], F32)
            Sbf[d] = c.sb("Sbf_%d" % d, [128, 4, 128], BF16)
            p.I("vector", "memset", [], ["S32_%d" % d], S32[d][:], 0.0)
            p.I("vector", "memset", [], ["Sbf_%d" % d], Sbf[d][:], 0.0)

        def b4(ap2):
            return ap2.unsqueeze(2).broadcast_to([128, 4, 128])

        def bh(ap2):
            return ap2.unsqueeze(1).broadcast_to([128, 4, 128])

        def rep2(ap3):
            return ap3.unsqueeze(2).broadcast_to([128, 2, 2, 128])

        def v4(ap3):
            return ap3.rearrange("p (q r) d -> p q r d", q=2)

        def pre(d, i, par, second):
            dd = D[d, par]
            sfx = "_%d%d" % (d, par)
            al = dd["_alias"]
            N = lambda x: al.get(x, x) + sfx
            b0, b1, b2 = 3 * d, 3 * d + 1, 3 * d + 2
            n0, n1, n2 = "B%d" % b0, "B%d" % b1, "B%d" % b2
            r0 = i * 128
            qTc, kTc, ktm, vtm, zc, gb = (dd[k] for k in ("qTc", "kTc", "ktm", "vtm", "zc", "gb"))
            p.dma("sync", qTc[:], qT_d[:, :, r0:r0 + 128], writes=[N("qTc")])
            p.dma("sync", kTc[:], kT_d[:, :, r0:r0 + 128], writes=[N("kTc")])
            p.dma("sync", ktm[:], ktm_d[r0:r0 + 128, :], writes=[N("ktm")])
            p.dma("sync", vtm[:], vtm_d[r0:r0 + 128, :], writes=[N("vtm")])
            p.dma("sync", zc[:], z_d[r0:r0 + 128, :], writes=[N("zc")])
            p.dma("sync", gb[:], gb_d[r0:r0 + 128, :], writes=[N("gb")])
            Mm = masks["LE"] if d == 0 else masks["GE"]
            Vm = masks["GT"] if d == 0 else masks["LT"]
            mS = masks["GT"] if d == 0 else masks["LT"]
            mI = masks["GE"] if d == 0 else masks["LE"]
            g_ = gb[:, d * 4:(d + 1) * 4]
            be = gb[:, 8 + d * 4:8 + (d + 1) * 4]
            gs, ex, nbe = dd["gs"], dd["ex"], dd["nbe"]
            for q in range(2):
                p.I("tensor", "matmul", [N("kTc")], [n0], bk(b0)[:, q, :], lhsT=kTc[:, q, :], rhs=kTc[:, q, :],
                    start=True, stop=True)
                p.I("tensor", "matmul", [N("kTc"), N("qTc")], [n0], bk(b0)[:, 2 + q, :], lhsT=qTc[:, q, :],
                    rhs=kTc[:, q, :], start=True, stop=True)
            p.I("scalar", "copy", [], [n0, N("KQ")], out=dd["KQ"][:], in_=bk(b0))
            p.I("tensor", "matmul", [N("gb")], [n1], B[b1][:, 0:4], lhsT=Mm[:, :], rhs=g_, start=True, stop=True)
            p.I("tensor", "matmul", [N("gb"), "ones"], [n1], B[b1][:, 4:8], lhsT=ones[:, :], rhs=g_, start=True, stop=True)
            p.I("vector", "tensor_copy", [], [n1, N("gs")], out=gs[:, :], in_=B[b1][:, 0:8])
            p.I("scalar", "activation", [N("gs")], [N("ex")], out=ex[:, 0:4], in_=gs[:, 0:4], func=AF.Exp)
            p.I("vector", "tensor_tensor", [N("gs")], [N("gs")], out=gs[:, 0:4], in0=gs[:, 4:8], in1=gs[:, 0:4], op=ALU.subtract)
            p.I("scalar", "activation", [N("gs")], [N("ex")], out=ex[:, 4:12], in_=gs[:, 0:8], func=AF.Exp)
            p.I("vector", "tensor_scalar", [N("gb")], [N("nbe")], out=nbe[:, :], in0=be, scalar1=-1.0, scalar2=None, op0=ALU.mult)
            p.I("vector", "tensor_tensor", [N("nbe"), N("ex")], [N("sso")], out=dd["sso"][:, :], in0=nbe[:, :], in1=ex[:, 0:4], op=ALU.mult)
            k3 = ktm[:, :].rearrange("p (q d) -> p q d", q=2)
            for h in range(4):
                p.I("scalar", "activation", [N("ktm"), N("ex")], [N("kdec")], out=dd["kdec"][:, h, :], in_=k3[:, h // 2, :],
                    func=AF.Identity, scale=ex[:, 4 + h:5 + h])
                p.I("scalar", "activation", [N("vtm"), N("gb")], [N("vb")], out=dd["vb"][:, h, :], in_=vtm[:, h * 128:(h + 1) * 128],
                    func=AF.Identity, scale=be[:, h:h + 1])
            p.I("gpsimd", "tensor_tensor", [N("gb")], [N("rhsD")], out=dd["rhsD"][:], in0=bh(Vm[:, :]), in1=b4(g_), op=ALU.mult)
            p.I("tensor", "matmul", [N("rhsD")], [n2], B[b2][:, :], lhsT=Mm[:, :], rhs=dd["rhsD"][:].rearrange("p h s -> p (h s)"),
                start=True, stop=True)
            p.I("scalar", "activation", [], [n2, N("E")], out=dd["E"][:], in_=bk(b2), func=AF.Exp)
            p.I("gpsimd", "tensor_tensor", [N("E")], [N("Ei")], out=dd["Ei"][:], in0=dd["E"][:], in1=bh(mI[:, :]), op=ALU.mult)
            p.I("vector", "tensor_tensor", [N("E"), N("nbe")], [N("Es")], out=dd["Es"][:], in0=dd["E"][:], in1=b4(nbe[:, :]), op=ALU.mult)
            Y0 = dd["Y0"]
            p.I("vector", "tensor_tensor", [N("Es"), N("KQ")], [N("Es")], out=v4(dd["Es"][:]), in0=v4(dd["Es"][:]),
                in1=rep2(dd["KQ"][:, 0:2, :]), op=ALU.mult)
            p.I("gpsimd", "tensor_tensor", [N("Es"), "bm"], [N("Y0")], out=Y0[:], in0=dd["Es"][:], in1=bh(bm[:, 3 * d + 0, :]), op=ALU.mult)
            p.I("gpsimd", "tensor_tensor", [N("Es"), "bm"], [N("N1")], out=dd["N1"][:], in0=dd["Es"][:], in1=bh(bm[:, 3 * d + 1, :]), op=ALU.mult)
            p.I("vector", "tensor_tensor", [N("Es"), "bm"], [N("N2")], out=dd["N2"][:], in0=dd["Es"][:], in1=bh(bm[:, 3 * d + 2, :]), op=ALU.mult)
            p.I("gpsimd", "tensor_tensor", [N("Ei"), N("KQ")], [N("attn")], out=v4(dd["attn"][:]), in0=v4(dd["Ei"][:]),
                in1=rep2(dd["KQ"][:, 2:4, :]), op=ALU.mult)
            TRb = B[b0].bitcast(BF16)
            TRb1 = B[b1].bitcast(BF16)
            for h in range(4):
                p.I("tensor", "transpose", [N("Y0")], [n0], out=TRb[:, h * 128:(h + 1) * 128], in_=Y0[:, h, :],
                    identity=ident[:, :])
            for h in range(4):
                p.I("tensor", "transpose", [N("attn")], [n0], out=TRb[:, (4 + h) * 128:(5 + h) * 128], in_=dd["attn"][:, h, :],
                    identity=ident[:, :])
            for h in range(4):
                p.I("tensor", "transpose", [N("N1")], [n1], out=TRb1[:, h * 128:(h + 1) * 128], in_=dd["N1"][:, h, :],
                    identity=ident[:, :])
            XA = dd["XA"]
            p.I("scalar", "copy", [], [n0, N("XA")], out=XA[:], in_=TRb[:, :].rearrange("p (h s) -> p h s", h=8))
            p.I("scalar", "copy", [], [n1, N("N1T")], out=dd["N1T"][:],
                in_=TRb1[:, 0:512].rearrange("p (h s) -> p h s", h=4))
            P1 = dd["P0"]
            p.I("vector", "tensor_tensor", [N("XA")], [N("Pf0")], out=dd["Pf0"][:], in0=XA[:, 0:4, :], in1=bh(ident[:, :]), op=ALU.add)
            p.I("gpsimd", "tensor_copy", [N("Pf0")], [N("P0")], out=P1[:], in_=dd["Pf0"][:])
            Pf, Pfn = dd["Pf0"], N("Pf0")
            Xc, Xn_ = XA[:, 0:4, :], N("XA")
            Yc, Yn_ = Y0, N("Y0")
            Pc, Pn_ = P1, N("P0")
            for it in range(5):
                nb = (it + 1) % 2
                doP = it >= 1
                doX = it <= 2
                doY = it <= 3
                if doP:
                    for h in range(4):
                        p.I("tensor", "matmul", [Yn_, Pn_], [n2], bk(b2)[:, h, :], lhsT=Yc[:, h, :], rhs=Pc[:, h, :], start=True, stop=True)
                if doX:
                    for h in range(4):
                        p.I("tensor", "matmul", [Yn_, Xn_], [n0], bk(b0)[:, h, :], lhsT=Yc[:, h, :], rhs=Xc[:, h, :], start=True, stop=True)
                if doY:
                    for h in range(4):
                        p.I("tensor", "matmul", [Xn_, Yn_], [n1], bk(b1)[:, h, :], lhsT=Xc[:, h, :], rhs=Yc[:, h, :], start=True, stop=True)
                if doP:
                    Pnew, Pnn = dd["P%d" % nb], N("P%d" % nb)
                    Pfnew, Pfnn = dd["Pf%d" % nb], N("Pf%d" % nb)
                    p.I("vector", "tensor_tensor", [Pfn], [n2, Pfnn], out=Pfnew[:], in0=bk(b2), in1=Pf[:], op=ALU.add)
                    p.I("scalar", "copy", [Pfnn], [Pnn], out=Pnew[:], in_=Pfnew[:])
                    Pf, Pfn = Pfnew, Pfnn
                    Pc, Pn_ = Pnew, Pnn
                if doX:
                    Xnew, Xnn = dd["X%d" % nb], N("X%d" % nb)
                    p.I("scalar", "copy", [], [n0, Xnn], out=Xnew[:], in_=bk(b0))
                if doY:
                    Ynew, Ynn = dd["Y%d" % nb], N("Y%d" % nb)
                    if it % 2 == 0:
                        p.I("scalar", "copy", [], [n1, Ynn], out=Ynew[:], in_=bk(b1))
                    else:
                        p.I("vector", "tensor_copy", [], [n1, Ynn], out=Ynew[:], in_=bk(b1))
                if doX:
                    Xc, Xn_ = Xnew[:], Xnn
                if doY:
                    Yc, Yn_ = Ynew, Ynn
            for h in range(4):
                p.I("tensor", "transpose", [Pn_], [n0], out=TRb[:, h * 128:(h + 1) * 128], in_=Pc[:, h, :], identity=ident[:, :])
            Q0, Q1 = dd["Q0"], dd["Q1"]
            p.I("scalar", "copy", [], [n0, N("Q0")], out=Q0[:], in_=TRb[:, 0:512].rearrange("p (h s) -> p h s", h=4))
            for h in range(4):
                p.I("tensor", "matmul", [N("N1"), Pn_], [n1], bk(b1)[:, h, :], lhsT=dd["N1"][:, h, :], rhs=Pc[:, h, :], start=True, stop=True)
            for h in range(4):
                p.I("tensor", "matmul", [N("N1T"), N("Q0")], [n2], bk(b2)[:, h, :], lhsT=dd["N1T"][:, h, :], rhs=Q0[:, h, :], start=True, stop=True)
            p.I("scalar", "copy", [], [n1, N("WP")], out=dd["WP"][:], in_=bk(b1))
            p.I("vector", "tensor_copy", [], [n2, N("WQ")], out=dd["WQ"][:], in_=bk(b2))
            for h in range(4):
                p.I("tensor", "matmul", [N("Q0"), N("WP")], [n0], bk(b0)[:, h, :], lhsT=Q0[:, h, :], rhs=dd["WP"][:, h, :], start=True, stop=True)
            for h in range(4):
                p.I("tensor", "matmul", [Pn_, N("WQ")], [n1], bk(b1)[:, h, :], lhsT=Pc[:, h, :], rhs=dd["WQ"][:, h, :], start=True, stop=True)
            nbp = 0 if Pc is dd["P1"] else 1
            Pnew, Pnn = dd["P%d" % nbp], N("P%d" % nbp)
            Pfnew, Pfnn = dd["Pf%d" % nbp], N("Pf%d" % nbp)
            p.I("vector", "tensor_tensor", [Pfn], [n0, Pfnn], out=Pfnew[:], in0=bk(b0), in1=Pf[:], op=ALU.add)
            p.I("scalar", "copy", [Pfnn], [Pnn], out=Pnew[:], in_=Pfnew[:])
            p.I("vector", "tensor_tensor", [N("Q0")], [n1, N("Q1")], out=Q1[:], in0=bk(b1), in1=Q0[:], op=ALU.add)
            Pf, Pfn, Pc, Pn_ = Pfnew, Pfnn, Pnew, Pnn
            for h in range(4):
                p.I("tensor", "matmul", [N("N2"), Pn_], [n2], bk(b2)[:, h, :], lhsT=dd["N2"][:, h, :], rhs=Pc[:, h, :], start=True, stop=True)
            p.I("scalar", "copy", [], [n2, N("WP")], out=dd["WP"][:], in_=bk(b2))
            for h in range(4):
                p.I("tensor", "matmul", [N("Q1"), N("WP")], [n0], bk(b0)[:, h, :], lhsT=Q1[:, h, :], rhs=dd["WP"][:, h, :], start=True, stop=True)
            nbp = 1 - nbp
            Pnew, Pnn = dd["P%d" % nbp], N("P%d" % nbp)
            p.I("vector", "tensor_tensor", [Pfn], [n0, Pnn], out=Pnew[:], in0=bk(b0), in1=Pf[:], op=ALU.add)
            Pc, Pn_ = Pnew, Pnn
            return Pc, Pn_

        def scan(d, i, par, second, Pc, Pn_):
            dd = D[d, par]
            sfx = "_%d%d" % (d, par)
            al = dd["_alias"]
            N = lambda x: al.get(x, x) + sfx
            Sn, Sb = "S32_%d" % d, "Sbf_%d" % d
            XA, qTc, zc, ex = dd["XA"], dd["qTc"], dd["zc"], dd["ex"]
            if second:
                p.dma("sync", dd["ol"][:], ost_d[i * 128:(i + 1) * 128, :].rearrange("p (h d) -> p h d", h=4),
                      reads=["ost%d" % i], writes=[N("ol")])
            kTc = dd["kTc"]
            for h in range(4):
                p.I("tensor", "matmul", [N("kTc"), Sb], ["B6"], bk(6)[:, h, :], lhsT=kTc[:, h // 2, :], rhs=Sbf[d][:, h, :],
                    start=True, stop=True)
            for h in range(4):
                p.I("tensor", "matmul", [N("qTc"), Sb], ["B7"], bk(7)[:, h, :], lhsT=qTc[:, h // 2, :], rhs=Sbf[d][:, h, :],
                    start=True, stop=True)
            rr = dd["kbg"]
            for h in range(4):
                p.I("vector", "scalar_tensor_tensor", [N("sso"), N("vb")], ["B6", N("kbg")], out=rr[:, h, :], in0=bk(6)[:, h, :],
                    scalar=dd["sso"][:, h:h + 1], in1=dd["vb"][:, h, :], op0=ALU.mult, op1=ALU.add)
            for h in range(4):
                p.I("tensor", "matmul", [Pn_, N("kbg")], ["B6"], bk(6)[:, h, :], lhsT=Pc[:, h, :], rhs=rr[:, h, :],
                    start=True, stop=True)
            p.I("vector", "tensor_copy", [], ["B6", N("vn")], out=dd["vn"][:], in_=bk(6))
            p.I("vector", "tensor_tensor", [N("ex")], ["B7", N("tq")], out=dd["tq"][:], in0=bk(7), in1=b4(ex[:, 0:4]), op=ALU.mult)
            for h in range(4):
                p.I("tensor", "matmul", [N("kdec"), N("vn")], ["B7"], bk(7)[:, h, :], lhsT=dd["kdec"][:, h, :], rhs=dd["vn"][:, h, :],
                    start=True, stop=True)
            for h in range(4):
                p.I("tensor", "matmul", [N("XA"), N("vn")], ["B6"], bk(6)[:, h, :], lhsT=XA[:, 4 + h, :], rhs=dd["vn"][:, h, :],
                    start=True, stop=True)
            for h in range(4):
                p.I("vector", "scalar_tensor_tensor", [N("ex")], ["B7", Sn], out=S32[d][:, h, :], in0=S32[d][:, h, :],
                    scalar=ex[:, 8 + h:9 + h], in1=bk(7)[:, h, :], op0=ALU.mult, op1=ALU.add)
            p.I("scalar", "copy", [Sn], [Sb], out=Sbf[d][:], in_=S32[d][:])
            od = dd["od"]
            p.I("vector", "tensor_tensor", [N("tq")], ["B6", N("od")], out=od[:], in0=bk(6), in1=dd["tq"][:], op=ALU.add)
            if not second:
                p.dma("sync", ost_d[i * 128:(i + 1) * 128, :], od[:].rearrange("p h d -> p (h d)"), reads=[N("od")],
                      writes=["ost%d" % i])
                return
            p.I("gpsimd", "tensor_tensor", [N("od"), N("ol")], [N("od")], out=od[:], in0=od[:], in1=dd["ol"][:], op=ALU.add)
            p.I("scalar", "activation", [N("od")], [N("sqo")], out=dd["sqo"][:], in_=od[:], func=AF.Square)
            p.I("vector", "tensor_reduce", [N("sqo")], [N("sso")], out=dd["sso"][:, :], in_=dd["sqo"][:], axis=AX.X, op=ALU.add)
            p.I("scalar", "activation", [N("sso"), "eps"], [N("sso")], out=dd["sso"][:, :], in_=dd["sso"][:, :], func=AF.Sqrt,
                scale=1.0 / 128, bias=eps[:, :])
            p.I("vector", "reciprocal", [N("sso")], [N("sso")], out=dd["sso"][:, :], in_=dd["sso"][:, :])
            p.I("vector", "tensor_tensor", [N("sso"), N("od")], [N("od")], out=od[:], in0=od[:], in1=b4(dd["sso"][:, :]), op=ALU.mult)
            p.I("gpsimd", "tensor_tensor", [N("od"), "on"], [N("od")], out=od[:], in0=od[:], in1=bh(on[:, :]), op=ALU.mult)
            p.I("gpsimd", "tensor_tensor", [N("od"), N("zc")], [N("ogc")], out=dd["ogc"][:], in0=od[:],
                in1=zc[:, :].rearrange("p (h d) -> p h d", h=4), op=ALU.mult)
            outs.append(p.dma("sync", og_d[i * 128:(i + 1) * 128, :], dd["ogc"][:].rearrange("p h d -> p (h d)"), reads=[N("ogc")]))

        pend = {}
        pend[0, 0] = pre(0, 0, 0, False)
        pend[1, nchunks - 1] = pre(1, nchunks - 1, 0, False)
        for t in range(nchunks):
            par = t % 2
            if t + 1 < nchunks:
                sec1 = (t + 1) >= nchunks // 2
                pend[0, t + 1] = pre(0, t + 1, 1 - par, sec1)
                pend[1, nchunks - 2 - t] = pre(1, nchunks - 2 - t, 1 - par, sec1)
            second = t >= nchunks // 2
            scan(0, t, par, second, *pend.pop((0, t)))
            scan(1, nchunks - 1 - t, par, second, *pend.pop((1, nchunks - 1 - t)))
        p.emit(final_wait_ops=outs)
    return nc, prog


def build_gdn_out_program():
    nc = bass.Bass("TRN2", target_bir_lowering=False)
    og_d = nc.dram_tensor("og", [TOK, 2048], BF16, kind="ExternalInput").ap()
    h_d = nc.dram_tensor("h", [TOK, D_MODEL], F32, kind="ExternalInput").ap()
    w_d = nc.dram_tensor("w_out", [2048, D_MODEL], F32, kind="ExternalInput").ap()
    out_d = nc.dram_tensor("out", [TOK, D_MODEL], F32, kind="ExternalOutput").ap()
    prog = Prog(nc)
    p = prog
    outs = []
    with contextlib.ExitStack() as st:
        c = Ctx(nc, prog, st)
        ident = make_ident(c)
        p.fence()
        w = c.sb("w", [128, 16, D_MODEL], BF16)
        wv = w_d.rearrange("(k p) c -> p k c", p=128)
        for k in range(0, 16, 4):
            p.dma("gpsimd", w[:, k:k + 4, :], wv[:, k:k + 4, :], writes=["w"])
        ogt = [c.sb("ogt%d" % i, [128, 2048], BF16) for i in range(2)]
        ht = [c.sb("ht%d" % i, [128, D_MODEL], F32) for i in range(2)]
        ho = [c.sb("ho%d" % i, [128, D_MODEL], F32) for i in range(2)]
        ogT = [c.sb("ogT%d" % i, [128, 16, 128], BF16) for i in range(2)]
        TP = c.ps("TP", [128, 16, 128], BF16)
        AC = c.ps("AC", [128, 1024], F32)
        for t in range(16):
            b = t % 2
            p.dma("sync", ogt[b][:], og_d[t * 128:(t + 1) * 128, :], writes=["ogt%d" % b])
            p.dma("sync", ht[b][:], h_d[t * 128:(t + 1) * 128, :], writes=["ht%d" % b])
            for k in range(16):
                p.I("tensor", "transpose", ["ogt%d" % b], ["TP"], out=TP[:, k, :], in_=ogt[b][:, k * 128:(k + 1) * 128],
                    identity=ident[:, :])
            p.I("vector", "tensor_copy", [], ["TP", "ogT%d" % b], out=ogT[b][:], in_=TP[:])
            for hf in range(2):
                for k in range(16):
                    p.I("tensor", "matmul", ["ogT%d" % b, "w"], ["AC"], AC[:, hf * 512:(hf + 1) * 512], lhsT=ogT[b][:, k, :],
                        rhs=w[:, k, hf * 512:(hf + 1) * 512], start=(k == 0), stop=(k == 15))
            p.I("vector", "tensor_tensor", ["ht%d" % b], ["AC", "ho%d" % b], out=ho[b][:], in0=AC[:, :], in1=ht[b][:], op=ALU.add)
            outs.append(p.dma("sync", out_d[t * 128:(t + 1) * 128, :], ho[b][:], reads=["ho%d" % b]))
        p.emit(final_wait_ops=outs)
    return nc, prog


_CACHE = {}


def _prog(name, builder, *a):
    key = (name,) + a
    if key not in _CACHE:
        _CACHE[key] = builder(*a)[0]
    return _CACHE[key]


def _rep(v, n=128):
    return np.ascontiguousarray(np.broadcast_to(np.asarray(v, np.float32)[None], (n,) + tuple(np.shape(v))))


def _halo(hb, c, pad):
    b, j = c // 4, c % 4
    z = np.zeros((pad, hb.shape[-1]), hb.dtype)
    ext = np.concatenate([z, hb[b], z], 0)
    return np.ascontiguousarray(ext[j * TOK:j * TOK + TOK + 2 * pad])


def _run(nc, maps):
    return run_bass_kernel_spmd(nc, maps, core_ids=list(range(NCORES))).results


def _ffn_layer(h, g, gf, wup, cwt, cb, wdn, final):
    nc = _prog("ffn", build_ffn_program, final)
    cw = np.concatenate([cwt, cb[None]], 0)
    cw_l = np.ascontiguousarray(cw.reshape(4, 44, 128).transpose(2, 1, 0))
    maps = []
    for c in range(NCORES):
        hh = _halo(h, c, 1)
        maps.append(dict(h=hh, g_ffn=_rep(g), g_fin=_rep(gf), w_up=wup, w_down=wdn, cw=cw_l))
    res = _run(nc, maps)
    return np.stack([np.concatenate([res[b * 4 + j]["out"] for j in range(4)], 0) for b in range(BATCH)], 0)


def kernel(x, norm_mix, norm_ffn, norm_final, attn_w_in, attn_q_norm, attn_k_norm, attn_w_out, gdn_w_in,
           gdn_conv_w, gdn_conv_b, gdn_a_log, gdn_dt_bias, gdn_o_norm, gdn_w_out, ffn_w_up, ffn_conv_w,
           ffn_conv_b, ffn_w_down):
    f32 = lambda a: np.ascontiguousarray(np.asarray(a, dtype=np.float32))
    x = f32(x)
    rope = rope_tables_np()
    nc = _prog("ap", build_attn_proj_program)
    maps = []
    for c in range(NCORES):
        b, j = c // 4, c % 4
        maps.append(dict(x=np.ascontiguousarray(x[b, j * TOK:(j + 1) * TOK]), g_mix=_rep(norm_mix[0]), w_in=f32(attn_w_in[0]),
                         qg=_rep(attn_q_norm[0]), kg=_rep(attn_k_norm[0]), rope=np.ascontiguousarray(rope[j * TOK:(j + 1) * TOK])))
    r1 = _run(nc, maps)
    nc = _prog("ac", build_attn_core_program)
    maps = []
    for c in range(NCORES):
        b, j = c // 4, c % 4
        kT = np.ascontiguousarray(np.concatenate([r1[b * 4 + i]["kT"] for i in range(4)], axis=2))
        v = np.ascontiguousarray(np.concatenate([r1[b * 4 + i]["v"] for i in range(4)], axis=1))
        maps.append(dict(qT=r1[c]["qT"], kT=kT, v=v, gate=r1[c]["gate"], x=np.ascontiguousarray(x[b, j * TOK:(j + 1) * TOK]),
                         w_out=f32(attn_w_out[0]), qg=_rep(attn_q_norm[0]), kg=_rep(attn_k_norm[0])))
    r2 = _run(nc, maps)
    h = np.stack([np.concatenate([r2[b * 4 + j]["out"] for j in range(4)], 0) for b in range(BATCH)], 0)
    h = _ffn_layer(h, f32(norm_ffn[0]), f32(norm_final), f32(ffn_w_up[0]), f32(ffn_conv_w[0]), f32(ffn_conv_b[0]),
                   f32(ffn_w_down[0]), False)
    nc = _prog("gp", build_gdn_proj_program)
    cwf = np.concatenate([f32(gdn_conv_w[0]), f32(gdn_conv_b[0])[None]], 0)
    cw_l = np.ascontiguousarray(cwf.reshape(6, 32, 128).transpose(2, 1, 0))
    ad = np.stack([f32(gdn_a_log[0]).reshape(32), f32(gdn_dt_bias[0]).reshape(32)], 0)
    maps = []
    for c in range(NCORES):
        maps.append(dict(h=_halo(h, c, 2), g_mix=_rep(norm_mix[1]), w_in=f32(gdn_w_in[0]), cw=cw_l, ad=_rep(ad)))
    r3 = _run(nc, maps)
    nc = _prog("gs", build_gdn_scan_program, 64)
    maps = []
    for c in range(NCORES):
        b, j = c // 4, c % 4
        grp = [r3[b * 4 + i] for i in range(4)]
        qT = np.ascontiguousarray(np.concatenate([g_["qT"][:, 2 * j:2 * j + 2] for g_ in grp], axis=2))
        kT = np.ascontiguousarray(np.concatenate([g_["kT"][:, 2 * j:2 * j + 2] for g_ in grp], axis=2))
        ktm = np.ascontiguousarray(np.concatenate([g_["k_tm"][:, 256 * j:256 * (j + 1)] for g_ in grp], axis=0))
        vtm = np.ascontiguousarray(np.concatenate([g_["v_tm"][:, 512 * j:512 * (j + 1)] for g_ in grp], axis=0))
        zz = np.ascontiguousarray(np.concatenate([g_["z"][:, 512 * j:512 * (j + 1)] for g_ in grp], axis=0))
        gball = np.concatenate([g_["gb"] for g_ in grp], axis=0)
        hs = slice(4 * j, 4 * j + 4)
        gb = np.ascontiguousarray(np.concatenate([gball[:, 0:16][:, hs], gball[:, 16:32][:, hs],
                                                  gball[:, 32:48][:, hs], gball[:, 48:64][:, hs]], axis=1))
        maps.append(dict(qT=qT, kT=kT, k_tm=ktm, v_tm=vtm, z=zz, gb=gb, on=_rep(gdn_o_norm[0]), bm=block_masks_np()))
    r4 = _run(nc, maps)
    nc = _prog("go", build_gdn_out_program)
    maps = []
    for c in range(NCORES):
        b, j = c // 4, c % 4
        og = np.ascontiguousarray(np.concatenate([r4[b * 4 + i]["og"][j * TOK:(j + 1) * TOK] for i in range(4)], axis=1))
        maps.append(dict(og=og, h=np.ascontiguousarray(h[b, j * TOK:(j + 1) * TOK]), w_out=f32(gdn_w_out[0])))
    r5 = _run(nc, maps)
    h = np.stack([np.concatenate([r5[b * 4 + j]["out"] for j in range(4)], 0) for b in range(BATCH)], 0)
    out = _ffn_layer(h, f32(norm_ffn[1]), f32(norm_final), f32(ffn_w_up[1]), f32(ffn_conv_w[1]), f32(ffn_conv_b[1]),
                     f32(ffn_w_down[1]), True)
    return out.astype(np.float32)
```

```python
import contextlib
import numpy as np
import concourse.bass as bass
import concourse.mybir as mybir
from concourse.bass_utils import run_bass_kernel_spmd

F32 = mybir.dt.float32
BF16 = mybir.dt.bfloat16
AF = mybir.ActivationFunctionType
ALU = mybir.AluOpType
AX = mybir.AxisListType

D_MODEL = 1024
BATCH = 2
SEQ = 8192
EPS = 1e-6
D_FF = 2816
NCORES = 8
TOK = 2048

COMPUTE = ("tensor", "vector", "scalar", "gpsimd")
N_DMA_SEMS = 24


class Prog:
    def __init__(self, nc, same_engine_sync=True, store_defer=0.0):
        self.nc = nc
        self.store_defer = store_defer
        self.ops = []
        self.last_w = {}
        self.readers = {}
        self.same_engine_sync = same_engine_sync
        self.n_dma = 0
        self.dma_sem_last = {}
        self.fence_deps = set()

    def fence(self):
        self.fence_deps = set(i for i, o in enumerate(self.ops) if not o["dma"])

    def op(self, eng, fn, reads=(), writes=(), dma=False):
        idx = len(self.ops)
        deps = set()
        for r in reads:
            if r in self.last_w:
                deps.add(self.last_w[r])
        for w in writes:
            if w in self.last_w:
                deps.add(self.last_w[w])
            for rd in self.readers.get(w, ()):
                deps.add(rd)
        sem_slot = None
        if dma:
            if eng == "gpsimd":
                self.n_dma_sw = getattr(self, "n_dma_sw", 0) + 1
                sem_slot = 16 + self.n_dma_sw % (N_DMA_SEMS - 16)
            else:
                sem_slot = self.n_dma % 16
            self.n_dma += 1
            prev = self.dma_sem_last.get(sem_slot)
            if prev is not None:
                deps.add(prev)
            self.dma_sem_last[sem_slot] = idx
        deps |= self.fence_deps
        deps.discard(idx)
        self.ops.append(dict(eng=eng, fn=fn, deps=deps, dma=dma, sem_slot=sem_slot))
        for w in writes:
            self.last_w[w] = idx
            self.readers[w] = []
        for r in reads:
            if r not in writes:
                self.readers.setdefault(r, []).append(idx)
        return idx

    def I(self, eng, method, reads, writes, *args, **kwargs):
        def fn(e, method=method, args=args, kwargs=kwargs):
            return getattr(e, method)(*args, **kwargs)
        idx = self.op(eng, fn, reads, writes)
        try:
            if method == "matmul":
                r = kwargs["rhs"]
                n = int(np.prod(r.shape[1:]))
                dur = 40 + n * (2.0 if r.dtype == F32 else 0.5)
            elif method == "transpose":
                dur = 110.0
            else:
                o = kwargs.get("out", args[0] if args else None)
                n = int(np.prod(o.shape[1:]))
                dur = {"scalar": 220 + n / 1.4, "vector": 90 + n / 0.96, "gpsimd": 160 + n / 0.45}[eng]
        except Exception:
            dur = 300.0
        self.ops[idx]["dur"] = dur
        return idx

    def dma(self, queue, out, in_, reads=(), writes=()):
        def fn(e, out=out, in_=in_):
            return e.dma_start(out=out, in_=in_)
        idx = self.op(queue, fn, reads, writes, dma=True)
        try:
            nbytes = int(np.prod(out.shape)) * (4 if out.dtype == F32 else 2)
        except Exception:
            nbytes = 1 << 16
        self.ops[idx]["dur"] = 2000 + nbytes / 150.0
        self.ops[idx]["store"] = (str(out.space) == "DRAM")
        return idx

    def reorder(self, final_wait_ops):
        import heapq
        ops = self.ops
        n = len(ops)
        succ = [[] for _ in range(n)]
        indeg = [0] * n
        for i, o in enumerate(ops):
            for d in o["deps"]:
                succ[d].append(i)
            indeg[i] = len(o["deps"])
        ready_t = [0.0] * n
        STORE_DEFER = self.store_defer
        heap = [(0.0, i) for i in range(n) if indeg[i] == 0]
        heapq.heapify(heap)
        eng_free = {}
        order = []
        LAT = 900.0
        while heap:
            rt, i = heapq.heappop(heap)
            o = ops[i]
            e = o["eng"]
            start = max(rt, eng_free.get(e, 0.0))
            dur = o.get("dur", 300.0)
            if o["dma"]:
                eng_free[e] = start + 60.0
                fin = start + dur
            else:
                eng_free[e] = start + dur
                fin = start + dur
            order.append(i)
            for s_ in succ[i]:
                so = ops[s_]
                lat = 0.0 if (so["eng"] == e and not o["dma"] and e == "tensor") else LAT
                t = fin + lat
                if t > ready_t[s_]:
                    ready_t[s_] = t
                indeg[s_] -= 1
                if indeg[s_] == 0:
                    heapq.heappush(heap, (ready_t[s_] + (STORE_DEFER if so.get("store") else 0.0), s_))
        assert len(order) == n
        pos = {old: new for new, old in enumerate(order)}
        new_ops = []
        for old in order:
            o = ops[old]
            o["deps"] = set(pos[d] for d in o["deps"])
            new_ops.append(o)
        self.ops = new_ops
        self.est_ns = max(eng_free.values()) if eng_free else 0
        return [pos[d] for d in final_wait_ops]

    def emit(self, final_wait_ops=(), schedule=True):
        nc = self.nc
        if schedule:
            final_wait_ops = self.reorder(list(final_wait_ops))
        ops = self.ops
        engines = ("sync",) + COMPUTE
        waited_eng = {e: {p: -1 for p in COMPUTE} for e in engines}
        waited_dma = {e: set() for e in engines}
        for i, o in enumerate(ops):
            e = o["eng"]
            need_eng = {}
            need_dma = []
            for d in sorted(o["deps"]):
                po = ops[d]
                if po["dma"]:
                    if d not in waited_dma[e]:
                        need_dma.append(d)
                        waited_dma[e].add(d)
                else:
                    pe = po["eng"]
                    if pe == e and (pe == "tensor" or not self.same_engine_sync):
                        continue
                    if d > waited_eng[e][pe]:
                        need_eng[pe] = max(need_eng.get(pe, -1), d)
            for pe, d in need_eng.items():
                waited_eng[e][pe] = d
            o["waits"] = list(need_eng.values()) + need_dma
        final_e = "sync"
        fw = []
        for d in final_wait_ops:
            fw.append(d)
        signal = set()
        for o in ops:
            for d in o["waits"]:
                signal.add(d)
        for d in fw:
            signal.add(d)
        cnt = {e: 0 for e in COMPUTE}
        dma_cnt = {}
        for i, o in enumerate(ops):
            if o["dma"]:
                s = o["sem_slot"]
                dma_cnt[s] = dma_cnt.get(s, 0) + 16
                o["sig"] = ("dma", s, dma_cnt[s])
            elif i in signal:
                cnt[o["eng"]] += 1
                o["sig"] = ("eng", o["eng"], cnt[o["eng"]])
            else:
                o["sig"] = None
        self.stats = dict(n_ops=len(ops), signals=dict(cnt), n_dma=self.n_dma)
        with contextlib.ExitStack() as st:
            esem = {e: st.enter_context(nc.semaphore("s_" + e)) for e in COMPUTE}
            dsem = [st.enter_context(nc.semaphore("d_%d" % k)) for k in range(N_DMA_SEMS)]
            block = st.enter_context(nc.Block())

            def semof(sig):
                if sig[0] == "dma":
                    return dsem[sig[1]], sig[2]
                return esem[sig[1]], sig[2]

            def run(ename):
                def body(eng):
                    for i, o in enumerate(ops):
                        if o["eng"] != ename:
                            continue
                        for d in o["waits"]:
                            s, v = semof(ops[d]["sig"])
                            eng.wait_ge(s, v)
                        ins = o["fn"](eng)
                        sig = o["sig"]
                        if sig is not None:
                            s, v = semof(sig)
                            ins.then_inc(s, 16 if sig[0] == "dma" else 1)
                    if ename == final_e:
                        for d in fw:
                            s, v = semof(ops[d]["sig"])
                            eng.wait_ge(s, v)
                return body

            block.sync(run("sync"))
            block.tensor(run("tensor"))
            block.vector(run("vector"))
            block.scalar(run("scalar"))
            block.gpsimd(run("gpsimd"))


class Ctx:
    def __init__(self, nc, prog, st):
        self.nc, self.p, self.st = nc, prog, st
        self.k = 0

    def sb(self, name, shape, dt):
        return self.st.enter_context(self.nc.sbuf_tensor(name, list(shape), dt))

    def ps(self, name, shape, dt):
        return self.st.enter_context(self.nc.psum_tensor(name, list(shape), dt))


def emit_rmsnorm_T(c, src_ap, n, g_sb, dstT, col0, tag, bufs):
    p = c.p
    sq, ss, rs, yb, pT, ident = (bufs[k] for k in ("sq", "ss", "rs", "yb", "pT", "ident"))
    nm = bufs["names"]
    p.I("scalar", "activation", [tag], [nm["sq"], nm["ss"]],
        out=sq[0:n, :], in_=src_ap, func=AF.Square, accum_out=ss[0:n, :])
    p.I("scalar", "activation", [nm["ss"], "eps"], [nm["rs"]],
        out=rs[0:n, :], in_=ss[0:n, :], func=AF.Sqrt, scale=1.0 / D_MODEL, bias=bufs["eps"][0:n, :])
    p.I("vector", "reciprocal", [nm["rs"]], [nm["rs"]], out=rs[0:n, :], in_=rs[0:n, :])
    p.I("vector", "scalar_tensor_tensor", [tag, nm["rs"], "g_sb"], [nm["yb"]],
        out=yb[0:n, :], in0=src_ap, scalar=rs[0:n, :], in1=g_sb[0:n, :], op0=ALU.mult, op1=ALU.mult)
    for k in range(8):
        p.I("tensor", "transpose", [nm["yb"]], [nm["pT"]],
            out=pT[:, k, 0:n], in_=yb[0:n, k * 128:(k + 1) * 128], identity=ident[0:n, 0:n])
    p.I("vector", "tensor_copy", [nm["pT"]], [bufs["dst_name"]],
        out=dstT[:, :, col0:col0 + n], in_=pT[:, :, 0:n])


def make_ident(c, name="ident"):
    ident = c.sb(name, [128, 128], BF16)
    c.p.I("gpsimd", "memset", [], [name], ident[:], 1.0)
    c.p.I("gpsimd", "affine_select", [name], [name], out=ident[:], in_=ident[:], pattern=[[-1, 128]],
          compare_op=ALU.is_equal, fill=0.0, base=0, channel_multiplier=1)
    return ident


def ffn_blocks(ntok):
    out = []
    t = 0
    while t < ntok:
        n = min(254, ntok - t)
        out.append((t, n))
        t += n
    return out


def build_ffn_program(final_norm, pre=None):
    nc = bass.Bass("TRN2", target_bir_lowering=False)
    h_d = nc.dram_tensor("h", [TOK + 2, D_MODEL], F32, kind="ExternalInput").ap()
    g_d = nc.dram_tensor("g_ffn", [128, D_MODEL], F32, kind="ExternalInput").ap()
    gf_d = nc.dram_tensor("g_fin", [128, D_MODEL], F32, kind="ExternalInput").ap()
    wup_d = nc.dram_tensor("w_up", [D_MODEL, 2 * D_FF], F32, kind="ExternalInput").ap()
    wdn_d = nc.dram_tensor("w_down", [D_FF, D_MODEL], F32, kind="ExternalInput").ap()
    cw_d = nc.dram_tensor("cw", [128, 44, 4], F32, kind="ExternalInput").ap()
    out_d = nc.dram_tensor("out", [TOK, D_MODEL], F32, kind="ExternalOutput").ap()
    prog = Prog(nc, store_defer=15000.0)
    with contextlib.ExitStack() as st:
        c = Ctx(nc, prog, st)
        emit_ffn(c, h_d, g_d, gf_d, wup_d, wdn_d, cw_d, out_d, final_norm)
    return nc, prog


def emit_ffn(c, h_d, g_d, gf_d, wup_d, wdn_d, cw_d, out_d, final_norm):
    p = c.p
    ident = make_ident(c)
    wup = c.sb("wup", [128, 8, 2 * D_FF], BF16)
    wdn = c.sb("wdn", [128, 22, D_MODEL], BF16)
    cw = c.sb("cw_sb", [128, 44, 4], F32)
    g_sb = c.sb("g_sb", [128, D_MODEL], F32)
    gf_sb = c.sb("gf_sb", [128, D_MODEL], F32)
    eps = c.sb("eps", [128, 1], F32)
    p.I("vector", "memset", [], ["eps"], eps[:], EPS)
    p.fence()
    p.dma("sync", g_sb[:], g_d, writes=["g_sb"])
    p.dma("sync", gf_sb[:], gf_d, writes=["gf_sb"])
    p.dma("sync", cw[:], cw_d, writes=["cw"])
    wup_v = wup_d.rearrange("(k p) c -> p k c", p=128)
    wdn_v = wdn_d.rearrange("(j p) c -> p j c", p=128)
    WG = 2
    for j0 in range(0, 22, WG):
        j1 = min(22, j0 + WG)
        for hh in range(2):
            c0, c1 = (hh * 22 + j0) * 128, (hh * 22 + j1) * 128
            p.dma("gpsimd", wup[:, :, c0:c1], wup_v[:, :, c0:c1], writes=["wup%d" % (j0 // WG)])
        p.dma("gpsimd", wdn[:, j0:j1, :], wdn_v[:, j0:j1, :], writes=["wdn%d" % (j0 // WG)])

    NB = 2
    xt = [c.sb("xt%d" % i, [128, D_MODEL], F32) for i in range(NB)]
    sq = c.sb("sq", [128, D_MODEL], F32)
    ss = c.sb("ss", [128, 1], F32)
    rs = c.sb("rs", [128, 1], F32)
    yb = c.sb("yb", [128, D_MODEL], BF16)
    yT = [c.sb("yT%d" % i, [128, 8, 256], BF16) for i in range(2)]
    NR = 3
    t1 = [c.sb("t1_%d" % i, [128, 254], F32) for i in range(2 * NR)]
    t2 = [c.sb("t2_%d" % i, [128, 254], F32) for i in range(2 * NR)]
    t3 = [c.sb("t3_%d" % i, [128, 254], F32) for i in range(2 * NR)]
    sas = [c.sb("sa%d" % i, [128, 254], F32) for i in range(NR)]
    mT = [c.sb("mT%d" % i, [128, 254], BF16) for i in range(NR)]
    hres = [c.sb("hres%d" % i, [128, D_MODEL], F32) for i in range(2)]
    hout = [c.sb("hout%d" % i, [128, D_MODEL], F32) for i in range(2)]
    pT = c.ps("pT", [128, 8, 128], BF16)
    U = [c.ps("U%d" % i, [128, 512], F32) for i in range(3)]
    Dp = [c.ps("D%d" % i, [128, 512], F32) for i in range(4)]
    nbufs = dict(sq=sq, ss=ss, rs=rs, yb=yb, pT=pT, ident=ident, eps=eps,
                 names=dict(sq="sq", ss="ss", rs="rs", yb="yb", pT="pT"))

    out_ops = []
    xi = 0
    for bi, (t0, n) in enumerate(ffn_blocks(TOK)):
        ncol = n + 2
        yTb = yT[bi % 2]
        yname = "yT%d" % (bi % 2)
        nbufs["dst_name"] = yname
        r = 0
        while r < ncol:
            rn = min(128, ncol - r)
            x = xt[xi % NB]
            xname = "xt%d" % (xi % NB)
            xi += 1
            p.dma("sync", x[0:rn, :], h_d[t0 + r:t0 + r + rn, :], writes=[xname])
            emit_rmsnorm_T(c, x[0:rn, :], rn, g_sb, yTb, r, xname, nbufs)
            r += rn
        subs = []
        s0 = 0
        while s0 < n:
            sn = min(127, n - s0)
            subs.append((s0, sn))
            s0 += sn

        def up(j):
            Uj = U[j % 3]
            un = "U%d" % (j % 3)
            for half, cj in ((0, j), (1, j + 22)):
                for k in range(8):
                    p.I("tensor", "matmul", ["wup%d" % (j // 2), yname], [un],
                        Uj[:, half * 256:half * 256 + ncol], lhsT=wup[:, k, cj * 128:(cj + 1) * 128],
                        rhs=yTb[:, k, 0:ncol], start=(k == 0), stop=(k == 7))

        def ew(j):
            Uj = U[j % 3]
            un = "U%d" % (j % 3)
            b = j % NR
            for half, cj in ((0, j), (1, j + 22)):
                o = half * 256
                ti_ = half * NR + b
                tt1, tt2, tt3 = t1[ti_], t2[ti_], t3[ti_]
                p.I("scalar", "activation", ["cw"], [un, "t1_%d" % ti_],
                    out=tt1[:, 0:n], in_=Uj[:, o + 1:o + 1 + n], func=AF.Identity,
                    scale=cw[:, cj, 1:2], bias=cw[:, cj, 3:4])
                p.I("vector", "scalar_tensor_tensor", ["cw", "t1_%d" % ti_], [un, "t2_%d" % ti_],
                    out=tt2[:, 0:n], in0=Uj[:, o:o + n], scalar=cw[:, cj, 0:1], in1=tt1[:, 0:n],
                    op0=ALU.mult, op1=ALU.add)
                p.I("vector", "scalar_tensor_tensor", ["cw", "t2_%d" % ti_], [un, "t3_%d" % ti_],
                    out=tt3[:, 0:n], in0=Uj[:, o + 2:o + 2 + n], scalar=cw[:, cj, 2:3], in1=tt2[:, 0:n],
                    op0=ALU.mult, op1=ALU.add)
            p.I("scalar", "activation", ["t3_%d" % b], ["sa%d" % b], out=sas[b][:, 0:n], in_=t3[b][:, 0:n], func=AF.Silu)
            p.I("gpsimd", "tensor_tensor", ["sa%d" % b, "t3_%d" % (NR + b)], ["mT%d" % b],
                out=mT[b][:, 0:n], in0=sas[b][:, 0:n], in1=t3[NR + b][:, 0:n], op=ALU.mult)

        def down(j):
            b = j % NR
            m = mT[b]
            for si, (s0, sn) in enumerate(subs):
                for hf in range(2):
                    p.I("tensor", "matmul", ["mT%d" % b, "wdn%d" % (j // 2)], ["D%d" % (si * 2 + hf)],
                        Dp[si * 2 + hf][0:sn, :], lhsT=m[:, s0:s0 + sn],
                        rhs=wdn[:, j, hf * 512:(hf + 1) * 512], start=(j == 0), stop=(j == 21))

        for si, (s0, sn) in enumerate(subs):
            p.dma("sync", hres[si][0:sn, :], h_d[1 + t0 + s0:1 + t0 + s0 + sn, :], writes=["hres%d" % si])
        up(0)
        for j in range(22):
            if j + 1 < 22:
                up(j + 1)
            ew(j)
            down(j)
        for si, (s0, sn) in enumerate(subs):
            hr = hres[si]
            ho = hout[si]
            hn = "hout%d" % si
            for hf in range(2):
                p.I("vector", "tensor_tensor", ["hres%d" % si], ["D%d" % (si * 2 + hf), hn],
                    out=ho[0:sn, hf * 512:(hf + 1) * 512], in0=Dp[si * 2 + hf][0:sn, :],
                    in1=hr[0:sn, hf * 512:(hf + 1) * 512], op=ALU.add)
            if final_norm:
                p.I("scalar", "activation", [hn], ["sq", "ss"],
                    out=sq[0:sn, :], in_=ho[0:sn, :], func=AF.Square, accum_out=ss[0:sn, :])
                p.I("scalar", "activation", ["ss", "eps"], ["rs"],
                    out=rs[0:sn, :], in_=ss[0:sn, :], func=AF.Sqrt, scale=1.0 / D_MODEL, bias=eps[0:sn, :])
                p.I("vector", "reciprocal", ["rs"], ["rs"], out=rs[0:sn, :], in_=rs[0:sn, :])
                p.I("vector", "scalar_tensor_tensor", ["rs", "gf_sb"], [hn],
                    out=ho[0:sn, :], in0=ho[0:sn, :], scalar=rs[0:sn, :], in1=gf_sb[0:sn, :],
                    op0=ALU.mult, op1=ALU.mult)
            d = p.dma("sync", out_d[t0 + s0:t0 + s0 + sn, :], ho[0:sn, :], reads=[hn])
            out_ops.append(d)
    p.emit(final_wait_ops=out_ops)


def build_attn_proj_program():
    nc = bass.Bass("TRN2", target_bir_lowering=False)
    x_d = nc.dram_tensor("x", [TOK, D_MODEL], F32, kind="ExternalInput").ap()
    g_d = nc.dram_tensor("g_mix", [128, D_MODEL], F32, kind="ExternalInput").ap()
    w_d = nc.dram_tensor("w_in", [D_MODEL, 2560], F32, kind="ExternalInput").ap()
    qg_d = nc.dram_tensor("qg", [128, 128], F32, kind="ExternalInput").ap()
    kg_d = nc.dram_tensor("kg", [128, 128], F32, kind="ExternalInput").ap()
    rope_d = nc.dram_tensor("rope", [TOK, 128], F32, kind="ExternalInput").ap()
    qT_d = nc.dram_tensor("qT", [128, 8, TOK], BF16, kind="ExternalOutput").ap()
    kT_d = nc.dram_tensor("kT", [128, 2, TOK], BF16, kind="ExternalOutput").ap()
    v_d = nc.dram_tensor("v", [128, 16, 256], BF16, kind="ExternalOutput").ap()
    gt_d = nc.dram_tensor("gate", [128, 16, 1024], BF16, kind="ExternalOutput").ap()
    prog = Prog(nc)
    p = prog
    with contextlib.ExitStack() as st:
        c = Ctx(nc, prog, st)
        ident = make_ident(c)
        w = c.sb("w", [128, 8, 2560], BF16)
        g_sb = c.sb("g_sb", [128, D_MODEL], F32)
        gains = c.sb("gains", [128, 2, 128], F32)
        eps = c.sb("eps", [128, 1], F32)
        p.I("vector", "memset", [], ["eps"], eps[:], EPS)
        p.fence()
        p.dma("sync", g_sb[:], g_d, writes=["g_sb"])
        p.dma("sync", gains[:, 0, :], qg_d, writes=["gains"])
        p.dma("sync", gains[:, 1, :], kg_d, writes=["gains"])
        p.I("scalar", "mul", ["gains"], ["gains"], out=gains[:, 0, :], in_=gains[:, 0, :], mul=128.0 ** -0.5)
        w_v = w_d.rearrange("(k p) c -> p k c", p=128)
        for k in range(8):
            p.dma("gpsimd", w[:, k, :], w_v[:, k, :], writes=["w"])
        xt = [c.sb("xt%d" % i, [128, D_MODEL], F32) for i in range(2)]
        rp = [c.sb("rp%d" % i, [128, 2, 2, 32], F32) for i in range(2)]
        sq = c.sb("sq", [128, D_MODEL], F32)
        ss = c.sb("ss", [128, 1], F32)
        rs = c.sb("rs", [128, 1], F32)
        yb = c.sb("yb", [128, D_MODEL], BF16)
        yT = [c.sb("yT%d" % i, [128, 8, 128], BF16) for i in range(2)]
        sq10 = c.sb("sq10", [128, 10, 128], F32)
        ss10 = c.sb("ss10", [128, 10], F32)
        rs10 = c.sb("rs10", [128, 10], F32)
        z0 = c.sb("z0", [128, 10, 128], F32)
        zz = c.sb("zz", [128, 10, 128], F32)
        ra = [c.sb("ra%d" % i, [128, 10, 2, 32], F32) for i in range(4)]
        zr = c.sb("zr", [128, 10, 128], BF16)
        qT_all = c.sb("qT_all", [128, 8, TOK], BF16)
        kT_all = c.sb("kT_all", [128, 2, TOK], BF16)
        v_all = c.sb("v_all", [128, 16, 256], BF16)
        gt_all = c.sb("gt_all", [128, 16, 1024], BF16)
        Q2 = c.ps("Q2", [128, 1024], F32)
        G2 = c.ps("G2", [128, 1024], F32)
        KV = c.ps("KV", [128, 512], F32)
        pT = c.ps("pT", [128, 8, 128], BF16)
        TQ = c.ps("TQ", [128, 8, 128], BF16)
        TK = c.ps("TK", [128, 8, 128], BF16)
        nbufs = dict(sq=sq, ss=ss, rs=rs, yb=yb, pT=pT, ident=ident, eps=eps,
                     names=dict(sq="sq", ss="ss", rs="rs", yb="yb", pT="pT"))
        for t in range(16):
            b = t % 2
            x = xt[b]
            p.dma("sync", x[:], x_d[t * 128:(t + 1) * 128, :], writes=["xt%d" % b])
            p.dma("sync", rp[b][:], rope_d[t * 128:(t + 1) * 128, :].rearrange("p (a r e) -> p a r e", a=2, r=2),
                  writes=["rp%d" % b])
            nbufs["dst_name"] = "yT%d" % b
            emit_rmsnorm_T(c, x[:], 128, g_sb, yT[b], 0, "xt%d" % b, nbufs)
            for (dst, dn, c0) in ((Q2[:, 0:512], "Q2", 0), (Q2[:, 512:1024], "Q2", 512), (KV[:, :], "KV", 1024),
                                  (G2[:, 0:512], "G2", 1536), (G2[:, 512:1024], "G2", 2048)):
                for k in range(8):
                    p.I("tensor", "matmul", ["w", "yT%d" % b], [dn], dst, lhsT=yT[b][:, k, :],
                        rhs=w[:, k, c0:c0 + 512], start=(k == 0), stop=(k == 7))
            p.I("scalar", "activation", [], ["Q2", "sq10"], out=sq10[:, 0:8, :],
                in_=Q2[:, :].rearrange("p (h d) -> p h d", h=8), func=AF.Square)
            p.I("scalar", "activation", [], ["KV", "sq10"], out=sq10[:, 8:10, :],
                in_=KV[:, 0:256].rearrange("p (h d) -> p h d", h=2), func=AF.Square)
            p.I("vector", "tensor_reduce", ["sq10"], ["ss10"], out=ss10[:, :], in_=sq10[:, :, :], axis=AX.X, op=ALU.add)
            p.I("scalar", "activation", ["ss10", "eps"], ["rs10"], out=rs10[:, :], in_=ss10[:, :], func=AF.Sqrt,
                scale=1.0 / 128, bias=eps[:, :])
            p.I("vector", "reciprocal", ["rs10"], ["rs10"], out=rs10[:, :], in_=rs10[:, :])
            p.I("vector", "tensor_tensor", ["rs10"], ["Q2", "z0"], out=z0[:, 0:8, :],
                in0=Q2[:, :].rearrange("p (h d) -> p h d", h=8),
                in1=rs10[:, 0:8].unsqueeze(2).broadcast_to([128, 8, 128]), op=ALU.mult)
            p.I("vector", "tensor_tensor", ["rs10"], ["KV", "z0"], out=z0[:, 8:10, :],
                in0=KV[:, 0:256].rearrange("p (h d) -> p h d", h=2),
                in1=rs10[:, 8:10].unsqueeze(2).broadcast_to([128, 2, 128]), op=ALU.mult)
            p.I("gpsimd", "tensor_tensor", ["z0", "gains"], ["zz"], out=zz[:, 0:8, :], in0=z0[:, 0:8, :],
                in1=gains[:, 0:1, :].broadcast_to([128, 8, 128]), op=ALU.mult)
            p.I("gpsimd", "tensor_tensor", ["z0", "gains"], ["zz"], out=zz[:, 8:10, :], in0=z0[:, 8:10, :],
                in1=gains[:, 1:2, :].broadcast_to([128, 2, 128]), op=ALU.mult)
            zv = zz[:, :, :].rearrange("p h (r f e) -> p h r f e", r=2, f=2)
            ov = zr[:, :, :].rearrange("p h (r f e) -> p h r f e", r=2, f=2)
            z1, z2 = zv[:, :, :, 0, :], zv[:, :, :, 1, :]
            cosb = rp[b][:, 0:1, :, :].broadcast_to([128, 10, 2, 32])
            sinb = rp[b][:, 1:2, :, :].broadcast_to([128, 10, 2, 32])
            rn = "rp%d" % b
            p.I("vector", "tensor_tensor", ["zz", rn], ["ra0"], out=ra[0][:], in0=z1, in1=cosb, op=ALU.mult)
            p.I("gpsimd", "tensor_tensor", ["zz", rn], ["ra1"], out=ra[1][:], in0=z2, in1=sinb, op=ALU.mult)
            p.I("gpsimd", "tensor_tensor", ["zz", rn], ["ra2"], out=ra[2][:], in0=z1, in1=sinb, op=ALU.mult)
            p.I("vector", "tensor_tensor", ["zz", rn], ["ra3"], out=ra[3][:], in0=z2, in1=cosb, op=ALU.mult)
            p.I("vector", "tensor_tensor", ["ra0", "ra1"], ["zr"], out=ov[:, :, :, 0, :], in0=ra[0][:], in1=ra[1][:],
                op=ALU.subtract)
            p.I("gpsimd", "tensor_tensor", ["ra2", "ra3"], ["zr"], out=ov[:, :, :, 1, :], in0=ra[2][:], in1=ra[3][:],
                op=ALU.add)
            for h in range(8):
                p.I("tensor", "transpose", ["zr"], ["TQ"], out=TQ[:, h, :], in_=zr[:, h, :], identity=ident[:, :])
            for h in range(2):
                p.I("tensor", "transpose", ["zr"], ["TK"], out=TK[:, h, :], in_=zr[:, 8 + h, :], identity=ident[:, :])
            p.I("scalar", "copy", [], ["TQ", "qT_all"], out=qT_all[:, :, t * 128:(t + 1) * 128], in_=TQ[:, :, :])
            p.I("vector", "tensor_copy", [], ["TK", "kT_all"], out=kT_all[:, :, t * 128:(t + 1) * 128], in_=TK[:, 0:2, :])
            p.I("scalar", "copy", [], ["KV", "v_all"], out=v_all[:, t, :], in_=KV[:, 256:512])
            p.I("scalar", "activation", [], ["G2", "gt_all"], out=gt_all[:, t, :], in_=G2[:, :], func=AF.Sigmoid)
        outs = [p.dma("sync", qT_d, qT_all[:], reads=["qT_all"]),
                p.dma("sync", kT_d, kT_all[:], reads=["kT_all"]),
                p.dma("sync", v_d, v_all[:], reads=["v_all"]),
                p.dma("sync", gt_d, gt_all[:], reads=["gt_all"])]
        p.emit(final_wait_ops=outs)
    return nc, prog


def block_masks_np():
    i = np.arange(128)
    b32 = (i[:, None] // 32) == (i[None, :] // 32)
    b64 = (i[:, None] // 64) == (i[None, :] // 64)
    lo = i[:, None] > i[None, :]
    up = i[:, None] < i[None, :]
    ms = [t & m for t in (lo, up) for m in (b32, b64 & ~b32, ~b64)]
    return np.ascontiguousarray(np.stack(ms, axis=1).astype(np.float32))


def rope_tables_np():
    t = np.arange(SEQ)
    row = (t // 64).astype(np.float32)
    col = (t % 64).astype(np.float32)
    inv = (np.float32(10000.0) ** (-(np.arange(32, dtype=np.float32) * np.float32(2.0) / np.float32(64)))).astype(np.float32)
    ar = row[:, None] * inv[None, :]
    ac = col[:, None] * inv[None, :]
    return np.concatenate([np.cos(ar), np.cos(ac), np.sin(ar), np.sin(ac)], axis=1).astype(np.float32)


def build_attn_core_program(nheads=8, nqb=4, nsp=32):
    nc = bass.Bass("TRN2", target_bir_lowering=False)
    qT_d = nc.dram_tensor("qT", [128, 8, TOK], BF16, kind="ExternalInput").ap()
    kT_d = nc.dram_tensor("kT", [128, 2, SEQ], BF16, kind="ExternalInput").ap()
    v_d = nc.dram_tensor("v", [128, 64, 256], BF16, kind="ExternalInput").ap()
    gt_d = nc.dram_tensor("gate", [128, 16, 1024], BF16, kind="ExternalInput").ap()
    x_d = nc.dram_tensor("x", [TOK, D_MODEL], F32, kind="ExternalInput").ap()
    wo_d = nc.dram_tensor("w_out", [D_MODEL, D_MODEL], F32, kind="ExternalInput").ap()
    qg_d = nc.dram_tensor("qg", [128, 128], F32, kind="ExternalInput").ap()
    kg_d = nc.dram_tensor("kg", [128, 128], F32, kind="ExternalInput").ap()
    out_d = nc.dram_tensor("out", [TOK, D_MODEL], F32, kind="ExternalOutput").ap()
    prog = Prog(nc)
    p = prog
    with contextlib.ExitStack() as st:
        c = Ctx(nc, prog, st)
        ident = make_ident(c)
        qT = c.sb("qT_sb", [128, 8, TOK], BF16)
        kT = c.sb("kT_sb", [128, 2, SEQ], BF16)
        va = c.sb("v_aug", [128, 64, 2, 129], BF16)
        gt = c.sb("gt_sb", [128, 16, 1024], BF16)
        wo = c.sb("wo_sb", [128, 8, D_MODEL], BF16)
        gq = c.sb("gq", [128, 2, 128], F32)
        m2 = c.sb("m2", [128, 2], F32)
        negb = c.sb("negb", [128, 1], F32)
        p.fence()
        p.dma("sync", gq[:, 0, :], qg_d, writes=["gq"])
        p.dma("sync", gq[:, 1, :], kg_d, writes=["gq"])
        for h in range(8):
            p.dma("sync", qT[:, h, :], qT_d[:, h, :], writes=["qT"])
        for h in range(2):
            for s4 in range(4):
                p.dma("sync", kT[:, h, s4 * 2048:(s4 + 1) * 2048], kT_d[:, h, s4 * 2048:(s4 + 1) * 2048], writes=["kT"])
        p.I("gpsimd", "memset", [], ["va"], va[:, :, :, 128:129], 1.0)
        for s4 in range(4):
            p.dma("sync", va[:, s4 * 16:(s4 + 1) * 16, :, 0:128],
                  v_d[:, s4 * 16:(s4 + 1) * 16, :].rearrange("p s (h d) -> p s h d", h=2), writes=["va"])
        for t4 in range(4):
            p.dma("sync", gt[:, t4 * 4:(t4 + 1) * 4, :], gt_d[:, t4 * 4:(t4 + 1) * 4, :], writes=["gt"])
        wo_v = wo_d.rearrange("(k p) c -> p k c", p=128)
        for k in range(0, 8, 2):
            p.dma("gpsimd", wo[:, k:k + 2, :], wo_v[:, k:k + 2, :], writes=["wo"])
        p.I("vector", "tensor_tensor", ["gq"], ["gq"], out=gq[:], in0=gq[:], in1=gq[:], op=ALU.mult)
        p.I("vector", "tensor_reduce", ["gq"], ["m2"], out=m2[:, :], in_=gq[:, :, :], axis=AX.X, op=ALU.max)
        p.I("vector", "tensor_tensor", ["m2"], ["negb"], out=negb[:, :], in0=m2[:, 0:1], in1=m2[:, 1:2], op=ALU.mult)
        p.I("scalar", "activation", ["negb"], ["negb"], out=negb[:, :], in_=negb[:, :], func=AF.Sqrt, scale=128.0)
        p.I("scalar", "mul", ["negb"], ["negb"], out=negb[:, :], in_=negb[:, :], mul=-1.0)

        SC = [c.ps("SC%d" % i, [128, 1024], F32) for i in range(2)]
        O = [c.ps("O%d" % i, [128, 512], F32) for i in range(4)]
        NP = 3
        pT = [c.sb("pT%d" % i, [128, 1024], BF16) for i in range(NP)]
        rinv = c.sb("rinv", [128, 4], F32)
        step = 0
        for h in range(nheads):
            kv = h // 4
            for qb in range(nqb):
                def qk(sp, st_):
                    b = st_ % 2
                    for cc in range(2):
                        s = 2 * sp + cc
                        p.I("tensor", "matmul", ["kT", "qT"], ["SC%d" % b], SC[b][:, cc * 512:(cc + 1) * 512],
                            lhsT=kT[:, kv, s * 128:(s + 1) * 128], rhs=qT[:, h, qb * 512:(qb + 1) * 512],
                            start=True, stop=True)

                def ex(sp, st_):
                    b = st_ % 2
                    pb = st_ % NP
                    p.I("scalar", "activation", ["negb"], ["SC%d" % b, "pT%d" % pb], out=pT[pb][:, :], in_=SC[b][:, :],
                        func=AF.Exp, bias=negb[:, :])

                def pv(sp, st_):
                    pb = st_ % NP
                    for cc in range(2):
                        s = 2 * sp + cc
                        for qs in range(4):
                            p.I("tensor", "matmul", ["pT%d" % pb, "va"], ["O%d" % qs], O[qs][:, 0:129],
                                lhsT=pT[pb][:, cc * 512 + qs * 128:cc * 512 + (qs + 1) * 128], rhs=va[:, s, kv, :],
                                start=(sp == 0 and cc == 0), stop=(sp == nsp - 1 and cc == 1))

                qk(0, step)
                for sp in range(nsp):
                    if sp + 1 < nsp:
                        qk(sp + 1, step + 1)
                    ex(sp, step)
                    pv(sp, step)
                    step += 1
                for qs in range(4):
                    tile = qb * 4 + qs
                    p.I("vector", "reciprocal", [], ["O%d" % qs, "rinv"], out=rinv[:, qs:qs + 1], in_=O[qs][:, 128:129])
                    p.I("vector", "scalar_tensor_tensor", ["rinv"], ["O%d" % qs, "gt"],
                        out=gt[:, tile, h * 128:(h + 1) * 128], in0=O[qs][:, 0:128], scalar=rinv[:, qs:qs + 1],
                        in1=gt[:, tile, h * 128:(h + 1) * 128], op0=ALU.mult, op1=ALU.mult)
        xt = [c.sb("xt%d" % i, [128, D_MODEL], F32) for i in range(2)]
        ho = [c.sb("ho%d" % i, [128, D_MODEL], F32) for i in range(2)]
        ogT = [c.sb("ogT%d" % i, [128, 8, 128], BF16) for i in range(2)]
        TP = O[0].bitcast(BF16) if hasattr(O[0], "bitcast") else None
        outs = []
        for t in range(16):
            b = t % 2
            p.dma("sync", xt[b][:], x_d[t * 128:(t + 1) * 128, :], writes=["xt%d" % b])
            for k in range(8):
                p.I("tensor", "transpose", ["gt"], ["O0"], out=TP[:, k * 128:(k + 1) * 128],
                    in_=gt[:, t, k * 128:(k + 1) * 128], identity=ident[:, :])
            p.I("vector", "tensor_copy", [], ["O0", "ogT%d" % b], out=ogT[b][:, :, :],
                in_=TP[:, :].rearrange("p (k t) -> p k t", k=8))
            for hf in range(2):
                for k in range(8):
                    p.I("tensor", "matmul", ["ogT%d" % b, "wo"], ["SC0"], SC[0][:, hf * 512:(hf + 1) * 512],
                        lhsT=ogT[b][:, k, :], rhs=wo[:, k, hf * 512:(hf + 1) * 512], start=(k == 0), stop=(k == 7))
            p.I("vector", "tensor_tensor", ["xt%d" % b], ["SC0", "ho%d" % b], out=ho[b][:, :], in0=SC[0][:, :],
                in1=xt[b][:, :], op=ALU.add)
            outs.append(p.dma("sync", out_d[t * 128:(t + 1) * 128, :], ho[b][:, :], reads=["ho%d" % b]))
        p.emit(final_wait_ops=outs)
    return nc, prog


GDN_IN = 6208


def build_gdn_proj_program():
    nc = bass.Bass("TRN2", target_bir_lowering=False)
    HT = TOK + 4
    h_d = nc.dram_tensor("h", [HT, D_MODEL], F32, kind="ExternalInput").ap()
    g_d = nc.dram_tensor("g_mix", [128, D_MODEL], F32, kind="ExternalInput").ap()
    w_d = nc.dram_tensor("w_in", [D_MODEL, GDN_IN], F32, kind="ExternalInput").ap()
    cw_d = nc.dram_tensor("cw", [128, 32, 6], F32, kind="ExternalInput").ap()
    ad_d = nc.dram_tensor("ad", [128, 2, 32], F32, kind="ExternalInput").ap()
    qT_d = nc.dram_tensor("qT", [128, 8, TOK], BF16, kind="ExternalOutput").ap()
    kT_d = nc.dram_tensor("kT", [128, 8, TOK], BF16, kind="ExternalOutput").ap()
    ktm_d = nc.dram_tensor("k_tm", [TOK, 1024], BF16, kind="ExternalOutput").ap()
    vtm_d = nc.dram_tensor("v_tm", [TOK, 2048], BF16, kind="ExternalOutput").ap()
    z_d = nc.dram_tensor("z", [TOK, 2048], BF16, kind="ExternalOutput").ap()
    gb_d = nc.dram_tensor("gb", [TOK, 64], F32, kind="ExternalOutput").ap()
    prog = Prog(nc)
    p = prog
    outs = []
    with contextlib.ExitStack() as st:
        c = Ctx(nc, prog, st)
        ident = make_ident(c)
        ones = c.sb("ones", [128, 128], F32)
        p.I("gpsimd", "memset", [], ["ones"], ones[:], 1.0)
        g_sb = c.sb("g_sb", [128, D_MODEL], F32)
        cw = c.sb("cw_sb", [128, 32, 6], F32)
        ad = c.sb("ad_sb", [128, 2, 32], F32)
        eps = c.sb("eps", [128, 1], F32)
        one1 = c.sb("one1", [128, 1], F32)
        p.I("vector", "memset", [], ["eps"], eps[:], EPS)
        p.I("vector", "memset", [], ["one1"], one1[:], 1.0)
        p.fence()
        p.dma("sync", g_sb[:], g_d, writes=["g_sb"])
        p.dma("sync", cw[:], cw_d, writes=["cw"])
        p.dma("sync", ad[:], ad_d, writes=["ad"])
        p.I("scalar", "activation", ["ad"], ["ad"], out=ad[:, 0, :], in_=ad[:, 0, :], func=AF.Exp)
        p.I("scalar", "mul", ["ad"], ["ad"], out=ad[:, 0, :], in_=ad[:, 0, :], mul=-1.0)
        wv = w_d.rearrange("(k p) c -> p k c", p=128)
        wb = [c.sb("wb%d" % i, [128, 8, 1024], BF16) for i in range(2)]
        yT = c.sb("yT_all", [128, 8, HT], BF16)
        xt = [c.sb("xt%d" % i, [128, D_MODEL], F32) for i in range(2)]
        sq = c.sb("sq", [128, D_MODEL], F32)
        ss = c.sb("ss", [128, 1], F32)
        rs = c.sb("rs", [128, 1], F32)
        yb = c.sb("yb", [128, D_MODEL], BF16)
        pT = c.ps("pT", [128, 8, 128], BF16)
        U = [c.ps("U%d" % i, [128, 512], F32) for i in range(3)]
        L = c.ps("L", [128, 512], F32)
        TT = c.ps("TT", [128, 8, 128], BF16)
        Z = [U[0], U[1]]
        nbufs = dict(sq=sq, ss=ss, rs=rs, yb=yb, pT=pT, ident=ident, eps=eps, dst_name="yT",
                     names=dict(sq="sq", ss="ss", rs="rs", yb="yb", pT="pT"))
        r = 0
        xi = 0
        while r < HT:
            rn = min(128, HT - r)
            b = xi % 2
            xi += 1
            p.dma("sync", xt[b][0:rn, :], h_d[r:r + rn, :], writes=["xt%d" % b])
            emit_rmsnorm_T(c, xt[b][0:rn, :], rn, g_sb, yT, r, "xt%d" % b, nbufs)
            r += rn
        NR = 3
        tAs = [c.sb("tA%d" % i, [128, 508], F32) for i in range(NR)]
        tBs = [c.sb("tB%d" % i, [128, 508], F32) for i in range(NR)]
        acts = [c.sb("act%d" % i, [128, 508], F32) for i in range(NR)]
        sqvs = [c.sb("sqv%d" % i, [128, 508], F32) for i in range(NR)]
        rts = [c.sb("rt%d" % i, [128, 508], F32) for i in range(NR)]
        fm = [c.sb("fm%d" % i, [128, 8, 508], BF16) for i in range(2)]
        tm = [c.sb("tm%d" % i, [128, 1024], BF16) for i in range(2)]
        blocks = []
        t0 = 0
        while t0 < TOK:
            n = min(508, TOK - t0)
            blocks.append((t0, n))
            t0 += n
        fi = 0
        ti = 0
        ui = 0
        for grp in range(4):
            wbuf = wb[grp % 2]
            wn = "wb%d" % (grp % 2)
            for k in range(0, 8, 2):
                p.dma("gpsimd", wbuf[:, k:k + 2, :], wv[:, k:k + 2, grp * 1024:(grp + 1) * 1024], writes=[wn])
            for (t0, n) in blocks:
                ncol = n + 4
                fmb = fm[fi % 2]
                fn_ = "fm%d" % (fi % 2)
                fi += 1
                for j in range(8):
                    cj = grp * 8 + j
                    Uj = U[ui % 3]
                    un = "U%d" % (ui % 3)
                    rr = ui % NR
                    tA, tB, act, sqv, rt = tAs[rr], tBs[rr], acts[rr], sqvs[rr], rts[rr]
                    nA, nB, nact, nsqv, nrt = "tA%d" % rr, "tB%d" % rr, "act%d" % rr, "sqv%d" % rr, "rt%d" % rr
                    ui += 1
                    for k in range(8):
                        p.I("tensor", "matmul", [wn, "yT"], [un], Uj[:, 0:ncol], lhsT=wbuf[:, k, j * 128:(j + 1) * 128],
                            rhs=yT[:, k, t0:t0 + ncol], start=(k == 0), stop=(k == 7))
                    p.I("scalar", "activation", ["cw"], [un, nA], out=tA[:, 0:n], in_=Uj[:, 2:2 + n], func=AF.Identity,
                        scale=cw[:, cj, 2:3], bias=cw[:, cj, 5:6])
                    src, dst = tA, tB
                    sn_, dn_ = nA, nB
                    for tap in (0, 1, 3, 4):
                        p.I("vector", "scalar_tensor_tensor", ["cw", sn_], [un, dn_], out=dst[:, 0:n],
                            in0=Uj[:, tap:tap + n], scalar=cw[:, cj, tap:tap + 1], in1=src[:, 0:n],
                            op0=ALU.mult, op1=ALU.add)
                        src, dst = dst, src
                        sn_, dn_ = dn_, sn_
                    if grp < 2:
                        p.I("scalar", "activation", [nA], [nact], out=act[:, 0:n], in_=tA[:, 0:n], func=AF.Silu)
                        p.I("gpsimd", "tensor_tensor", [nact], [nsqv], out=sqv[:, 0:n], in0=act[:, 0:n], in1=act[:, 0:n], op=ALU.mult)
                        p.I("tensor", "matmul", ["ones", nsqv], ["L"], L[:, 0:n], lhsT=ones[:, :], rhs=sqv[:, 0:n],
                            start=True, stop=True)
                        p.I("scalar", "activation", ["eps"], ["L", nrt], out=rt[:, 0:n], in_=L[:, 0:n], func=AF.Sqrt,
                            bias=eps[:, :])
                        p.I("vector", "reciprocal", [nrt], [nrt], out=rt[:, 0:n], in_=rt[:, 0:n])
                        p.I("gpsimd", "scalar_tensor_tensor" if False else "tensor_tensor", [nact, nrt], [nsqv], out=sqv[:, 0:n],
                            in0=act[:, 0:n], in1=rt[:, 0:n], op=ALU.mult)
                        p.I("scalar", "mul", [nsqv], [fn_], out=fmb[:, j, 0:n], in_=sqv[:, 0:n],
                            mul=(128.0 ** -0.5 if grp == 0 else 1.0))
                    else:
                        p.I("scalar", "activation", [nA], [fn_], out=fmb[:, j, 0:n], in_=tA[:, 0:n], func=AF.Silu)
                if grp == 0:
                    outs.append(p.dma("sync", qT_d[:, :, t0:t0 + n], fmb[:, :, 0:n], reads=[fn_]))
                if grp == 1:
                    outs.append(p.dma("sync", kT_d[:, :, t0:t0 + n], fmb[:, :, 0:n], reads=[fn_]))
                if grp >= 1:
                    s0 = 0
                    while s0 < n:
                        sn = min(128, n - s0)
                        tmb = tm[ti % 2]
                        tn = "tm%d" % (ti % 2)
                        ti += 1
                        for j in range(8):
                            p.I("tensor", "transpose", [fn_], ["TT"], out=TT[0:sn, j, :], in_=fmb[:, j, s0:s0 + sn],
                                identity=ident[:, :])
                        p.I("vector", "tensor_copy", [], ["TT", tn], out=tmb[0:sn, :],
                            in_=TT[0:sn, :, :].rearrange("p j d -> p (j d)"))
                        if grp == 1:
                            dst_ap = ktm_d[t0 + s0:t0 + s0 + sn, :]
                        else:
                            dst_ap = vtm_d[t0 + s0:t0 + s0 + sn, (grp - 2) * 1024:(grp - 1) * 1024]
                        outs.append(p.dma("sync", dst_ap, tmb[0:sn, :], reads=[tn]))
                        s0 += sn
        wz = wb
        zs = [c.sb("zs%d" % i, [128, 1024], BF16) for i in range(2)]
        for half in range(2):
            wbuf = wz[half % 2]
            wn = "wb%d" % (half % 2)
            for k in range(0, 8, 2):
                p.dma("gpsimd", wbuf[:, k:k + 2, :], wv[:, k:k + 2, 4096 + half * 1024:4096 + (half + 1) * 1024], writes=[wn])
            for t in range(16):
                zb = zs[t % 2]
                zn = "zs%d" % (t % 2)
                for hf in range(2):
                    for k in range(8):
                        p.I("tensor", "matmul", [wn, "yT"], ["U%d" % hf], Z[hf][:, :], lhsT=yT[:, k, 2 + t * 128:2 + (t + 1) * 128],
                            rhs=wbuf[:, k, hf * 512:(hf + 1) * 512], start=(k == 0), stop=(k == 7))
                    p.I("scalar", "activation", [], ["U%d" % hf, zn], out=zb[:, hf * 512:(hf + 1) * 512], in_=Z[hf][:, :],
                        func=AF.Silu)
                outs.append(p.dma("sync", z_d[t * 128:(t + 1) * 128, half * 1024:(half + 1) * 1024], zb[:, :], reads=[zn]))
        wab = c.sb("wab", [128, 8, 64], BF16)
        p.dma("gpsimd", wab[:, :, :], wv[:, :, 6144:6208], writes=["wab"])
        xs = c.sb("xs", [128, 32], F32)
        ax = c.sb("ax", [128, 32], F32)
        gbs = [c.sb("gbs%d" % i, [128, 64], F32) for i in range(2)]
        for t in range(16):
            gbt = gbs[t % 2]
            gn = "gbs%d" % (t % 2)
            for k in range(8):
                p.I("tensor", "matmul", ["wab", "yT"], ["U0"], Z[0][:, 0:64], lhsT=yT[:, k, 2 + t * 128:2 + (t + 1) * 128],
                    rhs=wab[:, k, :], start=(k == 0), stop=(k == 7))
            Zv = Z[0][:, 0:64].rearrange("p (d a h) -> p d a h", d=2, a=2)
            p.I("vector", "tensor_tensor", ["ad"], ["U0", "xs"], out=xs[:, :].rearrange("p (d h) -> p d h", d=2),
                in0=Zv[:, :, 0, :], in1=ad[:, 1, :].rearrange("p (d h) -> p d h", d=2), op=ALU.add)
            p.I("scalar", "activation", [], ["U0", gn], out=gbt[:, 32:64].rearrange("p (d h) -> p d h", d=2),
                in_=Zv[:, :, 1, :], func=AF.Sigmoid)
            p.I("scalar", "activation", ["xs"], ["ax"], out=ax[:, :], in_=xs[:, :], func=AF.Abs)
            p.I("scalar", "activation", ["ax"], ["ax"], out=ax[:, :], in_=ax[:, :], func=AF.Exp, scale=-1.0)
            p.I("scalar", "activation", ["ax", "one1"], ["ax"], out=ax[:, :], in_=ax[:, :], func=AF.Ln, bias=one1[:, :])
            p.I("vector", "scalar_tensor_tensor", ["xs", "ax"], ["xs"], out=xs[:, :], in0=xs[:, :], scalar=0.0,
                in1=ax[:, :], op0=ALU.max, op1=ALU.add)
            p.I("vector", "tensor_tensor", ["xs", "ad"], [gn], out=gbt[:, 0:32], in0=xs[:, :], in1=ad[:, 0, :], op=ALU.mult)
            outs.append(p.dma("sync", gb_d[t * 128:(t + 1) * 128, :], gbt[:, :], reads=[gn]))
        p.emit(final_wait_ops=outs)
    return nc, prog


def build_gdn_scan_program(nchunks=64):
    nc = bass.Bass("TRN2", target_bir_lowering=False)
    S_ = nchunks * 128
    qT_d = nc.dram_tensor("qT", [128, 2, S_], BF16, kind="ExternalInput").ap()
    kT_d = nc.dram_tensor("kT", [128, 2, S_], BF16, kind="ExternalInput").ap()
    ktm_d = nc.dram_tensor("k_tm", [S_, 256], BF16, kind="ExternalInput").ap()
    vtm_d = nc.dram_tensor("v_tm", [S_, 512], BF16, kind="ExternalInput").ap()
    z_d = nc.dram_tensor("z", [S_, 512], BF16, kind="ExternalInput").ap()
    gb_d = nc.dram_tensor("gb", [S_, 16], F32, kind="ExternalInput").ap()
    on_d = nc.dram_tensor("on", [128, 128], F32, kind="ExternalInput").ap()
    bm_d = nc.dram_tensor("bm", [128, 6, 128], F32, kind="ExternalInput").ap()
    og_d = nc.dram_tensor("og", [S_, 512], BF16, kind="ExternalOutput").ap()
    ost_d = nc.dram_tensor("ost", [S_, 512], F32, kind="Internal").ap()
    prog = Prog(nc)
    p = prog
    outs = []
    with contextlib.ExitStack() as st:
        c = Ctx(nc, prog, st)
        ident = make_ident(c)
        ones = c.sb("ones", [128, 128], F32)
        p.I("gpsimd", "memset", [], ["ones"], ones[:], 1.0)
        masks = {}
        for nm_, cmp, sg in (("LE", ALU.is_ge, -1), ("GT", ALU.is_gt, 1), ("GE", ALU.is_ge, 1), ("LT", ALU.is_gt, -1)):
            m = c.sb("m" + nm_, [128, 128], F32)
            p.I("gpsimd", "memset", [], ["m" + nm_], m[:], 1.0)
            p.I("gpsimd", "affine_select", ["m" + nm_], ["m" + nm_], out=m[:], in_=m[:], pattern=[[-sg, 128]],
                compare_op=cmp, fill=0.0, base=0, channel_multiplier=sg)
            masks[nm_] = m
        on = c.sb("on_sb", [128, 128], F32)
        eps = c.sb("eps", [128, 1], F32)
        p.I("vector", "memset", [], ["eps"], eps[:], EPS)
        p.fence()
        p.dma("sync", on[:], on_d, writes=["on"])
        bm = c.sb("bm_sb", [128, 6, 128], F32)
        p.dma("sync", bm[:], bm_d, writes=["bm"])
        B = [c.ps("B%d" % i, [128, 512], F32) for i in range(8)]

        def bk(i):
            return B[i][:, :].rearrange("p (h d) -> p h d", h=4)

        D = {}
        for d in range(2):
            for par in range(2):
                dd = {}
                sfx = "_%d%d" % (d, par)
                for nm_, shp, dt_ in (("qTc", [128, 2, 128], BF16), ("kTc", [128, 2, 128], BF16), ("ktm", [128, 256], BF16),
                                      ("vtm", [128, 512], BF16), ("zc", [128, 512], BF16), ("gb", [128, 16], F32),
                                      ("X0", [128, 4, 128], BF16), ("X1", [128, 4, 128], BF16),
                                      ("Y0", [128, 4, 128], BF16), ("Y1", [128, 4, 128], BF16),
                                      ("P0", [128, 4, 128], BF16), ("P1", [128, 4, 128], BF16),
                                      ("rhsD", [128, 4, 128], F32), ("E", [128, 4, 128], F32), ("Es", [128, 4, 128], F32),
                                      ("Ei", [128, 4, 128], F32), ("KQ", [128, 4, 128], F32), ("attn", [128, 4, 128], BF16),
                                      ("XA", [128, 8, 128], BF16), ("kbg", [128, 4, 128], BF16), ("vb", [128, 4, 128], BF16),
                                      ("kdec", [128, 4, 128], BF16), ("nwT", [128, 4, 128], BF16),
                                      ("gs", [128, 8], F32), ("ex", [128, 12], F32), ("nbe", [128, 4], F32),
                                      ("sso", [128, 4], F32), ("ol", [128, 4, 128], F32),
                                      ("Pf0", [128, 4, 128], F32), ("Pf1", [128, 4, 128], F32),
                                      ("N1", [128, 4, 128], BF16), ("N2", [128, 4, 128], BF16), ("N1T", [128, 4, 128], BF16),
                                      ("WP", [128, 4, 128], BF16), ("WQ", [128, 4, 128], BF16),
                                      ("Q0", [128, 4, 128], BF16), ("Q1", [128, 4, 128], BF16)):
                    dd[nm_] = c.sb(nm_ + sfx, shp, dt_)
                dd["tq"], dd["od"], dd["sqo"] = dd["rhsD"], dd["E"], dd["Es"]
                dd["vn"], dd["ogc"] = dd["attn"], dd["kbg"]
                dd["_alias"] = dict(tq="rhsD", od="E", sqo="Es", vn="attn", ogc="kbg")
                D[d, par] = dd
        S32, Sbf = {}, {}
        for d in range(2):
            S32[d] = c.sb("S32_%d" % d, [128, 4, 128], F32)
            Sbf[d] = c.sb("Sbf_%d" % d, [128, 4, 128], BF16)
            p.I("vector", "memset", [], ["S32_%d" % d], S32[d][:], 0.0)
            p.I("vector", "memset", [], ["Sbf_%d" % d], Sbf[d][:], 0.0)

        def b4(ap2):
            return ap2.unsqueeze(2).broadcast_to([128, 4, 128])

        def bh(ap2):
            return ap2.unsqueeze(1).broadcast_to([128, 4, 128])

        def rep2(ap3):
            return ap3.unsqueeze(2).broadcast_to([128, 2, 2, 128])

        def v4(ap3):
            return ap3.rearrange("p (q r) d -> p q r d", q=2)

        def pre(d, i, par, second):
            dd = D[d, par]
            sfx = "_%d%d" % (d, par)
            al = dd["_alias"]
            N = lambda x: al.get(x, x) + sfx
            b0, b1, b2 = 3 * d, 3 * d + 1, 3 * d + 2
            n0, n1, n2 = "B%d" % b0, "B%d" % b1, "B%d" % b2
            r0 = i * 128
            qTc, kTc, ktm, vtm, zc, gb = (dd[k] for k in ("qTc", "kTc", "ktm", "vtm", "zc", "gb"))
            p.dma("sync", qTc[:], qT_d[:, :, r0:r0 + 128], writes=[N("qTc")])
            p.dma("sync", kTc[:], kT_d[:, :, r0:r0 + 128], writes=[N("kTc")])
            p.dma("sync", ktm[:], ktm_d[r0:r0 + 128, :], writes=[N("ktm")])
            p.dma("sync", vtm[:], vtm_d[r0:r0 + 128, :], writes=[N("vtm")])
            p.dma("sync", zc[:], z_d[r0:r0 + 128, :], writes=[N("zc")])
            p.dma("sync", gb[:], gb_d[r0:r0 + 128, :], writes=[N("gb")])
            Mm = masks["LE"] if d == 0 else masks["GE"]
            Vm = masks["GT"] if d == 0 else masks["LT"]
            mS = masks["GT"] if d == 0 else masks["LT"]
            mI = masks["GE"] if d == 0 else masks["LE"]
            g_ = gb[:, d * 4:(d + 1) * 4]
            be = gb[:, 8 + d * 4:8 + (d + 1) * 4]
            gs, ex, nbe = dd["gs"], dd["ex"], dd["nbe"]
            for q in range(2):
                p.I("tensor", "matmul", [N("kTc")], [n0], bk(b0)[:, q, :], lhsT=kTc[:, q, :], rhs=kTc[:, q, :],
                    start=True, stop=True)
                p.I("tensor", "matmul", [N("kTc"), N("qTc")], [n0], bk(b0)[:, 2 + q, :], lhsT=qTc[:, q, :],
                    rhs=kTc[:, q, :], start=True, stop=True)
            p.I("scalar", "copy", [], [n0, N("KQ")], out=dd["KQ"][:], in_=bk(b0))
            p.I("tensor", "matmul", [N("gb")], [n1], B[b1][:, 0:4], lhsT=Mm[:, :], rhs=g_, start=True, stop=True)
            p.I("tensor", "matmul", [N("gb"), "ones"], [n1], B[b1][:, 4:8], lhsT=ones[:, :], rhs=g_, start=True, stop=True)
            p.I("vector", "tensor_copy", [], [n1, N("gs")], out=gs[:, :], in_=B[b1][:, 0:8])
            p.I("scalar", "activation", [N("gs")], [N("ex")], out=ex[:, 0:4], in_=gs[:, 0:4], func=AF.Exp)
            p.I("vector", "tensor_tensor", [N("gs")], [N("gs")], out=gs[:, 0:4], in0=gs[:, 4:8], in1=gs[:, 0:4], op=ALU.subtract)
            p.I("scalar", "activation", [N("gs")], [N("ex")], out=ex[:, 4:12], in_=gs[:, 0:8], func=AF.Exp)
            p.I("vector", "tensor_scalar", [N("gb")], [N("nbe")], out=nbe[:, :], in0=be, scalar1=-1.0, scalar2=None, op0=ALU.mult)
            p.I("vector", "tensor_tensor", [N("nbe"), N("ex")], [N("sso")], out=dd["sso"][:, :], in0=nbe[:, :], in1=ex[:, 0:4], op=ALU.mult)
            k3 = ktm[:, :].rearrange("p (q d) -> p q d", q=2)
            for h in range(4):
                p.I("scalar", "activation", [N("ktm"), N("ex")], [N("kdec")], out=dd["kdec"][:, h, :], in_=k3[:, h // 2, :],
                    func=AF.Identity, scale=ex[:, 4 + h:5 + h])
                p.I("scalar", "activation", [N("vtm"), N("gb")], [N("vb")], out=dd["vb"][:, h, :], in_=vtm[:, h * 128:(h + 1) * 128],
                    func=AF.Identity, scale=be[:, h:h + 1])
            p.I("gpsimd", "tensor_tensor", [N("gb")], [N("rhsD")], out=dd["rhsD"][:], in0=bh(Vm[:, :]), in1=b4(g_), op=ALU.mult)
            p.I("tensor", "matmul", [N("rhsD")], [n2], B[b2][:, :], lhsT=Mm[:, :], rhs=dd["rhsD"][:].rearrange("p h s -> p (h s)"),
                start=True, stop=True)
            p.I("scalar", "activation", [], [n2, N("E")], out=dd["E"][:], in_=bk(b2), func=AF.Exp)
            p.I("gpsimd", "tensor_tensor", [N("E")], [N("Ei")], out=dd["Ei"][:], in0=dd["E"][:], in1=bh(mI[:, :]), op=ALU.mult)
            p.I("vector", "tensor_tensor", [N("E"), N("nbe")], [N("Es")], out=dd["Es"][:], in0=dd["E"][:], in1=b4(nbe[:, :]), op=ALU.mult)
            Y0 = dd["Y0"]
            p.I("vector", "tensor_tensor", [N("Es"), N("KQ")], [N("Es")], out=v4(dd["Es"][:]), in0=v4(dd["Es"][:]),
                in1=rep2(dd["KQ"][:, 0:2, :]), op=ALU.mult)
            p.I("gpsimd", "tensor_tensor", [N("Es"), "bm"], [N("Y0")], out=Y0[:], in0=dd["Es"][:], in1=bh(bm[:, 3 * d + 0, :]), op=ALU.mult)
            p.I("gpsimd", "tensor_tensor", [N("Es"), "bm"], [N("N1")], out=dd["N1"][:], in0=dd["Es"][:], in1=bh(bm[:, 3 * d + 1, :]), op=ALU.mult)
            p.I("vector", "tensor_tensor", [N("Es"), "bm"], [N("N2")], out=dd["N2"][:], in0=dd["Es"][:], in1=bh(bm[:, 3 * d + 2, :]), op=ALU.mult)
            p.I("gpsimd", "tensor_tensor", [N("Ei"), N("KQ")], [N("attn")], out=v4(dd["attn"][:]), in0=v4(dd["Ei"][:]),
                in1=rep2(dd["KQ"][:, 2:4, :]), op=ALU.mult)
            TRb = B[b0].bitcast(BF16)
            TRb1 = B[b1].bitcast(BF16)
            for h in range(4):
                p.I("tensor", "transpose", [N("Y0")], [n0], out=TRb[:, h * 128:(h + 1) * 128], in_=Y0[:, h, :],
                    identity=ident[:, :])
            for h in range(4):
                p.I("tensor", "transpose", [N("attn")], [n0], out=TRb[:, (4 + h) * 128:(5 + h) * 128], in_=dd["attn"][:, h, :],
                    identity=ident[:, :])
            for h in range(4):
                p.I("tensor", "transpose", [N("N1")], [n1], out=TRb1[:, h * 128:(h + 1) * 128], in_=dd["N1"][:, h, :],
                    identity=ident[:, :])
            XA = dd["XA"]
            p.I("scalar", "copy", [], [n0, N("XA")], out=XA[:], in_=TRb[:, :].rearrange("p (h s) -> p h s", h=8))
            p.I("scalar", "copy", [], [n1, N("N1T")], out=dd["N1T"][:],
                in_=TRb1[:, 0:512].rearrange("p (h s) -> p h s", h=4))
            P1 = dd["P0"]
            p.I("vector", "tensor_tensor", [N("XA")], [N("Pf0")], out=dd["Pf0"][:], in0=XA[:, 0:4, :], in1=bh(ident[:, :]), op=ALU.add)
            p.I("gpsimd", "tensor_copy", [N("Pf0")], [N("P0")], out=P1[:], in_=dd["Pf0"][:])
            Pf, Pfn = dd["Pf0"], N("Pf0")
            Xc, Xn_ = XA[:, 0:4, :], N("XA")
            Yc, Yn_ = Y0, N("Y0")
            Pc, Pn_ = P1, N("P0")
            for it in range(5):
                nb = (it + 1) % 2
                doP = it >= 1
                doX = it <= 2
                doY = it <= 3
                if doP:
                    for h in range(4):
                        p.I("tensor", "matmul", [Yn_, Pn_], [n2], bk(b2)[:, h, :], lhsT=Yc[:, h, :], rhs=Pc[:, h, :], start=True, stop=True)
                if doX:
                    for h in range(4):
                        p.I("tensor", "matmul", [Yn_, Xn_], [n0], bk(b0)[:, h, :], lhsT=Yc[:, h, :], rhs=Xc[:, h, :], start=True, stop=True)
                if doY:
                    for h in range(4):
                        p.I("tensor", "matmul", [Xn_, Yn_], [n1], bk(b1)[:, h, :], lhsT=Xc[:, h, :], rhs=Yc[:, h, :], start=True, stop=True)
                if doP:
                    Pnew, Pnn = dd["P%d" % nb], N("P%d" % nb)
                    Pfnew, Pfnn = dd["Pf%d" % nb], N("Pf%d" % nb)
                    p.I("vector", "tensor_tensor", [Pfn], [n2, Pfnn], out=Pfnew[:], in0=bk(b2), in1=Pf[:], op=ALU.add)
                    p.I("scalar", "copy", [Pfnn], [Pnn], out=Pnew[:], in_=Pfnew[:])
                    Pf, Pfn = Pfnew, Pfnn
                    Pc, Pn_ = Pnew, Pnn
                if doX:
                    Xnew, Xnn = dd["X%d" % nb], N("X%d" % nb)
                    p.I("scalar", "copy", [], [n0, Xnn], out=Xnew[:], in_=bk(b0))
                if doY:
                    Ynew, Ynn = dd["Y%d" % nb], N("Y%d" % nb)
                    if it % 2 == 0:
                        p.I("scalar", "copy", [], [n1, Ynn], out=Ynew[:], in_=bk(b1))
                    else:
                        p.I("vector", "tensor_copy", [], [n1, Ynn], out=Ynew[:], in_=bk(b1))
                if doX:
                    Xc, Xn_ = Xnew[:], Xnn
                if doY:
                    Yc, Yn_ = Ynew, Ynn
            for h in range(4):
                p.I("tensor", "transpose", [Pn_], [n0], out=TRb[:, h * 128:(h + 1) * 128], in_=Pc[:, h, :], identity=ident[:, :])
            Q0, Q1 = dd["Q0"], dd["Q1"]
            p.I("scalar", "copy", [], [n0, N("Q0")], out=Q0[:], in_=TRb[:, 0:512].rearrange("p (h s) -> p h s", h=4))
            for h in range(4):
                p.I("tensor", "matmul", [N("N1"), Pn_], [n1], bk(b1)[:, h, :], lhsT=dd["N1"][:, h, :], rhs=Pc[:, h, :], start=True, stop=True)
            for h in range(4):
                p.I("tensor", "matmul", [N("N1T"), N("Q0")], [n2], bk(b2)[:, h, :], lhsT=dd["N1T"][:, h, :], rhs=Q0[:, h, :], start=True, stop=True)
            p.I("scalar", "copy", [], [n1, N("WP")], out=dd["WP"][:], in_=bk(b1))
            p.I("vector", "tensor_copy", [], [n2, N("WQ")], out=dd["WQ"][:], in_=bk(b2))
            for h in range(4):
                p.I("tensor", "matmul", [N("Q0"), N("WP")], [n0], bk(b0)[:, h, :], lhsT=Q0[:, h, :], rhs=dd["WP"][:, h, :], start=True, stop=True)
            for h in range(4):
                p.I("tensor", "matmul", [Pn_, N("WQ")], [n1], bk(b1)[:, h, :], lhsT=Pc[:, h, :], rhs=dd["WQ"][:, h, :], start=True, stop=True)
            nbp = 0 if Pc is dd["P1"] else 1
            Pnew, Pnn = dd["P%d" % nbp], N("P%d" % nbp)
            Pfnew, Pfnn = dd["Pf%d" % nbp], N("Pf%d" % nbp)
            p.I("vector", "tensor_tensor", [Pfn], [n0, Pfnn], out=Pfnew[:], in0=bk(b0), in1=Pf[:], op=ALU.add)
            p.I("scalar", "copy", [Pfnn], [Pnn], out=Pnew[:], in_=Pfnew[:])
            p.I("vector", "tensor_tensor", [N("Q0")], [n1, N("Q1")], out=Q1[:], in0=bk(b1), in1=Q0[:], op=ALU.add)
            Pf, Pfn, Pc, Pn_ = Pfnew, Pfnn, Pnew, Pnn
            for h in range(4):
                p.I("tensor", "matmul", [N("N2"), Pn_], [n2], bk(b2)[:, h, :], lhsT=dd["N2"][:, h, :], rhs=Pc[:, h, :], start=True, stop=True)
            p.I("scalar", "copy", [], [n2, N("WP")], out=dd["WP"][:], in_=bk(b2))
            for h in range(4):
                p.I("tensor", "matmul", [N("Q1"), N("WP")], [n0], bk(b0)[:, h, :], lhsT=Q1[:, h, :], rhs=dd["WP"][:, h, :], start=True, stop=True)
            nbp = 1 - nbp
            Pnew, Pnn = dd["P%d" % nbp], N("P%d" % nbp)
            p.I("vector", "tensor_tensor", [Pfn], [n0, Pnn], out=Pnew[:], in0=bk(b0), in1=Pf[:], op=ALU.add)
            Pc, Pn_ = Pnew, Pnn
            return Pc, Pn_

        def scan(d, i, par, second, Pc, Pn_):
            dd = D[d, par]
            sfx = "_%d%d" % (d, par)
            al = dd["_alias"]
            N = lambda x: al.get(x, x) + sfx
            Sn, Sb = "S32_%d" % d, "Sbf_%d" % d
            XA, qTc, zc, ex = dd["XA"], dd["qTc"], dd["zc"], dd["ex"]
            if second:
                p.dma("sync", dd["ol"][:], ost_d[i * 128:(i + 1) * 128, :].rearrange("p (h d) -> p h d", h=4),
                      reads=["ost%d" % i], writes=[N("ol")])
            kTc = dd["kTc"]
            for h in range(4):
                p.I("tensor", "matmul", [N("kTc"), Sb], ["B6"], bk(6)[:, h, :], lhsT=kTc[:, h // 2, :], rhs=Sbf[d][:, h, :],
                    start=True, stop=True)
            for h in range(4):
                p.I("tensor", "matmul", [N("qTc"), Sb], ["B7"], bk(7)[:, h, :], lhsT=qTc[:, h // 2, :], rhs=Sbf[d][:, h, :],
                    start=True, stop=True)
            rr = dd["kbg"]
            for h in range(4):
                p.I("vector", "scalar_tensor_tensor", [N("sso"), N("vb")], ["B6", N("kbg")], out=rr[:, h, :], in0=bk(6)[:, h, :],
                    scalar=dd["sso"][:, h:h + 1], in1=dd["vb"][:, h, :], op0=ALU.mult, op1=ALU.add)
            for h in range(4):
                p.I("tensor", "matmul", [Pn_, N("kbg")], ["B6"], bk(6)[:, h, :], lhsT=Pc[:, h, :], rhs=rr[:, h, :],
                    start=True, stop=True)
            p.I("vector", "tensor_copy", [], ["B6", N("vn")], out=dd["vn"][:], in_=bk(6))
            p.I("vector", "tensor_tensor", [N("ex")], ["B7", N("tq")], out=dd["tq"][:], in0=bk(7), in1=b4(ex[:, 0:4]), op=ALU.mult)
            for h in range(4):
                p.I("tensor", "matmul", [N("kdec"), N("vn")], ["B7"], bk(7)[:, h, :], lhsT=dd["kdec"][:, h, :], rhs=dd["vn"][:, h, :],
                    start=True, stop=True)
            for h in range(4):
                p.I("tensor", "matmul", [N("XA"), N("vn")], ["B6"], bk(6)[:, h, :], lhsT=XA[:, 4 + h, :], rhs=dd["vn"][:, h, :],
                    start=True, stop=True)
            for h in range(4):
                p.I("vector", "scalar_tensor_tensor", [N("ex")], ["B7", Sn], out=S32[d][:, h, :], in0=S32[d][:, h, :],
                    scalar=ex[:, 8 + h:9 + h], in1=bk(7)[:, h, :], op0=ALU.mult, op1=ALU.add)
            p.I("scalar", "copy", [Sn], [Sb], out=Sbf[d][:], in_=S32[d][:])
            od = dd["od"]
            p.I("vector", "tensor_tensor", [N("tq")], ["B6", N("od")], out=od[:], in0=bk(6), in1=dd["tq"][:], op=ALU.add)
            if not second:
                p.dma("sync", ost_d[i * 128:(i + 1) * 128, :], od[:].rearrange("p h d -> p (h d)"), reads=[N("od")],
                      writes=["ost%d" % i])
                return
            p.I("gpsimd", "tensor_tensor", [N("od"), N("ol")], [N("od")], out=od[:], in0=od[:], in1=dd["ol"][:], op=ALU.add)
            p.I("scalar", "activation", [N("od")], [N("sqo")], out=dd["sqo"][:], in_=od[:], func=AF.Square)
            p.I("vector", "tensor_reduce", [N("sqo")], [N("sso")], out=dd["sso"][:, :], in_=dd["sqo"][:], axis=AX.X, op=ALU.add)
            p.I("scalar", "activation", [N("sso"), "eps"], [N("sso")], out=dd["sso"][:, :], in_=dd["sso"][:, :], func=AF.Sqrt,
                scale=1.0 / 128, bias=eps[:, :])
            p.I("vector", "reciprocal", [N("sso")], [N("sso")], out=dd["sso"][:, :], in_=dd["sso"][:, :])
            p.I("vector", "tensor_tensor", [N("sso"), N("od")], [N("od")], out=od[:], in0=od[:], in1=b4(dd["sso"][:, :]), op=ALU.mult)
            p.I("gpsimd", "tensor_tensor", [N("od"), "on"], [N("od")], out=od[:], in0=od[:], in1=bh(on[:, :]), op=ALU.mult)
            p.I("gpsimd", "tensor_tensor", [N("od"), N("zc")], [N("ogc")], out=dd["ogc"][:], in0=od[:],
                in1=zc[:, :].rearrange("p (h d) -> p h d", h=4), op=ALU.mult)
            outs.append(p.dma("sync", og_d[i * 128:(i + 1) * 128, :], dd["ogc"][:].rearrange("p h d -> p (h d)"), reads=[N("ogc")]))

        pend = {}
        pend[0, 0] = pre(0, 0, 0, False)
        pend[1, nchunks - 1] = pre(1, nchunks - 1, 0, False)
        for t in range(nchunks):
            par = t % 2
            if t + 1 < nchunks:
                sec1 = (t + 1) >= nchunks // 2
                pend[0, t + 1] = pre(0, t + 1, 1 - par, sec1)
                pend[1, nchunks - 2 - t] = pre(1, nchunks - 2 - t, 1 - par, sec1)
            second = t >= nchunks // 2
            scan(0, t, par, second, *pend.pop((0, t)))
            scan(1, nchunks - 1 - t, par, second, *pend.pop((1, nchunks - 1 - t)))
        p.emit(final_wait_ops=outs)
    return nc, prog


def build_gdn_out_program():
    nc = bass.Bass("TRN2", target_bir_lowering=False)
    og_d = nc.dram_tensor("og", [TOK, 2048], BF16, kind="ExternalInput").ap()
    h_d = nc.dram_tensor("h", [TOK, D_MODEL], F32, kind="ExternalInput").ap()
    w_d = nc.dram_tensor("w_out", [2048, D_MODEL], F32, kind="ExternalInput").ap()
    out_d = nc.dram_tensor("out", [TOK, D_MODEL], F32, kind="ExternalOutput").ap()
    prog = Prog(nc)
    p = prog
    outs = []
    with contextlib.ExitStack() as st:
        c = Ctx(nc, prog, st)
        ident = make_ident(c)
        p.fence()
        w = c.sb("w", [128, 16, D_MODEL], BF16)
        wv = w_d.rearrange("(k p) c -> p k c", p=128)
        for k in range(0, 16, 4):
            p.dma("gpsimd", w[:, k:k + 4, :], wv[:, k:k + 4, :], writes=["w"])
        ogt = [c.sb("ogt%d" % i, [128, 2048], BF16) for i in range(2)]
        ht = [c.sb("ht%d" % i, [128, D_MODEL], F32) for i in range(2)]
        ho = [c.sb("ho%d" % i, [128, D_MODEL], F32) for i in range(2)]
        ogT = [c.sb("ogT%d" % i, [128, 16, 128], BF16) for i in range(2)]
        TP = c.ps("TP", [128, 16, 128], BF16)
        AC = c.ps("AC", [128, 1024], F32)
        for t in range(16):
            b = t % 2
            p.dma("sync", ogt[b][:], og_d[t * 128:(t + 1) * 128, :], writes=["ogt%d" % b])
            p.dma("sync", ht[b][:], h_d[t * 128:(t + 1) * 128, :], writes=["ht%d" % b])
            for k in range(16):
                p.I("tensor", "transpose", ["ogt%d" % b], ["TP"], out=TP[:, k, :], in_=ogt[b][:, k * 128:(k + 1) * 128],
                    identity=ident[:, :])
            p.I("vector", "tensor_copy", [], ["TP", "ogT%d" % b], out=ogT[b][:], in_=TP[:])
            for hf in range(2):
                for k in range(16):
                    p.I("tensor", "matmul", ["ogT%d" % b, "w"], ["AC"], AC[:, hf * 512:(hf + 1) * 512], lhsT=ogT[b][:, k, :],
                        rhs=w[:, k, hf * 512:(hf + 1) * 512], start=(k == 0), stop=(k == 15))
            p.I("vector", "tensor_tensor", ["ht%d" % b], ["AC", "ho%d" % b], out=ho[b][:], in0=AC[:, :], in1=ht[b][:], op=ALU.add)
            outs.append(p.dma("sync", out_d[t * 128:(t + 1) * 128, :], ho[b][:], reads=["ho%d" % b]))
        p.emit(final_wait_ops=outs)
    return nc, prog


_CACHE = {}


def _prog(name, builder, *a):
    key = (name,) + a
    if key not in _CACHE:
        _CACHE[key] = builder(*a)[0]
    return _CACHE[key]


def _rep(v, n=128):
    return np.ascontiguousarray(np.broadcast_to(np.asarray(v, np.float32)[None], (n,) + tuple(np.shape(v))))


def _halo(hb, c, pad):
    b, j = c // 4, c % 4
    z = np.zeros((pad, hb.shape[-1]), hb.dtype)
    ext = np.concatenate([z, hb[b], z], 0)
    return np.ascontiguousarray(ext[j * TOK:j * TOK + TOK + 2 * pad])


def _run(nc, maps):
    return run_bass_kernel_spmd(nc, maps, core_ids=list(range(NCORES))).results


def _ffn_layer(h, g, gf, wup, cwt, cb, wdn, final):
    nc = _prog("ffn", build_ffn_program, final)
    cw = np.concatenate([cwt, cb[None]], 0)
    cw_l = np.ascontiguousarray(cw.reshape(4, 44, 128).transpose(2, 1, 0))
    maps = []
    for c in range(NCORES):
        hh = _halo(h, c, 1)
        maps.append(dict(h=hh, g_ffn=_rep(g), g_fin=_rep(gf), w_up=wup, w_down=wdn, cw=cw_l))
    res = _run(nc, maps)
    return np.stack([np.concatenate([res[b * 4 + j]["out"] for j in range(4)], 0) for b in range(BATCH)], 0)


def kernel(x, norm_mix, norm_ffn, norm_final, attn_w_in, attn_q_norm, attn_k_norm, attn_w_out, gdn_w_in,
           gdn_conv_w, gdn_conv_b, gdn_a_log, gdn_dt_bias, gdn_o_norm, gdn_w_out, ffn_w_up, ffn_conv_w,
           ffn_conv_b, ffn_w_down):
    f32 = lambda a: np.ascontiguousarray(np.asarray(a, dtype=np.float32))
    x = f32(x)
    rope = rope_tables_np()
    nc = _prog("ap", build_attn_proj_program)
    maps = []
    for c in range(NCORES):
        b, j = c // 4, c % 4
        maps.append(dict(x=np.ascontiguousarray(x[b, j * TOK:(j + 1) * TOK]), g_mix=_rep(norm_mix[0]), w_in=f32(attn_w_in[0]),
                         qg=_rep(attn_q_norm[0]), kg=_rep(attn_k_norm[0]), rope=np.ascontiguousarray(rope[j * TOK:(j + 1) * TOK])))
    r1 = _run(nc, maps)
    nc = _prog("ac", build_attn_core_program)
    maps = []
    for c in range(NCORES):
        b, j = c // 4, c % 4
        kT = np.ascontiguousarray(np.concatenate([r1[b * 4 + i]["kT"] for i in range(4)], axis=2))
        v = np.ascontiguousarray(np.concatenate([r1[b * 4 + i]["v"] for i in range(4)], axis=1))
        maps.append(dict(qT=r1[c]["qT"], kT=kT, v=v, gate=r1[c]["gate"], x=np.ascontiguousarray(x[b, j * TOK:(j + 1) * TOK]),
                         w_out=f32(attn_w_out[0]), qg=_rep(attn_q_norm[0]), kg=_rep(attn_k_norm[0])))
    r2 = _run(nc, maps)
    h = np.stack([np.concatenate([r2[b * 4 + j]["out"] for j in range(4)], 0) for b in range(BATCH)], 0)
    h = _ffn_layer(h, f32(norm_ffn[0]), f32(norm_final), f32(ffn_w_up[0]), f32(ffn_conv_w[0]), f32(ffn_conv_b[0]),
                   f32(ffn_w_down[0]), False)
    nc = _prog("gp", build_gdn_proj_program)
    cwf = np.concatenate([f32(gdn_conv_w[0]), f32(gdn_conv_b[0])[None]], 0)
    cw_l = np.ascontiguousarray(cwf.reshape(6, 32, 128).transpose(2, 1, 0))
    ad = np.stack([f32(gdn_a_log[0]).reshape(32), f32(gdn_dt_bias[0]).reshape(32)], 0)
    maps = []
    for c in range(NCORES):
        maps.append(dict(h=_halo(h, c, 2), g_mix=_rep(norm_mix[1]), w_in=f32(gdn_w_in[0]), cw=cw_l, ad=_rep(ad)))
    r3 = _run(nc, maps)
    nc = _prog("gs", build_gdn_scan_program, 64)
    maps = []
    for c in range(NCORES):
        b, j = c // 4, c % 4
        grp = [r3[b * 4 + i] for i in range(4)]
        qT = np.ascontiguousarray(np.concatenate([g_["qT"][:, 2 * j:2 * j + 2] for g_ in grp], axis=2))
        kT = np.ascontiguousarray(np.concatenate([g_["kT"][:, 2 * j:2 * j + 2] for g_ in grp], axis=2))
        ktm = np.ascontiguousarray(np.concatenate([g_["k_tm"][:, 256 * j:256 * (j + 1)] for g_ in grp], axis=0))
        vtm = np.ascontiguousarray(np.concatenate([g_["v_tm"][:, 512 * j:512 * (j + 1)] for g_ in grp], axis=0))
        zz = np.ascontiguousarray(np.concatenate([g_["z"][:, 512 * j:512 * (j + 1)] for g_ in grp], axis=0))
        gball = np.concatenate([g_["gb"] for g_ in grp], axis=0)
        hs = slice(4 * j, 4 * j + 4)
        gb = np.ascontiguousarray(np.concatenate([gball[:, 0:16][:, hs], gball[:, 16:32][:, hs],
                                                  gball[:, 32:48][:, hs], gball[:, 48:64][:, hs]], axis=1))
        maps.append(dict(qT=qT, kT=kT, k_tm=ktm, v_tm=vtm, z=zz, gb=gb, on=_rep(gdn_o_norm[0]), bm=block_masks_np()))
    r4 = _run(nc, maps)
    nc = _prog("go", build_gdn_out_program)
    maps = []
    for c in range(NCORES):
        b, j = c // 4, c % 4
        og = np.ascontiguousarray(np.concatenate([r4[b * 4 + i]["og"][j * TOK:(j + 1) * TOK] for i in range(4)], axis=1))
        maps.append(dict(og=og, h=np.ascontiguousarray(h[b, j * TOK:(j + 1) * TOK]), w_out=f32(gdn_w_out[0])))
    r5 = _run(nc, maps)
    h = np.stack([np.concatenate([r5[b * 4 + j]["out"] for j in range(4)], 0) for b in range(BATCH)], 0)
    out = _ffn_layer(h, f32(norm_ffn[1]), f32(norm_final), f32(ffn_w_up[1]), f32(ffn_conv_w[1]), f32(ffn_conv_b[1]),
                     f32(ffn_w_down[1]), True)
    return out.astype(np.float32)
```

```python
import contextlib
import numpy as np
import concourse.bass as bass
import concourse.mybir as mybir
from concourse.bass_utils import run_bass_kernel_spmd

F32 = mybir.dt.float32
BF16 = mybir.dt.bfloat16
AF = mybir.ActivationFunctionType
ALU = mybir.AluOpType
AX = mybir.AxisListType

D_MODEL = 1024
BATCH = 2
SEQ = 8192
EPS = 1e-6
D_FF = 2816
NCORES = 8
TOK = 2048

COMPUTE = ("tensor", "vector", "scalar", "gpsimd")
N_DMA_SEMS = 24


class Prog:
    def __init__(self, nc, same_engine_sync=True, store_defer=0.0):
        self.nc = nc
        self.store_defer = store_defer
        self.ops = []
        self.last_w = {}
        self.readers = {}
        self.same_engine_sync = same_engine_sync
        self.n_dma = 0
        self.dma_sem_last = {}
        self.fence_deps = set()

    def fence(self):
        self.fence_deps = set(i for i, o in enumerate(self.ops) if not o["dma"])

    def op(self, eng, fn, reads=(), writes=(), dma=False):
        idx = len(self.ops)
        deps = set()
        for r in reads:
            if r in self.last_w:
                deps.add(self.last_w[r])
        for w in writes:
            if w in self.last_w:
                deps.add(self.last_w[w])
            for rd in self.readers.get(w, ()):
                deps.add(rd)
        sem_slot = None
        if dma:
            if eng == "gpsimd":
                self.n_dma_sw = getattr(self, "n_dma_sw", 0) + 1
                sem_slot = 16 + self.n_dma_sw % (N_DMA_SEMS - 16)
            else:
                sem_slot = self.n_dma % 16
            self.n_dma += 1
            prev = self.dma_sem_last.get(sem_slot)
            if prev is not None:
                deps.add(prev)
            self.dma_sem_last[sem_slot] = idx
        deps |= self.fence_deps
        deps.discard(idx)
        self.ops.append(dict(eng=eng, fn=fn, deps=deps, dma=dma, sem_slot=sem_slot))
        for w in writes:
            self.last_w[w] = idx
            self.readers[w] = []
        for r in reads:
            if r not in writes:
                self.readers.setdefault(r, []).append(idx)
        return idx

    def I(self, eng, method, reads, writes, *args, **kwargs):
        def fn(e, method=method, args=args, kwargs=kwargs):
            return getattr(e, method)(*args, **kwargs)
        idx = self.op(eng, fn, reads, writes)
        try:
            if method == "matmul":
                r = kwargs["rhs"]
                n = int(np.prod(r.shape[1:]))
                dur = 40 + n * (2.0 if r.dtype == F32 else 0.5)
            elif method == "transpose":
                dur = 110.0
            else:
                o = kwargs.get("out", args[0] if args else None)
                n = int(np.prod(o.shape[1:]))
                dur = {"scalar": 220 + n / 1.4, "vector": 90 + n / 0.96, "gpsimd": 160 + n / 0.45}[eng]
        except Exception:
            dur = 300.0
        self.ops[idx]["dur"] = dur
        return idx

    def dma(self, queue, out, in_, reads=(), writes=()):
        def fn(e, out=out, in_=in_):
            return e.dma_start(out=out, in_=in_)
        idx = self.op(queue, fn, reads, writes, dma=True)
        try:
            nbytes = int(np.prod(out.shape)) * (4 if out.dtype == F32 else 2)
        except Exception:
            nbytes = 1 << 16
        self.ops[idx]["dur"] = 2000 + nbytes / 150.0
        self.ops[idx]["store"] = (str(out.space) == "DRAM")
        return idx

    def reorder(self, final_wait_ops):
        import heapq
        ops = self.ops
        n = len(ops)
        succ = [[] for _ in range(n)]
        indeg = [0] * n
        for i, o in enumerate(ops):
            for d in o["deps"]:
                succ[d].append(i)
            indeg[i] = len(o["deps"])
        ready_t = [0.0] * n
        STORE_DEFER = self.store_defer
        heap = [(0.0, i) for i in range(n) if indeg[i] == 0]
        heapq.heapify(heap)
        eng_free = {}
        order = []
        LAT = 900.0
        while heap:
            rt, i = heapq.heappop(heap)
            o = ops[i]
            e = o["eng"]
            start = max(rt, eng_free.get(e, 0.0))
            dur = o.get("dur", 300.0)
            if o["dma"]:
                eng_free[e] = start + 60.0
                fin = start + dur
            else:
                eng_free[e] = start + dur
                fin = start + dur
            order.append(i)
            for s_ in succ[i]:
                so = ops[s_]
                lat = 0.0 if (so["eng"] == e and not o["dma"] and e == "tensor") else LAT
                t = fin + lat
                if t > ready_t[s_]:
                    ready_t[s_] = t
                indeg[s_] -= 1
                if indeg[s_] == 0:
                    heapq.heappush(heap, (ready_t[s_] + (STORE_DEFER if so.get("store") else 0.0), s_))
        assert len(order) == n
        pos = {old: new for new, old in enumerate(order)}
        new_ops = []
        for old in order:
            o = ops[old]
            o["deps"] = set(pos[d] for d in o["deps"])
            new_ops.append(o)
        self.ops = new_ops
        self.est_ns = max(eng_free.values()) if eng_free else 0
        return [pos[d] for d in final_wait_ops]

    def emit(self, final_wait_ops=(), schedule=True):
        nc = self.nc
        if schedule:
            final_wait_ops = self.reorder(list(final_wait_ops))
        ops = self.ops
        engines = ("sync",) + COMPUTE
        waited_eng = {e: {p: -1 for p in COMPUTE} for e in engines}
        waited_dma = {e: set() for e in engines}
        for i, o in enumerate(ops):
            e = o["eng"]
            need_eng = {}
            need_dma = []
            for d in sorted(o["deps"]):
                po = ops[d]
                if po["dma"]:
                    if d not in waited_dma[e]:
                        need_dma.append(d)
                        waited_dma[e].add(d)
                else:
                    pe = po["eng"]
                    if pe == e and (pe == "tensor" or not self.same_engine_sync):
                        continue
                    if d > waited_eng[e][pe]:
                        need_eng[pe] = max(need_eng.get(pe, -1), d)
            for pe, d in need_eng.items():
                waited_eng[e][pe] = d
            o["waits"] = list(need_eng.values()) + need_dma
        final_e = "sync"
        fw = []
        for d in final_wait_ops:
            fw.append(d)
        signal = set()
        for o in ops:
            for d in o["waits"]:
                signal.add(d)
        for d in fw:
            signal.add(d)
        cnt = {e: 0 for e in COMPUTE}
        dma_cnt = {}
        for i, o in enumerate(ops):
            if o["dma"]:
                s = o["sem_slot"]
                dma_cnt[s] = dma_cnt.get(s, 0) + 16
                o["sig"] = ("dma", s, dma_cnt[s])
            elif i in signal:
                cnt[o["eng"]] += 1
                o["sig"] = ("eng", o["eng"], cnt[o["eng"]])
            else:
                o["sig"] = None
        self.stats = dict(n_ops=len(ops), signals=dict(cnt), n_dma=self.n_dma)
        with contextlib.ExitStack() as st:
            esem = {e: st.enter_context(nc.semaphore("s_" + e)) for e in COMPUTE}
            dsem = [st.enter_context(nc.semaphore("d_%d" % k)) for k in range(N_DMA_SEMS)]
            block = st.enter_context(nc.Block())

            def semof(sig):
                if sig[0] == "dma":
                    return dsem[sig[1]], sig[2]
                return esem[sig[1]], sig[2]

            def run(ename):
                def body(eng):
                    for i, o in enumerate(ops):
                        if o["eng"] != ename:
                            continue
                        for d in o["waits"]:
                            s, v = semof(ops[d]["sig"])
                            eng.wait_ge(s, v)
                        ins = o["fn"](eng)
                        sig = o["sig"]
                        if sig is not None:
                            s, v = semof(sig)
                            ins.then_inc(s, 16 if sig[0] == "dma" else 1)
                    if ename == final_e:
                        for d in fw:
                            s, v = semof(ops[d]["sig"])
                            eng.wait_ge(s, v)
                return body

            block.sync(run("sync"))
            block.tensor(run("tensor"))
            block.vector(run("vector"))
            block.scalar(run("scalar"))
            block.gpsimd(run("gpsimd"))


class Ctx:
    def __init__(self, nc, prog, st):
        self.nc, self.p, self.st = nc, prog, st
        self.k = 0

    def sb(self, name, shape, dt):
        return self.st.enter_context(self.nc.sbuf_tensor(name, list(shape), dt))

    def ps(self, name, shape, dt):
        return self.st.enter_context(self.nc.psum_tensor(name, list(shape), dt))


def emit_rmsnorm_T(c, src_ap, n, g_sb, dstT, col0, tag, bufs):
    p = c.p
    sq, ss, rs, yb, pT, ident = (bufs[k] for k in ("sq", "ss", "rs", "yb", "pT", "ident"))
    nm = bufs["names"]
    p.I("scalar", "activation", [tag], [nm["sq"], nm["ss"]],
        out=sq[0:n, :], in_=src_ap, func=AF.Square, accum_out=ss[0:n, :])
    p.I("scalar", "activation", [nm["ss"], "eps"], [nm["rs"]],
        out=rs[0:n, :], in_=ss[0:n, :], func=AF.Sqrt, scale=1.0 / D_MODEL, bias=bufs["eps"][0:n, :])
    p.I("vector", "reciprocal", [nm["rs"]], [nm["rs"]], out=rs[0:n, :], in_=rs[0:n, :])
    p.I("vector", "scalar_tensor_tensor", [tag, nm["rs"], "g_sb"], [nm["yb"]],
        out=yb[0:n, :], in0=src_ap, scalar=rs[0:n, :], in1=g_sb[0:n, :], op0=ALU.mult, op1=ALU.mult)
    for k in range(8):
        p.I("tensor", "transpose", [nm["yb"]], [nm["pT"]],
            out=pT[:, k, 0:n], in_=yb[0:n, k * 128:(k + 1) * 128], identity=ident[0:n, 0:n])
    p.I("vector", "tensor_copy", [nm["pT"]], [bufs["dst_name"]],
        out=dstT[:, :, col0:col0 + n], in_=pT[:, :, 0:n])


def make_ident(c, name="ident"):
    ident = c.sb(name, [128, 128], BF16)
    c.p.I("gpsimd", "memset", [], [name], ident[:], 1.0)
    c.p.I("gpsimd", "affine_select", [name], [name], out=ident[:], in_=ident[:], pattern=[[-1, 128]],
          compare_op=ALU.is_equal, fill=0.0, base=0, channel_multiplier=1)
    return ident


def ffn_blocks(ntok):
    out = []
    t = 0
    while t < ntok:
        n = min(254, ntok - t)
        out.append((t, n))
        t += n
    return out


def build_ffn_program(final_norm, pre=None):
    nc = bass.Bass("TRN2", target_bir_lowering=False)
    h_d = nc.dram_tensor("h", [TOK + 2, D_MODEL], F32, kind="ExternalInput").ap()
    g_d = nc.dram_tensor("g_ffn", [128, D_MODEL], F32, kind="ExternalInput").ap()
    gf_d = nc.dram_tensor("g_fin", [128, D_MODEL], F32, kind="ExternalInput").ap()
    wup_d = nc.dram_tensor("w_up", [D_MODEL, 2 * D_FF], F32, kind="ExternalInput").ap()
    wdn_d = nc.dram_tensor("w_down", [D_FF, D_MODEL], F32, kind="ExternalInput").ap()
    cw_d = nc.dram_tensor("cw", [128, 44, 4], F32, kind="ExternalInput").ap()
    out_d = nc.dram_tensor("out", [TOK, D_MODEL], F32, kind="ExternalOutput").ap()
    prog = Prog(nc, store_defer=15000.0)
    with contextlib.ExitStack() as st:
        c = Ctx(nc, prog, st)
        emit_ffn(c, h_d, g_d, gf_d, wup_d, wdn_d, cw_d, out_d, final_norm)
    return nc, prog


def emit_ffn(c, h_d, g_d, gf_d, wup_d, wdn_d, cw_d, out_d, final_norm):
    p = c.p
    ident = make_ident(c)
    wup = c.sb("wup", [128, 8, 2 * D_FF], BF16)
    wdn = c.sb("wdn", [128, 22, D_MODEL], BF16)
    cw = c.sb("cw_sb", [128, 44, 4], F32)
    g_sb = c.sb("g_sb", [128, D_MODEL], F32)
    gf_sb = c.sb("gf_sb", [128, D_MODEL], F32)
    eps = c.sb("eps", [128, 1], F32)
    p.I("vector", "memset", [], ["eps"], eps[:], EPS)
    p.fence()
    p.dma("sync", g_sb[:], g_d, writes=["g_sb"])
    p.dma("sync", gf_sb[:], gf_d, writes=["gf_sb"])
    p.dma("sync", cw[:], cw_d, writes=["cw"])
    wup_v = wup_d.rearrange("(k p) c -> p k c", p=128)
    wdn_v = wdn_d.rearrange("(j p) c -> p j c", p=128)
    WG = 2
    for j0 in range(0, 22, WG):
        j1 = min(22, j0 + WG)
        for hh in range(2):
            c0, c1 = (hh * 22 + j0) * 128, (hh * 22 + j1) * 128
            p.dma("gpsimd", wup[:, :, c0:c1], wup_v[:, :, c0:c1], writes=["wup%d" % (j0 // WG)])
        p.dma("gpsimd", wdn[:, j0:j1, :], wdn_v[:, j0:j1, :], writes=["wdn%d" % (j0 // WG)])

    NB = 2
    xt = [c.sb("xt%d" % i, [128, D_MODEL], F32) for i in range(NB)]
    sq = c.sb("sq", [128, D_MODEL], F32)
    ss = c.sb("ss", [128, 1], F32)
    rs = c.sb("rs", [128, 1], F32)
    yb = c.sb("yb", [128, D_MODEL], BF16)
    yT = [c.sb("yT%d" % i, [128, 8, 256], BF16) for i in range(2)]
    NR = 3
    t1 = [c.sb("t1_%d" % i, [128, 254], F32) for i in range(2 * NR)]
    t2 = [c.sb("t2_%d" % i, [128, 254], F32) for i in range(2 * NR)]
    t3 = [c.sb("t3_%d" % i, [128, 254], F32) for i in range(2 * NR)]
    sas = [c.sb("sa%d" % i, [128, 254], F32) for i in range(NR)]
    mT = [c.sb("mT%d" % i, [128, 254], BF16) for i in range(NR)]
    hres = [c.sb("hres%d" % i, [128, D_MODEL], F32) for i in range(2)]
    hout = [c.sb("hout%d" % i, [128, D_MODEL], F32) for i in range(2)]
    pT = c.ps("pT", [128, 8, 128], BF16)
    U = [c.ps("U%d" % i, [128, 512], F32) for i in range(3)]
    Dp = [c.ps("D%d" % i, [128, 512], F32) for i in range(4)]
    nbufs = dict(sq=sq, ss=ss, rs=rs, yb=yb, pT=pT, ident=ident, eps=eps,
                 names=dict(sq="sq", ss="ss", rs="rs", yb="yb", pT="pT"))

    out_ops = []
    xi = 0
    for bi, (t0, n) in enumerate(ffn_blocks(TOK)):
        ncol = n + 2
        yTb = yT[bi % 2]
        yname = "yT%d" % (bi % 2)
        nbufs["dst_name"] = yname
        r = 0
        while r < ncol:
            rn = min(128, ncol - r)
            x = xt[xi % NB]
            xname = "xt%d" % (xi % NB)
            xi += 1
            p.dma("sync", x[0:rn, :], h_d[t0 + r:t0 + r + rn, :], writes=[xname])
            emit_rmsnorm_T(c, x[0:rn, :], rn, g_sb, yTb, r, xname, nbufs)
            r += rn
        subs = []
        s0 = 0
        while s0 < n:
            sn = min(127, n - s0)
            subs.append((s0, sn))
            s0 += sn

        def up(j):
            Uj = U[j % 3]
            un = "U%d" % (j % 3)
            for half, cj in ((0, j), (1, j + 22)):
                for k in range(8):
                    p.I("tensor", "matmul", ["wup%d" % (j // 2), yname], [un],
                        Uj[:, half * 256:half * 256 + ncol], lhsT=wup[:, k, cj * 128:(cj + 1) * 128],
                        rhs=yTb[:, k, 0:ncol], start=(k == 0), stop=(k == 7))

        def ew(j):
            Uj = U[j % 3]
            un = "U%d" % (j % 3)
            b = j % NR
            for half, cj in ((0, j), (1, j + 22)):
                o = half * 256
                ti_ = half * NR + b
                tt1, tt2, tt3 = t1[ti_], t2[ti_], t3[ti_]
                p.I("scalar", "activation", ["cw"], [un, "t1_%d" % ti_],
                    out=tt1[:, 0:n], in_=Uj[:, o + 1:o + 1 + n], func=AF.Identity,
                    scale=cw[:, cj, 1:2], bias=cw[:, cj, 3:4])
                p.I("vector", "scalar_tensor_tensor", ["cw", "t1_%d" % ti_], [un, "t2_%d" % ti_],
                    out=tt2[:, 0:n], in0=Uj[:, o:o + n], scalar=cw[:, cj, 0:1], in1=tt1[:, 0:n],
                    op0=ALU.mult, op1=ALU.add)
                p.I("vector", "scalar_tensor_tensor", ["cw", "t2_%d" % ti_], [un, "t3_%d" % ti_],
                    out=tt3[:, 0:n], in0=Uj[:, o + 2:o + 2 + n], scalar=cw[:, cj, 2:3], in1=tt2[:, 0:n],
                    op0=ALU.mult, op1=ALU.add)
            p.I("scalar", "activation", ["t3_%d" % b], ["sa%d" % b], out=sas[b][:, 0:n], in_=t3[b][:, 0:n], func=AF.Silu)
            p.I("gpsimd", "tensor_tensor", ["sa%d" % b, "t3_%d" % (NR + b)], ["mT%d" % b],
                out=mT[b][:, 0:n], in0=sas[b][:, 0:n], in1=t3[NR + b][:, 0:n], op=ALU.mult)

        def down(j):
            b = j % NR
            m = mT[b]
            for si, (s0, sn) in enumerate(subs):
                for hf in range(2):
                    p.I("tensor", "matmul", ["mT%d" % b, "wdn%d" % (j // 2)], ["D%d" % (si * 2 + hf)],
                        Dp[si * 2 + hf][0:sn, :], lhsT=m[:, s0:s0 + sn],
                        rhs=wdn[:, j, hf * 512:(hf + 1) * 512], start=(j == 0), stop=(j == 21))

        for si, (s0, sn) in enumerate(subs):
            p.dma("sync", hres[si][0:sn, :], h_d[1 + t0 + s0:1 + t0 + s0 + sn, :], writes=["hres%d" % si])
        up(0)
        for j in range(22):
            if j + 1 < 22:
                up(j + 1)
            ew(j)
            down(j)
        for si, (s0, sn) in enumerate(subs):
            hr = hres[si]
            ho = hout[si]
            hn = "hout%d" % si
            for hf in range(2):
                p.I("vector", "tensor_tensor", ["hres%d" % si], ["D%d" % (si * 2 + hf), hn],
                    out=ho[0:sn, hf * 512:(hf + 1) * 512], in0=Dp[si * 2 + hf][0:sn, :],
                    in1=hr[0:sn, hf * 512:(hf + 1) * 512], op=ALU.add)
            if final_norm:
                p.I("scalar", "activation", [hn], ["sq", "ss"],
                    out=sq[0:sn, :], in_=ho[0:sn, :], func=AF.Square, accum_out=ss[0:sn, :])
                p.I("scalar", "activation", ["ss", "eps"], ["rs"],
                    out=rs[0:sn, :], in_=ss[0:sn, :], func=AF.Sqrt, scale=1.0 / D_MODEL, bias=eps[0:sn, :])
                p.I("vector", "reciprocal", ["rs"], ["rs"], out=rs[0:sn, :], in_=rs[0:sn, :])
                p.I("vector", "scalar_tensor_tensor", ["rs", "gf_sb"], [hn],
                    out=ho[0:sn, :], in0=ho[0:sn, :], scalar=rs[0:sn, :], in1=gf_sb[0:sn, :],
                    op0=ALU.mult, op1=ALU.mult)
            d = p.dma("sync", out_d[t0 + s0:t0 + s0 + sn, :], ho[0:sn, :], reads=[hn])
            out_ops.append(d)
    p.emit(final_wait_ops=out_ops)


def build_attn_proj_program():
    nc = bass.Bass("TRN2", target_bir_lowering=False)
    x_d = nc.dram_tensor("x", [TOK, D_MODEL], F32, kind="ExternalInput").ap()
    g_d = nc.dram_tensor("g_mix", [128, D_MODEL], F32, kind="ExternalInput").ap()
    w_d = nc.dram_tensor("w_in", [D_MODEL, 2560], F32, kind="ExternalInput").ap()
    qg_d = nc.dram_tensor("qg", [128, 128], F32, kind="ExternalInput").ap()
    kg_d = nc.dram_tensor("kg", [128, 128], F32, kind="ExternalInput").ap()
    rope_d = nc.dram_tensor("rope", [TOK, 128], F32, kind="ExternalInput").ap()
    qT_d = nc.dram_tensor("qT", [128, 8, TOK], BF16, kind="ExternalOutput").ap()
    kT_d = nc.dram_tensor("kT", [128, 2, TOK], BF16, kind="ExternalOutput").ap()
    v_d = nc.dram_tensor("v", [128, 16, 256], BF16, kind="ExternalOutput").ap()
    gt_d = nc.dram_tensor("gate", [128, 16, 1024], BF16, kind="ExternalOutput").ap()
    prog = Prog(nc)
    p = prog
    with contextlib.ExitStack() as st:
        c = Ctx(nc, prog, st)
        ident = make_ident(c)
        w = c.sb("w", [128, 8, 2560], BF16)
        g_sb = c.sb("g_sb", [128, D_MODEL], F32)
        gains = c.sb("gains", [128, 2, 128], F32)
        eps = c.sb("eps", [128, 1], F32)
        p.I("vector", "memset", [], ["eps"], eps[:], EPS)
        p.fence()
        p.dma("sync", g_sb[:], g_d, writes=["g_sb"])
        p.dma("sync", gains[:, 0, :], qg_d, writes=["gains"])
        p.dma("sync", gains[:, 1, :], kg_d, writes=["gains"])
        p.I("scalar", "mul", ["gains"], ["gains"], out=gains[:, 0, :], in_=gains[:, 0, :], mul=128.0 ** -0.5)
        w_v = w_d.rearrange("(k p) c -> p k c", p=128)
        for k in range(8):
            p.dma("gpsimd", w[:, k, :], w_v[:, k, :], writes=["w"])
        xt = [c.sb("xt%d" % i, [128, D_MODEL], F32) for i in range(2)]
        rp = [c.sb("rp%d" % i, [128, 2, 2, 32], F32) for i in range(2)]
        sq = c.sb("sq", [128, D_MODEL], F32)
        ss = c.sb("ss", [128, 1], F32)
        rs = c.sb("rs", [128, 1], F32)
        yb = c.sb("yb", [128, D_MODEL], BF16)
        yT = [c.sb("yT%d" % i, [128, 8, 128], BF16) for i in range(2)]
        sq10 = c.sb("sq10", [128, 10, 128], F32)
        ss10 = c.sb("ss10", [128, 10], F32)
        rs10 = c.sb("rs10", [128, 10], F32)
        z0 = c.sb("z0", [128, 10, 128], F32)
        zz = c.sb("zz", [128, 10, 128], F32)
        ra = [c.sb("ra%d" % i, [128, 10, 2, 32], F32) for i in range(4)]
        zr = c.sb("zr", [128, 10, 128], BF16)
        qT_all = c.sb("qT_all", [128, 8, TOK], BF16)
        kT_all = c.sb("kT_all", [128, 2, TOK], BF16)
        v_all = c.sb("v_all", [128, 16, 256], BF16)
        gt_all = c.sb("gt_all", [128, 16, 1024], BF16)
        Q2 = c.ps("Q2", [128, 1024], F32)
        G2 = c.ps("G2", [128, 1024], F32)
        KV = c.ps("KV", [128, 512], F32)
        pT = c.ps("pT", [128, 8, 128], BF16)
        TQ = c.ps("TQ", [128, 8, 128], BF16)
        TK = c.ps("TK", [128, 8, 128], BF16)
        nbufs = dict(sq=sq, ss=ss, rs=rs, yb=yb, pT=pT, ident=ident, eps=eps,
                     names=dict(sq="sq", ss="ss", rs="rs", yb="yb", pT="pT"))
        for t in range(16):
            b = t % 2
            x = xt[b]
            p.dma("sync", x[:], x_d[t * 128:(t + 1) * 128, :], writes=["xt%d" % b])
            p.dma("sync", rp[b][:], rope_d[t * 128:(t + 1) * 128, :].rearrange("p (a r e) -> p a r e", a=2, r=2),
                  writes=["rp%d" % b])
            nbufs["dst_name"] = "yT%d" % b
            emit_rmsnorm_T(c, x[:], 128, g_sb, yT[b], 0, "xt%d" % b, nbufs)
            for (dst, dn, c0) in ((Q2[:, 0:512], "Q2", 0), (Q2[:, 512:1024], "Q2", 512), (KV[:, :], "KV", 1024),
                                  (G2[:, 0:512], "G2", 1536), (G2[:, 512:1024], "G2", 2048)):
                for k in range(8):
                    p.I("tensor", "matmul", ["w", "yT%d" % b], [dn], dst, lhsT=yT[b][:, k, :],
                        rhs=w[:, k, c0:c0 + 512], start=(k == 0), stop=(k == 7))
            p.I("scalar", "activation", [], ["Q2", "sq10"], out=sq10[:, 0:8, :],
                in_=Q2[:, :].rearrange("p (h d) -> p h d", h=8), func=AF.Square)
            p.I("scalar", "activation", [], ["KV", "sq10"], out=sq10[:, 8:10, :],
                in_=KV[:, 0:256].rearrange("p (h d) -> p h d", h=2), func=AF.Square)
            p.I("vector", "tensor_reduce", ["sq10"], ["ss10"], out=ss10[:, :], in_=sq10[:, :, :], axis=AX.X, op=ALU.add)
            p.I("scalar", "activation", ["ss10", "eps"], ["rs10"], out=rs10[:, :], in_=ss10[:, :], func=AF.Sqrt,
                scale=1.0 / 128, bias=eps[:, :])
            p.I("vector", "reciprocal", ["rs10"], ["rs10"], out=rs10[:, :], in_=rs10[:, :])
            p.I("vector", "tensor_tensor", ["rs10"], ["Q2", "z0"], out=z0[:, 0:8, :],
                in0=Q2[:, :].rearrange("p (h d) -> p h d", h=8),
                in1=rs10[:, 0:8].unsqueeze(2).broadcast_to([128, 8, 128]), op=ALU.mult)
            p.I("vector", "tensor_tensor", ["rs10"], ["KV", "z0"], out=z0[:, 8:10, :],
                in0=KV[:, 0:256].rearrange("p (h d) -> p h d", h=2),
                in1=rs10[:, 8:10].unsqueeze(2).broadcast_to([128, 2, 128]), op=ALU.mult)
            p.I("gpsimd", "tensor_tensor", ["z0", "gains"], ["zz"], out=zz[:, 0:8, :], in0=z0[:, 0:8, :],
                in1=gains[:, 0:1, :].broadcast_to([128, 8, 128]), op=ALU.mult)
            p.I("gpsimd", "tensor_tensor", ["z0", "gains"], ["zz"], out=zz[:, 8:10, :], in0=z0[:, 8:10, :],
                in1=gains[:, 1:2, :].broadcast_to([128, 2, 128]), op=ALU.mult)
            zv = zz[:, :, :].rearrange("p h (r f e) -> p h r f e", r=2, f=2)
            ov = zr[:, :, :].rearrange("p h (r f e) -> p h r f e", r=2, f=2)
            z1, z2 = zv[:, :, :, 0, :], zv[:, :, :, 1, :]
            cosb = rp[b][:, 0:1, :, :].broadcast_to([128, 10, 2, 32])
            sinb = rp[b][:, 1:2, :, :].broadcast_to([128, 10, 2, 32])
            rn = "rp%d" % b
            p.I("vector", "tensor_tensor", ["zz", rn], ["ra0"], out=ra[0][:], in0=z1, in1=cosb, op=ALU.mult)
            p.I("gpsimd", "tensor_tensor", ["zz", rn], ["ra1"], out=ra[1][:], in0=z2, in1=sinb, op=ALU.mult)
            p.I("gpsimd", "tensor_tensor", ["zz", rn], ["ra2"], out=ra[2][:], in0=z1, in1=sinb, op=ALU.mult)
            p.I("vector", "tensor_tensor", ["zz", rn], ["ra3"], out=ra[3][:], in0=z2, in1=cosb, op=ALU.mult)
            p.I("vector", "tensor_tensor", ["ra0", "ra1"], ["zr"], out=ov[:, :, :, 0, :], in0=ra[0][:], in1=ra[1][:],
                op=ALU.subtract)
            p.I("gpsimd", "tensor_tensor", ["ra2", "ra3"], ["zr"], out=ov[:, :, :, 1, :], in0=ra[2][:], in1=ra[3][:],
                op=ALU.add)
            for h in range(8):
                p.I("tensor", "transpose", ["zr"], ["TQ"], out=TQ[:, h, :], in_=zr[:, h, :], identity=ident[:, :])
            for h in range(2):
                p.I("tensor", "transpose", ["zr"], ["TK"], out=TK[:, h, :], in_=zr[:, 8 + h, :], identity=ident[:, :])
            p.I("scalar", "copy", [], ["TQ", "qT_all"], out=qT_all[:, :, t * 128:(t + 1) * 128], in_=TQ[:, :, :])
            p.I("vector", "tensor_copy", [], ["TK", "kT_all"], out=kT_all[:, :, t * 128:(t + 1) * 128], in_=TK[:, 0:2, :])
            p.I("scalar", "copy", [], ["KV", "v_all"], out=v_all[:, t, :], in_=KV[:, 256:512])
            p.I("scalar", "activation", [], ["G2", "gt_all"], out=gt_all[:, t, :], in_=G2[:, :], func=AF.Sigmoid)
        outs = [p.dma("sync", qT_d, qT_all[:], reads=["qT_all"]),
                p.dma("sync", kT_d, kT_all[:], reads=["kT_all"]),
                p.dma("sync", v_d, v_all[:], reads=["v_all"]),
                p.dma("sync", gt_d, gt_all[:], reads=["gt_all"])]
        p.emit(final_wait_ops=outs)
    return nc, prog


def block_masks_np():
    i = np.arange(128)
    b32 = (i[:, None] // 32) == (i[None, :] // 32)
    b64 = (i[:, None] // 64) == (i[None, :] // 64)
    lo = i[:, None] > i[None, :]
    up = i[:, None] < i[None, :]
    ms = [t & m for t in (lo, up) for m in (b32, b64 & ~b32, ~b64)]
    return np.ascontiguousarray(np.stack(ms, axis=1).astype(np.float32))


def rope_tables_np():
    t = np.arange(SEQ)
    row = (t // 64).astype(np.float32)
    col = (t % 64).astype(np.float32)
    inv = (np.float32(10000.0) ** (-(np.arange(32, dtype=np.float32) * np.float32(2.0) / np.float32(64)))).astype(np.float32)
    ar = row[:, None] * inv[None, :]
    ac = col[:, None] * inv[None, :]
    return np.concatenate([np.cos(ar), np.cos(ac), np.sin(ar), np.sin(ac)], axis=1).astype(np.float32)


def build_attn_core_program(nheads=8, nqb=4, nsp=32):
    nc = bass.Bass("TRN2", target_bir_lowering=False)
    qT_d = nc.dram_tensor("qT", [128, 8, TOK], BF16, kind="ExternalInput").ap()
    kT_d = nc.dram_tensor("kT", [128, 2, SEQ], BF16, kind="ExternalInput").ap()
    v_d = nc.dram_tensor("v", [128, 64, 256], BF16, kind="ExternalInput").ap()
    gt_d = nc.dram_tensor("gate", [128, 16, 1024], BF16, kind="ExternalInput").ap()
    x_d = nc.dram_tensor("x", [TOK, D_MODEL], F32, kind="ExternalInput").ap()
    wo_d = nc.dram_tensor("w_out", [D_MODEL, D_MODEL], F32, kind="ExternalInput").ap()
    qg_d = nc.dram_tensor("qg", [128, 128], F32, kind="ExternalInput").ap()
    kg_d = nc.dram_tensor("kg", [128, 128], F32, kind="ExternalInput").ap()
    out_d = nc.dram_tensor("out", [TOK, D_MODEL], F32, kind="ExternalOutput").ap()
    prog = Prog(nc)
    p = prog
    with contextlib.ExitStack() as st:
        c = Ctx(nc, prog, st)
        ident = make_ident(c)
        qT = c.sb("qT_sb", [128, 8, TOK], BF16)
        kT = c.sb("kT_sb", [128, 2, SEQ], BF16)
        va = c.sb("v_aug", [128, 64, 2, 129], BF16)
        gt = c.sb("gt_sb", [128, 16, 1024], BF16)
        wo = c.sb("wo_sb", [128, 8, D_MODEL], BF16)
        gq = c.sb("gq", [128, 2, 128], F32)
        m2 = c.sb("m2", [128, 2], F32)
        negb = c.sb("negb", [128, 1], F32)
        p.fence()
        p.dma("sync", gq[:, 0, :], qg_d, writes=["gq"])
        p.dma("sync", gq[:, 1, :], kg_d, writes=["gq"])
        for h in range(8):
            p.dma("sync", qT[:, h, :], qT_d[:, h, :], writes=["qT"])
        for h in range(2):
            for s4 in range(4):
                p.dma("sync", kT[:, h, s4 * 2048:(s4 + 1) * 2048], kT_d[:, h, s4 * 2048:(s4 + 1) * 2048], writes=["kT"])
        p.I("gpsimd", "memset", [], ["va"], va[:, :, :, 128:129], 1.0)
        for s4 in range(4):
            p.dma("sync", va[:, s4 * 16:(s4 + 1) * 16, :, 0:128],
                  v_d[:, s4 * 16:(s4 + 1) * 16, :].rearrange("p s (h d) -> p s h d", h=2), writes=["va"])
        for t4 in range(4):
            p.dma("sync", gt[:, t4 * 4:(t4 + 1) * 4, :], gt_d[:, t4 * 4:(t4 + 1) * 4, :], writes=["gt"])
        wo_v = wo_d.rearrange("(k p) c -> p k c", p=128)
        for k in range(0, 8, 2):
            p.dma("gpsimd", wo[:, k:k + 2, :], wo_v[:, k:k + 2, :], writes=["wo"])
        p.I("vector", "tensor_tensor", ["gq"], ["gq"], out=gq[:], in0=gq[:], in1=gq[:], op=ALU.mult)
        p.I("vector", "tensor_reduce", ["gq"], ["m2"], out=m2[:, :], in_=gq[:, :, :], axis=AX.X, op=ALU.max)
        p.I("vector", "tensor_tensor", ["m2"], ["negb"], out=negb[:, :], in0=m2[:, 0:1], in1=m2[:, 1:2], op=ALU.mult)
        p.I("scalar", "activation", ["negb"], ["negb"], out=negb[:, :], in_=negb[:, :], func=AF.Sqrt, scale=128.0)
        p.I("scalar", "mul", ["negb"], ["negb"], out=negb[:, :], in_=negb[:, :], mul=-1.0)

        SC = [c.ps("SC%d" % i, [128, 1024], F32) for i in range(2)]
        O = [c.ps("O%d" % i, [128, 512], F32) for i in range(4)]
        NP = 3
        pT = [c.sb("pT%d" % i, [128, 1024], BF16) for i in range(NP)]
        rinv = c.sb("rinv", [128, 4], F32)
        step = 0
        for h in range(nheads):
            kv = h // 4
            for qb in range(nqb):
                def qk(sp, st_):
                    b = st_ % 2
                    for cc in range(2):
                        s = 2 * sp + cc
                        p.I("tensor", "matmul", ["kT", "qT"], ["SC%d" % b], SC[b][:, cc * 512:(cc + 1) * 512],
                            lhsT=kT[:, kv, s * 128:(s + 1) * 128], rhs=qT[:, h, qb * 512:(qb + 1) * 512],
                            start=True, stop=True)

                def ex(sp, st_):
                    b = st_ % 2
                    pb = st_ % NP
                    p.I("scalar", "activation", ["negb"], ["SC%d" % b, "pT%d" % pb], out=pT[pb][:, :], in_=SC[b][:, :],
                        func=AF.Exp, bias=negb[:, :])

                def pv(sp, st_):
                    pb = st_ % NP
                    for cc in range(2):
                        s = 2 * sp + cc
                        for qs in range(4):
                            p.I("tensor", "matmul", ["pT%d" % pb, "va"], ["O%d" % qs], O[qs][:, 0:129],
                                lhsT=pT[pb][:, cc * 512 + qs * 128:cc * 512 + (qs + 1) * 128], rhs=va[:, s, kv, :],
                                start=(sp == 0 and cc == 0), stop=(sp == nsp - 1 and cc == 1))

                qk(0, step)
                for sp in range(nsp):
                    if sp + 1 < nsp:
                        qk(sp + 1, step + 1)
                    ex(sp, step)
                    pv(sp, step)
                    step += 1
                for qs in range(4):
                    tile = qb * 4 + qs
                    p.I("vector", "reciprocal", [], ["O%d" % qs, "rinv"], out=rinv[:, qs:qs + 1], in_=O[qs][:, 128:129])
                    p.I("vector", "scalar_tensor_tensor", ["rinv"], ["O%d" % qs, "gt"],
                        out=gt[:, tile, h * 128:(h + 1) * 128], in0=O[qs][:, 0:128], scalar=rinv[:, qs:qs + 1],
                        in1=gt[:, tile, h * 128:(h + 1) * 128], op0=ALU.mult, op1=ALU.mult)
        xt = [c.sb("xt%d" % i, [128, D_MODEL], F32) for i in range(2)]
        ho = [c.sb("ho%d" % i, [128, D_MODEL], F32) for i in range(2)]
        ogT = [c.sb("ogT%d" % i, [128, 8, 128], BF16) for i in range(2)]
        TP = O[0].bitcast(BF16) if hasattr(O[0], "bitcast") else None
        outs = []
        for t in range(16):
            b = t % 2
            p.dma("sync", xt[b][:], x_d[t * 128:(t + 1) * 128, :], writes=["xt%d" % b])
            for k in range(8):
                p.I("tensor", "transpose", ["gt"], ["O0"], out=TP[:, k * 128:(k + 1) * 128],
                    in_=gt[:, t, k * 128:(k + 1) * 128], identity=ident[:, :])
            p.I("vector", "tensor_copy", [], ["O0", "ogT%d" % b], out=ogT[b][:, :, :],
                in_=TP[:, :].rearrange("p (k t) -> p k t", k=8))
            for hf in range(2):
                for k in range(8):
                    p.I("tensor", "matmul", ["ogT%d" % b, "wo"], ["SC0"], SC[0][:, hf * 512:(hf + 1) * 512],
                        lhsT=ogT[b][:, k, :], rhs=wo[:, k, hf * 512:(hf + 1) * 512], start=(k == 0), stop=(k == 7))
            p.I("vector", "tensor_tensor", ["xt%d" % b], ["SC0", "ho%d" % b], out=ho[b][:, :], in0=SC[0][:, :],
                in1=xt[b][:, :], op=ALU.add)
            outs.append(p.dma("sync", out_d[t * 128:(t + 1) * 128, :], ho[b][:, :], reads=["ho%d" % b]))
        p.emit(final_wait_ops=outs)
    return nc, prog


GDN_IN = 6208


def build_gdn_proj_program():
    nc = bass.Bass("TRN2", target_bir_lowering=False)
    HT = TOK + 4
    h_d = nc.dram_tensor("h", [HT, D_MODEL], F32, kind="ExternalInput").ap()
    g_d = nc.dram_tensor("g_mix", [128, D_MODEL], F32, kind="ExternalInput").ap()
    w_d = nc.dram_tensor("w_in", [D_MODEL, GDN_IN], F32, kind="ExternalInput").ap()
    cw_d = nc.dram_tensor("cw", [128, 32, 6], F32, kind="ExternalInput").ap()
    ad_d = nc.dram_tensor("ad", [128, 2, 32], F32, kind="ExternalInput").ap()
    qT_d = nc.dram_tensor("qT", [128, 8, TOK], BF16, kind="ExternalOutput").ap()
    kT_d = nc.dram_tensor("kT", [128, 8, TOK], BF16, kind="ExternalOutput").ap()
    ktm_d = nc.dram_tensor("k_tm", [TOK, 1024], BF16, kind="ExternalOutput").ap()
    vtm_d = nc.dram_tensor("v_tm", [TOK, 2048], BF16, kind="ExternalOutput").ap()
    z_d = nc.dram_tensor("z", [TOK, 2048], BF16, kind="ExternalOutput").ap()
    gb_d = nc.dram_tensor("gb", [TOK, 64], F32, kind="ExternalOutput").ap()
    prog = Prog(nc)
    p = prog
    outs = []
    with contextlib.ExitStack() as st:
        c = Ctx(nc, prog, st)
        ident = make_ident(c)
        ones = c.sb("ones", [128, 128], F32)
        p.I("gpsimd", "memset", [], ["ones"], ones[:], 1.0)
        g_sb = c.sb("g_sb", [128, D_MODEL], F32)
        cw = c.sb("cw_sb", [128, 32, 6], F32)
        ad = c.sb("ad_sb", [128, 2, 32], F32)
        eps = c.sb("eps", [128, 1], F32)
        one1 = c.sb("one1", [128, 1], F32)
        p.I("vector", "memset", [], ["eps"], eps[:], EPS)
        p.I("vector", "memset", [], ["one1"], one1[:], 1.0)
        p.fence()
        p.dma("sync", g_sb[:], g_d, writes=["g_sb"])
        p.dma("sync", cw[:], cw_d, writes=["cw"])
        p.dma("sync", ad[:], ad_d, writes=["ad"])
        p.I("scalar", "activation", ["ad"], ["ad"], out=ad[:, 0, :], in_=ad[:, 0, :], func=AF.Exp)
        p.I("scalar", "mul", ["ad"], ["ad"], out=ad[:, 0, :], in_=ad[:, 0, :], mul=-1.0)
        wv = w_d.rearrange("(k p) c -> p k c", p=128)
        wb = [c.sb("wb%d" % i, [128, 8, 1024], BF16) for i in range(2)]
        yT = c.sb("yT_all", [128, 8, HT], BF16)
        xt = [c.sb("xt%d" % i, [128, D_MODEL], F32) for i in range(2)]
        sq = c.sb("sq", [128, D_MODEL], F32)
        ss = c.sb("ss", [128, 1], F32)
        rs = c.sb("rs", [128, 1], F32)
        yb = c.sb("yb", [128, D_MODEL], BF16)
        pT = c.ps("pT", [128, 8, 128], BF16)
        U = [c.ps("U%d" % i, [128, 512], F32) for i in range(3)]
        L = c.ps("L", [128, 512], F32)
        TT = c.ps("TT", [128, 8, 128], BF16)
        Z = [c.ps("Z%d" % i, [128, 512], F32) for i in range(2)]
        nbufs = dict(sq=sq, ss=ss, rs=rs, yb=yb, pT=pT, ident=ident, eps=eps, dst_name="yT",
                     names=dict(sq="sq", ss="ss", rs="rs", yb="yb", pT="pT"))
        r = 0
        xi = 0
        while r < HT:
            rn = min(128, HT - r)
            b = xi % 2
            xi += 1
            p.dma("sync", xt[b][0:rn, :], h_d[r:r + rn, :], writes=["xt%d" % b])
            emit_rmsnorm_T(c, xt[b][0:rn, :], rn, g_sb, yT, r, "xt%d" % b, nbufs)
            r += rn
        NR = 3
        tAs = [c.sb("tA%d" % i, [128, 508], F32) for i in range(NR)]
        tBs = [c.sb("tB%d" % i, [128, 508], F32) for i in range(NR)]
        acts = [c.sb("act%d" % i, [128, 508], F32) for i in range(NR)]
        sqvs = [c.sb("sqv%d" % i, [128, 508], F32) for i in range(NR)]
        rts = [c.sb("rt%d" % i, [128, 508], F32) for i in range(NR)]
        fm = [c.sb("fm%d" % i, [128, 8, 508], BF16) for i in range(2)]
        tm = [c.sb("tm%d" % i, [128, 1024], BF16) for i in range(2)]
        blocks = []
        t0 = 0
        while t0 < TOK:
            n = min(508, TOK - t0)
            blocks.append((t0, n))
            t0 += n
        fi = 0
        ti = 0
        ui = 0
        for grp in range(4):
            wbuf = wb[grp % 2]
            wn = "wb%d" % (grp % 2)
            for k in range(0, 8, 2):
                p.dma("gpsimd", wbuf[:, k:k + 2, :], wv[:, k:k + 2, grp * 1024:(grp + 1) * 1024], writes=[wn])
            for (t0, n) in blocks:
                ncol = n + 4
                fmb = fm[fi % 2]
                fn_ = "fm%d" % (fi % 2)
                fi += 1
                for j in range(8):
                    cj = grp * 8 + j
                    Uj = U[ui % 3]
                    un = "U%d" % (ui % 3)
                    rr = ui % NR
                    tA, tB, act, sqv, rt = tAs[rr], tBs[rr], acts[rr], sqvs[rr], rts[rr]
                    nA, nB, nact, nsqv, nrt = "tA%d" % rr, "tB%d" % rr, "act%d" % rr, "sqv%d" % rr, "rt%d" % rr
                    ui += 1
                    for k in range(8):
                        p.I("tensor", "matmul", [wn, "yT"], [un], Uj[:, 0:ncol], lhsT=wbuf[:, k, j * 128:(j + 1) * 128],
                            rhs=yT[:, k, t0:t0 + ncol], start=(k == 0), stop=(k == 7))
                    p.I("scalar", "activation", ["cw"], [un, nA], out=tA[:, 0:n], in_=Uj[:, 2:2 + n], func=AF.Identity,
                        scale=cw[:, cj, 2:3], bias=cw[:, cj, 5:6])
                    src, dst = tA, tB
                    sn_, dn_ = nA, nB
                    for tap in (0, 1, 3, 4):
                        p.I("vector", "scalar_tensor_tensor", ["cw", sn_], [un, dn_], out=dst[:, 0:n],
                            in0=Uj[:, tap:tap + n], scalar=cw[:, cj, tap:tap + 1], in1=src[:, 0:n],
                            op0=ALU.mult, op1=ALU.add)
                        src, dst = dst, src
                        sn_, dn_ = dn_, sn_
                    if grp < 2:
                        p.I("scalar", "activation", [nA], [nact], out=act[:, 0:n], in_=tA[:, 0:n], func=AF.Silu)
                        p.I("gpsimd", "tensor_tensor", [nact], [nsqv], out=sqv[:, 0:n], in0=act[:, 0:n], in1=act[:, 0:n], op=ALU.mult)
                        p.I("tensor", "matmul", ["ones", nsqv], ["L"], L[:, 0:n], lhsT=ones[:, :], rhs=sqv[:, 0:n],
                            start=True, stop=True)
                        p.I("scalar", "activation", ["eps"], ["L", nrt], out=rt[:, 0:n], in_=L[:, 0:n], func=AF.Sqrt,
                            bias=eps[:, :])
                        p.I("vector", "reciprocal", [nrt], [nrt], out=rt[:, 0:n], in_=rt[:, 0:n])
                        p.I("gpsimd", "scalar_tensor_tensor" if False else "tensor_tensor", [nact, nrt], [nsqv], out=sqv[:, 0:n],
                            in0=act[:, 0:n], in1=rt[:, 0:n], op=ALU.mult)
                        p.I("scalar", "mul", [nsqv], [fn_], out=fmb[:, j, 0:n], in_=sqv[:, 0:n],
                            mul=(128.0 ** -0.5 if grp == 0 else 1.0))
                    else:
                        p.I("scalar", "activation", [nA], [fn_], out=fmb[:, j, 0:n], in_=tA[:, 0:n], func=AF.Silu)
                if grp == 0:
                    outs.append(p.dma("sync", qT_d[:, :, t0:t0 + n], fmb[:, :, 0:n], reads=[fn_]))
                if grp == 1:
                    outs.append(p.dma("sync", kT_d[:, :, t0:t0 + n], fmb[:, :, 0:n], reads=[fn_]))
                if grp >= 1:
                    s0 = 0
                    while s0 < n:
                        sn = min(128, n - s0)
                        tmb = tm[ti % 2]
                        tn = "tm%d" % (ti % 2)
                        ti += 1
                        for j in range(8):
                            p.I("tensor", "transpose", [fn_], ["TT"], out=TT[0:sn, j, :], in_=fmb[:, j, s0:s0 + sn],
                                identity=ident[:, :])
                        p.I("vector", "tensor_copy", [], ["TT", tn], out=tmb[0:sn, :],
                            in_=TT[0:sn, :, :].rearrange("p j d -> p (j d)"))
                        if grp == 1:
                            dst_ap = ktm_d[t0 + s0:t0 + s0 + sn, :]
                        else:
                            dst_ap = vtm_d[t0 + s0:t0 + s0 + sn, (grp - 2) * 1024:(grp - 1) * 1024]
                        outs.append(p.dma("sync", dst_ap, tmb[0:sn, :], reads=[tn]))
                        s0 += sn
        wz = [c.sb("wz%d" % i, [128, 8, 1024], BF16) for i in range(2)]
        zs = [c.sb("zs%d" % i, [128, 1024], BF16) for i in range(2)]
        for half in range(2):
            wbuf = wz[half % 2]
            wn = "wz%d" % (half % 2)
            for k in range(0, 8, 2):
                p.dma("gpsimd", wbuf[:, k:k + 2, :], wv[:, k:k + 2, 4096 + half * 1024:4096 + (half + 1) * 1024], writes=[wn])
            for t in range(16):
                zb = zs[t % 2]
                zn = "zs%d" % (t % 2)
                for hf in range(2):
                    for k in range(8):
                        p.I("tensor", "matmul", [wn, "yT"], ["Z%d" % hf], Z[hf][:, :], lhsT=yT[:, k, 2 + t * 128:2 + (t + 1) * 128],
                            rhs=wbuf[:, k, hf * 512:(hf + 1) * 512], start=(k == 0), stop=(k == 7))
                    p.I("scalar", "activation", [], ["Z%d" % hf, zn], out=zb[:, hf * 512:(hf + 1) * 512], in_=Z[hf][:, :],
                        func=AF.Silu)
                outs.append(p.dma("sync", z_d[t * 128:(t + 1) * 128, half * 1024:(half + 1) * 1024], zb[:, :], reads=[zn]))
        wab = c.sb("wab", [128, 8, 64], BF16)
        p.dma("gpsimd", wab[:, :, :], wv[:, :, 6144:6208], writes=["wab"])
        xs = c.sb("xs", [128, 32], F32)
        ax = c.sb("ax", [128, 32], F32)
        gbs = [c.sb("gbs%d" % i, [128, 64], F32) for i in range(2)]
        for t in range(16):
            gbt = gbs[t % 2]
            gn = "gbs%d" % (t % 2)
            for k in range(8):
                p.I("tensor", "matmul", ["wab", "yT"], ["Z0"], Z[0][:, 0:64], lhsT=yT[:, k, 2 + t * 128:2 + (t + 1) * 128],
                    rhs=wab[:, k, :], start=(k == 0), stop=(k == 7))
            Zv = Z[0][:, 0:64].rearrange("p (d a h) -> p d a h", d=2, a=2)
            p.I("vector", "tensor_tensor", ["ad"], ["Z0", "xs"], out=xs[:, :].rearrange("p (d h) -> p d h", d=2),
                in0=Zv[:, :, 0, :], in1=ad[:, 1, :].rearrange("p (d h) -> p d h", d=2), op=ALU.add)
            p.I("scalar", "activation", [], ["Z0", gn], out=gbt[:, 32:64].rearrange("p (d h) -> p d h", d=2),
                in_=Zv[:, :, 1, :], func=AF.Sigmoid)
            p.I("scalar", "activation", ["xs"], ["ax"], out=ax[:, :], in_=xs[:, :], func=AF.Abs)
            p.I("scalar", "activation", ["ax"], ["ax"], out=ax[:, :], in_=ax[:, :], func=AF.Exp, scale=-1.0)
            p.I("scalar", "activation", ["ax", "one1"], ["ax"], out=ax[:, :], in_=ax[:, :], func=AF.Ln, bias=one1[:, :])
            p.I("vector", "scalar_tensor_tensor", ["xs", "ax"], ["xs"], out=xs[:, :], in0=xs[:, :], scalar=0.0,
                in1=ax[:, :], op0=ALU.max, op1=ALU.add)
            p.I("vector", "tensor_tensor", ["xs", "ad"], [gn], out=gbt[:, 0:32], in0=xs[:, :], in1=ad[:, 0, :], op=ALU.mult)
            outs.append(p.dma("sync", gb_d[t * 128:(t + 1) * 128, :], gbt[:, :], reads=[gn]))
        p.emit(final_wait_ops=outs)
    return nc, prog


def build_gdn_scan_program(nchunks=64):
    nc = bass.Bass("TRN2", target_bir_lowering=False)
    S_ = nchunks * 128
    qT_d = nc.dram_tensor("qT", [128, 2, S_], BF16, kind="ExternalInput").ap()
    kT_d = nc.dram_tensor("kT", [128, 2, S_], BF16, kind="ExternalInput").ap()
    ktm_d = nc.dram_tensor("k_tm", [S_, 256], BF16, kind="ExternalInput").ap()
    vtm_d = nc.dram_tensor("v_tm", [S_, 512], BF16, kind="ExternalInput").ap()
    z_d = nc.dram_tensor("z", [S_, 512], BF16, kind="ExternalInput").ap()
    gb_d = nc.dram_tensor("gb", [S_, 16], F32, kind="ExternalInput").ap()
    on_d = nc.dram_tensor("on", [128, 128], F32, kind="ExternalInput").ap()
    bm_d = nc.dram_tensor("bm", [128, 6, 128], F32, kind="ExternalInput").ap()
    og_d = nc.dram_tensor("og", [S_, 512], BF16, kind="ExternalOutput").ap()
    ost_d = nc.dram_tensor("ost", [S_, 512], F32, kind="Internal").ap()
    prog = Prog(nc)
    p = prog
    outs = []
    with contextlib.ExitStack() as st:
        c = Ctx(nc, prog, st)
        ident = make_ident(c)
        ones = c.sb("ones", [128, 128], F32)
        p.I("gpsimd", "memset", [], ["ones"], ones[:], 1.0)
        masks = {}
        for nm_, cmp, sg in (("LE", ALU.is_ge, -1), ("GT", ALU.is_gt, 1), ("GE", ALU.is_ge, 1), ("LT", ALU.is_gt, -1)):
            m = c.sb("m" + nm_, [128, 128], F32)
            p.I("gpsimd", "memset", [], ["m" + nm_], m[:], 1.0)
            p.I("gpsimd", "affine_select", ["m" + nm_], ["m" + nm_], out=m[:], in_=m[:], pattern=[[-sg, 128]],
                compare_op=cmp, fill=0.0, base=0, channel_multiplier=sg)
            masks[nm_] = m
        on = c.sb("on_sb", [128, 128], F32)
        eps = c.sb("eps", [128, 1], F32)
        p.I("vector", "memset", [], ["eps"], eps[:], EPS)
        p.fence()
        p.dma("sync", on[:], on_d, writes=["on"])
        bm = c.sb("bm_sb", [128, 6, 128], F32)
        p.dma("sync", bm[:], bm_d, writes=["bm"])
        B = [c.ps("B%d" % i, [128, 512], F32) for i in range(8)]

        def bk(i):
            return B[i][:, :].rearrange("p (h d) -> p h d", h=4)

        D = {}
        for d in range(2):
            for par in range(2):
                dd = {}
                sfx = "_%d%d" % (d, par)
                for nm_, shp, dt_ in (("qTc", [128, 2, 128], BF16), ("kTc", [128, 2, 128], BF16), ("ktm", [128, 256], BF16),
                                      ("vtm", [128, 512], BF16), ("zc", [128, 512], BF16), ("gb", [128, 16], F32),
                                      ("X0", [128, 4, 128], BF16), ("X1", [128, 4, 128], BF16),
                                      ("Y0", [128, 4, 128], BF16), ("Y1", [128, 4, 128], BF16),
                                      ("P0", [128, 4, 128], BF16), ("P1", [128, 4, 128], BF16),
                                      ("rhsD", [128, 4, 128], F32), ("E", [128, 4, 128], F32), ("Es", [128, 4, 128], F32),
                                      ("Ei", [128, 4, 128], F32), ("KQ", [128, 4, 128], F32), ("attn", [128, 4, 128], BF16),
                                      ("XA", [128, 8, 128], BF16), ("kbg", [128, 4, 128], BF16), ("vb", [128, 4, 128], BF16),
                                      ("kdec", [128, 4, 128], BF16), ("nwT", [128, 4, 128], BF16),
                                      ("gs", [128, 8], F32), ("ex", [128, 12], F32), ("nbe", [128, 4], F32),
                                      ("sso", [128, 4], F32), ("ol", [128, 4, 128], F32),
                                      ("Pf0", [128, 4, 128], F32), ("Pf1", [128, 4, 128], F32),
                                      ("N1", [128, 4, 128], BF16), ("N2", [128, 4, 128], BF16), ("N1T", [128, 4, 128], BF16),
                                      ("WP", [128, 4, 128], BF16), ("WQ", [128, 4, 128], BF16),
                                      ("Q0", [128, 4, 128], BF16), ("Q1", [128, 4, 128], BF16)):
                    dd[nm_] = c.sb(nm_ + sfx, shp, dt_)
                dd["tq"], dd["od"], dd["sqo"] = dd["rhsD"], dd["E"], dd["Es"]
                dd["vn"], dd["ogc"] = dd["attn"], dd["kbg"]
                dd["_alias"] = dict(tq="rhsD", od="E", sqo="Es", vn="attn", ogc="kbg")
                D[d, par] = dd
        S32, Sbf = {}, {}
        for d in range(2):
            S32[d] = c.sb("S32_%d" % d, [128, 4, 128], F32)
            Sbf[d] = c.sb("Sbf_%d" % d, [128, 4, 128], BF16)
            p.I("vector", "memset", [], ["S32_%d" % d], S32[d][:], 0.0)
            p.I("vector", "memset", [], ["Sbf_%d" % d], Sbf[d][:], 0.0)

        def b4(ap2):
            return ap2.unsqueeze(2).broadcast_to([128, 4, 128])

        def bh(ap2):
            return ap2.unsqueeze(1).broadcast_to([128, 4, 128])

        def rep2(ap3):
            return ap3.unsqueeze(2).broadcast_to([128, 2, 2, 128])

        def v4(ap3):
            return ap3.rearrange("p (q r) d -> p q r d", q=2)

        def pre(d, i, par, second):
            dd = D[d, par]
            sfx = "_%d%d" % (d, par)
            al = dd["_alias"]
            N = lambda x: al.get(x, x) + sfx
            b0, b1, b2 = 3 * d, 3 * d + 1, 3 * d + 2
            n0, n1, n2 = "B%d" % b0, "B%d" % b1, "B%d" % b2
            r0 = i * 128
            qTc, kTc, ktm, vtm, zc, gb = (dd[k] for k in ("qTc", "kTc", "ktm", "vtm", "zc", "gb"))
            p.dma("sync", qTc[:], qT_d[:, :, r0:r0 + 128], writes=[N("qTc")])
            p.dma("sync", kTc[:], kT_d[:, :, r0:r0 + 128], writes=[N("kTc")])
            p.dma("sync", ktm[:], ktm_d[r0:r0 + 128, :], writes=[N("ktm")])
            p.dma("sync", vtm[:], vtm_d[r0:r0 + 128, :], writes=[N("vtm")])
            p.dma("sync", zc[:], z_d[r0:r0 + 128, :], writes=[N("zc")])
            p.dma("sync", gb[:], gb_d[r0:r0 + 128, :], writes=[N("gb")])
            Mm = masks["LE"] if d == 0 else masks["GE"]
            Vm = masks["GT"] if d == 0 else masks["LT"]
            mS = masks["GT"] if d == 0 else masks["LT"]
            mI = masks["GE"] if d == 0 else masks["LE"]
            g_ = gb[:, d * 4:(d + 1) * 4]
            be = gb[:, 8 + d * 4:8 + (d + 1) * 4]
            gs, ex, nbe = dd["gs"], dd["ex"], dd["nbe"]
            for q in range(2):
                p.I("tensor", "matmul", [N("kTc")], [n0], bk(b0)[:, q, :], lhsT=kTc[:, q, :], rhs=kTc[:, q, :],
                    start=True, stop=True)
                p.I("tensor", "matmul", [N("kTc"), N("qTc")], [n0], bk(b0)[:, 2 + q, :], lhsT=qTc[:, q, :],
                    rhs=kTc[:, q, :], start=True, stop=True)
            p.I("scalar", "copy", [], [n0, N("KQ")], out=dd["KQ"][:], in_=bk(b0))
            p.I("tensor", "matmul", [N("gb")], [n1], B[b1][:, 0:4], lhsT=Mm[:, :], rhs=g_, start=True, stop=True)
            p.I("tensor", "matmul", [N("gb"), "ones"], [n1], B[b1][:, 4:8], lhsT=ones[:, :], rhs=g_, start=True, stop=True)
            p.I("vector", "tensor_copy", [], [n1, N("gs")], out=gs[:, :], in_=B[b1][:, 0:8])
            p.I("scalar", "activation", [N("gs")], [N("ex")], out=ex[:, 0:4], in_=gs[:, 0:4], func=AF.Exp)
            p.I("vector", "tensor_tensor", [N("gs")], [N("gs")], out=gs[:, 0:4], in0=gs[:, 4:8], in1=gs[:, 0:4], op=ALU.subtract)
            p.I("scalar", "activation", [N("gs")], [N("ex")], out=ex[:, 4:12], in_=gs[:, 0:8], func=AF.Exp)
            p.I("vector", "tensor_scalar", [N("gb")], [N("nbe")], out=nbe[:, :], in0=be, scalar1=-1.0, scalar2=None, op0=ALU.mult)
            p.I("vector", "tensor_tensor", [N("nbe"), N("ex")], [N("sso")], out=dd["sso"][:, :], in0=nbe[:, :], in1=ex[:, 0:4], op=ALU.mult)
            k3 = ktm[:, :].rearrange("p (q d) -> p q d", q=2)
            for h in range(4):
                p.I("scalar", "activation", [N("ktm"), N("ex")], [N("kdec")], out=dd["kdec"][:, h, :], in_=k3[:, h // 2, :],
                    func=AF.Identity, scale=ex[:, 4 + h:5 + h])
                p.I("scalar", "activation", [N("vtm"), N("gb")], [N("vb")], out=dd["vb"][:, h, :], in_=vtm[:, h * 128:(h + 1) * 128],
                    func=AF.Identity, scale=be[:, h:h + 1])
            p.I("gpsimd", "tensor_tensor", [N("gb")], [N("rhsD")], out=dd["rhsD"][:], in0=bh(Vm[:, :]), in1=b4(g_), op=ALU.mult)
            p.I("tensor", "matmul", [N("rhsD")], [n2], B[b2][:, :], lhsT=Mm[:, :], rhs=dd["rhsD"][:].rearrange("p h s -> p (h s)"),
                start=True, stop=True)
            p.I("scalar", "activation", [], [n2, N("E")], out=dd["E"][:], in_=bk(b2), func=AF.Exp)
            p.I("gpsimd", "tensor_tensor", [N("E")], [N("Ei")], out=dd["Ei"][:], in0=dd["E"][:], in1=bh(mI[:, :]), op=ALU.mult)
            p.I("vector", "tensor_tensor", [N("E"), N("nbe")], [N("Es")], out=dd["Es"][:], in0=dd["E"][:], in1=b4(nbe[:, :]), op=ALU.mult)
            Y0 = dd["Y0"]
            p.I("vector", "tensor_tensor", [N("Es"), N("KQ")], [N("Es")], out=v4(dd["Es"][:]), in0=v4(dd["Es"][:]),
                in1=rep2(dd["KQ"][:, 0:2, :]), op=ALU.mult)
            p.I("gpsimd", "tensor_tensor", [N("Es"), "bm"], [N("Y0")], out=Y0[:], in0=dd["Es"][:], in1=bh(bm[:, 3 * d + 0, :]), op=ALU.mult)
            p.I("gpsimd", "tensor_tensor", [N("Es"), "bm"], [N("N1")], out=dd["N1"][:], in0=dd["Es"][:], in1=bh(bm[:, 3 * d + 1, :]), op=ALU.mult)
            p.I("vector", "tensor_tensor", [N("Es"), "bm"], [N("N2")], out=dd["N2"][:], in0=dd["Es"][:], in1=bh(bm[:, 3 * d + 2, :]), op=ALU.mult)
            p.I("gpsimd", "tensor_tensor", [N("Ei"), N("KQ")], [N("attn")], out=v4(dd["attn"][:]), in0=v4(dd["Ei"][:]),
                in1=rep2(dd["KQ"][:, 2:4, :]), op=ALU.mult)
            TRb = B[b0].bitcast(BF16)
            TRb1 = B[b1].bitcast(BF16)
            for h in range(4):
                p.I("tensor", "transpose", [N("Y0")], [n0], out=TRb[:, h * 128:(h + 1) * 128], in_=Y0[:, h, :],
                    identity=ident[:, :])
            for h in range(4):
                p.I("tensor", "transpose", [N("attn")], [n0], out=TRb[:, (4 + h) * 128:(5 + h) * 128], in_=dd["attn"][:, h, :],
                    identity=ident[:, :])
            for h in range(4):
                p.I("tensor", "transpose", [N("N1")], [n1], out=TRb1[:, h * 128:(h + 1) * 128], in_=dd["N1"][:, h, :],
                    identity=ident[:, :])
            XA = dd["XA"]
            p.I("scalar", "copy", [], [n0, N("XA")], out=XA[:], in_=TRb[:, :].rearrange("p (h s) -> p h s", h=8))
            p.I("scalar", "copy", [], [n1, N("N1T")], out=dd["N1T"][:],
                in_=TRb1[:, 0:512].rearrange("p (h s) -> p h s", h=4))
            P1 = dd["P0"]
            p.I("vector", "tensor_tensor", [N("XA")], [N("Pf0")], out=dd["Pf0"][:], in0=XA[:, 0:4, :], in1=bh(ident[:, :]), op=ALU.add)
            p.I("gpsimd", "tensor_copy", [N("Pf0")], [N("P0")], out=P1[:], in_=dd["Pf0"][:])
            Pf, Pfn = dd["Pf0"], N("Pf0")
            Xc, Xn_ = XA[:, 0:4, :], N("XA")
            Yc, Yn_ = Y0, N("Y0")
            Pc, Pn_ = P1, N("P0")
            for it in range(5):
                nb = (it + 1) % 2
                doP = it >= 1
                doX = it <= 2
                doY = it <= 3
                if doP:
                    for h in range(4):
                        p.I("tensor", "matmul", [Yn_, Pn_], [n2], bk(b2)[:, h, :], lhsT=Yc[:, h, :], rhs=Pc[:, h, :], start=True, stop=True)
                if doX:
                    for h in range(4):
                        p.I("tensor", "matmul", [Yn_, Xn_], [n0], bk(b0)[:, h, :], lhsT=Yc[:, h, :], rhs=Xc[:, h, :], start=True, stop=True)
                if doY:
                    for h in range(4):
                        p.I("tensor", "matmul", [Xn_, Yn_], [n1], bk(b1)[:, h, :], lhsT=Xc[:, h, :], rhs=Yc[:, h, :], start=True, stop=True)
                if doP:
                    Pnew, Pnn = dd["P%d" % nb], N("P%d" % nb)
                    Pfnew, Pfnn = dd["Pf%d" % nb], N("Pf%d" % nb)
                    p.I("vector", "tensor_tensor", [Pfn], [n2, Pfnn], out=Pfnew[:], in0=bk(b2), in1=Pf[:], op=ALU.add)
                    p.I("scalar", "copy", [Pfnn], [Pnn], out=Pnew[:], in_=Pfnew[:])
                    Pf, Pfn = Pfnew, Pfnn
                    Pc, Pn_ = Pnew, Pnn
                if doX:
                    Xnew, Xnn = dd["X%d" % nb], N("X%d" % nb)
                    p.I("scalar", "copy", [], [n0, Xnn], out=Xnew[:], in_=bk(b0))
                if doY:
                    Ynew, Ynn = dd["Y%d" % nb], N("Y%d" % nb)
                    if it % 2 == 0:
                        p.I("scalar", "copy", [], [n1, Ynn], out=Ynew[:], in_=bk(b1))
                    else:
                        p.I("vector", "tensor_copy", [], [n1, Ynn], out=Ynew[:], in_=bk(b1))
                if doX:
                    Xc, Xn_ = Xnew[:], Xnn
                if doY:
                    Yc, Yn_ = Ynew, Ynn
            for h in range(4):
                p.I("tensor", "transpose", [Pn_], [n0], out=TRb[:, h * 128:(h + 1) * 128], in_=Pc[:, h, :], identity=ident[:, :])
            Q0, Q1 = dd["Q0"], dd["Q1"]
            p.I("scalar", "copy", [], [n0, N("Q0")], out=Q0[:], in_=TRb[:, 0:512].rearrange("p (h s) -> p h s", h=4))
            for h in range(4):
                p.I("tensor", "matmul", [N("N1"), Pn_], [n1], bk(b1)[:, h, :], lhsT=dd["N1"][:, h, :], rhs=Pc[:, h, :], start=True, stop=True)
            for h in range(4):
                p.I("tensor", "matmul", [N("N1T"), N("Q0")], [n2], bk(b2)[:, h, :], lhsT=dd["N1T"][:, h, :], rhs=Q0[:, h, :], start=True, stop=True)
            p.I("scalar", "copy", [], [n1, N("WP")], out=dd["WP"][:], in_=bk(b1))
            p.I("vector", "tensor_copy", [], [n2, N("WQ")], out=dd["WQ"][:], in_=bk(b2))
            for h in range(4):
                p.I("tensor", "matmul", [N("Q0"), N("WP")], [n0], bk(b0)[:, h, :], lhsT=Q0[:, h, :], rhs=dd["WP"][:, h, :], start=True, stop=True)
            for h in range(4):
                p.I("tensor", "matmul", [Pn_, N("WQ")], [n1], bk(b1)[:, h, :], lhsT=Pc[:, h, :], rhs=dd["WQ"][:, h, :], start=True, stop=True)
            nbp = 0 if Pc is dd["P1"] else 1
            Pnew, Pnn = dd["P%d" % nbp], N("P%d" % nbp)
            Pfnew, Pfnn = dd["Pf%d" % nbp], N("Pf%d" % nbp)
            p.I("vector", "tensor_tensor", [Pfn], [n0, Pfnn], out=Pfnew[:], in0=bk(b0), in1=Pf[:], op=ALU.add)
            p.I("scalar", "copy", [Pfnn], [Pnn], out=Pnew[:], in_=Pfnew[:])
            p.I("vector", "tensor_tensor", [N("Q0")], [n1, N("Q1")], out=Q1[:], in0=bk(b1), in1=Q0[:], op=ALU.add)
            Pf, Pfn, Pc, Pn_ = Pfnew, Pfnn, Pnew, Pnn
            for h in range(4):
                p.I("tensor", "matmul", [N("N2"), Pn_], [n2], bk(b2)[:, h, :], lhsT=dd["N2"][:, h, :], rhs=Pc[:, h, :], start=True, stop=True)
            p.I("scalar", "copy", [], [n2, N("WP")], out=dd["WP"][:], in_=bk(b2))
            for h in range(4):
                p.I("tensor", "matmul", [N("Q1"), N("WP")], [n0], bk(b0)[:, h, :], lhsT=Q1[:, h, :], rhs=dd["WP"][:, h, :], start=True, stop=True)
            nbp = 1 - nbp
            Pnew, Pnn = dd["P%d" % nbp], N("P%d" % nbp)
            p.I("vector", "tensor_tensor", [Pfn], [n0, Pnn], out=Pnew[:], in0=bk(b0), in1=Pf[:], op=ALU.add)
            Pc, Pn_ = Pnew, Pnn
            return Pc, Pn_

        def scan(d, i, par, second, Pc, Pn_):
            dd = D[d, par]
            sfx = "_%d%d" % (d, par)
            al = dd["_alias"]
            N = lambda x: al.get(x, x) + sfx
            Sn, Sb = "S32_%d" % d, "Sbf_%d" % d
            XA, qTc, zc, ex = dd["XA"], dd["qTc"], dd["zc"], dd["ex"]
            if second:
                p.dma("sync", dd["ol"][:], ost_d[i * 128:(i + 1) * 128, :].rearrange("p (h d) -> p h d", h=4),
                      reads=["ost%d" % i], writes=[N("ol")])
            kTc = dd["kTc"]
            for h in range(4):
                p.I("tensor", "matmul", [N("kTc"), Sb], ["B6"], bk(6)[:, h, :], lhsT=kTc[:, h // 2, :], rhs=Sbf[d][:, h, :],
                    start=True, stop=True)
            for h in range(4):
                p.I("tensor", "matmul", [N("qTc"), Sb], ["B7"], bk(7)[:, h, :], lhsT=qTc[:, h // 2, :], rhs=Sbf[d][:, h, :],
                    start=True, stop=True)
            rr = dd["kbg"]
            for h in range(4):
                p.I("vector", "scalar_tensor_tensor", [N("sso"), N("vb")], ["B6", N("kbg")], out=rr[:, h, :], in0=bk(6)[:, h, :],
                    scalar=dd["sso"][:, h:h + 1], in1=dd["vb"][:, h, :], op0=ALU.mult, op1=ALU.add)
            for h in range(4):
                p.I("tensor", "matmul", [Pn_, N("kbg")], ["B6"], bk(6)[:, h, :], lhsT=Pc[:, h, :], rhs=rr[:, h, :],
                    start=True, stop=True)
            p.I("vector", "tensor_copy", [], ["B6", N("vn")], out=dd["vn"][:], in_=bk(6))
            p.I("vector", "tensor_tensor", [N("ex")], ["B7", N("tq")], out=dd["tq"][:], in0=bk(7), in1=b4(ex[:, 0:4]), op=ALU.mult)
            for h in range(4):
                p.I("tensor", "matmul", [N("kdec"), N("vn")], ["B7"], bk(7)[:, h, :], lhsT=dd["kdec"][:, h, :], rhs=dd["vn"][:, h, :],
                    start=True, stop=True)
            for h in range(4):
                p.I("tensor", "matmul", [N("XA"), N("vn")], ["B6"], bk(6)[:, h, :], lhsT=XA[:, 4 + h, :], rhs=dd["vn"][:, h, :],
                    start=True, stop=True)
            for h in range(4):
                p.I("vector", "scalar_tensor_tensor", [N("ex")], ["B7", Sn], out=S32[d][:, h, :], in0=S32[d][:, h, :],
                    scalar=ex[:, 8 + h:9 + h], in1=bk(7)[:, h, :], op0=ALU.mult, op1=ALU.add)
            p.I("scalar", "copy", [Sn], [Sb], out=Sbf[d][:], in_=S32[d][:])
            od = dd["od"]
            p.I("vector", "tensor_tensor", [N("tq")], ["B6", N("od")], out=od[:], in0=bk(6), in1=dd["tq"][:], op=ALU.add)
            if not second:
                p.dma("sync", ost_d[i * 128:(i + 1) * 128, :], od[:].rearrange("p h d -> p (h d)"), reads=[N("od")],
                      writes=["ost%d" % i])
                return
            p.I("gpsimd", "tensor_tensor", [N("od"), N("ol")], [N("od")], out=od[:], in0=od[:], in1=dd["ol"][:], op=ALU.add)
            p.I("scalar", "activation", [N("od")], [N("sqo")], out=dd["sqo"][:], in_=od[:], func=AF.Square)
            p.I("vector", "tensor_reduce", [N("sqo")], [N("sso")], out=dd["sso"][:, :], in_=dd["sqo"][:], axis=AX.X, op=ALU.add)
            p.I("scalar", "activation", [N("sso"), "eps"], [N("sso")], out=dd["sso"][:, :], in_=dd["sso"][:, :], func=AF.Sqrt,
                scale=1.0 / 128, bias=eps[:, :])
            p.I("vector", "reciprocal", [N("sso")], [N("sso")], out=dd["sso"][:, :], in_=dd["sso"][:, :])
            p.I("vector", "tensor_tensor", [N("sso"), N("od")], [N("od")], out=od[:], in0=od[:], in1=b4(dd["sso"][:, :]), op=ALU.mult)
            p.I("gpsimd", "tensor_tensor", [N("od"), "on"], [N("od")], out=od[:], in0=od[:], in1=bh(on[:, :]), op=ALU.mult)
            p.I("gpsimd", "tensor_tensor", [N("od"), N("zc")], [N("ogc")], out=dd["ogc"][:], in0=od[:],
                in1=zc[:, :].rearrange("p (h d) -> p h d", h=4), op=ALU.mult)
            outs.append(p.dma("sync", og_d[i * 128:(i + 1) * 128, :], dd["ogc"][:].rearrange("p h d -> p (h d)"), reads=[N("ogc")]))

        pend = {}
        pend[0, 0] = pre(0, 0, 0, False)
        pend[1, nchunks - 1] = pre(1, nchunks - 1, 0, False)
        for t in range(nchunks):
            par = t % 2
            if t + 1 < nchunks:
                sec1 = (t + 1) >= nchunks // 2
                pend[0, t + 1] = pre(0, t + 1, 1 - par, sec1)
                pend[1, nchunks - 2 - t] = pre(1, nchunks - 2 - t, 1 - par, sec1)
            second = t >= nchunks // 2
            scan(0, t, par, second, *pend.pop((0, t)))
            scan(1, nchunks - 1 - t, par, second, *pend.pop((1, nchunks - 1 - t)))
        p.emit(final_wait_ops=outs)
    return nc, prog


def build_gdn_out_program():
    nc = bass.Bass("TRN2", target_bir_lowering=False)
    og_d = nc.dram_tensor("og", [TOK, 2048], BF16, kind="ExternalInput").ap()
    h_d = nc.dram_tensor("h", [TOK, D_MODEL], F32, kind="ExternalInput").ap()
    w_d = nc.dram_tensor("w_out", [2048, D_MODEL], F32, kind="ExternalInput").ap()
    out_d = nc.dram_tensor("out", [TOK, D_MODEL], F32, kind="ExternalOutput").ap()
    prog = Prog(nc)
    p = prog
    outs = []
    with contextlib.ExitStack() as st:
        c = Ctx(nc, prog, st)
        ident = make_ident(c)
        p.fence()
        w = c.sb("w", [128, 16, D_MODEL], BF16)
        wv = w_d.rearrange("(k p) c -> p k c", p=128)
        for k in range(0, 16, 4):
            p.dma("gpsimd", w[:, k:k + 4, :], wv[:, k:k + 4, :], writes=["w"])
        ogt = [c.sb("ogt%d" % i, [128, 2048], BF16) for i in range(2)]
        ht = [c.sb("ht%d" % i, [128, D_MODEL], F32) for i in range(2)]
        ho = [c.sb("ho%d" % i, [128, D_MODEL], F32) for i in range(2)]
        ogT = [c.sb("ogT%d" % i, [128, 16, 128], BF16) for i in range(2)]
        TP = c.ps("TP", [128, 16, 128], BF16)
        AC = c.ps("AC", [128, 1024], F32)
        for t in range(16):
            b = t % 2
            p.dma("sync", ogt[b][:], og_d[t * 128:(t + 1) * 128, :], writes=["ogt%d" % b])
            p.dma("sync", ht[b][:], h_d[t * 128:(t + 1) * 128, :], writes=["ht%d" % b])
            for k in range(16):
                p.I("tensor", "transpose", ["ogt%d" % b], ["TP"], out=TP[:, k, :], in_=ogt[b][:, k * 128:(k + 1) * 128],
                    identity=ident[:, :])
            p.I("vector", "tensor_copy", [], ["TP", "ogT%d" % b], out=ogT[b][:], in_=TP[:])
            for hf in range(2):
                for k in range(16):
                    p.I("tensor", "matmul", ["ogT%d" % b, "w"], ["AC"], AC[:, hf * 512:(hf + 1) * 512], lhsT=ogT[b][:, k, :],
                        rhs=w[:, k, hf * 512:(hf + 1) * 512], start=(k == 0), stop=(k == 15))
            p.I("vector", "tensor_tensor", ["ht%d" % b], ["AC", "ho%d" % b], out=ho[b][:], in0=AC[:, :], in1=ht[b][:], op=ALU.add)
            outs.append(p.dma("sync", out_d[t * 128:(t + 1) * 128, :], ho[b][:], reads=["ho%d" % b]))
        p.emit(final_wait_ops=outs)
    return nc, prog


_CACHE = {}


def _prog(name, builder, *a):
    key = (name,) + a
    if key not in _CACHE:
        _CACHE[key] = builder(*a)[0]
    return _CACHE[key]


def _rep(v, n=128):
    return np.ascontiguousarray(np.broadcast_to(np.asarray(v, np.float32)[None], (n,) + tuple(np.shape(v))))


def _halo(hb, c, pad):
    b, j = c // 4, c % 4
    z = np.zeros((pad, hb.shape[-1]), hb.dtype)
    ext = np.concatenate([z, hb[b], z], 0)
    return np.ascontiguousarray(ext[j * TOK:j * TOK + TOK + 2 * pad])


def _run(nc, maps):
    return run_bass_kernel_spmd(nc, maps, core_ids=list(range(NCORES))).results


def _ffn_layer(h, g, gf, wup, cwt, cb, wdn, final):
    nc = _prog("ffn", build_ffn_program, final)
    cw = np.concatenate([cwt, cb[None]], 0)
    cw_l = np.ascontiguousarray(cw.reshape(4, 44, 128).transpose(2, 1, 0))
    maps = []
    for c in range(NCORES):
        hh = _halo(h, c, 1)
        maps.append(dict(h=hh, g_ffn=_rep(g), g_fin=_rep(gf), w_up=wup, w_down=wdn, cw=cw_l))
    res = _run(nc, maps)
    return np.stack([np.concatenate([res[b * 4 + j]["out"] for j in range(4)], 0) for b in range(BATCH)], 0)


def kernel(x, norm_mix, norm_ffn, norm_final, attn_w_in, attn_q_norm, attn_k_norm, attn_w_out, gdn_w_in,
           gdn_conv_w, gdn_conv_b, gdn_a_log, gdn_dt_bias, gdn_o_norm, gdn_w_out, ffn_w_up, ffn_conv_w,
           ffn_conv_b, ffn_w_down):
    f32 = lambda a: np.ascontiguousarray(np.asarray(a, dtype=np.float32))
    x = f32(x)
    rope = rope_tables_np()
    nc = _prog("ap", build_attn_proj_program)
    maps = []
    for c in range(NCORES):
        b, j = c // 4, c % 4
        maps.append(dict(x=np.ascontiguousarray(x[b, j * TOK:(j + 1) * TOK]), g_mix=_rep(norm_mix[0]), w_in=f32(attn_w_in[0]),
                         qg=_rep(attn_q_norm[0]), kg=_rep(attn_k_norm[0]), rope=np.ascontiguousarray(rope[j * TOK:(j + 1) * TOK])))
    r1 = _run(nc, maps)
    nc = _prog("ac", build_attn_core_program)
    maps = []
    for c in range(NCORES):
        b, j = c // 4, c % 4
        kT = np.ascontiguousarray(np.concatenate([r1[b * 4 + i]["kT"] for i in range(4)], axis=2))
        v = np.ascontiguousarray(np.concatenate([r1[b * 4 + i]["v"] for i in range(4)], axis=1))
        maps.append(dict(qT=r1[c]["qT"], kT=kT, v=v, gate=r1[c]["gate"], x=np.ascontiguousarray(x[b, j * TOK:(j + 1) * TOK]),
                         w_out=f32(attn_w_out[0]), qg=_rep(attn_q_norm[0]), kg=_rep(attn_k_norm[0])))
    r2 = _run(nc, maps)
    h = np.stack([np.concatenate([r2[b * 4 + j]["out"] for j in range(4)], 0) for b in range(BATCH)], 0)
    h = _ffn_layer(h, f32(norm_ffn[0]), f32(norm_final), f32(ffn_w_up[0]), f32(ffn_conv_w[0]), f32(ffn_conv_b[0]),
                   f32(ffn_w_down[0]), False)
    nc = _prog("gp", build_gdn_proj_program)
    cwf = np.concatenate([f32(gdn_conv_w[0]), f32(gdn_conv_b[0])[None]], 0)
    cw_l = np.ascontiguousarray(cwf.reshape(6, 32, 128).transpose(2, 1, 0))
    ad = np.stack([f32(gdn_a_log[0]).reshape(32), f32(gdn_dt_bias[0]).reshape(32)], 0)
    maps = []
    for c in range(NCORES):
        maps.append(dict(h=_halo(h, c, 2), g_mix=_rep(norm_mix[1]), w_in=f32(gdn_w_in[0]), cw=cw_l, ad=_rep(ad)))
    r3 = _run(nc, maps)
    nc = _prog("gs", build_gdn_scan_program, 64)
    maps = []
    for c in range(NCORES):
        b, j = c // 4, c % 4
        grp = [r3[b * 4 + i] for i in range(4)]
        qT = np.ascontiguousarray(np.concatenate([g_["qT"][:, 2 * j:2 * j + 2] for g_ in grp], axis=2))
        kT = np.ascontiguousarray(np.concatenate([g_["kT"][:, 2 * j:2 * j + 2] for g_ in grp], axis=2))
        ktm = np.ascontiguousarray(np.concatenate([g_["k_tm"][:, 256 * j:256 * (j + 1)] for g_ in grp], axis=0))
        vtm = np.ascontiguousarray(np.concatenate([g_["v_tm"][:, 512 * j:512 * (j + 1)] for g_ in grp], axis=0))
        zz = np.ascontiguousarray(np.concatenate([g_["z"][:, 512 * j:512 * (j + 1)] for g_ in grp], axis=0))
        gball = np.concatenate([g_["gb"] for g_ in grp], axis=0)
        hs = slice(4 * j, 4 * j + 4)
        gb = np.ascontiguousarray(np.concatenate([gball[:, 0:16][:, hs], gball[:, 16:32][:, hs],
                                                  gball[:, 32:48][:, hs], gball[:, 48:64][:, hs]], axis=1))
        maps.append(dict(qT=qT, kT=kT, k_tm=ktm, v_tm=vtm, z=zz, gb=gb, on=_rep(gdn_o_norm[0]), bm=block_masks_np()))
    r4 = _run(nc, maps)
    nc = _prog("go", build_gdn_out_program)
    maps = []
    for c in range(NCORES):
        b, j = c // 4, c % 4
        og = np.ascontiguousarray(np.concatenate([r4[b * 4 + i]["og"][j * TOK:(j + 1) * TOK] for i in range(4)], axis=1))
        maps.append(dict(og=og, h=np.ascontiguousarray(h[b, j * TOK:(j + 1) * TOK]), w_out=f32(gdn_w_out[0])))
    r5 = _run(nc, maps)
    h = np.stack([np.concatenate([r5[b * 4 + j]["out"] for j in range(4)], 0) for b in range(BATCH)], 0)
    out = _ffn_layer(h, f32(norm_ffn[1]), f32(norm_final), f32(ffn_w_up[1]), f32(ffn_conv_w[1]), f32(ffn_conv_b[1]),
                     f32(ffn_w_down[1]), True)
    return out.astype(np.float32)
```
